# Optimizing a Trainium2 kernel written in Bass

```python
import jax, jax.numpy as jnp
from jax import lax
import numpy as np

D_MODEL = 1024
BATCH = 16
SEQ = 256
DEPTH = 2
DEC_BATCH = 2
DEC_SEQ = 2048
PAST_LEN = 512

GRID_W = 64
N_EVEN = (DEPTH + 1) // 2
N_ODD = DEPTH // 2
N_DIR = 2
CONV_W = D_MODEL
CONV_TAPS = 3
WKV_W = D_MODEL
WKV_HEAD_DIM = 64
WKV_HEADS = WKV_W // WKV_HEAD_DIM
DECAY_RANK = 64
ICLR_RANK = 64
GN_EPS = 64e-5
GLA_HEADS = 4
GLA_DK_TOTAL = D_MODEL // 2
GLA_DV_TOTAL = D_MODEL
GLA_DK = GLA_DK_TOTAL // GLA_HEADS
GLA_DV = GLA_DV_TOTAL // GLA_HEADS
GLA_GATE_RANK = 16
GLA_GATE_NORM = 16.0
GLA_CHUNK = 64
EVEN_IN = 4 * CONV_W + 4 * WKV_W
ODD_IN = 2 * GLA_DK_TOTAL + 2 * GLA_DV_TOTAL
EVEN_SPLITS = [CONV_W, 2 * CONV_W, 3 * CONV_W, 4 * CONV_W, 4 * CONV_W + WKV_W, 4 * CONV_W + 2 * WKV_W, 4 * CONV_W + 3 * WKV_W]
ODD_SPLITS = [GLA_DK_TOTAL, 2 * GLA_DK_TOTAL, 2 * GLA_DK_TOTAL + GLA_DV_TOTAL]
NORM_EPS = 1e-6

kernel_name = 'bidir_conv_rwkv7_gla_diffusion_step'


def _rmsnorm(x, g):
    xf = x.astype(jnp.float32)
    y = xf * lax.rsqrt(jnp.mean(xf * xf, axis=-1, keepdims=True) + NORM_EPS)
    return (y * g.astype(jnp.float32)).astype(x.dtype)


def _short_conv(u, w, row_len):
    b, t, ch = u.shape
    ug = u.reshape(b, t // row_len, row_len, ch)
    up = jnp.pad(ug, ((0, 0), (0, 0), (1, 1), (0, 0)))
    y = w[0] * up[:, :, :-2] + w[1] * up[:, :, 1:-1] + w[2] * up[:, :, 2:]
    return y.reshape(b, t, ch)


def _orient(xf, xb):
    return jnp.stack([xf, jnp.flip(xb, axis=1)], axis=0)


def _unorient(o):
    return o[0] + jnp.flip(o[1], axis=1)


def _wkv7_scan(r, w, k, kk, a, v, s0):
    def step(s, inp):
        r_t, w_t, k_t, kk_t, a_t, v_t = inp
        s_kk = jnp.einsum('zbhvk,zbhk->zbhv', s, kk_t)
        s = (s * w_t[..., None, :] - s_kk[..., :, None] * (kk_t * a_t)[..., None, :]
             + v_t[..., :, None] * k_t[..., None, :])
        return s, jnp.einsum('zbhvk,zbhk->zbhv', s, r_t)
    xs = tuple(jnp.moveaxis(z, 2, 0) for z in (r, w, k, kk, a, v))
    s_fin, o = lax.scan(step, s0, xs)
    return jnp.moveaxis(o, 0, 2), s_fin


def _gla_chunk_scan(q, k, v, g, s0):
    z, b, t, h, _ = q.shape
    n = t // GLA_CHUNK

    def chunks(x):
        return x.reshape(z, b, n, GLA_CHUNK, h, x.shape[-1]).transpose(2, 0, 1, 4, 3, 5)

    mask = jnp.tril(jnp.ones((GLA_CHUNK, GLA_CHUNK), dtype=bool))

    def step(s, inp):
        q_c, k_c, v_c, g_c = inp
        bc = jnp.cumsum(g_c, axis=-2)
        b_last = bc[..., -1:, :]
        qe = q_c * jnp.exp(bc)
        ke = k_c * jnp.exp(-bc)
        att = jnp.where(mask, jnp.einsum('zbhid,zbhjd->zbhij', qe, ke), 0.0)
        o = jnp.einsum('zbhij,zbhje->zbhie', att, v_c) + jnp.einsum('zbhid,zbhde->zbhie', qe, s)
        s = (jnp.exp(b_last)[..., 0, :, None] * s
             + jnp.einsum('zbhjd,zbhje->zbhde', k_c * jnp.exp(b_last - bc), v_c))
        return s, o

    s_fin, o = lax.scan(step, s0, (chunks(q), chunks(k), chunks(v), chunks(g)))
    o = o.transpose(1, 2, 0, 4, 3, 5).reshape(z, b, t, h, v.shape[-1])
    return o, s_fin


def _mixer_conv_wkv(h, s0, row_len, w_in, w_out, conv_w, w0, w1, w2, a0, a1, a2,
                    k_k, k_a, r_k, ln_w, ln_b):
    f32 = jnp.float32
    b, t, _ = h.shape
    heads = lambda z: z.reshape(z.shape[:-1] + (WKV_HEADS, WKV_HEAD_DIM))
    u, gb, gc, zc, r, k, v, zw = jnp.split(h @ w_in, EVEN_SPLITS, axis=-1)
    o_conv = jax.nn.silu(zc) * gb * _short_conv(gc * u, conv_w, row_len)
    hf = h.astype(f32)
    w_raw = w0[:, None, None, :] + jnp.einsum('zbtr,zrc->zbtc', jnp.tanh(jnp.einsum('btd,zdr->zbtr', hf, w1)), w2)
    decay = jnp.exp(-jnp.exp(-jax.nn.softplus(-w_raw) - 0.5))
    iclr = jax.nn.sigmoid(a0[:, None, None, :] + jnp.einsum('zbtr,zrc->zbtc', jnp.einsum('btd,zdr->zbtr', hf, a1), a2))
    rf, kf, vf = r.astype(f32), k.astype(f32), v.astype(f32)
    kk = heads(kf * k_k)
    kk = kk / jnp.maximum(jnp.sqrt(jnp.sum(kk * kk, axis=-1, keepdims=True)), 1e-12)
    k_mod = heads(kf[None] * (1.0 + (iclr - 1.0) * k_a))
    decay_h, iclr_h = heads(decay), heads(iclr)
    rh, vh = heads(rf), heads(vf)
    o, s_fin = _wkv7_scan(_orient(rh, rh), _orient(decay_h[0], decay_h[1]), _orient(k_mod[0], k_mod[1]),
                          _orient(kk, kk), _orient(iclr_h[0], iclr_h[1]), _orient(vh, vh), s0.astype(f32))
    o = _unorient(o)
    mu = jnp.mean(o, axis=-1, keepdims=True)
    var = jnp.mean(jnp.square(o - mu), axis=-1, keepdims=True)
    gn = ((o - mu) * lax.rsqrt(var + GN_EPS)).reshape(b, t, WKV_W) * ln_w + ln_b
    bonus = jnp.sum(rh[None] * k_mod * r_k, axis=(0, -1))[..., None] * vh
    o_wkv = (gn + bonus.reshape(b, t, WKV_W)) * jax.nn.silu(zw.astype(f32))
    y = jnp.concatenate([o_conv, o_wkv.astype(h.dtype)], axis=-1) @ w_out
    return y, s_fin


def _mixer_gla(h, s0, w_in, w_out, gk1, gk2, gk_b, g_norm):
    f32 = jnp.float32
    b, t, _ = h.shape
    kh = lambda z: z.reshape(z.shape[:-1] + (GLA_HEADS, GLA_DK))
    vhd = lambda z: z.reshape(z.shape[:-1] + (GLA_HEADS, GLA_DV))
    q, k, v, zg = jnp.split(h @ w_in, ODD_SPLITS, axis=-1)
    hf = h.astype(f32)
    g = jax.nn.log_sigmoid(jnp.einsum('zbtr,zrc->zbtc', jnp.einsum('btd,zdr->zbtr', hf, gk1), gk2)
                           + gk_b[:, None, None, :]) / GLA_GATE_NORM
    g = kh(g)
    qh = kh(q.astype(f32) * GLA_DK ** -0.5)
    khh = kh(k.astype(f32))
    vh = vhd(v.astype(f32))
    o, s_fin = _gla_chunk_scan(_orient(qh, qh), _orient(khh, khh), _orient(vh, vh),
                               _orient(g[0], g[1]), s0.astype(f32))
    o = _unorient(o)
    o = o * lax.rsqrt(jnp.mean(o * o, axis=-1, keepdims=True) + NORM_EPS) * g_norm
    o = o.reshape(b, t, GLA_DV_TOTAL) * jax.nn.silu(zg.astype(f32))
    return o.astype(h.dtype) @ w_out, s_fin


def _trunk(x, cvec, s_wkv, s_gla, row_len, norm_g, ada_w, ada_b, final_g, even, odd):
    f32 = jnp.float32
    new_wkv, new_gla = [], []
    cf = jax.nn.silu(cvec.astype(f32))
    for l in range(DEPTH):
        mod = cf @ ada_w[l].astype(f32) + ada_b[l].astype(f32)
        shift, scale, gate = jnp.split(mod[:, None, :], 3, axis=-1)
        h = (_rmsnorm(x, norm_g[l]).astype(f32) * (1.0 + scale) + shift).astype(x.dtype)
        i = l // 2
        if l % 2 == 0:
            y, s = _mixer_conv_wkv(h, jnp.swapaxes(s_wkv[:, i], 0, 1), row_len, *[p[i] for p in even])
            new_wkv.append(jnp.swapaxes(s, 0, 1))
        else:
            y, s = _mixer_gla(h, jnp.swapaxes(s_gla[:, i], 0, 1), *[p[i] for p in odd])
            new_gla.append(jnp.swapaxes(s, 0, 1))
        x = (x.astype(f32) + gate * y.astype(f32)).astype(x.dtype)
    return _rmsnorm(x, final_g), jnp.stack(new_wkv, axis=1), jnp.stack(new_gla, axis=1)


def setup_inputs(seed: int = 0) -> dict:
    key = jax.random.key(seed)
    ks = iter(jax.random.split(key, 40))
    f32 = jnp.float32
    nrm = lambda shape, s: jax.random.normal(next(ks), shape, f32) * s
    D = D_MODEL
    return {
        'x_prompt': nrm((BATCH, SEQ, D), 1.0),
        'x_sample': nrm((DEC_BATCH, DEC_SEQ, D), 1.0),
        'state_wkv': nrm((DEC_BATCH, N_EVEN, N_DIR, WKV_HEADS, WKV_HEAD_DIM, WKV_HEAD_DIM), 0.3),
        'state_gla': nrm((DEC_BATCH, N_ODD, N_DIR, GLA_HEADS, GLA_DK, GLA_DV), 0.3),
        'c': nrm((DEC_BATCH, D), 1.0),
        'c_ctx': nrm((D,), 1.0),
        'norm_g': 1.0 + nrm((DEPTH, D), 0.02),
        'ada_w': nrm((DEPTH, D, 3 * D), 0.5 * D ** -0.5),
        'ada_b': nrm((DEPTH, 3 * D), 0.01),
        'final_g': 1.0 + nrm((D,), 0.02),
        'e_w_in': nrm((N_EVEN, D, EVEN_IN), D ** -0.5),
        'e_w_out': nrm((N_EVEN, CONV_W + WKV_W, D), (CONV_W + WKV_W) ** -0.5),
        'conv_w': nrm((N_EVEN, CONV_TAPS, CONV_W), 0.5),
        'wkv_w0': jax.random.uniform(next(ks), (N_EVEN, N_DIR, WKV_W), f32, -6.0, 1.0),
        'wkv_w1': nrm((N_EVEN, N_DIR, D, DECAY_RANK), D ** -0.5),
        'wkv_w2': nrm((N_EVEN, N_DIR, DECAY_RANK, WKV_W), 0.1 * DECAY_RANK ** -0.5),
        'wkv_a0': nrm((N_EVEN, N_DIR, WKV_W), 0.5),
        'wkv_a1': nrm((N_EVEN, N_DIR, D, ICLR_RANK), D ** -0.5),
        'wkv_a2': nrm((N_EVEN, N_DIR, ICLR_RANK, WKV_W), 0.1 * ICLR_RANK ** -0.5),
        'wkv_k_k': 0.85 + nrm((N_EVEN, WKV_W), 0.1),
        'wkv_k_a': 1.0 + nrm((N_EVEN, WKV_W), 0.1),
        'wkv_r_k': nrm((N_EVEN, WKV_HEADS, WKV_HEAD_DIM), 0.1),
        'wkv_ln_w': 1.0 + nrm((N_EVEN, WKV_W), 0.02),
        'wkv_ln_b': nrm((N_EVEN, WKV_W), 0.01),
        'o_w_in': nrm((N_ODD, D, ODD_IN), D ** -0.5),
        'o_w_out': nrm((N_ODD, GLA_DV_TOTAL, D), GLA_DV_TOTAL ** -0.5),
        'gla_gk1': nrm((N_ODD, N_DIR, D, GLA_GATE_RANK), D ** -0.5),
        'gla_gk2': nrm((N_ODD, N_DIR, GLA_GATE_RANK, GLA_DK_TOTAL), GLA_GATE_RANK ** -0.5),
        'gla_gk_b': nrm((N_ODD, N_DIR, GLA_DK_TOTAL), 0.5),
        'gla_g_norm': 1.0 + nrm((N_ODD, GLA_DV), 0.02),
    }


def reference(x_prompt, x_sample, state_wkv, state_gla, c, c_ctx, norm_g, ada_w, ada_b, final_g,
              e_w_in, e_w_out, conv_w, wkv_w0, wkv_w1, wkv_w2, wkv_a0, wkv_a1, wkv_a2,
              wkv_k_k, wkv_k_a, wkv_r_k, wkv_ln_w, wkv_ln_b,
              o_w_in, o_w_out, gla_gk1, gla_gk2, gla_gk_b, gla_g_norm):
    even = (e_w_in, e_w_out, conv_w, wkv_w0, wkv_w1, wkv_w2, wkv_a0, wkv_a1, wkv_a2,
            wkv_k_k, wkv_k_a, wkv_r_k, wkv_ln_w, wkv_ln_b)
    odd = (o_w_in, o_w_out, gla_gk1, gla_gk2, gla_gk_b, gla_g_norm)
    n_ctx_req, ctx_len = x_prompt.shape[0], x_prompt.shape[1]
    zero_wkv = jnp.zeros((n_ctx_req,) + state_wkv.shape[1:], jnp.float32)
    zero_gla = jnp.zeros((n_ctx_req,) + state_gla.shape[1:], jnp.float32)
    y_prompt, new_state_wkv, new_state_gla = _trunk(x_prompt, c_ctx[None, :], zero_wkv, zero_gla, ctx_len,
                                                    norm_g, ada_w, ada_b, final_g, even, odd)
    y_sample, _, _ = _trunk(x_sample, c, state_wkv, state_gla, GRID_W,
                            norm_g, ada_w, ada_b, final_g, even, odd)
    return (y_prompt, y_sample, new_state_wkv, new_state_gla)
```

```python
import contextlib
import numpy as np
import ml_dtypes
import concourse.bass as bass
import concourse.mybir as mybir
from concourse.bass_utils import run_bass_kernel_spmd

ACT = mybir.ActivationFunctionType
ALU = mybir.AluOpType
F32 = mybir.dt.float32
BF16 = mybir.dt.bfloat16
AX = mybir.AxisListType

ENGS = ['pe', 'act', 'dve', 'pool', 'sp']
EPOCH = 4000
NDS = 8
NT = 2048
NCH = 16
LAM = 0.6065306597126334
NORM_EPS = 1e-6
GN_EPS = 64e-5


class Prog:
    def __init__(self, nc):
        self.nc = nc
        self.ops = {e: [] for e in ENGS}
        self.count = {e: 0 for e in ENGS}
        self.dcount = {e: 0 for e in ENGS}
        self.last_w = {}
        self.readers = {}
        self.waited = {e: {} for e in ENGS}
        self.pending = {e: [] for e in ENGS}

    def _deps(self, eng, reads, writes):
        deps = set()
        for r in reads:
            if r in self.last_w:
                deps.add(self.last_w[r])
        for w in writes:
            if w in self.last_w:
                deps.add(self.last_w[w])
            for rd in self.readers.get(w, ()):
                deps.add(rd)
        best = {}
        for d in deps:
            if eng == 'pe' and d[:-1] == ('e', 'pe'):
                continue
            best[d[:-1]] = max(best.get(d[:-1], 0), d[-1])
        for d in self.pending[eng]:
            best[d[:-1]] = max(best.get(d[:-1], 0), d[-1])
        self.pending[eng] = []
        final = []
        for key, i in best.items():
            if self.waited[eng].get(key, 0) < i:
                self.waited[eng][key] = i
                final.append(key + (i,))
        return final

    def _mark(self, tok, reads, writes):
        for r in reads:
            self.readers.setdefault(r, []).append(tok)
        for w in writes:
            self.last_w[w] = tok
            self.readers[w] = []

    def op(self, eng, fn, reads=(), writes=()):
        writes = list(writes) + [r for r in reads if isinstance(r, tuple) and r[0] == 'ps']
        waits = self._deps(eng, reads, writes)
        idx = self.count[eng] + 1
        self.count[eng] = idx
        self.ops[eng].append(('c', fn, waits, idx))
        self._mark(('e', eng, idx), reads, writes)

    def dma(self, eng, fn, reads=(), writes=()):
        waits = self._deps(eng, reads, writes)
        j = self.dcount[eng]
        self.dcount[eng] = j + 1
        slot = j % NDS
        if j >= NDS:
            key = ('d', eng, slot)
            need = j // NDS
            if self.waited[eng].get(key, 0) < need:
                self.waited[eng][key] = need
                waits.append(key + (need,))
        self.ops[eng].append(('d', fn, waits, (slot, j // NDS + 1)))
        self._mark(('d', eng, slot, j // NDS + 1), reads, writes)

    def barrier(self):
        snap = [('e', e, self.count[e]) for e in ENGS if self.count[e]]
        for q in ENGS:
            n = self.dcount[q]
            for slot in range(min(n, NDS)):
                snap.append(('d', q, slot, (n - 1 - slot) // NDS + 1))
        for e in ENGS:
            self.pending[e] = list(snap)

    def finish_waits(self, eng='sp'):
        waits = []
        for q in ENGS:
            n = self.dcount[q]
            for slot in range(min(n, NDS)):
                waits.append(('d', q, slot, (n - 1 - slot) // NDS + 1))
        self.ops[eng].append(('w', None, waits, None))

    def emit(self):
        nc = self.nc
        with contextlib.ExitStack() as st:
            esem = {e: [st.enter_context(nc.semaphore(f"s_{e}_{k}")) for k in range(self.count[e] // EPOCH + 1)]
                    for e in ENGS}
            dsem = {e: [st.enter_context(nc.semaphore(f"d_{e}_{k}")) for k in range(NDS)]
                    for e in ENGS if self.dcount[e]}
            block = st.enter_context(nc.Block())

            def run(handle, e):
                for kind, fn, waits, info in self.ops[e]:
                    for w in waits:
                        if w[0] == 'e':
                            handle.wait_ge(esem[w[1]][(w[2] - 1) // EPOCH], (w[2] - 1) % EPOCH + 1)
                        else:
                            handle.wait_ge(dsem[w[1]][w[2]], 16 * w[3])
                    if kind == 'c':
                        fn(handle).then_inc(esem[e][(info - 1) // EPOCH], 1)
                    elif kind == 'd':
                        fn(handle).then_inc(dsem[e][info[0]], 16)

            @block.tensor
            def _(h):
                run(h, 'pe')

            @block.scalar
            def _(h):
                run(h, 'act')

            @block.vector
            def _(h):
                run(h, 'dve')

            @block.gpsimd
            def _(h):
                run(h, 'pool')

            @block.sync
            def _(h):
                run(h, 'sp')


V_NG, V_FG, V_CW, V_W0, V_A0, V_KK, V_KA, V_RK, V_LW, V_LB, V_AB, V_GB, V_GN, V_CV = \
    0, 16, 24, 48, 64, 80, 88, 96, 104, 112, 120, 168, 176, 178
NV = 186
C_ID, C_BD, C_HM, C_ONE = 0, 128, 256, 258
NCF = 386
B_RST, B_MAB, B_MN, B_MG = 0, 512, 1536, 2048
NCB = 2304


def _col(v):
    v = np.asarray(v, np.float32).reshape(-1, 128)
    return np.ascontiguousarray(v.T)


def _consts():
    u = np.arange(128)[:, None]
    t = np.arange(128)[None, :]
    LT, LE, GT, GE = (u < t), (u <= t), (u > t), (u >= t)
    cf = np.zeros((128, NCF), np.float32)
    cf[:, C_ID:C_ID + 128] = np.eye(128)
    cf[:, C_BD:C_BD + 128] = (u // 64 == t // 64)
    cf[:, C_HM] = (np.arange(128) < 64)
    cf[:, C_HM + 1] = (np.arange(128) >= 64)
    cf[:, C_ONE:C_ONE + 128] = 1.0
    cb = np.zeros((128, NCB), np.float32)
    rst = np.ones(512, np.float32)
    rst[::128] = 0
    cb[:, B_RST:B_RST + 512] = rst[None]
    cb[:, B_MAB:B_MAB + 512] = np.concatenate([LT, LE, LT, LE], 1)
    cb[:, B_MAB + 512:B_MAB + 1024] = np.concatenate([GT, GE, GT, GE], 1)
    cb[:, B_MN:B_MN + 256] = np.concatenate([GT, GT], 1)
    cb[:, B_MN + 256:B_MN + 512] = np.concatenate([LT, LT], 1)
    cb[:, B_MG:B_MG + 128] = LE
    cb[:, B_MG + 128:B_MG + 256] = GE
    return cf, cb


def _assign():
    plan = [[('s', 0)], [('s', 1)]]
    p = 0
    for n in (3, 3, 3, 3, 2, 2):
        plan.append([('p', p + i) for i in range(n)])
        p += n
    return plan


def build(stop_after=99):
    nc = bass.Bass("TRN2", target_bir_lowering=False)
    dt_in = lambda n, s: nc.dram_tensor(n, s, F32, kind="ExternalInput").ap()
    dt_out = lambda n, s: nc.dram_tensor(n, s, F32, kind="ExternalOutput").ap()
    d_x = dt_in("xT", [128, 8, NT])
    d_vec = dt_in("vec", [128, NV])
    d_cf = dt_in("cf", [128, NCF])
    d_cb = nc.dram_tensor("cb", [128, NCB], BF16, kind="ExternalInput").ap()
    d_msk = dt_in("msk", [128, 4, 32])
    d_ada = dt_in("ada", [2, 128, 8, 3072])
    d_ewin = dt_in("ewin", [128, 8, 8192])
    d_ewout = dt_in("ewout", [128, 16, 1024])
    d_w1 = dt_in("w1c", [128, 8, 256])
    d_w2 = dt_in("w2p", [128, 4, 1024])
    d_sw = dt_in("s_wkv", [128, 8, 2, 128])
    d_owin = dt_in("owin", [128, 8, 3072])
    d_owout = dt_in("owout", [128, 8, 1024])
    d_g1 = dt_in("g1c", [128, 8, 32])
    d_g2 = dt_in("g2p", [32, 2, 512])
    d_sg = dt_in("s_gla", [128, 4, 2, 256])
    o_y = dt_out("yT", [128, 8, NT])
    o_sw = dt_out("o_wkv", [8, 2, 8, 128, 128])
    o_sg = dt_out("o_gla", [8, 2, 4, 128, 256])

    st = contextlib.ExitStack()
    sb = lambda n, s, d=F32: st.enter_context(nc.sbuf_tensor(n, s, d))
    X = sb("X", [128, 8, NT])
    HT = sb("HT", [128, 8, NT], BF16)
    VEC = sb("VEC", [128, NV])
    CF = sb("CF", [128, NCF])
    CB = sb("CB", [128, NCB], BF16)
    IDB = sb("IDB", [128, 128], BF16)
    ONEB = sb("ONEB", [128, 128], BF16)
    BDB = sb("BDB", [128, 128], BF16)
    MSK = sb("MSK", [128, 4, 32])
    MOD = sb("MOD", [128, 2, 24])
    G1 = sb("G1", [128, 2, 8])
    CS_ = sb("CSIL", [128, 8])
    EPS = sb("EPS", [128, 2])
    FA = sb("FA", [128, NT])
    FB = sb("FB", [128, NT])
    BA = [sb(f"BA{i}", [128, NT], BF16) for i in range(4)]
    WRAW = sb("WRAW", [128, 3072])
    WS = WRAW[:, 0:1024].rearrange("p (a b) -> p a b", a=8)
    WB = WRAW[:, 1024:3072].bitcast(BF16).rearrange("p (a b c) -> p a b c", a=4, b=8)
    WOB = WB[:, 3, :, :].rearrange("p a b -> p (a b)")
    UNI = sb("UNI", [128, 8960])
    ub = lambda a, b, p=128: UNI[0:p, a:b].bitcast(BF16)
    T512 = [sb(f"T512_{i}", [128, 512]) for i in range(4)] + [WRAW[:, 512 * i:512 * (i + 1)] for i in range(6)]
    G1B = ub(1024, 1152).rearrange("p (a b) -> p a b", a=8)
    G2B = ub(1152, 1664, 32).rearrange("p (a b) -> p a b", a=2)
    GT1 = ub(0, 1024, 32)
    VTG = sb("VTG", [128, 16, 256], BF16)
    KKF = sb("KKF", [128, NT], BF16)
    GS = sb("GS", [128, 3, 256])
    WLG = sb("WLG", [128, 2, 16])
    NGB = sb("NGB", [128, 8])
    RSG = sb("RSG", [128, 16])
    PRB = ub(0, 2560)
    CHB = ub(2560, 6080)
    BKT = ub(6080, 7104).rearrange("p (a b) -> p a b", a=4)
    PRS = ub(7104, 8128)
    W2S = WRAW[:, 0:512].rearrange("p (a b) -> p a b", a=4)
    W2B = ub(8128, 8384).rearrange("p (a b) -> p a b", a=4)
    W_XAM = 8384
    WLW = sb("WLW", [128, 2, 16])
    OMKA = sb("OMKA", [128, 8])
    GNS = sb("GNS", [128, 8])
    WOT = [T512[2], T512[3]]
    PS = [st.enter_context(nc.psum_tensor(f"ps{i}", [128, 512], F32)) for i in range(8)]

    P = Prog(nc)
    cnt = {'rr': 0}

    def rr(engs=('act', 'dve')):
        cnt['rr'] += 1
        return engs[cnt['rr'] % len(engs)]

    def mm(out, lhsT, rhs, start, stop, r, w):
        P.op('pe', lambda e: e.matmul(out, lhsT, rhs, start=start, stop=stop), reads=r, writes=w)

    def copy(eng, out, in_, r, w):
        if eng == 'act':
            P.op('act', lambda e: e.activation(out=out, in_=in_, func=ACT.Copy), reads=r, writes=w)
        else:
            P.op(eng, lambda e: e.tensor_copy(out=out, in_=in_), reads=r, writes=w)

    def tt(eng, out, a, b, op, r, w):
        P.op(eng, lambda e: e.tensor_tensor(out=out, in0=a, in1=b, op=op), reads=r, writes=w)

    def ts(eng, out, a, s1, s2, op0, op1, r, w):
        if s2 is None:
            P.op(eng, lambda e: e.tensor_scalar(out=out, in0=a, scalar1=s1, scalar2=None, op0=op0), reads=r, writes=w)
        else:
            P.op(eng, lambda e: e.tensor_scalar(out=out, in0=a, scalar1=s1, scalar2=s2, op0=op0, op1=op1), reads=r, writes=w)

    def stt(out, a, s, b, op0, op1, r, w):
        P.op('dve', lambda e: e.scalar_tensor_tensor(out=out, in0=a, scalar=s, in1=b, op0=op0, op1=op1), reads=r, writes=w)

    def act(out, in_, func, r, w, bias=None, scale=None):
        kw = {}
        if bias is not None:
            kw['bias'] = bias
        if scale is not None:
            kw['scale'] = scale
        P.op('act', lambda e: e.activation(out=out, in_=in_, func=func, **kw), reads=r, writes=w)

    def ld(out, in_, w, r=()):
        P.dma('sp', lambda e: e.dma_start(out=out, in_=in_), reads=r, writes=w)

    for c in range(8):
        ld(X[:, c, :], d_x[:, c, :], [('X', c)])
    ld(VEC[:], d_vec, ['VEC'])
    ld(CF[:], d_cf, ['CF'])
    ld(CB[:], d_cb, ['CB'])
    ld(MSK[:], d_msk, ['MSK'])
    copy('pool', IDB[:], CF[:, C_ID:C_ID + 128], ['CF'], ['IDB'])
    copy('pool', ONEB[:], CF[:, C_ONE:C_ONE + 128], ['CF'], ['ONEB'])
    copy('pool', BDB[:], CF[:, C_BD:C_BD + 128], ['CF'], ['BDB'])
    P.op('pool', lambda e: e.memset(EPS[:, 0:1], NORM_EPS), writes=['EPS'])
    P.op('pool', lambda e: e.memset(EPS[:, 1:2], GN_EPS), writes=['EPS'])
    act(CS_[:], VEC[:, V_CV:V_CV + 8], ACT.Silu, ['VEC'], ['CSIL'])
    def ada_layer(l, ACCQ, an, STG, sn):
        steps = []
        for c in range(8):
            for q in range(3):
                def st_(c=c, q=q, i=len(steps)):
                    sg, sr = STG[i % 4], (sn, i % 4)
                    ld(sg, d_ada[l, :, c, q * 1024:(q + 1) * 1024], [sr])
                    if c == 0:
                        ts('dve', ACCQ[q], sg, CS_[:, c:c + 1], None, ALU.mult, None, [sr, 'CSIL'], [(an, q)])
                    else:
                        stt(ACCQ[q], sg, CS_[:, c:c + 1], ACCQ[q], ALU.mult, ALU.add, [sr, 'CSIL', (an, q)], [(an, q)])
                steps.append(st_)

        def fin():
            for j in range(24):
                mm(PS[0][:, j:j + 1], ACCQ[j // 8][:, (j % 8) * 128:(j % 8 + 1) * 128], CF[:, C_ONE:C_ONE + 1],
                   True, True, [(an, j // 8), 'CF'], [('ps', 0)])
            tt('dve', MOD[:, l, :], PS[0][:, 0:24], VEC[:, V_AB + 24 * l:V_AB + 24 * l + 24], ALU.add,
               [('ps', 0), 'VEC'], ['MOD'])
            ts('dve', G1[:, l, :], MOD[:, l, 8:16], 1.0, None, ALU.add, None, ['MOD'], ['G1'])
            tt('dve', G1[:, l, :], G1[:, l, :], VEC[:, V_NG + 8 * l:V_NG + 8 * l + 8], ALU.mult, ['G1', 'VEC'], ['G1'])
        return steps, fin

    st0, fin0 = ada_layer(0, [FA[:, 0:1024], FA[:, 1024:2048], FB[:, 1024:2048]], 'ACC',
                          [FB[:, 0:1024], WRAW[:, 0:1024], WRAW[:, 1024:2048], WRAW[:, 2048:3072]], 'STG')
    for f_ in st0:
        f_()
    fin0()
    ada1_steps, ada1_fin = ada_layer(1, [UNI[:, 1024 * i:1024 * (i + 1)] for i in range(3)], 'uACC',
                                     [UNI[:, 3072 + 1024 * i:4096 + 1024 * i] for i in range(4)], 'uSTG')

    XR = [('X', c) for c in range(8)]
    HR = [('HT', c) for c in range(8)]

    SQB = [WRAW[:, 0:256].bitcast(BF16), WRAW[:, 1024:1280].bitcast(BF16)]
    SQR = ['WS', ('WB', 0)]

    RSTD = [(T512[2], ('T', 2)), (T512[3], ('T', 3))]

    def sumsq_rstd(tsl, ri=0):
        rb_, rn_ = RSTD[ri]
        for c in range(8):
            act(SQB[c % 2], X[:, c, tsl], ACT.Square, [('X', c)], [SQR[c % 2]])
            mm(PS[1][:], ONEB[:], SQB[c % 2], c == 0, c == 7, [SQR[c % 2], 'ONEB'], [('ps', 1)])
        act(rb_[:], PS[1][:], ACT.Sqrt, [('ps', 1), 'EPS'], [rn_], bias=EPS[:, 0:1], scale=1.0 / 1024)
        P.op('dve', lambda e: e.reciprocal(out=rb_[:], in_=rb_[:]), reads=[rn_], writes=[rn_])

    def norm_mod(gfn, sfn, out_fn, out_res):
        tsls = [slice(t4 * 512, (t4 + 1) * 512) for t4 in range(4)]
        sumsq_rstd(tsls[0], 0)
        for t4 in range(4):
            tsl = tsls[t4]
            if t4 + 1 < 4:
                sumsq_rstd(tsls[t4 + 1], (t4 + 1) % 2)
            rb_, rn_ = RSTD[t4 % 2]
            for c in range(8):
                tmp = T512[c % 2]
                o = out_fn(c, tsl)
                if c % 2 == 0:
                    tt('pool', tmp[:], X[:, c, tsl], rb_[:], ALU.mult, [('X', c), rn_], [('T', 0)])
                    ts('dve', o, tmp[:], gfn(c), sfn(c), ALU.mult, ALU.add, [('T', 0), 'VEC', 'G1', 'MOD'], out_res(c, t4))
                else:
                    tt('dve', tmp[:], X[:, c, tsl], rb_[:], ALU.mult, [('X', c), rn_], [('T', 1)])
                    act(o, tmp[:], ACT.Identity, [('T', 1), 'VEC', 'G1', 'MOD'], out_res(c, t4), bias=sfn(c), scale=gfn(c))

    def load_w(dram, col0, br):
        ld(WS[:], dram[:, :, col0:col0 + 128], ['WS'])
        copy(rr(('act', 'dve')), WB[:, br, :, :], WS[:], ['WS'], [('WB', br)])

    def proj(br, evac, banks=(2, 3, 4, 5)):
        for t4 in range(4):
            tsl = slice(t4 * 512, (t4 + 1) * 512)
            pb = banks[cnt['rr'] % len(banks)]
            cnt['rr'] += 1
            for c in range(8):
                mm(PS[pb][:], WB[:, br, c, :], HT[:, c, tsl], c == 0, c == 7, [('WB', br), ('HT', c)], [('ps', pb)])
            evac(t4, tsl, PS[pb][:], ('ps', pb))

    def wout_partial(l, dram_wout, j, OB, ores):
        ld(WS[:].rearrange("p a b -> p (a b)"), dram_wout[:, j, :], ['WS'])
        copy('pool', WOB[:], WS[:].rearrange("p a b -> p (a b)"), ['WS'], [('WB', 3)])
        for ft in range(8):
            for t4 in range(4):
                tsl = slice(t4 * 512, (t4 + 1) * 512)
                pb = 2 + (cnt['rr'] % 4)
                cnt['rr'] += 1
                mm(PS[pb][:], WOB[:, ft * 128:(ft + 1) * 128], OB[:, tsl], True, True, [('WB', 3), ores], [('ps', pb)])
                if (ft * 4 + t4) % 5 < 3:
                    stt(X[:, ft, tsl], PS[pb][:], MOD[:, l, 16 + ft:17 + ft], X[:, ft, tsl], ALU.mult, ALU.add,
                        [('ps', pb), 'MOD', ('X', ft)], [('X', ft)])
                else:
                    wt = WOT[t4 % 2]
                    act(wt[:], PS[pb][:], ACT.Copy, [('ps', pb), 'MOD'], [('T', 2 + t4 % 2)], scale=MOD[:, l, 16 + ft:17 + ft])
                    tt('pool', X[:, ft, tsl], X[:, ft, tsl], wt[:], ALU.add, [('T', 2 + t4 % 2), ('X', ft)], [('X', ft)])

    P.barrier()
    norm_mod(lambda c: G1[:, 0, c:c + 1], lambda c: MOD[:, 0, c:c + 1], lambda c, tsl: HT[:, c, tsl],
             lambda c, t4: [('HT', c)])

    def v3(t, a, b):
        return t[:].rearrange("p (g w) -> p g w", w=64)[:, a, b]

    for j in range(8):
        for br in range(4):
            load_w(d_ewin, br * 1024 + j * 128, br)
        for f_ in ada1_steps[3 * j:3 * j + 3]:
            f_()
        U, Pm, Y = FA, FB, FA
        proj(0, lambda t4, tsl, ps, pr: copy(rr(), FA[:, tsl], ps, [pr], ['FA']))
        proj(2, lambda t4, tsl, ps, pr: tt('dve', FB[:, tsl], ps, FA[:, tsl], ALU.mult, [pr, 'FA'], ['FB']))
        proj(1, lambda t4, tsl, ps, pr: copy(rr(), BA[0][:, tsl], ps, [pr], ['BA0']))
        proj(3, lambda t4, tsl, ps, pr: act(BA[1][:, tsl], ps, ACT.Silu, [pr], ['BA1']))
        w0, w1, w2 = (VEC[:, V_CW + 8 * k + j:V_CW + 8 * k + j + 1] for k in range(3))
        act(FA[:], FB[:], ACT.Copy, ['FB', 'VEC'], ['FA'], scale=w1)
        g_all, g_lo, g_hi = slice(0, 32), slice(0, 31), slice(1, 32)
        stt(v3(FA, g_all, slice(1, 64)), v3(FB, g_all, slice(0, 63)), w0, v3(FA, g_all, slice(1, 64)),
            ALU.mult, ALU.add, ['FB', 'FA', 'VEC'], ['FA'])
        stt(v3(FA, g_all, slice(0, 63)), v3(FB, g_all, slice(1, 64)), w2, v3(FA, g_all, slice(0, 63)),
            ALU.mult, ALU.add, ['FB', 'FA', 'VEC'], ['FA'])
        tb = T512[0]
        tt('pool', tb[:, 0:31], v3(FB, g_lo, 63), MSK[:, 0, 1:32], ALU.mult, ['FB', 'MSK'], [('T', 0)])
        stt(v3(FA, g_hi, 0), tb[:, 0:31], w0, v3(FA, g_hi, 0), ALU.mult, ALU.add, [('T', 0), 'FA', 'VEC'], ['FA'])
        tt('pool', tb[:, 32:63], v3(FB, g_hi, 0), MSK[:, 1, 1:32], ALU.mult, ['FB', 'MSK'], [('T', 0)])
        stt(v3(FA, g_lo, 63), tb[:, 32:63], w2, v3(FA, g_lo, 63), ALU.mult, ALU.add, [('T', 0), 'FA', 'VEC'], ['FA'])
        tt('pool', FA[:], FA[:], BA[0][:], ALU.mult, ['FA', 'BA0'], ['FA'])
        tt('dve', BA[2][:], FA[:], BA[1][:], ALU.mult, ['FA', 'BA1'], ['BA2'])
        wout_partial(0, d_ewout, j, BA[2], 'BA2')

    ada1_fin()
    P.barrier()
    load_w(d_w1, 0, 0)
    load_w(d_w1, 128, 1)
    proj(0, lambda t4, tsl, ps, pr: act(BA[2][:, tsl], ps, ACT.Tanh, [pr], ['BA2']))
    proj(1, lambda t4, tsl, ps, pr: copy('act', BA[3][:, tsl], ps, [pr], ['BA3']))
    ts('pool', OMKA[:], VEC[:, V_KA:V_KA + 8], -1.0, 1.0, ALU.mult, ALU.add, ['VEC'], ['OMKA'])
    KKf = KKF[:]
    Rb = FA[:, 0:1024].bitcast(BF16)
    Kb = FA[:, 1024:2048].bitcast(BF16)
    TB = T512
    c3 = lambda ap: ap.rearrange("p (k t) -> p k t", t=128)
    FBb = FB[:].bitcast(BF16)
    PRBs = [PRB, FBb]
    XAm = ub(W_XAM, W_XAM + 512)
    def opnd(par):
        base = PRBs[par]
        d = dict(AR=base[:, 0:1024], Bh=base[:, 1024:1536], Kh=base[:, 1536:2048],
                 Btm=[base[:, 2048:2560], base[:, 2560:3072]], Ktm=[base[:, 3072:3584], base[:, 3584:4096]])
        d['Am'] = [base[:, 4096:4608], base[:, 4608:5120]] if par == 0 else [XAm[:, 0:512], XAm[:, 512:1024]]
        d['AR4'] = d['AR'].rearrange("p (k s t) -> p k s t", s=2, t=128)
        return d
    OPN = [opnd(0), opnd(1)]
    NM0 = [CHB[:, 1024 * i:1024 * i + 512] for i in range(3)]
    NM1 = [CHB[:, 1024 * i + 512:1024 * i + 1024] for i in range(3)]
    lv4 = lambda ap: ap.rearrange("p (h s t) -> p h s t", h=2, s=2)
    LVs = [[lv4(CHB[:, 3072 + 1536 * c + 512 * i:3072 + 1536 * c + 512 * (i + 1)]) for i in range(2)] for c in range(2)]
    PPs = [[CHB[:, 4096 + 1536 * c + 256 * i:4096 + 1536 * c + 256 * (i + 1)] for i in range(2)] for c in range(2)]
    TTf = [CHB[:, 6144:6400], CHB[:, 6400:6656]]
    Z0B, UB, SBw = CHB[:, 6656:6784], CHB[:, 6784:6912], CHB[:, 6912:7040]
    BKTs = [BKT[:, 0:2, :], BKT[:, 2:4, :]]
    SFw = GS[:, 0, 0:128]
    BDm = CF[:, C_BD:C_BD + 128]
    HM = [CF[:, C_HM:C_HM + 1], CF[:, C_HM + 1:C_HM + 2]]
    hs = lambda h: slice(h * 64, (h + 1) * 64)

    def interleave(lists):
        lists = [l for l in lists if l]
        pos = [0] * len(lists)
        n = max(len(l) for l in lists) if lists else 0
        for step in range(n):
            for li, l in enumerate(lists):
                tgt = (step + 1) * len(l) // n
                while pos[li] < tgt:
                    l[pos[li]]()
                    pos[li] += 1

    KT = [UNI[:, 512 * i:512 * (i + 1)] for i in range(3)]
    ET = [FB[:, 512 * i:512 * (i + 1)] for i in range(2)]

    def start_rk(jj, banks=(2, 3, 4, 5), kb=1):
        vcol = lambda base: VEC[:, base + jj:base + jj + 1]
        G = []

        def g_load():
            load_w(d_ewin, 4096 + 0 * 1024 + jj * 128, 0)
            load_w(d_ewin, 4096 + 1 * 1024 + jj * 128, 1)
            ld(W2S[:], d_w2[:, :, jj * 128:(jj + 1) * 128], ['WS'])
            copy('pool', W2B[:], W2S[:], ['WS'], ['W2B'])
        G.append(g_load)
        G.append(lambda: proj(0, lambda t4, tsl, ps, pr: copy(rr(), Rb[:, tsl], ps, [pr], ['FA']), banks))
        G.append(lambda: proj(1, lambda t4, tsl, ps, pr: copy(rr(), Kb[:, tsl], ps, [pr], ['FA']), banks))

        def g_kk(t4):
            def g():
                tsl = slice(t4 * 512, (t4 + 1) * 512)
                o0 = ('OPN', 0)
                sqb = KT[1].bitcast(BF16)[:, 0:512]
                act(KT[0][:], Kb[:, tsl], ACT.Copy, ['FA', 'VEC'], [o0], scale=vcol(V_KK))
                act(sqb, KT[0][:], ACT.Square, [], [o0])
                mm(PS[kb][:], BDB[:], sqb, True, True, [o0, 'BDB'], [('ps', kb)])
                act(KT[2][:], PS[kb][:], ACT.Sqrt, [('ps', kb)], [o0])
                ts('dve', KT[2][:], KT[2][:], 1e-12, None, ALU.max, None, [], [o0])
                P.op('dve', lambda e: e.reciprocal(out=KT[2][:], in_=KT[2][:]), reads=[], writes=[o0])
                tt('pool', KKf[:, tsl], KT[0][:], KT[2][:], ALU.mult, [o0], ['KKf'])
            return g
        for t4 in range(4):
            G.append(g_kk(t4))
        return G

    def start_vz(jj):
        G = []

        def g_l():
            load_w(d_ewin, 4096 + 2 * 1024 + jj * 128, 2)
            load_w(d_ewin, 4096 + 3 * 1024 + jj * 128, 3)
        G.append(g_l)
        G.append(lambda: proj(2, lambda t4, tsl, ps, pr: copy(rr(), BA[0][:, tsl], ps, [pr], ['BA0'])))
        G.append(lambda: proj(3, lambda t4, tsl, ps, pr: act(BA[1][:, tsl], ps, ACT.Silu, [pr], ['BA1'])))

        def g_vt(g):
            def f():
                for k in range(4):
                    mm(PS[4][:, k * 128:(k + 1) * 128], BA[0][:, (4 * g + k) * 128:(4 * g + k + 1) * 128], IDB[:], True, True,
                       ['BA0', 'IDB'], [('ps', 4)])
                copy('act', VTG[:, 4 * g:4 * g + 4, 0:128], c3(PS[4][:]), [('ps', 4)], ['VTG'])
            return f
        for g in range(4):
            G.append(g_vt(g))
        return G

    NRK_AT = {26: [0], 27: [1], 28: [2], 29: [3], 30: [4, 5], 31: [6]}
    KICK = 6

    def pair_chain(jj, vz, nrk=None):
        vcol = lambda base: VEC[:, base + jj:base + jj + 1]
        if True:
            blocks = [(0, b) for b in range(4)] + [(1, b) for b in range(3, -1, -1)]
            chunks = [(0, b, k) for b in range(4) for k in range(4)] + [(1, b, k) for b in range(3, -1, -1) for k in range(3, -1, -1)]
            mab = lambda z: CB[:, B_MAB + 512 * z:B_MAB + 512 * z + 512]
            mnm = lambda z: CB[:, B_MN + 256 * z:B_MN + 256 * z + 256]

            def init_state(z, jj=jj):
                def g():
                    ld(SFw, d_sw[:, jj, z, :], ['SFw'])
                    copy('pool', SBw, SFw, ['SFw'], ['SBw'])
                return g

            def prep_groups(bi, jj=jj, vcol=vcol):
                z, blk = blocks[bi]
                par = bi % 2
                O_ = OPN[par]
                opr = ('OPN', par)
                bkt = BKTs[par]
                bkr = ('BKT', par)
                tsl = slice(blk * 512, (blk + 1) * 512)
                SIG, CSw, CRw, CSBw, AI, KM, BV = TB[0], TB[1], TB[3], TB[4], TB[9], TB[7], TB[8]
                rAI, rBV = ('WB', 3), ('WB', 2)
                if bi == 0:
                    AI, BV = FB[:, 1024:1536], FB[:, 1536:2048]
                    rAI = rBV = ('OPN', 1)
                E1, E3 = TB[2], TB[6]
                if z == 0:
                    inc, ex, rest = (CSw, ('T', 1)), (SIG, ('T', 0)), (CRw, ('T', 3))
                else:
                    inc, ex, rest = (CSBw, 'WS'), (CRw, ('T', 3)), (SIG, ('T', 0))
                def w0():
                    mm(PS[4][:], W2B[:, z, :], BA[2][:, tsl], True, True, ['W2B', 'BA2'], [('ps', 4)])
                    mm(PS[5][:], W2B[:, 2 + z, :], BA[3][:, tsl], True, True, ['W2B', 'BA3'], [('ps', 5)])

                def w1():
                    act(SIG[:], PS[4][:], ACT.Sigmoid, [('ps', 4), 'VEC'], [('T', 0)], bias=vcol(V_W0 + 8 * z))
                    act(AI[:], PS[5][:], ACT.Sigmoid, [('ps', 5), 'VEC'], [rAI], bias=vcol(V_A0 + 8 * z))

                def w2():
                    P.op('dve', lambda e: e.tensor_tensor_scan(out=CSw[:], data0=CB[:, B_RST:B_RST + 512], data1=SIG[:],
                                                               initial=0.0, op0=ALU.mult, op1=ALU.add),
                         reads=[('T', 0), 'CB'], writes=[('T', 1)])
                    act(E3[:], AI[:], ACT.Identity, [rAI, 'VEC', 'OMKA'], [('WB', 0)], bias=OMKA[:, jj:jj + 1], scale=vcol(V_KA))
                    stt(BV[:], KKf[:, tsl], -1.0, AI[:], ALU.mult, ALU.mult, ['KKf', rAI], [rBV])

                def w3():
                    totb = bass.AP(CSw, 127, [[512, 128], [128, 4], [0, 128]])
                    tt('pool', c3(CRw[:]), totb, c3(CSw[:]), ALU.subtract, [('T', 1)], [('T', 3)])
                    act(WLW[:, z, blk * 4:blk * 4 + 4], bass.AP(CSw, 127, [[512, 128], [128, 4]]), ACT.Exp, [('T', 1)], ['WLW'], scale=-LAM)
                    tt('pool', KM[:], Kb[:, tsl], E3[:], ALU.mult, ['FA', ('WB', 0)], [('WB', 1)])

                def w4():
                    if z == 1:
                        tt('pool', CSBw[:], CRw[:], SIG[:], ALU.add, [('T', 3), ('T', 0)], ['WS'])
                    tt('pool', SIG[:], CSw[:], SIG[:], ALU.subtract, [('T', 1), ('T', 0)], [('T', 0)])
                    if z == 0:
                        tt('pool', PRS[:, tsl], Rb[:, tsl], KM[:], ALU.mult, ['FA', ('WB', 1)], ['PRS'])
                    else:
                        tt('pool', E3[:], Rb[:, tsl], KM[:], ALU.mult, ['FA', ('WB', 1)], [('WB', 0)])
                        tt('pool', PRS[:, tsl], PRS[:, tsl], E3[:], ALU.add, ['PRS', ('WB', 0)], ['PRS'])

                def w5():
                    act(E1[:], ex[0][:], ACT.Exp, [ex[1]], [('T', 2)], scale=-LAM)
                    act(E3[:], inc[0][:], ACT.Exp, [inc[1]], [('WB', 0)], scale=-LAM)

                def w6():
                    tt('pool', O_['AR4'][:, :, 0, :], c3(KKf[:, tsl]), c3(E1[:]), ALU.mult, ['KKf', ('T', 2)], [opr])
                    for h in range(2):
                        stt(O_['Am'][h], KKf[:, tsl], HM[h], E1[:], ALU.mult, ALU.mult, ['KKf', 'CF', ('T', 2)], [opr])
                    tt('pool', O_['AR4'][:, :, 1, :], c3(Rb[:, tsl]), c3(E3[:]), ALU.mult, ['FA', ('WB', 0)], [opr])

                def w7():
                    act(E1[:], rest[0][:], ACT.Exp, [rest[1]], [('T', 2)], scale=-LAM)
                    act(E3[:], inc[0][:], ACT.Exp, [inc[1]], [('WB', 0)], scale=LAM)

                def w8():
                    for h in range(2):
                        stt(O_['Btm'][h], BV[:], HM[h], E3[:], ALU.mult, ALU.mult, [rBV, 'CF', ('WB', 0)], [opr])
                        stt(O_['Ktm'][h], KM[:], HM[h], E3[:], ALU.mult, ALU.mult, [('WB', 1), 'CF', ('WB', 0)], [opr])
                    tt('pool', O_['Bh'], BV[:], E1[:], ALU.mult, [rBV, ('T', 2)], [opr])
                    tt('pool', O_['Kh'], KM[:], E1[:], ALU.mult, [('WB', 1), ('T', 2)], [opr])

                def w9():
                    for k in range(4):
                        mm(PS[4][:, k * 128:(k + 1) * 128], O_['Bh'][:, k * 128:(k + 1) * 128], IDB[:], True, True, [opr, 'IDB'], [('ps', 4)])

                def w10():
                    copy('act', bkt[:, 0, :], PS[4][:], [('ps', 4)], [bkr])
                    for k in range(4):
                        mm(PS[5][:, k * 128:(k + 1) * 128], O_['Kh'][:, k * 128:(k + 1) * 128], IDB[:], True, True, [opr, 'IDB'], [('ps', 5)])

                def w11():
                    copy('act', bkt[:, 1, :], PS[5][:], [('ps', 5)], [bkr])
                return [w0, w1, w2, w3, w4, w5, w6, w7, w8, w9, w10, w11]

            def a_groups(ci):
                bi, (z, blk, k) = ci // 4, chunks[ci]
                MAB, MNm = mab(z), mnm(z)
                par, q, m, cx = bi % 2, ci % 2, ci % 3, ci % 2
                O_ = OPN[par]
                opr = ('OPN', par)
                ksl = slice(k * 128, (k + 1) * 128)
                ARk = O_['AR'][:, k * 256:(k + 1) * 256]
                nm0, nm1 = NM0[m], NM1[m]
                LV, PPp = LVs[cx], PPs[cx]
                na, nb = (6, 7) if cx == 0 else (0, 1)
                PA_, PB_ = PS[na], PS[nb]
                ra, rb = ('ps', na), ('ps', nb)
                lvp = lambda i: ('LVp', cx, i)
                lvt = lambda i: ('LVt', cx, i)
                ppr = lambda i: ('PPp', cx, i)
                G = []

                def g0():
                    for h in range(2):
                        mm(PA_[:, h * 256:(h + 1) * 256], O_['Btm'][h][:, ksl], ARk, True, True, [opr], [ra])
                        mm(PB_[:, h * 256:(h + 1) * 256], O_['Ktm'][h][:, ksl], ARk, True, True, [opr], [rb])
                        mm(PS[3][:, 256 + h * 128:384 + h * 128], O_['Am'][h][:, ksl], O_['Btm'][h][:, ksl], True, True, [opr], [('ps', 3)])
                    tt('dve', nm0, PA_[:], MAB, ALU.mult, [ra, 'CB'], [('NM0', m)])
                    tt('dve', PPp[0], PS[3][:, 256:512], MNm, ALU.mult, [('ps', 3), 'CB'], [ppr(0)])
                    tt('dve', nm1, PB_[:], MAB, ALU.mult, [rb, 'CB'], [('NM1', m)])
                G.append(g0)
                pt0 = nm0.rearrange("p (h s t) -> p h s t", h=2, s=2)[:, :, 0, :]

                def g1():
                    idb2 = bass.AP(IDB, 0, [[128, 128], [0, 2], [1, 128]])
                    tt('pool', LV[1][:, :, 1, :], pt0, idb2, ALU.add, [('NM0', m), 'IDB'], [lvt(1)])
                    for h in range(2):
                        mm(PA_[:, h * 128:(h + 1) * 128], nm0[:, h * 256:h * 256 + 128], PPp[0][:, h * 128:(h + 1) * 128], True, True,
                           [('NM0', m), ppr(0)], [ra])
                        mm(PB_[:, h * 256:h * 256 + 128], PPp[0][:, h * 128:(h + 1) * 128], nm0[:, h * 256:h * 256 + 128], True, True,
                           [('NM0', m), ppr(0)], [rb])
                    copy('act', PPp[1], PA_[:, 0:256], [ra], [ppr(1)])
                    copy('dve', LV[1][:, :, 0, :], PB_[:].rearrange("p (h s t) -> p h s t", h=2, s=2)[:, :, 0, :], [rb], [lvp(1)])
                G.append(g1)

                def lvl(kk_):
                    def g():
                        a, b = kk_ % 2, (kk_ + 1) % 2
                        psb = PB_[:].rearrange("p (h s t) -> p h s t", h=2, s=2)
                        for h in range(2):
                            pk = PPp[a][:, h * 128:(h + 1) * 128]
                            if kk_ < 5:
                                mm(PB_[:, h * 256:(h + 1) * 256], pk, LV[a][:, h, :, :].rearrange("p s t -> p (s t)"), True, True,
                                   [ppr(a), lvp(a), lvt(a)], [rb])
                            else:
                                mm(PB_[:, h * 256 + 128:(h + 1) * 256], pk, LV[a][:, h, 1, :], True, True, [ppr(a), lvt(a)], [rb])
                            mm(PA_[:, h * 128:(h + 1) * 128], LV[a][:, h, 0, :], pk, True, True, [ppr(a), lvp(a)], [ra])
                        copy('act', PPp[b], PA_[:, 0:256], [ra], [ppr(b)])
                        if kk_ < 5:
                            copy('dve', LV[b][:, :, 0, :], psb[:, :, 0, :], [rb], [lvp(b)])
                        tt('dve', LV[b][:, :, 1, :], psb[:, :, 1, :], LV[a][:, :, 1, :], ALU.add, [rb, lvt(a)], [lvt(b)])
                    return g
                for kk_ in range(1, 6):
                    G.append(lvl(kk_))

                def g7():
                    for h in range(2):
                        mm(PB_[:, h * 128:(h + 1) * 128], PPp[0][:, h * 128:(h + 1) * 128], LV[0][:, h, 1, :], True, True,
                           [ppr(0), lvt(0)], [rb])
                    tt('dve', TTf[q].rearrange("p (h t) -> p h t", h=2), PB_[:, 0:256].rearrange("p (h t) -> p h t", h=2),
                       LV[0][:, :, 1, :], ALU.add, [rb, lvt(0)], [('TTf', q)])
                G.append(g7)
                return G

            def b_groups(ci, jj=jj):
                bi, (z, blk, k) = ci // 4, chunks[ci]
                par, q, m = bi % 2, ci % 2, ci % 3
                O_ = OPN[par]
                opr = ('OPN', par)
                bkt, bkr = BKTs[par], ('BKT', par)
                c16 = blk * 4 + k
                ksl = slice(k * 128, (k + 1) * 128)
                ARk = O_['AR'][:, k * 256:(k + 1) * 256]
                nm0, nm1, TT = NM0[m], NM1[m], TTf[q]
                vt = lambda h: VTG[:, c16, h * 64:(h + 1) * 64]
                G = []

                def g0():
                    mm(PS[2][:, 0:128], ARk[:, 0:128], SBw, True, False, [opr, 'SBw'], [('ps', 2)])
                    for h in range(2):
                        mm(PS[2][:, hs(h)], nm1[:, h * 256:h * 256 + 128], vt(h), False, h == 1, [('NM1', m), 'VTG'], [('ps', 2)])
                    copy('act', Z0B, PS[2][:, 0:128], [('ps', 2)], ['Z0B'])
                G.append(g0)

                def g1():
                    for h in range(2):
                        mm(PS[2][:, 128 + h * 64:192 + h * 64], TT[:, h * 128:(h + 1) * 128], Z0B[:, hs(h)], True, True,
                           [('TTf', q), 'Z0B'], [('ps', 2)])
                    copy('act', UB, PS[2][:, 128:256], [('ps', 2)], ['UB'])
                G.append(g1)

                def g2():
                    mm(PS[3][:, 0:128], ARk[:, 128:256], SBw, True, False, [opr, 'SBw'], [('ps', 3)])
                    for h in range(2):
                        mm(PS[3][:, hs(h)], nm0[:, h * 256 + 128:h * 256 + 256], UB[:, hs(h)], False, False, [('NM0', m), 'UB'], [('ps', 3)])
                        mm(PS[3][:, hs(h)], nm1[:, h * 256 + 128:h * 256 + 256], vt(h), False, h == 1, [('NM1', m), 'VTG'], [('ps', 3)])
                    mm(PS[2][:, 256:384], bkt[:, 0, ksl], UB, True, False, [bkr, 'UB'], [('ps', 2)])
                    mm(PS[2][:, 256:384], bkt[:, 1, ksl], VTG[:, c16, 0:128], False, True, [bkr, 'VTG'], [('ps', 2)])
                G.append(g2)

                def g3():
                    tmpw = GS[:, 1 + (c16 % 2), 0:128]
                    tr = ('TMPw', c16 % 2)
                    stt(tmpw, SFw, WLW[:, z, c16:c16 + 1], PS[2][:, 256:384], ALU.mult, ALU.add, ['SFw', 'WLW', ('ps', 2)], [tr])
                    stt(SFw, tmpw, MSK[:, 2 + z, c16:c16 + 1], BDm, ALU.mult, ALU.mult, [tr, 'MSK', 'CF'], ['SFw'])
                    copy('act', SBw, SFw, ['SFw'], ['SBw'])
                    if (c16 % 2 == 1) == (z == 0):
                        P.dma('sp', lambda e, tmpw=tmpw, z=z, jj=jj, c16=c16: e.dma_start(out=o_sw[c16 // 2, z, jj], in_=tmpw), reads=[tr])
                    ofc = VTG[:, c16, 128:256]
                    vo = ('VO', c16)
                    if z == 0:
                        copy('act', ofc, PS[3][:, 0:128], [('ps', 3)], [vo])
                    else:
                        to, sqo = GS[:, 0, 128:256], GS[:, 1, 128:256]
                        h3 = lambda ap: ap.rearrange("p (g w) -> p g w", w=64)
                        gb = lambda off: bass.AP(GNS, off, [[8, 128], [1, 2], [0, 64]])
                        tt('dve', to, ofc, PS[3][:, 0:128], ALU.add, [('ps', 3), vo], ['TO'])
                        P.op('dve', lambda e: e.tensor_reduce(out=GNS[:, 0:2], in_=h3(to), axis=AX.X, op=ALU.add),
                             reads=['TO'], writes=['GNS'])
                        tt('pool', sqo, to, to, ALU.mult, ['TO'], ['SQO'])
                        P.op('dve', lambda e: e.tensor_reduce(out=GNS[:, 2:4], in_=h3(sqo), axis=AX.X, op=ALU.add),
                             reads=['SQO'], writes=['GNS'])
                        ts('dve', GNS[:, 0:2], GNS[:, 0:2], 1.0 / 64, None, ALU.mult, None, ['GNS'], ['GNS'])
                        tt('dve', GNS[:, 4:6], GNS[:, 0:2], GNS[:, 0:2], ALU.mult, ['GNS'], ['GNS'])
                        stt(GNS[:, 2:4], GNS[:, 2:4], 1.0 / 64, GNS[:, 4:6], ALU.mult, ALU.subtract, ['GNS'], ['GNS'])
                        act(GNS[:, 2:4], GNS[:, 2:4], ACT.Ln, ['GNS', 'EPS'], ['GNS'], bias=EPS[:, 1:2], scale=1.0)
                        act(GNS[:, 2:4], GNS[:, 2:4], ACT.Exp, ['GNS'], ['GNS'], scale=-0.5)
                        tt('pool', h3(to), h3(to), gb(0), ALU.subtract, ['TO', 'GNS'], ['TO'])
                        tt('pool', h3(ofc), h3(to), gb(2), ALU.mult, ['TO', 'GNS'], [vo])
                G.append(g3)
                return G

            NCK = 32
            init_state(0)()
            AG = {0: a_groups(0), 1: a_groups(1)}
            PG = {0: prep_groups(0), 1: prep_groups(1)}
            lock = [(lambda i=i: (AG[0][i](), AG[1][i]() if i < 4 else None)) for i in range(8)]
            interleave([vz, PG[0] + lock])
            for ci in range(NCK):
                bg = b_groups(ci)
                if ci == 16:
                    bg = [init_state(1)] + bg
                lists = [bg]
                if ci + 1 < NCK:
                    lists.append(AG[ci + 1][4:])
                if ci + 2 < NCK:
                    AG[ci + 2] = a_groups(ci + 2)
                    lists.append(AG[ci + 2][:4])
                b_, r_ = ci // 4, ci % 4
                if r_ < 2 and b_ + 1 < 8:
                    if b_ + 1 not in PG:
                        PG[b_ + 1] = prep_groups(b_ + 1)
                    if b_ == 0:
                        lists.append(PG[1][6 * r_:6 * r_ + 6])
                    else:
                        lists.append(PG[b_ + 1][6 + 3 * r_:9 + 3 * r_])
                elif r_ >= 2 and b_ + 2 < 8:
                    if b_ + 2 not in PG:
                        PG[b_ + 2] = prep_groups(b_ + 2)
                    lists.append(PG[b_ + 2][3 * (r_ - 2):3 * (r_ - 2) + 3])
                def kick():
                    for i in range(KICK):
                        mm(PS[5][:], IDB[:], HT[:, i % 8, 0:512], True, True, ['IDB', ('HT', i % 8)], [('ps', 5)])
                lists.append([kick])
                if nrk is not None and ci in NRK_AT:
                    lists.append([nrk[i] for i in NRK_AT[ci]])
                interleave(lists)

    def pair_end(jj):
        vcol = lambda base: VEC[:, base + jj:base + jj + 1]
        G = []
        o1 = ('OPN', 1)

        def g_t(t4):
            def g():
                tsl = slice(t4 * 512, (t4 + 1) * 512)
                for k in range(4):
                    mm(PS[4][:, k * 128:(k + 1) * 128], VTG[:, t4 * 4 + k, 128:256], IDB[:], True, True,
                       [('VO', t4 * 4 + k), 'IDB'], [('ps', 4)])
                ts('dve', TB[0][:], PS[4][:], vcol(V_LW), vcol(V_LB), ALU.mult, ALU.add, [('ps', 4), 'VEC'], [('T', 0)])
                etb = ET[0].bitcast(BF16)[:, 0:512]
                act(etb, PRS[:, tsl], ACT.Copy, ['PRS', 'VEC'], [o1], scale=vcol(V_RK))
                mm(PS[5][:], BDB[:], etb, True, True, [o1, 'BDB'], [('ps', 5)])
                tt('dve', ET[1][:], PS[5][:], BA[0][:, tsl], ALU.mult, [('ps', 5), 'BA0'], [o1])
                tt('pool', TB[0][:], TB[0][:], ET[1][:], ALU.add, [('T', 0), o1], [('T', 0)])
                tt('pool', BA[1][:, tsl], TB[0][:], BA[1][:, tsl], ALU.mult, [('T', 0), 'BA1'], ['BA1'])
            return g
        for t4 in range(4):
            G.append(g_t(t4))

        def g_w():
            ld(WS[:].rearrange("p a b -> p (a b)"), d_ewout[:, 8 + jj, :], ['WS'])
            copy('pool', WOB[:], WS[:].rearrange("p a b -> p (a b)"), ['WS'], [('WB', 3)])

        def g_o(t4, half):
            def g():
                tsl = slice(t4 * 512, (t4 + 1) * 512)
                for ft in range(4 * half, 4 * half + 4):
                    pb = 2 + (cnt['rr'] % 4)
                    cnt['rr'] += 1
                    mm(PS[pb][:], WOB[:, ft * 128:(ft + 1) * 128], BA[1][:, tsl], True, True, [('WB', 3), 'BA1'], [('ps', pb)])
                    if (ft * 4 + t4) % 5 < 3:
                        stt(X[:, ft, tsl], PS[pb][:], MOD[:, 0, 16 + ft:17 + ft], X[:, ft, tsl], ALU.mult, ALU.add,
                            [('ps', pb), 'MOD', ('X', ft)], [('X', ft)])
                    else:
                        wt = WOT[ft % 2]
                        act(wt[:], PS[pb][:], ACT.Copy, [('ps', pb), 'MOD'], [('T', 2 + ft % 2)], scale=MOD[:, 0, 16 + ft:17 + ft])
                        tt('pool', X[:, ft, tsl], X[:, ft, tsl], wt[:], ALU.add, [('T', 2 + ft % 2), ('X', ft)], [('X', ft)])
            return g
        G = [g_w, G[0], G[1], g_o(0, 0), g_o(0, 1), G[2], g_o(1, 0), g_o(1, 1), G[3], g_o(2, 0), g_o(2, 1), g_o(3, 0), g_o(3, 1)]
        return G

    for g in start_rk(0):
        g()
    vz = start_vz(0)
    for jj in range(8):
        nrk = start_rk(jj + 1, banks=(4, 5), kb=4) if jj + 1 < 8 else None
        pair_chain(jj, vz, nrk)
        for g in pair_end(jj):
            g()
        if jj + 1 < 8:
            vz = start_vz(jj + 1)
    P.barrier()

    norm_mod(lambda c: G1[:, 1, c:c + 1], lambda c: MOD[:, 1, c:c + 1], lambda c, tsl: HT[:, c, tsl],
             lambda c, t4: [('HT', c)])
    ld(WS[:].rearrange("p a b -> p (a b)")[:, 0:256], d_g1.rearrange('p a b -> p (a b)'), ['WS'])
    copy('pool', G1B[:].rearrange('p a b -> p (a b)'), WS[:].rearrange("p a b -> p (a b)")[:, 0:256], ['WS'], ['G1B'])
    ld(WS[:].rearrange("p a b -> p (a b)")[0:32, :], d_g2.rearrange('p a b -> p (a b)'), ['WS'])
    copy('pool', G2B[:].rearrange('p a b -> p (a b)'), WS[:].rearrange("p a b -> p (a b)")[0:32, :], ['WS'], ['G2B'])
    ts('pool', NGB[:], VEC[:, V_GB:V_GB + 8], -1.0, None, ALU.mult, None, ['VEC'], ['NGB'])
    for t4 in range(4):
        tsl = slice(t4 * 512, (t4 + 1) * 512)
        for c in range(8):
            mm(PS[0][0:32, :], G1B[:, c, :], HT[:, c, tsl], c == 0, c == 7, ['G1B', ('HT', c)], [('ps', 0)])
        copy('act', GT1[:, tsl], PS[0][0:32, :], [('ps', 0)], ['GT1'])
    SP, CSg, CR0, INC, RST, EXg = T512[:6]
    g2 = ub(2048, 3328)
    GSET = [(BA[0][:, 0:512], BA[0][:, 512:1024], BA[0][:, 1024:1536], BA[1][:, 0:512],
             BA[1][:, 512:1024].rearrange('p (k t) -> p k t', k=4)),
            (g2[:, 0:512], g2[:, 512:1024], g2[:, 1024:1536], g2[:, 1536:2048],
             g2[:, 2048:2560].rearrange('p (k t) -> p k t', k=4))]
    GEX = [UNI[:, 4096 + 512 * i:4096 + 512 * (i + 1)] for i in range(3)]
    GLTf = KKF[:].bitcast(F32)
    GLT = [GLTf[:, 0:512], GLTf[:, 512:1024]]
    SBg = BA[1][:, 1024:1280]
    SFg = GS[:, 0, :]
    for hd in range(4):
        load_w(d_owin, hd * 128, 0)
        load_w(d_owin, 512 + hd * 128, 1)
        load_w(d_owin, 1024 + hd * 256, 2)
        load_w(d_owin, 1024 + hd * 256 + 128, 3)
        proj(0, lambda t4, tsl, ps, pr: copy(rr(), FA[:, tsl], ps, [pr], ['FA']))
        proj(1, lambda t4, tsl, ps, pr: copy(rr(), FB[:, tsl], ps, [pr], ['FB']))
        vgr = []
        for half in range(2):
            vgr.append(lambda half=half: proj(2 + half, lambda t4, tsl, ps, pr: copy(rr(), BA[2][:, tsl], ps, [pr], ['BA2'])))

            def vt(g, half=half):
                def f():
                    for k in range(4):
                        mm(PS[1][:, k * 128:(k + 1) * 128], BA[2][:, (4 * g + k) * 128:(4 * g + k + 1) * 128], IDB[:], True, True,
                           ['BA2', 'IDB'], [('ps', 1)])
                    copy('act', VTG[:, 4 * g:4 * g + 4, half * 128:(half + 1) * 128], PS[1][:].rearrange("p (k t) -> p k t", k=4),
                         [('ps', 1)], [('VTG', 4 * g + i) for i in range(4)])
                return f
            for g in range(4):
                vgr.append(vt(g))
        gblocks = [(0, b) for b in range(4)] + [(1, b) for b in range(3, -1, -1)]

        def g_init(z, hd=hd):
            def g():
                ld(SFg, d_sg[:, hd, z, :], ['SFg'])
                copy('pool', SBg, SFg, ['SFg'], ['SBg', 'BA1'])
            return g

        def g_prep(bi, hd=hd):
            z, blk = gblocks[bi]
            sb_ = bi % 2
            QE, KE, KD, KDT, ATT4 = GSET[sb_]
            rq, rk, rd, rt, ra_ = (('gQE', sb_), ('gKE', sb_), ('gKD', sb_), ('gKDT', sb_), ('gATT', sb_))
            MB = [(PS[0], 0), (PS[1], 1)] if sb_ == 0 else [(PS[2], 2), (PS[5], 5)]
            tsl = slice(blk * 512, (blk + 1) * 512)
            EA, EB, EC = GEX
            if z == 0:
                inc, rst, ri, rr_ = CSg, CR0, ('T', 1), ('T', 2)
            else:
                inc, rst, ri, rr_ = INC, RST, ('T', 3), 'WS'

            def w0():
                mm(PS[4][:], G2B[:, z, hd * 128:(hd + 1) * 128], GT1[:, tsl], True, True, ['G2B', 'GT1'], [('ps', 4)])

            def w1():
                act(EXg[:], PS[4][:], ACT.Exp, [('ps', 4), 'NGB'], ['WS'], bias=NGB[:, 4 * z + hd:4 * z + hd + 1], scale=-1.0)

            def w2():
                act(SP[:], EXg[:], ACT.Ln, ['WS', 'CF'], [('T', 0)], bias=CF[:, C_ONE:C_ONE + 1], scale=1.0)

            def w3():
                P.op('dve', lambda e: e.tensor_tensor_scan(out=CSg[:], data0=CB[:, B_RST:B_RST + 512], data1=SP[:],
                                                           initial=0.0, op0=ALU.mult, op1=ALU.add),
                     reads=[('T', 0), 'CB'], writes=[('T', 1)])

            def w4():
                totb = bass.AP(CSg, 127, [[512, 128], [128, 4], [0, 128]])
                cs3 = bass.AP(CSg, 0, [[512, 128], [128, 4], [1, 128]])
                cr3 = bass.AP(CR0, 0, [[512, 128], [128, 4], [1, 128]])
                tt('pool', cr3, totb, cs3, ALU.subtract, [('T', 1)], [('T', 2)])
                tot4 = bass.AP(CSg, 127, [[512, 128], [128, 4]])
                act(WLG[:, z, blk * 4:blk * 4 + 4], tot4, ACT.Exp, [('T', 1)], ['WLG'], scale=-1.0 / 16)
                if z == 1:
                    tt('pool', INC[:], CR0[:], SP[:], ALU.add, [('T', 2), ('T', 0)], [('T', 3)])
                    tt('pool', RST[:], CSg[:], SP[:], ALU.subtract, [('T', 1), ('T', 0)], ['WS'])

            def w5():
                act(EA, inc[:], ACT.Exp, [ri], [('gE', 0)], scale=-1.0 / 16)
                act(EB, inc[:], ACT.Exp, [ri], [('gE', 1)], scale=1.0 / 16)
                act(EC, rst[:], ACT.Exp, [rr_], [('gE', 2)], scale=-1.0 / 16)

            def w6():
                stt(QE, FA[:, tsl], 128 ** -0.5, EA, ALU.mult, ALU.mult, ['FA', ('gE', 0)], [rq])
                tt('dve', KE, FB[:, tsl], EB, ALU.mult, ['FB', ('gE', 1)], [rk])
                tt('pool', KD, FB[:, tsl], EC, ALU.mult, ['FB', ('gE', 2)], [rd])

            def w7():
                for k in range(4):
                    ksl = slice(k * 128, (k + 1) * 128)
                    mm(PS[6][:, ksl], KE[:, ksl], QE[:, ksl], True, True, [rk, rq], [('ps', 6)])
                for k in range(4):
                    mm(PS[4][:, k * 128:(k + 1) * 128], KD[:, k * 128:(k + 1) * 128], IDB[:], True, True, [rd, 'IDB'], [('ps', 4)])

            def w8():
                mg = bass.AP(CB, B_MG + 128 * z, [[NCB, 128], [0, 4], [1, 128]])
                tt('dve', ATT4, PS[6][:].rearrange("p (k t) -> p k t", k=4), mg, ALU.mult, [('ps', 6), 'CB'], [ra_])
                copy('act', KDT, PS[4][:], [('ps', 4)], [rt])

            def w9():
                for k in range(4):
                    ksl = slice(k * 128, (k + 1) * 128)
                    mb, mbn = MB[k // 2]
                    mm(mb[:, (k % 2) * 256:(k % 2 + 1) * 256], KDT[:, ksl], VTG[:, blk * 4 + k, :], True, True,
                       [rt, ('VTG', blk * 4 + k)], [('ps', mbn)])
            return [w0, w1, w2, w3, w4, w5, w6, w7, w8, w9]

        def g_chain(bi, hd=hd):
            z, blk = gblocks[bi]
            sb_ = bi % 2
            QE, KE, KD, KDT, ATT4 = GSET[sb_]
            rq, ra_ = ('gQE', sb_), ('gATT', sb_)
            MB = [(PS[0], 0), (PS[1], 1)] if sb_ == 0 else [(PS[2], 2), (PS[5], 5)]
            G = []

            def ch(k):
                def g():
                    c16 = blk * 4 + k
                    ksl = slice(k * 128, (k + 1) * 128)
                    mb, mbn = MB[k // 2]
                    ob, obn = (PS[7], 7) if k % 2 == 0 else (PS[3], 3)
                    mm(ob[:, 0:256], ATT4[:, k, :], VTG[:, c16, :], True, False, [ra_, ('VTG', c16)], [('ps', obn)])
                    mm(ob[:, 0:256], QE[:, ksl], SBg, False, True, [rq, 'SBg'], [('ps', obn)])
                    tmpg = GS[:, 1 + (c16 % 2), :]
                    tr = ('TMPg', c16 % 2)
                    stt(tmpg, SFg, WLG[:, z, c16:c16 + 1], mb[:, (k % 2) * 256:(k % 2 + 1) * 256], ALU.mult, ALU.add,
                        ['SFg', 'WLG', ('ps', mbn)], [tr])
                    ts('dve', SFg, tmpg, MSK[:, 2 + z, c16:c16 + 1], None, ALU.mult, None, [tr, 'MSK'], ['SFg'])
                    act(SBg, tmpg, ACT.Copy, [tr, 'MSK'], ['SBg'], scale=MSK[:, 2 + z, c16:c16 + 1])
                    if (c16 % 2 == 1) == (z == 0):
                        P.dma('sp', lambda e, z=z, hd=hd, c16=c16, tmpg=tmpg: e.dma_start(out=o_sg[c16 // 2, z, hd], in_=tmpg), reads=[tr])
                    obf = BA[2 + c16 // 8][:, (c16 % 8) * 256:(c16 % 8 + 1) * 256]
                    obr = 'BA2' if c16 < 8 else 'BA3'
                    if z == 0:
                        copy('act', obf, ob[:, 0:256], [('ps', obn)], [obr])
                    else:
                        tog, sqg = GLT[c16 % 2][:, 0:256], GLT[c16 % 2][:, 256:512]
                        gr = ('GLT', c16 % 2)
                        tt('dve', tog, obf, ob[:, 0:256], ALU.add, [('ps', obn), obr], [gr])
                        tt('pool', sqg, tog, tog, ALU.mult, [gr], [gr])
                        P.op('dve', lambda e, sqg=sqg, c16=c16: e.tensor_reduce(out=RSG[:, c16:c16 + 1], in_=sqg, axis=AX.X, op=ALU.add),
                             reads=[gr], writes=[('RSG', c16)])
                        act(RSG[:, c16:c16 + 1], RSG[:, c16:c16 + 1], ACT.Ln, [('RSG', c16), 'EPS'], [('RSG', c16)], bias=EPS[:, 0:1], scale=1.0 / 256)
                        act(RSG[:, c16:c16 + 1], RSG[:, c16:c16 + 1], ACT.Exp, [('RSG', c16)], [('RSG', c16)], scale=-0.5)
                        act(VTG[:, c16, :], tog, ACT.Copy, [gr, ('RSG', c16)], [('VTG', c16)], scale=RSG[:, c16:c16 + 1])
                return g
            for k in (range(4) if z == 0 else range(3, -1, -1)):
                G.append(ch(k))
            return G

        p0_ = g_prep(0)
        interleave([vgr, [g_init(0)] + p0_[:9]])
        p0_[9]()
        for bi in range(8):
            cg = g_chain(bi)
            if bi == 4:
                cg = [g_init(1)] + cg
            lists = [cg]
            if bi + 1 < 8:
                lists.append(g_prep(bi + 1))
            interleave(lists)
        def half_groups(half, hd=hd):
            slot, SZ, nSZ, TF, nTF, OBh, nOB, pT, nT = ((0, BA[2], 'BA2', FB, 'FB', BA[3], 'BA3', PS[1], 1) if half == 0 else
                                                         (1, BA[0], 'BA0', FA, 'FA', BA[1], 'BA1', PS[0], 0))
            G = []

            def ga():
                load_w(d_owin, 2048 + hd * 256 + half * 128, slot)
                proj(slot, lambda t4, tsl, ps, pr: act(SZ[:, tsl], ps, ACT.Silu, [pr], [nSZ]))
            G.append(ga)

            def gt(t4):
                def f():
                    tsl = slice(t4 * 512, (t4 + 1) * 512)
                    for k in range(4):
                        mm(pT[:, k * 128:(k + 1) * 128], VTG[:, t4 * 4 + k, half * 128:(half + 1) * 128], IDB[:], True, True,
                           [('VTG', t4 * 4 + k), 'IDB'], [('ps', nT)])
                    ts('dve', TF[:, tsl], pT[:], VEC[:, V_GN + half:V_GN + half + 1], None, ALU.mult, None, [('ps', nT), 'VEC'], [nTF])
                return f
            for t4 in range(4):
                G.append(gt(t4))

            def gm():
                tt('pool', OBh[:], TF[:], SZ[:], ALU.mult, [nTF, nSZ], [nOB])
            G.append(gm)
            G.append(lambda: wout_partial(1, d_owout, hd * 2 + half, OBh, nOB))
            return G
        interleave([half_groups(0), half_groups(1)])

    ftsl = [slice(t4 * 512, (t4 + 1) * 512) for t4 in range(4)]
    sumsq_rstd(ftsl[0], 0)
    for t4 in range(4):
        tsl = ftsl[t4]
        if t4 + 1 < 4:
            sumsq_rstd(ftsl[t4 + 1], (t4 + 1) % 2)
        rb_, rn_ = RSTD[t4 % 2]
        for c in range(8):
            stt(FA[:, (c % 4) * 512:(c % 4 + 1) * 512], X[:, c, tsl], VEC[:, V_FG + c:V_FG + c + 1], rb_[:],
                ALU.mult, ALU.mult, [('X', c), 'VEC', rn_], [('FAq', c % 4)])
            P.dma('sp', lambda e, c=c, tsl=tsl: e.dma_start(out=o_y[:, c, tsl], in_=FA[:, (c % 4) * 512:(c % 4 + 1) * 512]),
                  reads=[('FAq', c % 4)])
    P.finish_waits('sp')
    P.emit()
    global _LAST_P
    _LAST_P = P
    st.close()
    return nc


def kernel(**inp):
    f = lambda k: np.asarray(inp[k], np.float32)
    plan = _assign()
    cf, cb = _consts()
    vec0 = np.zeros((128, NV), np.float32)
    vec0[:, V_NG:V_NG + 16] = np.concatenate([_col(f('norm_g')[0]), _col(f('norm_g')[1])], 1)
    vec0[:, V_FG:V_FG + 8] = _col(f('final_g'))
    cw = f('conv_w')[0]
    for k in range(3):
        vec0[:, V_CW + 8 * k:V_CW + 8 * k + 8] = _col(cw[k])
    for z in range(2):
        vec0[:, V_W0 + 8 * z:V_W0 + 8 * z + 8] = _col(f('wkv_w0')[0, z])
        vec0[:, V_A0 + 8 * z:V_A0 + 8 * z + 8] = _col(f('wkv_a0')[0, z])
        vec0[:, V_GB + 4 * z:V_GB + 4 * z + 4] = _col(f('gla_gk_b')[0, z])
    vec0[:, V_KK:V_KK + 8] = _col(f('wkv_k_k')[0])
    vec0[:, V_KA:V_KA + 8] = _col(f('wkv_k_a')[0])
    vec0[:, V_RK:V_RK + 8] = _col(f('wkv_r_k')[0].reshape(-1))
    vec0[:, V_LW:V_LW + 8] = _col(f('wkv_ln_w')[0])
    vec0[:, V_LB:V_LB + 8] = _col(f('wkv_ln_b')[0])
    for l in range(2):
        vec0[:, V_AB + 24 * l:V_AB + 24 * l + 24] = _col(f('ada_b')[l])
    vec0[:, V_GN:V_GN + 2] = _col(f('gla_g_norm')[0])
    r3 = lambda w: np.ascontiguousarray(w.reshape(-1, 128, w.shape[-1]).transpose(1, 0, 2))
    ada = np.stack([r3(f('ada_w')[l]) for l in range(2)])
    ewin = r3(f('e_w_in')[0])
    ewout = r3(f('e_w_out')[0])
    owin = r3(f('o_w_in')[0])
    owout = r3(f('o_w_out')[0])
    w1c = np.concatenate([r3(f('wkv_w1')[0, 0]), r3(f('wkv_w1')[0, 1]), r3(f('wkv_a1')[0, 0]), r3(f('wkv_a1')[0, 1])], 2)
    w2p = np.zeros((128, 4, 1024), np.float32)
    for z in range(2):
        w2p[64 * z:64 * z + 64, z] = f('wkv_w2')[0, z]
        w2p[64 * z:64 * z + 64, 2 + z] = f('wkv_a2')[0, z]
    g1c = np.concatenate([r3(f('gla_gk1')[0, 0]), r3(f('gla_gk1')[0, 1])], 2)
    g2p = np.zeros((32, 2, 512), np.float32)
    for z in range(2):
        g2p[16 * z:16 * z + 16, z] = f('gla_gk2')[0, z]
    xp, xs = f('x_prompt'), f('x_sample')
    swkv, sgla = f('state_wkv'), f('state_gla')
    in_maps = []
    for core, items in enumerate(plan):
        x = np.zeros((NT, 1024), np.float32)
        msk = np.zeros((128, 4, 32), np.float32)
        s_w = np.zeros((128, 8, 2, 128), np.float32)
        s_g = np.zeros((128, 4, 2, 256), np.float32)
        vec = vec0.copy()
        if items[0][0] == 's':
            b = items[0][1]
            x[:] = xs[b]
            cv = f('c')[b]
            msk[:, 2, :16] = 1.0
            msk[:, 3, :16] = 1.0
            for z in range(2):
                for h in range(16):
                    jj, hl = divmod(h, 2)
                    s_w[64 * hl:64 * hl + 64, jj, z, 64 * hl:64 * hl + 64] = swkv[b, 0, z, h].T
                for h in range(4):
                    s_g[:, h, z, :] = sgla[b, 0, z, h]
        else:
            for si in range(8):
                x[256 * si:256 * si + 256] = xp[items[si % len(items)][1]]
            cv = f('c_ctx')
            g = np.arange(32)
            inner = (g % 4 != 0).astype(np.float32)
            msk[:, 0, :] = inner[None]
            msk[:, 1, :] = inner[None]
            c16 = np.arange(16)
            msk[:, 2, :16] = (c16 % 2 == 0)[None]
            msk[:, 3, :16] = (c16 % 2 == 1)[None]
        vec[:, V_CV:V_CV + 8] = _col(cv)
        xT = np.ascontiguousarray(x.reshape(NT, 8, 128).transpose(2, 1, 0))
        in_maps.append(dict(xT=xT, vec=vec, cf=cf, cb=cb.astype(ml_dtypes.bfloat16), msk=msk, ada=ada, ewin=ewin, ewout=ewout, w1c=w1c,
                            w2p=w2p, s_wkv=s_w, owin=owin, owout=owout, g1c=g1c, g2p=g2p, s_gla=s_g))
    nc = build()
    res = run_bass_kernel_spmd(nc, in_maps, core_ids=list(range(8)))
    y_p = np.zeros((16, 256, 1024), np.float32)
    y_s = np.zeros((2, 2048, 1024), np.float32)
    n_w = np.zeros((16, 1, 2, 16, 64, 64), np.float32)
    n_g = np.zeros((16, 1, 2, 4, 128, 256), np.float32)
    for core, items in enumerate(plan):
        r = res.results[core]
        y = np.asarray(r["yT"]).transpose(2, 1, 0).reshape(NT, 1024)
        ow = np.asarray(r["o_wkv"])
        og = np.asarray(r["o_gla"])
        if items[0][0] == 's':
            y_s[items[0][1]] = y
        else:
            for si, (_, pi) in enumerate(items):
                y_p[pi] = y[256 * si:256 * si + 256]
                for z in range(2):
                    for h in range(16):
                        jj, hl = divmod(h, 2)
                        n_w[pi, 0, z, h] = ow[si, z, jj, 64 * hl:64 * hl + 64, 64 * hl:64 * hl + 64].T
                    n_g[pi, 0, z] = og[si, z]
    return (y_p, y_s, n_w, n_g)
```

```python
import contextlib
import numpy as np
import ml_dtypes
import concourse.bass as bass
import concourse.mybir as mybir
from concourse.bass_utils import run_bass_kernel_spmd

ACT = mybir.ActivationFunctionType
ALU = mybir.AluOpType
F32 = mybir.dt.float32
BF16 = mybir.dt.bfloat16
AX = mybir.AxisListType

ENGS = ['pe', 'act', 'dve', 'pool', 'sp']
EPOCH = 4000
NDS = 8
NT = 2048
NCH = 16
LAM = 0.6065306597126334
NORM_EPS = 1e-6
GN_EPS = 64e-5


class Prog:
    def __init__(self, nc):
        self.nc = nc
        self.ops = {e: [] for e in ENGS}
        self.count = {e: 0 for e in ENGS}
        self.dcount = {e: 0 for e in ENGS}
        self.last_w = {}
        self.readers = {}
        self.waited = {e: {} for e in ENGS}
        self.pending = {e: [] for e in ENGS}

    def _deps(self, eng, reads, writes):
        deps = set()
        for r in reads:
            if r in self.last_w:
                deps.add(self.last_w[r])
        for w in writes:
            if w in self.last_w:
                deps.add(self.last_w[w])
            for rd in self.readers.get(w, ()):
                deps.add(rd)
        best = {}
        for d in deps:
            if eng == 'pe' and d[:-1] == ('e', 'pe'):
                continue
            best[d[:-1]] = max(best.get(d[:-1], 0), d[-1])
        for d in self.pending[eng]:
            best[d[:-1]] = max(best.get(d[:-1], 0), d[-1])
        self.pending[eng] = []
        final = []
        for key, i in best.items():
            if self.waited[eng].get(key, 0) < i:
                self.waited[eng][key] = i
                final.append(key + (i,))
        return final

    def _mark(self, tok, reads, writes):
        for r in reads:
            self.readers.setdefault(r, []).append(tok)
        for w in writes:
            self.last_w[w] = tok
            self.readers[w] = []

    def op(self, eng, fn, reads=(), writes=()):
        writes = list(writes) + [r for r in reads if isinstance(r, tuple) and r[0] == 'ps']
        waits = self._deps(eng, reads, writes)
        idx = self.count[eng] + 1
        self.count[eng] = idx
        self.ops[eng].append(('c', fn, waits, idx))
        self._mark(('e', eng, idx), reads, writes)

    def dma(self, eng, fn, reads=(), writes=()):
        waits = self._deps(eng, reads, writes)
        j = self.dcount[eng]
        self.dcount[eng] = j + 1
        slot = j % NDS
        if j >= NDS:
            key = ('d', eng, slot)
            need = j // NDS
            if self.waited[eng].get(key, 0) < need:
                self.waited[eng][key] = need
                waits.append(key + (need,))
        self.ops[eng].append(('d', fn, waits, (slot, j // NDS + 1)))
        self._mark(('d', eng, slot, j // NDS + 1), reads, writes)

    def barrier(self):
        snap = [('e', e, self.count[e]) for e in ENGS if self.count[e]]
        for q in ENGS:
            n = self.dcount[q]
            for slot in range(min(n, NDS)):
                snap.append(('d', q, slot, (n - 1 - slot) // NDS + 1))
        for e in ENGS:
            self.pending[e] = list(snap)

    def finish_waits(self, eng='sp'):
        waits = []
        for q in ENGS:
            n = self.dcount[q]
            for slot in range(min(n, NDS)):
                waits.append(('d', q, slot, (n - 1 - slot) // NDS + 1))
        self.ops[eng].append(('w', None, waits, None))

    def emit(self):
        nc = self.nc
        with contextlib.ExitStack() as st:
            esem = {e: [st.enter_context(nc.semaphore(f"s_{e}_{k}")) for k in range(self.count[e] // EPOCH + 1)]
                    for e in ENGS}
            dsem = {e: [st.enter_context(nc.semaphore(f"d_{e}_{k}")) for k in range(NDS)]
                    for e in ENGS if self.dcount[e]}
            block = st.enter_context(nc.Block())

            def run(handle, e):
                for kind, fn, waits, info in self.ops[e]:
                    for w in waits:
                        if w[0] == 'e':
                            handle.wait_ge(esem[w[1]][(w[2] - 1) // EPOCH], (w[2] - 1) % EPOCH + 1)
                        else:
                            handle.wait_ge(dsem[w[1]][w[2]], 16 * w[3])
                    if kind == 'c':
                        fn(handle).then_inc(esem[e][(info - 1) // EPOCH], 1)
                    elif kind == 'd':
                        fn(handle).then_inc(dsem[e][info[0]], 16)

            @block.tensor
            def _(h):
                run(h, 'pe')

            @block.scalar
            def _(h):
                run(h, 'act')

            @block.vector
            def _(h):
                run(h, 'dve')

            @block.gpsimd
            def _(h):
                run(h, 'pool')

            @block.sync
            def _(h):
                run(h, 'sp')


V_NG, V_FG, V_CW, V_W0, V_A0, V_KK, V_KA, V_RK, V_LW, V_LB, V_AB, V_GB, V_GN, V_CV = \
    0, 16, 24, 48, 64, 80, 88, 96, 104, 112, 120, 168, 176, 178
NV = 186
C_ID, C_BD, C_HM, C_ONE = 0, 128, 256, 258
NCF = 386
B_RST, B_MAB, B_MN, B_MG = 0, 512, 1536, 2048
NCB = 2304


def _col(v):
    v = np.asarray(v, np.float32).reshape(-1, 128)
    return np.ascontiguousarray(v.T)


def _consts():
    u = np.arange(128)[:, None]
    t = np.arange(128)[None, :]
    LT, LE, GT, GE = (u < t), (u <= t), (u > t), (u >= t)
    cf = np.zeros((128, NCF), np.float32)
    cf[:, C_ID:C_ID + 128] = np.eye(128)
    cf[:, C_BD:C_BD + 128] = (u // 64 == t // 64)
    cf[:, C_HM] = (np.arange(128) < 64)
    cf[:, C_HM + 1] = (np.arange(128) >= 64)
    cf[:, C_ONE:C_ONE + 128] = 1.0
    cb = np.zeros((128, NCB), np.float32)
    rst = np.ones(512, np.float32)
    rst[::128] = 0
    cb[:, B_RST:B_RST + 512] = rst[None]
    cb[:, B_MAB:B_MAB + 512] = np.concatenate([LT, LE, LT, LE], 1)
    cb[:, B_MAB + 512:B_MAB + 1024] = np.concatenate([GT, GE, GT, GE], 1)
    cb[:, B_MN:B_MN + 256] = np.concatenate([GT, GT], 1)
    cb[:, B_MN + 256:B_MN + 512] = np.concatenate([LT, LT], 1)
    cb[:, B_MG:B_MG + 128] = LE
    cb[:, B_MG + 128:B_MG + 256] = GE
    return cf, cb


def _assign():
    plan = [[('s', 0)], [('s', 1)]]
    p = 0
    for n in (3, 3, 3, 3, 2, 2):
        plan.append([('p', p + i) for i in range(n)])
        p += n
    return plan


def build(stop_after=99):
    nc = bass.Bass("TRN2", target_bir_lowering=False)
    dt_in = lambda n, s: nc.dram_tensor(n, s, F32, kind="ExternalInput").ap()
    dt_out = lambda n, s: nc.dram_tensor(n, s, F32, kind="ExternalOutput").ap()
    d_x = dt_in("xT", [128, 8, NT])
    d_vec = dt_in("vec", [128, NV])
    d_cf = dt_in("cf", [128, NCF])
    d_cb = nc.dram_tensor("cb", [128, NCB], BF16, kind="ExternalInput").ap()
    d_msk = dt_in("msk", [128, 4, 32])
    d_ada = dt_in("ada", [2, 128, 8, 3072])
    d_ewin = dt_in("ewin", [128, 8, 8192])
    d_ewout = dt_in("ewout", [128, 16, 1024])
    d_w1 = dt_in("w1c", [128, 8, 256])
    d_w2 = dt_in("w2p", [128, 4, 1024])
    d_sw = dt_in("s_wkv", [128, 8, 2, 128])
    d_owin = dt_in("owin", [128, 8, 3072])
    d_owout = dt_in("owout", [128, 8, 1024])
    d_g1 = dt_in("g1c", [128, 8, 32])
    d_g2 = dt_in("g2p", [32, 2, 512])
    d_sg = dt_in("s_gla", [128, 4, 2, 256])
    o_y = dt_out("yT", [128, 8, NT])
    o_sw = dt_out("o_wkv", [8, 2, 8, 128, 128])
    o_sg = dt_out("o_gla", [8, 2, 4, 128, 256])

    st = contextlib.ExitStack()
    sb = lambda n, s, d=F32: st.enter_context(nc.sbuf_tensor(n, s, d))
    X = sb("X", [128, 8, NT])
    HT = sb("HT", [128, 8, NT], BF16)
    VEC = sb("VEC", [128, NV])
    CF = sb("CF", [128, NCF])
    CB = sb("CB", [128, NCB], BF16)
    IDB = sb("IDB", [128, 128], BF16)
    ONEB = sb("ONEB", [128, 128], BF16)
    BDB = sb("BDB", [128, 128], BF16)
    MSK = sb("MSK", [128, 4, 32])
    MOD = sb("MOD", [128, 2, 24])
    G1 = sb("G1", [128, 2, 8])
    CS_ = sb("CSIL", [128, 8])
    EPS = sb("EPS", [128, 2])
    FA = sb("FA", [128, NT])
    FB = sb("FB", [128, NT])
    BA = [sb(f"BA{i}", [128, NT], BF16) for i in range(4)]
    WRAW = sb("WRAW", [128, 3072])
    WS = WRAW[:, 0:1024].rearrange("p (a b) -> p a b", a=8)
    WB = WRAW[:, 1024:3072].bitcast(BF16).rearrange("p (a b c) -> p a b c", a=4, b=8)
    WOB = WB[:, 3, :, :].rearrange("p a b -> p (a b)")
    UNI = sb("UNI", [128, 8960])
    ub = lambda a, b, p=128: UNI[0:p, a:b].bitcast(BF16)
    T512 = [sb(f"T512_{i}", [128, 512]) for i in range(4)] + [WRAW[:, 512 * i:512 * (i + 1)] for i in range(6)]
    G1B = ub(1024, 1152).rearrange("p (a b) -> p a b", a=8)
    G2B = ub(1152, 1664, 32).rearrange("p (a b) -> p a b", a=2)
    GT1 = ub(0, 1024, 32)
    VTG = sb("VTG", [128, 16, 256], BF16)
    KKF = sb("KKF", [128, NT], BF16)
    GS = sb("GS", [128, 3, 256])
    WLG = sb("WLG", [128, 2, 16])
    NGB = sb("NGB", [128, 8])
    RSG = sb("RSG", [128, 16])
    PRB = ub(0, 2560)
    CHB = ub(2560, 6080)
    BKT = ub(6080, 7104).rearrange("p (a b) -> p a b", a=4)
    PRS = ub(7104, 8128)
    W2S = WRAW[:, 0:512].rearrange("p (a b) -> p a b", a=4)
    W2B = ub(8128, 8384).rearrange("p (a b) -> p a b", a=4)
    W_XAM = 8384
    WLW = sb("WLW", [128, 2, 16])
    OMKA = sb("OMKA", [128, 8])
    GNS = sb("GNS", [128, 8])
    WOT = [T512[2], T512[3]]
    PS = [st.enter_context(nc.psum_tensor(f"ps{i}", [128, 512], F32)) for i in range(8)]

    P = Prog(nc)
    cnt = {'rr': 0}

    def rr(engs=('act', 'dve')):
        cnt['rr'] += 1
        return engs[cnt['rr'] % len(engs)]

    def mm(out, lhsT, rhs, start, stop, r, w):
        P.op('pe', lambda e: e.matmul(out, lhsT, rhs, start=start, stop=stop), reads=r, writes=w)

    def copy(eng, out, in_, r, w):
        if eng == 'act':
            P.op('act', lambda e: e.activation(out=out, in_=in_, func=ACT.Copy), reads=r, writes=w)
        else:
            P.op(eng, lambda e: e.tensor_copy(out=out, in_=in_), reads=r, writes=w)

    def tt(eng, out, a, b, op, r, w):
        P.op(eng, lambda e: e.tensor_tensor(out=out, in0=a, in1=b, op=op), reads=r, writes=w)

    def ts(eng, out, a, s1, s2, op0, op1, r, w):
        if s2 is None:
            P.op(eng, lambda e: e.tensor_scalar(out=out, in0=a, scalar1=s1, scalar2=None, op0=op0), reads=r, writes=w)
        else:
            P.op(eng, lambda e: e.tensor_scalar(out=out, in0=a, scalar1=s1, scalar2=s2, op0=op0, op1=op1), reads=r, writes=w)

    def stt(out, a, s, b, op0, op1, r, w):
        P.op('dve', lambda e: e.scalar_tensor_tensor(out=out, in0=a, scalar=s, in1=b, op0=op0, op1=op1), reads=r, writes=w)

    def act(out, in_, func, r, w, bias=None, scale=None):
        kw = {}
        if bias is not None:
            kw['bias'] = bias
        if scale is not None:
            kw['scale'] = scale
        P.op('act', lambda e: e.activation(out=out, in_=in_, func=func, **kw), reads=r, writes=w)

    def ld(out, in_, w, r=()):
        P.dma('sp', lambda e: e.dma_start(out=out, in_=in_), reads=r, writes=w)

    for c in range(8):
        ld(X[:, c, :], d_x[:, c, :], [('X', c)])
    ld(VEC[:], d_vec, ['VEC'])
    ld(CF[:], d_cf, ['CF'])
    ld(CB[:], d_cb, ['CB'])
    ld(MSK[:], d_msk, ['MSK'])
    copy('pool', IDB[:], CF[:, C_ID:C_ID + 128], ['CF'], ['IDB'])
    copy('pool', ONEB[:], CF[:, C_ONE:C_ONE + 128], ['CF'], ['ONEB'])
    copy('pool', BDB[:], CF[:, C_BD:C_BD + 128], ['CF'], ['BDB'])
    P.op('pool', lambda e: e.memset(EPS[:, 0:1], NORM_EPS), writes=['EPS'])
    P.op('pool', lambda e: e.memset(EPS[:, 1:2], GN_EPS), writes=['EPS'])
    act(CS_[:], VEC[:, V_CV:V_CV + 8], ACT.Silu, ['VEC'], ['CSIL'])
    def ada_layer(l, ACCQ, an, STG, sn):
        steps = []
        for c in range(8):
            for q in range(3):
                def st_(c=c, q=q, i=len(steps)):
                    sg, sr = STG[i % 4], (sn, i % 4)
                    ld(sg, d_ada[l, :, c, q * 1024:(q + 1) * 1024], [sr])
                    if c == 0:
                        ts('dve', ACCQ[q], sg, CS_[:, c:c + 1], None, ALU.mult, None, [sr, 'CSIL'], [(an, q)])
                    else:
                        stt(ACCQ[q], sg, CS_[:, c:c + 1], ACCQ[q], ALU.mult, ALU.add, [sr, 'CSIL', (an, q)], [(an, q)])
                steps.append(st_)

        def fin():
            for j in range(24):
                mm(PS[0][:, j:j + 1], ACCQ[j // 8][:, (j % 8) * 128:(j % 8 + 1) * 128], CF[:, C_ONE:C_ONE + 1],
                   True, True, [(an, j // 8), 'CF'], [('ps', 0)])
            tt('dve', MOD[:, l, :], PS[0][:, 0:24], VEC[:, V_AB + 24 * l:V_AB + 24 * l + 24], ALU.add,
               [('ps', 0), 'VEC'], ['MOD'])
            ts('dve', G1[:, l, :], MOD[:, l, 8:16], 1.0, None, ALU.add, None, ['MOD'], ['G1'])
            tt('dve', G1[:, l, :], G1[:, l, :], VEC[:, V_NG + 8 * l:V_NG + 8 * l + 8], ALU.mult, ['G1', 'VEC'], ['G1'])
        return steps, fin

    st0, fin0 = ada_layer(0, [FA[:, 0:1024], FA[:, 1024:2048], FB[:, 1024:2048]], 'ACC',
                          [FB[:, 0:1024], WRAW[:, 0:1024], WRAW[:, 1024:2048], WRAW[:, 2048:3072]], 'STG')
    for f_ in st0:
        f_()
    fin0()
    ada1_steps, ada1_fin = ada_layer(1, [UNI[:, 1024 * i:1024 * (i + 1)] for i in range(3)], 'uACC',
                                     [UNI[:, 3072 + 1024 * i:4096 + 1024 * i] for i in range(4)], 'uSTG')

    XR = [('X', c) for c in range(8)]
    HR = [('HT', c) for c in range(8)]

    SQB = [WRAW[:, 0:256].bitcast(BF16), WRAW[:, 1024:1280].bitcast(BF16)]
    SQR = ['WS', ('WB', 0)]

    RSTD = [(T512[2], ('T', 2)), (T512[3], ('T', 3))]

    def sumsq_rstd(tsl, ri=0):
        rb_, rn_ = RSTD[ri]
        for c in range(8):
            act(SQB[c % 2], X[:, c, tsl], ACT.Square, [('X', c)], [SQR[c % 2]])
            mm(PS[1][:], ONEB[:], SQB[c % 2], c == 0, c == 7, [SQR[c % 2], 'ONEB'], [('ps', 1)])
        act(rb_[:], PS[1][:], ACT.Sqrt, [('ps', 1), 'EPS'], [rn_], bias=EPS[:, 0:1], scale=1.0 / 1024)
        P.op('dve', lambda e: e.reciprocal(out=rb_[:], in_=rb_[:]), reads=[rn_], writes=[rn_])

    def norm_mod(gfn, sfn, out_fn, out_res):
        tsls = [slice(t4 * 512, (t4 + 1) * 512) for t4 in range(4)]
        sumsq_rstd(tsls[0], 0)
        for t4 in range(4):
            tsl = tsls[t4]
            if t4 + 1 < 4:
                sumsq_rstd(tsls[t4 + 1], (t4 + 1) % 2)
            rb_, rn_ = RSTD[t4 % 2]
            for c in range(8):
                tmp = T512[c % 2]
                o = out_fn(c, tsl)
                if c % 2 == 0:
                    tt('pool', tmp[:], X[:, c, tsl], rb_[:], ALU.mult, [('X', c), rn_], [('T', 0)])
                    ts('dve', o, tmp[:], gfn(c), sfn(c), ALU.mult, ALU.add, [('T', 0), 'VEC', 'G1', 'MOD'], out_res(c, t4))
                else:
                    tt('dve', tmp[:], X[:, c, tsl], rb_[:], ALU.mult, [('X', c), rn_], [('T', 1)])
                    act(o, tmp[:], ACT.Identity, [('T', 1), 'VEC', 'G1', 'MOD'], out_res(c, t4), bias=sfn(c), scale=gfn(c))

    def load_w(dram, col0, br):
        ld(WS[:], dram[:, :, col0:col0 + 128], ['WS'])
        copy(rr(('act', 'dve')), WB[:, br, :, :], WS[:], ['WS'], [('WB', br)])

    def proj(br, evac, banks=(2, 3, 4, 5)):
        for t4 in range(4):
            tsl = slice(t4 * 512, (t4 + 1) * 512)
            pb = banks[cnt['rr'] % len(banks)]
            cnt['rr'] += 1
            for c in range(8):
                mm(PS[pb][:], WB[:, br, c, :], HT[:, c, tsl], c == 0, c == 7, [('WB', br), ('HT', c)], [('ps', pb)])
            evac(t4, tsl, PS[pb][:], ('ps', pb))

    def wout_partial(l, dram_wout, j, OB, ores):
        ld(WS[:].rearrange("p a b -> p (a b)"), dram_wout[:, j, :], ['WS'])
        copy('pool', WOB[:], WS[:].rearrange("p a b -> p (a b)"), ['WS'], [('WB', 3)])
        for ft in range(8):
            for t4 in range(4):
                tsl = slice(t4 * 512, (t4 + 1) * 512)
                pb = 2 + (cnt['rr'] % 4)
                cnt['rr'] += 1
                mm(PS[pb][:], WOB[:, ft * 128:(ft + 1) * 128], OB[:, tsl], True, True, [('WB', 3), ores], [('ps', pb)])
                if (ft * 4 + t4) % 5 < 3:
                    stt(X[:, ft, tsl], PS[pb][:], MOD[:, l, 16 + ft:17 + ft], X[:, ft, tsl], ALU.mult, ALU.add,
                        [('ps', pb), 'MOD', ('X', ft)], [('X', ft)])
                else:
                    wt = WOT[t4 % 2]
                    act(wt[:], PS[pb][:], ACT.Copy, [('ps', pb), 'MOD'], [('T', 2 + t4 % 2)], scale=MOD[:, l, 16 + ft:17 + ft])
                    tt('pool', X[:, ft, tsl], X[:, ft, tsl], wt[:], ALU.add, [('T', 2 + t4 % 2), ('X', ft)], [('X', ft)])

    P.barrier()
    norm_mod(lambda c: G1[:, 0, c:c + 1], lambda c: MOD[:, 0, c:c + 1], lambda c, tsl: HT[:, c, tsl],
             lambda c, t4: [('HT', c)])

    def v3(t, a, b):
        return t[:].rearrange("p (g w) -> p g w", w=64)[:, a, b]

    for j in range(8):
        for br in range(4):
            load_w(d_ewin, br * 1024 + j * 128, br)
        for f_ in ada1_steps[3 * j:3 * j + 3]:
            f_()
        U, Pm, Y = FA, FB, FA
        proj(0, lambda t4, tsl, ps, pr: copy(rr(), FA[:, tsl], ps, [pr], ['FA']))
        proj(2, lambda t4, tsl, ps, pr: tt('dve', FB[:, tsl], ps, FA[:, tsl], ALU.mult, [pr, 'FA'], ['FB']))
        proj(1, lambda t4, tsl, ps, pr: copy(rr(), BA[0][:, tsl], ps, [pr], ['BA0']))
        proj(3, lambda t4, tsl, ps, pr: act(BA[1][:, tsl], ps, ACT.Silu, [pr], ['BA1']))
        w0, w1, w2 = (VEC[:, V_CW + 8 * k + j:V_CW + 8 * k + j + 1] for k in range(3))
        act(FA[:], FB[:], ACT.Copy, ['FB', 'VEC'], ['FA'], scale=w1)
        g_all, g_lo, g_hi = slice(0, 32), slice(0, 31), slice(1, 32)
        stt(v3(FA, g_all, slice(1, 64)), v3(FB, g_all, slice(0, 63)), w0, v3(FA, g_all, slice(1, 64)),
            ALU.mult, ALU.add, ['FB', 'FA', 'VEC'], ['FA'])
        stt(v3(FA, g_all, slice(0, 63)), v3(FB, g_all, slice(1, 64)), w2, v3(FA, g_all, slice(0, 63)),
            ALU.mult, ALU.add, ['FB', 'FA', 'VEC'], ['FA'])
        tb = T512[0]
        tt('pool', tb[:, 0:31], v3(FB, g_lo, 63), MSK[:, 0, 1:32], ALU.mult, ['FB', 'MSK'], [('T', 0)])
        stt(v3(FA, g_hi, 0), tb[:, 0:31], w0, v3(FA, g_hi, 0), ALU.mult, ALU.add, [('T', 0), 'FA', 'VEC'], ['FA'])
        tt('pool', tb[:, 32:63], v3(FB, g_hi, 0), MSK[:, 1, 1:32], ALU.mult, ['FB', 'MSK'], [('T', 0)])
        stt(v3(FA, g_lo, 63), tb[:, 32:63], w2, v3(FA, g_lo, 63), ALU.mult, ALU.add, [('T', 0), 'FA', 'VEC'], ['FA'])
        tt('pool', FA[:], FA[:], BA[0][:], ALU.mult, ['FA', 'BA0'], ['FA'])
        tt('dve', BA[2][:], FA[:], BA[1][:], ALU.mult, ['FA', 'BA1'], ['BA2'])
        wout_partial(0, d_ewout, j, BA[2], 'BA2')

    ada1_fin()
    P.barrier()
    load_w(d_w1, 0, 0)
    load_w(d_w1, 128, 1)
    proj(0, lambda t4, tsl, ps, pr: act(BA[2][:, tsl], ps, ACT.Tanh, [pr], ['BA2']))
    proj(1, lambda t4, tsl, ps, pr: copy('act', BA[3][:, tsl], ps, [pr], ['BA3']))
    ts('pool', OMKA[:], VEC[:, V_KA:V_KA + 8], -1.0, 1.0, ALU.mult, ALU.add, ['VEC'], ['OMKA'])
    KKf = KKF[:]
    Rb = FA[:, 0:1024].bitcast(BF16)
    Kb = FA[:, 1024:2048].bitcast(BF16)
    TB = T512
    c3 = lambda ap: ap.rearrange("p (k t) -> p k t", t=128)
    FBb = FB[:].bitcast(BF16)
    PRBs = [PRB, FBb]
    XAm = ub(W_XAM, W_XAM + 512)
    def opnd(par):
        base = PRBs[par]
        d = dict(AR=base[:, 0:1024], Bh=base[:, 1024:1536], Kh=base[:, 1536:2048],
                 Btm=[base[:, 2048:2560], base[:, 2560:3072]], Ktm=[base[:, 3072:3584], base[:, 3584:4096]])
        d['Am'] = [base[:, 4096:4608], base[:, 4608:5120]] if par == 0 else [XAm[:, 0:512], XAm[:, 512:1024]]
        d['AR4'] = d['AR'].rearrange("p (k s t) -> p k s t", s=2, t=128)
        return d
    OPN = [opnd(0), opnd(1)]
    NM0 = [CHB[:, 1024 * i:1024 * i + 512] for i in range(3)]
    NM1 = [CHB[:, 1024 * i + 512:1024 * i + 1024] for i in range(3)]
    lv4 = lambda ap: ap.rearrange("p (h s t) -> p h s t", h=2, s=2)
    LVs = [[lv4(CHB[:, 3072 + 1536 * c + 512 * i:3072 + 1536 * c + 512 * (i + 1)]) for i in range(2)] for c in range(2)]
    PPs = [[CHB[:, 4096 + 1536 * c + 256 * i:4096 + 1536 * c + 256 * (i + 1)] for i in range(2)] for c in range(2)]
    TTf = [CHB[:, 6144:6400], CHB[:, 6400:6656]]
    Z0B, UB, SBw = CHB[:, 6656:6784], CHB[:, 6784:6912], CHB[:, 6912:7040]
    BKTs = [BKT[:, 0:2, :], BKT[:, 2:4, :]]
    SFw = GS[:, 0, 0:128]
    BDm = CF[:, C_BD:C_BD + 128]
    HM = [CF[:, C_HM:C_HM + 1], CF[:, C_HM + 1:C_HM + 2]]
    hs = lambda h: slice(h * 64, (h + 1) * 64)

    def interleave(lists):
        lists = [l for l in lists if l]
        pos = [0] * len(lists)
        n = max(len(l) for l in lists) if lists else 0
        for step in range(n):
            for li, l in enumerate(lists):
                tgt = (step + 1) * len(l) // n
                while pos[li] < tgt:
                    l[pos[li]]()
                    pos[li] += 1

    KT = [UNI[:, 512 * i:512 * (i + 1)] for i in range(3)]
    ET = [FB[:, 512 * i:512 * (i + 1)] for i in range(2)]

    def start_rk(jj, banks=(2, 3, 4, 5), kb=1):
        vcol = lambda base: VEC[:, base + jj:base + jj + 1]
        G = []

        def g_load():
            load_w(d_ewin, 4096 + 0 * 1024 + jj * 128, 0)
            load_w(d_ewin, 4096 + 1 * 1024 + jj * 128, 1)
            ld(W2S[:], d_w2[:, :, jj * 128:(jj + 1) * 128], ['WS'])
            copy('pool', W2B[:], W2S[:], ['WS'], ['W2B'])
        G.append(g_load)
        G.append(lambda: proj(0, lambda t4, tsl, ps, pr: copy(rr(), Rb[:, tsl], ps, [pr], ['FA']), banks))
        G.append(lambda: proj(1, lambda t4, tsl, ps, pr: copy(rr(), Kb[:, tsl], ps, [pr], ['FA']), banks))

        def g_kk(t4):
            def g():
                tsl = slice(t4 * 512, (t4 + 1) * 512)
                o0 = ('OPN', 0)
                sqb = KT[1].bitcast(BF16)[:, 0:512]
                act(KT[0][:], Kb[:, tsl], ACT.Copy, ['FA', 'VEC'], [o0], scale=vcol(V_KK))
                act(sqb, KT[0][:], ACT.Square, [], [o0])
                mm(PS[kb][:], BDB[:], sqb, True, True, [o0, 'BDB'], [('ps', kb)])
                act(KT[2][:], PS[kb][:], ACT.Sqrt, [('ps', kb)], [o0])
                ts('dve', KT[2][:], KT[2][:], 1e-12, None, ALU.max, None, [], [o0])
                P.op('dve', lambda e: e.reciprocal(out=KT[2][:], in_=KT[2][:]), reads=[], writes=[o0])
                tt('pool', KKf[:, tsl], KT[0][:], KT[2][:], ALU.mult, [o0], ['KKf'])
            return g
        for t4 in range(4):
            G.append(g_kk(t4))
        return G

    def start_vz(jj):
        G = []

        def g_l():
            load_w(d_ewin, 4096 + 2 * 1024 + jj * 128, 2)
            load_w(d_ewin, 4096 + 3 * 1024 + jj * 128, 3)
        G.append(g_l)
        G.append(lambda: proj(2, lambda t4, tsl, ps, pr: copy(rr(), BA[0][:, tsl], ps, [pr], ['BA0'])))
        G.append(lambda: proj(3, lambda t4, tsl, ps, pr: act(BA[1][:, tsl], ps, ACT.Silu, [pr], ['BA1'])))

        def g_vt(g):
            def f():
                for k in range(4):
                    mm(PS[4][:, k * 128:(k + 1) * 128], BA[0][:, (4 * g + k) * 128:(4 * g + k + 1) * 128], IDB[:], True, True,
                       ['BA0', 'IDB'], [('ps', 4)])
                copy('act', VTG[:, 4 * g:4 * g + 4, 0:128], c3(PS[4][:]), [('ps', 4)], ['VTG'])
            return f
        for g in range(4):
            G.append(g_vt(g))
        return G

    NRK_AT = {26: [0], 27: [1], 28: [2], 29: [3], 30: [4, 5], 31: [6]}
    KICK = 2

    def pair_chain(jj, vz, nrk=None):
        vcol = lambda base: VEC[:, base + jj:base + jj + 1]
        if True:
            blocks = [(0, b) for b in range(4)] + [(1, b) for b in range(3, -1, -1)]
            chunks = [(0, b, k) for b in range(4) for k in range(4)] + [(1, b, k) for b in range(3, -1, -1) for k in range(3, -1, -1)]
            mab = lambda z: CB[:, B_MAB + 512 * z:B_MAB + 512 * z + 512]
            mnm = lambda z: CB[:, B_MN + 256 * z:B_MN + 256 * z + 256]

            def init_state(z, jj=jj):
                def g():
                    ld(SFw, d_sw[:, jj, z, :], ['SFw'])
                    copy('pool', SBw, SFw, ['SFw'], ['SBw'])
                return g

            def prep_groups(bi, jj=jj, vcol=vcol):
                z, blk = blocks[bi]
                par = bi % 2
                O_ = OPN[par]
                opr = ('OPN', par)
                bkt = BKTs[par]
                bkr = ('BKT', par)
                tsl = slice(blk * 512, (blk + 1) * 512)
                SIG, CSw, CRw, CSBw, AI, KM, BV = TB[0], TB[1], TB[3], TB[4], TB[9], TB[7], TB[8]
                rAI, rBV = ('WB', 3), ('WB', 2)
                if bi == 0:
                    AI, BV = FB[:, 1024:1536], FB[:, 1536:2048]
                    rAI = rBV = ('OPN', 1)
                E1, E3 = TB[2], TB[6]
                if z == 0:
                    inc, ex, rest = (CSw, ('T', 1)), (SIG, ('T', 0)), (CRw, ('T', 3))
                else:
                    inc, ex, rest = (CSBw, 'WS'), (CRw, ('T', 3)), (SIG, ('T', 0))
                def w0():
                    mm(PS[4][:], W2B[:, z, :], BA[2][:, tsl], True, True, ['W2B', 'BA2'], [('ps', 4)])
                    mm(PS[5][:], W2B[:, 2 + z, :], BA[3][:, tsl], True, True, ['W2B', 'BA3'], [('ps', 5)])

                def w1():
                    act(SIG[:], PS[4][:], ACT.Sigmoid, [('ps', 4), 'VEC'], [('T', 0)], bias=vcol(V_W0 + 8 * z))
                    act(AI[:], PS[5][:], ACT.Sigmoid, [('ps', 5), 'VEC'], [rAI], bias=vcol(V_A0 + 8 * z))

                def w2():
                    P.op('dve', lambda e: e.tensor_tensor_scan(out=CSw[:], data0=CB[:, B_RST:B_RST + 512], data1=SIG[:],
                                                               initial=0.0, op0=ALU.mult, op1=ALU.add),
                         reads=[('T', 0), 'CB'], writes=[('T', 1)])
                    act(E3[:], AI[:], ACT.Identity, [rAI, 'VEC', 'OMKA'], [('WB', 0)], bias=OMKA[:, jj:jj + 1], scale=vcol(V_KA))
                    stt(BV[:], KKf[:, tsl], -1.0, AI[:], ALU.mult, ALU.mult, ['KKf', rAI], [rBV])

                def w3():
                    totb = bass.AP(CSw, 127, [[512, 128], [128, 4], [0, 128]])
                    tt('pool', c3(CRw[:]), totb, c3(CSw[:]), ALU.subtract, [('T', 1)], [('T', 3)])
                    act(WLW[:, z, blk * 4:blk * 4 + 4], bass.AP(CSw, 127, [[512, 128], [128, 4]]), ACT.Exp, [('T', 1)], ['WLW'], scale=-LAM)
                    tt('pool', KM[:], Kb[:, tsl], E3[:], ALU.mult, ['FA', ('WB', 0)], [('WB', 1)])

                def w4():
                    if z == 1:
                        tt('pool', CSBw[:], CRw[:], SIG[:], ALU.add, [('T', 3), ('T', 0)], ['WS'])
                    tt('pool', SIG[:], CSw[:], SIG[:], ALU.subtract, [('T', 1), ('T', 0)], [('T', 0)])
                    if z == 0:
                        tt('pool', PRS[:, tsl], Rb[:, tsl], KM[:], ALU.mult, ['FA', ('WB', 1)], ['PRS'])
                    else:
                        tt('pool', E3[:], Rb[:, tsl], KM[:], ALU.mult, ['FA', ('WB', 1)], [('WB', 0)])
                        tt('pool', PRS[:, tsl], PRS[:, tsl], E3[:], ALU.add, ['PRS', ('WB', 0)], ['PRS'])

                def w5():
                    act(E1[:], ex[0][:], ACT.Exp, [ex[1]], [('T', 2)], scale=-LAM)
                    act(E3[:], inc[0][:], ACT.Exp, [inc[1]], [('WB', 0)], scale=-LAM)

                def w6():
                    tt('pool', O_['AR4'][:, :, 0, :], c3(KKf[:, tsl]), c3(E1[:]), ALU.mult, ['KKf', ('T', 2)], [opr])
                    for h in range(2):
                        stt(O_['Am'][h], KKf[:, tsl], HM[h], E1[:], ALU.mult, ALU.mult, ['KKf', 'CF', ('T', 2)], [opr])
                    tt('pool', O_['AR4'][:, :, 1, :], c3(Rb[:, tsl]), c3(E3[:]), ALU.mult, ['FA', ('WB', 0)], [opr])

                def w7():
                    act(E1[:], rest[0][:], ACT.Exp, [rest[1]], [('T', 2)], scale=-LAM)
                    act(E3[:], inc[0][:], ACT.Exp, [inc[1]], [('WB', 0)], scale=LAM)

                def w8():
                    for h in range(2):
                        stt(O_['Btm'][h], BV[:], HM[h], E3[:], ALU.mult, ALU.mult, [rBV, 'CF', ('WB', 0)], [opr])
                        stt(O_['Ktm'][h], KM[:], HM[h], E3[:], ALU.mult, ALU.mult, [('WB', 1), 'CF', ('WB', 0)], [opr])
                    tt('pool', O_['Bh'], BV[:], E1[:], ALU.mult, [rBV, ('T', 2)], [opr])
                    tt('pool', O_['Kh'], KM[:], E1[:], ALU.mult, [('WB', 1), ('T', 2)], [opr])

                def w9():
                    for k in range(4):
                        mm(PS[4][:, k * 128:(k + 1) * 128], O_['Bh'][:, k * 128:(k + 1) * 128], IDB[:], True, True, [opr, 'IDB'], [('ps', 4)])

                def w10():
                    copy('act', bkt[:, 0, :], PS[4][:], [('ps', 4)], [bkr])
                    for k in range(4):
                        mm(PS[5][:, k * 128:(k + 1) * 128], O_['Kh'][:, k * 128:(k + 1) * 128], IDB[:], True, True, [opr, 'IDB'], [('ps', 5)])

                def w11():
                    copy('act', bkt[:, 1, :], PS[5][:], [('ps', 5)], [bkr])
                return [w0, w1, w2, w3, w4, w5, w6, w7, w8, w9, w10, w11]

            def a_groups(ci):
                bi, (z, blk, k) = ci // 4, chunks[ci]
                MAB, MNm = mab(z), mnm(z)
                par, q, m, cx = bi % 2, ci % 2, ci % 3, ci % 2
                O_ = OPN[par]
                opr = ('OPN', par)
                ksl = slice(k * 128, (k + 1) * 128)
                ARk = O_['AR'][:, k * 256:(k + 1) * 256]
                nm0, nm1 = NM0[m], NM1[m]
                LV, PPp = LVs[cx], PPs[cx]
                na, nb = (6, 7) if cx == 0 else (0, 1)
                PA_, PB_ = PS[na], PS[nb]
                ra, rb = ('ps', na), ('ps', nb)
                lvp = lambda i: ('LVp', cx, i)
                lvt = lambda i: ('LVt', cx, i)
                ppr = lambda i: ('PPp', cx, i)
                G = []

                def g0():
                    for h in range(2):
                        mm(PA_[:, h * 256:(h + 1) * 256], O_['Btm'][h][:, ksl], ARk, True, True, [opr], [ra])
                        mm(PB_[:, h * 256:(h + 1) * 256], O_['Ktm'][h][:, ksl], ARk, True, True, [opr], [rb])
                        mm(PS[3][:, 256 + h * 128:384 + h * 128], O_['Am'][h][:, ksl], O_['Btm'][h][:, ksl], True, True, [opr], [('ps', 3)])
                    tt('dve', nm0, PA_[:], MAB, ALU.mult, [ra, 'CB'], [('NM0', m)])
                    tt('dve', PPp[0], PS[3][:, 256:512], MNm, ALU.mult, [('ps', 3), 'CB'], [ppr(0)])
                    tt('dve', nm1, PB_[:], MAB, ALU.mult, [rb, 'CB'], [('NM1', m)])
                G.append(g0)
                pt0 = nm0.rearrange("p (h s t) -> p h s t", h=2, s=2)[:, :, 0, :]

                def g1():
                    idb2 = bass.AP(IDB, 0, [[128, 128], [0, 2], [1, 128]])
                    tt('pool', LV[1][:, :, 1, :], pt0, idb2, ALU.add, [('NM0', m), 'IDB'], [lvt(1)])
                    for h in range(2):
                        mm(PA_[:, h * 128:(h + 1) * 128], nm0[:, h * 256:h * 256 + 128], PPp[0][:, h * 128:(h + 1) * 128], True, True,
                           [('NM0', m), ppr(0)], [ra])
                        mm(PB_[:, h * 256:h * 256 + 128], PPp[0][:, h * 128:(h + 1) * 128], nm0[:, h * 256:h * 256 + 128], True, True,
                           [('NM0', m), ppr(0)], [rb])
                    copy('act', PPp[1], PA_[:, 0:256], [ra], [ppr(1)])
                    copy('dve', LV[1][:, :, 0, :], PB_[:].rearrange("p (h s t) -> p h s t", h=2, s=2)[:, :, 0, :], [rb], [lvp(1)])
                G.append(g1)

                def lvl(kk_):
                    def g():
                        a, b = kk_ % 2, (kk_ + 1) % 2
                        psb = PB_[:].rearrange("p (h s t) -> p h s t", h=2, s=2)
                        for h in range(2):
                            pk = PPp[a][:, h * 128:(h + 1) * 128]
                            if kk_ < 5:
                                mm(PB_[:, h * 256:(h + 1) * 256], pk, LV[a][:, h, :, :].rearrange("p s t -> p (s t)"), True, True,
                                   [ppr(a), lvp(a), lvt(a)], [rb])
                            else:
                                mm(PB_[:, h * 256 + 128:(h + 1) * 256], pk, LV[a][:, h, 1, :], True, True, [ppr(a), lvt(a)], [rb])
                            mm(PA_[:, h * 128:(h + 1) * 128], LV[a][:, h, 0, :], pk, True, True, [ppr(a), lvp(a)], [ra])
                        copy('act', PPp[b], PA_[:, 0:256], [ra], [ppr(b)])
                        if kk_ < 5:
                            copy('dve', LV[b][:, :, 0, :], psb[:, :, 0, :], [rb], [lvp(b)])
                        tt('dve', LV[b][:, :, 1, :], psb[:, :, 1, :], LV[a][:, :, 1, :], ALU.add, [rb, lvt(a)], [lvt(b)])
                    return g
                for kk_ in range(1, 6):
                    G.append(lvl(kk_))

                def g7():
                    for h in range(2):
                        mm(PB_[:, h * 128:(h + 1) * 128], PPp[0][:, h * 128:(h + 1) * 128], LV[0][:, h, 1, :], True, True,
                           [ppr(0), lvt(0)], [rb])
                    tt('dve', TTf[q].rearrange("p (h t) -> p h t", h=2), PB_[:, 0:256].rearrange("p (h t) -> p h t", h=2),
                       LV[0][:, :, 1, :], ALU.add, [rb, lvt(0)], [('TTf', q)])
                G.append(g7)
                return G

            def b_groups(ci, jj=jj):
                bi, (z, blk, k) = ci // 4, chunks[ci]
                par, q, m = bi % 2, ci % 2, ci % 3
                O_ = OPN[par]
                opr = ('OPN', par)
                bkt, bkr = BKTs[par], ('BKT', par)
                c16 = blk * 4 + k
                ksl = slice(k * 128, (k + 1) * 128)
                ARk = O_['AR'][:, k * 256:(k + 1) * 256]
                nm0, nm1, TT = NM0[m], NM1[m], TTf[q]
                vt = lambda h: VTG[:, c16, h * 64:(h + 1) * 64]
                G = []

                def g0():
                    mm(PS[2][:, 0:128], ARk[:, 0:128], SBw, True, False, [opr, 'SBw'], [('ps', 2)])
                    for h in range(2):
                        mm(PS[2][:, hs(h)], nm1[:, h * 256:h * 256 + 128], vt(h), False, h == 1, [('NM1', m), 'VTG'], [('ps', 2)])
                    copy('act', Z0B, PS[2][:, 0:128], [('ps', 2)], ['Z0B'])
                G.append(g0)

                def g1():
                    for h in range(2):
                        mm(PS[2][:, 128 + h * 64:192 + h * 64], TT[:, h * 128:(h + 1) * 128], Z0B[:, hs(h)], True, True,
                           [('TTf', q), 'Z0B'], [('ps', 2)])
                    copy('act', UB, PS[2][:, 128:256], [('ps', 2)], ['UB'])
                G.append(g1)

                def g2():
                    mm(PS[3][:, 0:128], ARk[:, 128:256], SBw, True, False, [opr, 'SBw'], [('ps', 3)])
                    for h in range(2):
                        mm(PS[3][:, hs(h)], nm0[:, h * 256 + 128:h * 256 + 256], UB[:, hs(h)], False, False, [('NM0', m), 'UB'], [('ps', 3)])
                        mm(PS[3][:, hs(h)], nm1[:, h * 256 + 128:h * 256 + 256], vt(h), False, h == 1, [('NM1', m), 'VTG'], [('ps', 3)])
                    mm(PS[2][:, 256:384], bkt[:, 0, ksl], UB, True, False, [bkr, 'UB'], [('ps', 2)])
                    mm(PS[2][:, 256:384], bkt[:, 1, ksl], VTG[:, c16, 0:128], False, True, [bkr, 'VTG'], [('ps', 2)])
                G.append(g2)

                def g3():
                    tmpw = GS[:, 1 + (c16 % 2), 0:128]
                    tr = ('TMPw', c16 % 2)
                    stt(tmpw, SFw, WLW[:, z, c16:c16 + 1], PS[2][:, 256:384], ALU.mult, ALU.add, ['SFw', 'WLW', ('ps', 2)], [tr])
                    stt(SFw, tmpw, MSK[:, 2 + z, c16:c16 + 1], BDm, ALU.mult, ALU.mult, [tr, 'MSK', 'CF'], ['SFw'])
                    copy('act', SBw, SFw, ['SFw'], ['SBw'])
                    if (c16 % 2 == 1) == (z == 0):
                        P.dma('sp', lambda e, tmpw=tmpw, z=z, jj=jj, c16=c16: e.dma_start(out=o_sw[c16 // 2, z, jj], in_=tmpw), reads=[tr])
                    ofc = VTG[:, c16, 128:256]
                    vo = ('VO', c16)
                    if z == 0:
                        copy('act', ofc, PS[3][:, 0:128], [('ps', 3)], [vo])
                    else:
                        to, sqo = GS[:, 0, 128:256], GS[:, 1, 128:256]
                        h3 = lambda ap: ap.rearrange("p (g w) -> p g w", w=64)
                        gb = lambda off: bass.AP(GNS, off, [[8, 128], [1, 2], [0, 64]])
                        tt('dve', to, ofc, PS[3][:, 0:128], ALU.add, [('ps', 3), vo], ['TO'])
                        P.op('dve', lambda e: e.tensor_reduce(out=GNS[:, 0:2], in_=h3(to), axis=AX.X, op=ALU.add),
                             reads=['TO'], writes=['GNS'])
                        tt('pool', sqo, to, to, ALU.mult, ['TO'], ['SQO'])
                        P.op('dve', lambda e: e.tensor_reduce(out=GNS[:, 2:4], in_=h3(sqo), axis=AX.X, op=ALU.add),
                             reads=['SQO'], writes=['GNS'])
                        ts('dve', GNS[:, 0:2], GNS[:, 0:2], 1.0 / 64, None, ALU.mult, None, ['GNS'], ['GNS'])
                        tt('dve', GNS[:, 4:6], GNS[:, 0:2], GNS[:, 0:2], ALU.mult, ['GNS'], ['GNS'])
                        stt(GNS[:, 2:4], GNS[:, 2:4], 1.0 / 64, GNS[:, 4:6], ALU.mult, ALU.subtract, ['GNS'], ['GNS'])
                        act(GNS[:, 2:4], GNS[:, 2:4], ACT.Ln, ['GNS', 'EPS'], ['GNS'], bias=EPS[:, 1:2], scale=1.0)
                        act(GNS[:, 2:4], GNS[:, 2:4], ACT.Exp, ['GNS'], ['GNS'], scale=-0.5)
                        tt('pool', h3(to), h3(to), gb(0), ALU.subtract, ['TO', 'GNS'], ['TO'])
                        tt('pool', h3(ofc), h3(to), gb(2), ALU.mult, ['TO', 'GNS'], [vo])
                G.append(g3)
                return G

            NCK = 32
            init_state(0)()
            AG = {0: a_groups(0), 1: a_groups(1)}
            PG = {0: prep_groups(0), 1: prep_groups(1)}
            lock = [(lambda i=i: (AG[0][i](), AG[1][i]() if i < 4 else None)) for i in range(8)]
            interleave([vz, PG[0] + lock])
            for ci in range(NCK):
                bg = b_groups(ci)
                if ci == 16:
                    bg = [init_state(1)] + bg
                lists = [bg]
                if ci + 1 < NCK:
                    lists.append(AG[ci + 1][4:])
                if ci + 2 < NCK:
                    AG[ci + 2] = a_groups(ci + 2)
                    lists.append(AG[ci + 2][:4])
                b_, r_ = ci // 4, ci % 4
                if r_ < 2 and b_ + 1 < 8:
                    if b_ + 1 not in PG:
                        PG[b_ + 1] = prep_groups(b_ + 1)
                    if b_ == 0:
                        lists.append(PG[1][6 * r_:6 * r_ + 6])
                    else:
                        lists.append(PG[b_ + 1][6 + 3 * r_:9 + 3 * r_])
                elif r_ >= 2 and b_ + 2 < 8:
                    if b_ + 2 not in PG:
                        PG[b_ + 2] = prep_groups(b_ + 2)
                    lists.append(PG[b_ + 2][3 * (r_ - 2):3 * (r_ - 2) + 3])
                def kick():
                    for i in range(KICK):
                        mm(PS[5][:], IDB[:], HT[:, i % 8, 0:512], True, True, ['IDB', ('HT', i % 8)], [('ps', 5)])
                lists.append([kick])
                if nrk is not None and ci in NRK_AT:
                    lists.append([nrk[i] for i in NRK_AT[ci]])
                interleave(lists)

    def pair_end(jj):
        vcol = lambda base: VEC[:, base + jj:base + jj + 1]
        G = []
        o1 = ('OPN', 1)

        def g_t(t4):
            def g():
                tsl = slice(t4 * 512, (t4 + 1) * 512)
                for k in range(4):
                    mm(PS[4][:, k * 128:(k + 1) * 128], VTG[:, t4 * 4 + k, 128:256], IDB[:], True, True,
                       [('VO', t4 * 4 + k), 'IDB'], [('ps', 4)])
                ts('dve', TB[0][:], PS[4][:], vcol(V_LW), vcol(V_LB), ALU.mult, ALU.add, [('ps', 4), 'VEC'], [('T', 0)])
                etb = ET[0].bitcast(BF16)[:, 0:512]
                act(etb, PRS[:, tsl], ACT.Copy, ['PRS', 'VEC'], [o1], scale=vcol(V_RK))
                mm(PS[5][:], BDB[:], etb, True, True, [o1, 'BDB'], [('ps', 5)])
                tt('dve', ET[1][:], PS[5][:], BA[0][:, tsl], ALU.mult, [('ps', 5), 'BA0'], [o1])
                tt('pool', TB[0][:], TB[0][:], ET[1][:], ALU.add, [('T', 0), o1], [('T', 0)])
                tt('pool', BA[1][:, tsl], TB[0][:], BA[1][:, tsl], ALU.mult, [('T', 0), 'BA1'], ['BA1'])
            return g
        for t4 in range(4):
            G.append(g_t(t4))

        def g_w():
            ld(WS[:].rearrange("p a b -> p (a b)"), d_ewout[:, 8 + jj, :], ['WS'])
            copy('pool', WOB[:], WS[:].rearrange("p a b -> p (a b)"), ['WS'], [('WB', 3)])

        def g_o(t4, half):
            def g():
                tsl = slice(t4 * 512, (t4 + 1) * 512)
                for ft in range(4 * half, 4 * half + 4):
                    pb = 2 + (cnt['rr'] % 4)
                    cnt['rr'] += 1
                    mm(PS[pb][:], WOB[:, ft * 128:(ft + 1) * 128], BA[1][:, tsl], True, True, [('WB', 3), 'BA1'], [('ps', pb)])
                    if (ft * 4 + t4) % 5 < 3:
                        stt(X[:, ft, tsl], PS[pb][:], MOD[:, 0, 16 + ft:17 + ft], X[:, ft, tsl], ALU.mult, ALU.add,
                            [('ps', pb), 'MOD', ('X', ft)], [('X', ft)])
                    else:
                        wt = WOT[ft % 2]
                        act(wt[:], PS[pb][:], ACT.Copy, [('ps', pb), 'MOD'], [('T', 2 + ft % 2)], scale=MOD[:, 0, 16 + ft:17 + ft])
                        tt('pool', X[:, ft, tsl], X[:, ft, tsl], wt[:], ALU.add, [('T', 2 + ft % 2), ('X', ft)], [('X', ft)])
            return g
        G = [g_w, G[0], G[1], g_o(0, 0), g_o(0, 1), G[2], g_o(1, 0), g_o(1, 1), G[3], g_o(2, 0), g_o(2, 1), g_o(3, 0), g_o(3, 1)]
        return G

    for g in start_rk(0):
        g()
    vz = start_vz(0)
    for jj in range(8):
        nrk = start_rk(jj + 1, banks=(4, 5), kb=4) if jj + 1 < 8 else None
        pair_chain(jj, vz, nrk)
        for g in pair_end(jj):
            g()
        if jj + 1 < 8:
            vz = start_vz(jj + 1)
    P.barrier()

    norm_mod(lambda c: G1[:, 1, c:c + 1], lambda c: MOD[:, 1, c:c + 1], lambda c, tsl: HT[:, c, tsl],
             lambda c, t4: [('HT', c)])
    ld(WS[:].rearrange("p a b -> p (a b)")[:, 0:256], d_g1.rearrange('p a b -> p (a b)'), ['WS'])
    copy('pool', G1B[:].rearrange('p a b -> p (a b)'), WS[:].rearrange("p a b -> p (a b)")[:, 0:256], ['WS'], ['G1B'])
    ld(WS[:].rearrange("p a b -> p (a b)")[0:32, :], d_g2.rearrange('p a b -> p (a b)'), ['WS'])
    copy('pool', G2B[:].rearrange('p a b -> p (a b)'), WS[:].rearrange("p a b -> p (a b)")[0:32, :], ['WS'], ['G2B'])
    ts('pool', NGB[:], VEC[:, V_GB:V_GB + 8], -1.0, None, ALU.mult, None, ['VEC'], ['NGB'])
    for t4 in range(4):
        tsl = slice(t4 * 512, (t4 + 1) * 512)
        for c in range(8):
            mm(PS[0][0:32, :], G1B[:, c, :], HT[:, c, tsl], c == 0, c == 7, ['G1B', ('HT', c)], [('ps', 0)])
        copy('act', GT1[:, tsl], PS[0][0:32, :], [('ps', 0)], ['GT1'])
    SP, CSg, CR0, INC, RST, EXg = T512[:6]
    g2 = ub(2048, 3328)
    GSET = [(BA[0][:, 0:512], BA[0][:, 512:1024], BA[0][:, 1024:1536], BA[1][:, 0:512],
             BA[1][:, 512:1024].rearrange('p (k t) -> p k t', k=4)),
            (g2[:, 0:512], g2[:, 512:1024], g2[:, 1024:1536], g2[:, 1536:2048],
             g2[:, 2048:2560].rearrange('p (k t) -> p k t', k=4))]
    GEX = [UNI[:, 4096 + 512 * i:4096 + 512 * (i + 1)] for i in range(3)]
    GLTf = KKF[:].bitcast(F32)
    GLT = [GLTf[:, 0:512], GLTf[:, 512:1024]]
    SBg = BA[1][:, 1024:1280]
    SFg = GS[:, 0, :]
    for hd in range(4):
        load_w(d_owin, hd * 128, 0)
        load_w(d_owin, 512 + hd * 128, 1)
        load_w(d_owin, 1024 + hd * 256, 2)
        load_w(d_owin, 1024 + hd * 256 + 128, 3)
        proj(0, lambda t4, tsl, ps, pr: copy(rr(), FA[:, tsl], ps, [pr], ['FA']))
        proj(1, lambda t4, tsl, ps, pr: copy(rr(), FB[:, tsl], ps, [pr], ['FB']))
        vgr = []
        for half in range(2):
            vgr.append(lambda half=half: proj(2 + half, lambda t4, tsl, ps, pr: copy(rr(), BA[2][:, tsl], ps, [pr], ['BA2'])))

            def vt(g, half=half):
                def f():
                    for k in range(4):
                        mm(PS[1][:, k * 128:(k + 1) * 128], BA[2][:, (4 * g + k) * 128:(4 * g + k + 1) * 128], IDB[:], True, True,
                           ['BA2', 'IDB'], [('ps', 1)])
                    copy('act', VTG[:, 4 * g:4 * g + 4, half * 128:(half + 1) * 128], PS[1][:].rearrange("p (k t) -> p k t", k=4),
                         [('ps', 1)], [('VTG', 4 * g + i) for i in range(4)])
                return f
            for g in range(4):
                vgr.append(vt(g))
        gblocks = [(0, b) for b in range(4)] + [(1, b) for b in range(3, -1, -1)]

        def g_init(z, hd=hd):
            def g():
                ld(SFg, d_sg[:, hd, z, :], ['SFg'])
                copy('pool', SBg, SFg, ['SFg'], ['SBg', 'BA1'])
            return g

        def g_prep(bi, hd=hd):
            z, blk = gblocks[bi]
            sb_ = bi % 2
            QE, KE, KD, KDT, ATT4 = GSET[sb_]
            rq, rk, rd, rt, ra_ = (('gQE', sb_), ('gKE', sb_), ('gKD', sb_), ('gKDT', sb_), ('gATT', sb_))
            MB = [(PS[0], 0), (PS[1], 1)] if sb_ == 0 else [(PS[2], 2), (PS[5], 5)]
            tsl = slice(blk * 512, (blk + 1) * 512)
            EA, EB, EC = GEX
            if z == 0:
                inc, rst, ri, rr_ = CSg, CR0, ('T', 1), ('T', 2)
            else:
                inc, rst, ri, rr_ = INC, RST, ('T', 3), 'WS'

            def w0():
                mm(PS[4][:], G2B[:, z, hd * 128:(hd + 1) * 128], GT1[:, tsl], True, True, ['G2B', 'GT1'], [('ps', 4)])

            def w1():
                act(EXg[:], PS[4][:], ACT.Exp, [('ps', 4), 'NGB'], ['WS'], bias=NGB[:, 4 * z + hd:4 * z + hd + 1], scale=-1.0)

            def w2():
                act(SP[:], EXg[:], ACT.Ln, ['WS', 'CF'], [('T', 0)], bias=CF[:, C_ONE:C_ONE + 1], scale=1.0)

            def w3():
                P.op('dve', lambda e: e.tensor_tensor_scan(out=CSg[:], data0=CB[:, B_RST:B_RST + 512], data1=SP[:],
                                                           initial=0.0, op0=ALU.mult, op1=ALU.add),
                     reads=[('T', 0), 'CB'], writes=[('T', 1)])

            def w4():
                totb = bass.AP(CSg, 127, [[512, 128], [128, 4], [0, 128]])
                cs3 = bass.AP(CSg, 0, [[512, 128], [128, 4], [1, 128]])
                cr3 = bass.AP(CR0, 0, [[512, 128], [128, 4], [1, 128]])
                tt('pool', cr3, totb, cs3, ALU.subtract, [('T', 1)], [('T', 2)])
                tot4 = bass.AP(CSg, 127, [[512, 128], [128, 4]])
                act(WLG[:, z, blk * 4:blk * 4 + 4], tot4, ACT.Exp, [('T', 1)], ['WLG'], scale=-1.0 / 16)
                if z == 1:
                    tt('pool', INC[:], CR0[:], SP[:], ALU.add, [('T', 2), ('T', 0)], [('T', 3)])
                    tt('pool', RST[:], CSg[:], SP[:], ALU.subtract, [('T', 1), ('T', 0)], ['WS'])

            def w5():
                act(EA, inc[:], ACT.Exp, [ri], [('gE', 0)], scale=-1.0 / 16)
                act(EB, inc[:], ACT.Exp, [ri], [('gE', 1)], scale=1.0 / 16)
                act(EC, rst[:], ACT.Exp, [rr_], [('gE', 2)], scale=-1.0 / 16)

            def w6():
                stt(QE, FA[:, tsl], 128 ** -0.5, EA, ALU.mult, ALU.mult, ['FA', ('gE', 0)], [rq])
                tt('dve', KE, FB[:, tsl], EB, ALU.mult, ['FB', ('gE', 1)], [rk])
                tt('pool', KD, FB[:, tsl], EC, ALU.mult, ['FB', ('gE', 2)], [rd])

            def w7():
                for k in range(4):
                    ksl = slice(k * 128, (k + 1) * 128)
                    mm(PS[6][:, ksl], KE[:, ksl], QE[:, ksl], True, True, [rk, rq], [('ps', 6)])
                for k in range(4):
                    mm(PS[4][:, k * 128:(k + 1) * 128], KD[:, k * 128:(k + 1) * 128], IDB[:], True, True, [rd, 'IDB'], [('ps', 4)])

            def w8():
                mg = bass.AP(CB, B_MG + 128 * z, [[NCB, 128], [0, 4], [1, 128]])
                tt('dve', ATT4, PS[6][:].rearrange("p (k t) -> p k t", k=4), mg, ALU.mult, [('ps', 6), 'CB'], [ra_])
                copy('act', KDT, PS[4][:], [('ps', 4)], [rt])

            def w9():
                for k in range(4):
                    ksl = slice(k * 128, (k + 1) * 128)
                    mb, mbn = MB[k // 2]
                    mm(mb[:, (k % 2) * 256:(k % 2 + 1) * 256], KDT[:, ksl], VTG[:, blk * 4 + k, :], True, True,
                       [rt, ('VTG', blk * 4 + k)], [('ps', mbn)])
            return [w0, w1, w2, w3, w4, w5, w6, w7, w8, w9]

        def g_chain(bi, hd=hd):
            z, blk = gblocks[bi]
            sb_ = bi % 2
            QE, KE, KD, KDT, ATT4 = GSET[sb_]
            rq, ra_ = ('gQE', sb_), ('gATT', sb_)
            MB = [(PS[0], 0), (PS[1], 1)] if sb_ == 0 else [(PS[2], 2), (PS[5], 5)]
            G = []

            def ch(k):
                def g():
                    c16 = blk * 4 + k
                    ksl = slice(k * 128, (k + 1) * 128)
                    mb, mbn = MB[k // 2]
                    ob, obn = (PS[7], 7) if k % 2 == 0 else (PS[3], 3)
                    mm(ob[:, 0:256], ATT4[:, k, :], VTG[:, c16, :], True, False, [ra_, ('VTG', c16)], [('ps', obn)])
                    mm(ob[:, 0:256], QE[:, ksl], SBg, False, True, [rq, 'SBg'], [('ps', obn)])
                    tmpg = GS[:, 1 + (c16 % 2), :]
                    tr = ('TMPg', c16 % 2)
                    stt(tmpg, SFg, WLG[:, z, c16:c16 + 1], mb[:, (k % 2) * 256:(k % 2 + 1) * 256], ALU.mult, ALU.add,
                        ['SFg', 'WLG', ('ps', mbn)], [tr])
                    ts('dve', SFg, tmpg, MSK[:, 2 + z, c16:c16 + 1], None, ALU.mult, None, [tr, 'MSK'], ['SFg'])
                    act(SBg, tmpg, ACT.Copy, [tr, 'MSK'], ['SBg'], scale=MSK[:, 2 + z, c16:c16 + 1])
                    if (c16 % 2 == 1) == (z == 0):
                        P.dma('sp', lambda e, z=z, hd=hd, c16=c16, tmpg=tmpg: e.dma_start(out=o_sg[c16 // 2, z, hd], in_=tmpg), reads=[tr])
                    obf = BA[2 + c16 // 8][:, (c16 % 8) * 256:(c16 % 8 + 1) * 256]
                    obr = 'BA2' if c16 < 8 else 'BA3'
                    if z == 0:
                        copy('act', obf, ob[:, 0:256], [('ps', obn)], [obr])
                    else:
                        tog, sqg = GLT[c16 % 2][:, 0:256], GLT[c16 % 2][:, 256:512]
                        gr = ('GLT', c16 % 2)
                        tt('dve', tog, obf, ob[:, 0:256], ALU.add, [('ps', obn), obr], [gr])
                        tt('pool', sqg, tog, tog, ALU.mult, [gr], [gr])
                        P.op('dve', lambda e, sqg=sqg, c16=c16: e.tensor_reduce(out=RSG[:, c16:c16 + 1], in_=sqg, axis=AX.X, op=ALU.add),
                             reads=[gr], writes=[('RSG', c16)])
                        act(RSG[:, c16:c16 + 1], RSG[:, c16:c16 + 1], ACT.Ln, [('RSG', c16), 'EPS'], [('RSG', c16)], bias=EPS[:, 0:1], scale=1.0 / 256)
                        act(RSG[:, c16:c16 + 1], RSG[:, c16:c16 + 1], ACT.Exp, [('RSG', c16)], [('RSG', c16)], scale=-0.5)
                        act(VTG[:, c16, :], tog, ACT.Copy, [gr, ('RSG', c16)], [('VTG', c16)], scale=RSG[:, c16:c16 + 1])
                return g
            for k in (range(4) if z == 0 else range(3, -1, -1)):
                G.append(ch(k))
            return G

        p0_ = g_prep(0)
        interleave([vgr, [g_init(0)] + p0_[:9]])
        p0_[9]()
        for bi in range(8):
            cg = g_chain(bi)
            if bi == 4:
                cg = [g_init(1)] + cg
            lists = [cg]
            if bi + 1 < 8:
                lists.append(g_prep(bi + 1))
            interleave(lists)
        def half_groups(half, hd=hd):
            slot, SZ, nSZ, TF, nTF, OBh, nOB, pT, nT = ((0, BA[2], 'BA2', FB, 'FB', BA[3], 'BA3', PS[1], 1) if half == 0 else
                                                         (1, BA[0], 'BA0', FA, 'FA', BA[1], 'BA1', PS[0], 0))
            G = []

            def ga():
                load_w(d_owin, 2048 + hd * 256 + half * 128, slot)
                proj(slot, lambda t4, tsl, ps, pr: act(SZ[:, tsl], ps, ACT.Silu, [pr], [nSZ]))
            G.append(ga)

            def gt(t4):
                def f():
                    tsl = slice(t4 * 512, (t4 + 1) * 512)
                    for k in range(4):
                        mm(pT[:, k * 128:(k + 1) * 128], VTG[:, t4 * 4 + k, half * 128:(half + 1) * 128], IDB[:], True, True,
                           [('VTG', t4 * 4 + k), 'IDB'], [('ps', nT)])
                    ts('dve', TF[:, tsl], pT[:], VEC[:, V_GN + half:V_GN + half + 1], None, ALU.mult, None, [('ps', nT), 'VEC'], [nTF])
                return f
            for t4 in range(4):
                G.append(gt(t4))

            def gm():
                tt('pool', OBh[:], TF[:], SZ[:], ALU.mult, [nTF, nSZ], [nOB])
            G.append(gm)
            G.append(lambda: wout_partial(1, d_owout, hd * 2 + half, OBh, nOB))
            return G
        interleave([half_groups(0), half_groups(1)])

    ftsl = [slice(t4 * 512, (t4 + 1) * 512) for t4 in range(4)]
    sumsq_rstd(ftsl[0], 0)
    for t4 in range(4):
        tsl = ftsl[t4]
        if t4 + 1 < 4:
            sumsq_rstd(ftsl[t4 + 1], (t4 + 1) % 2)
        rb_, rn_ = RSTD[t4 % 2]
        for c in range(8):
            stt(FA[:, (c % 4) * 512:(c % 4 + 1) * 512], X[:, c, tsl], VEC[:, V_FG + c:V_FG + c + 1], rb_[:],
                ALU.mult, ALU.mult, [('X', c), 'VEC', rn_], [('FAq', c % 4)])
            P.dma('sp', lambda e, c=c, tsl=tsl: e.dma_start(out=o_y[:, c, tsl], in_=FA[:, (c % 4) * 512:(c % 4 + 1) * 512]),
                  reads=[('FAq', c % 4)])
    P.finish_waits('sp')
    P.emit()
    global _LAST_P
    _LAST_P = P
    st.close()
    return nc


def kernel(**inp):
    f = lambda k: np.asarray(inp[k], np.float32)
    plan = _assign()
    cf, cb = _consts()
    vec0 = np.zeros((128, NV), np.float32)
    vec0[:, V_NG:V_NG + 16] = np.concatenate([_col(f('norm_g')[0]), _col(f('norm_g')[1])], 1)
    vec0[:, V_FG:V_FG + 8] = _col(f('final_g'))
    cw = f('conv_w')[0]
    for k in range(3):
        vec0[:, V_CW + 8 * k:V_CW + 8 * k + 8] = _col(cw[k])
    for z in range(2):
        vec0[:, V_W0 + 8 * z:V_W0 + 8 * z + 8] = _col(f('wkv_w0')[0, z])
        vec0[:, V_A0 + 8 * z:V_A0 + 8 * z + 8] = _col(f('wkv_a0')[0, z])
        vec0[:, V_GB + 4 * z:V_GB + 4 * z + 4] = _col(f('gla_gk_b')[0, z])
    vec0[:, V_KK:V_KK + 8] = _col(f('wkv_k_k')[0])
    vec0[:, V_KA:V_KA + 8] = _col(f('wkv_k_a')[0])
    vec0[:, V_RK:V_RK + 8] = _col(f('wkv_r_k')[0].reshape(-1))
    vec0[:, V_LW:V_LW + 8] = _col(f('wkv_ln_w')[0])
    vec0[:, V_LB:V_LB + 8] = _col(f('wkv_ln_b')[0])
    for l in range(2):
        vec0[:, V_AB + 24 * l:V_AB + 24 * l + 24] = _col(f('ada_b')[l])
    vec0[:, V_GN:V_GN + 2] = _col(f('gla_g_norm')[0])
    r3 = lambda w: np.ascontiguousarray(w.reshape(-1, 128, w.shape[-1]).transpose(1, 0, 2))
    ada = np.stack([r3(f('ada_w')[l]) for l in range(2)])
    ewin = r3(f('e_w_in')[0])
    ewout = r3(f('e_w_out')[0])
    owin = r3(f('o_w_in')[0])
    owout = r3(f('o_w_out')[0])
    w1c = np.concatenate([r3(f('wkv_w1')[0, 0]), r3(f('wkv_w1')[0, 1]), r3(f('wkv_a1')[0, 0]), r3(f('wkv_a1')[0, 1])], 2)
    w2p = np.zeros((128, 4, 1024), np.float32)
    for z in range(2):
        w2p[64 * z:64 * z + 64, z] = f('wkv_w2')[0, z]
        w2p[64 * z:64 * z + 64, 2 + z] = f('wkv_a2')[0, z]
    g1c = np.concatenate([r3(f('gla_gk1')[0, 0]), r3(f('gla_gk1')[0, 1])], 2)
    g2p = np.zeros((32, 2, 512), np.float32)
    for z in range(2):
        g2p[16 * z:16 * z + 16, z] = f('gla_gk2')[0, z]
    xp, xs = f('x_prompt'), f('x_sample')
    swkv, sgla = f('state_wkv'), f('state_gla')
    in_maps = []
    for core, items in enumerate(plan):
        x = np.zeros((NT, 1024), np.float32)
        msk = np.zeros((128, 4, 32), np.float32)
        s_w = np.zeros((128, 8, 2, 128), np.float32)
        s_g = np.zeros((128, 4, 2, 256), np.float32)
        vec = vec0.copy()
        if items[0][0] == 's':
            b = items[0][1]
            x[:] = xs[b]
            cv = f('c')[b]
            msk[:, 2, :16] = 1.0
            msk[:, 3, :16] = 1.0
            for z in range(2):
                for h in range(16):
                    jj, hl = divmod(h, 2)
                    s_w[64 * hl:64 * hl + 64, jj, z, 64 * hl:64 * hl + 64] = swkv[b, 0, z, h].T
                for h in range(4):
                    s_g[:, h, z, :] = sgla[b, 0, z, h]
        else:
            for si in range(8):
                x[256 * si:256 * si + 256] = xp[items[si % len(items)][1]]
            cv = f('c_ctx')
            g = np.arange(32)
            inner = (g % 4 != 0).astype(np.float32)
            msk[:, 0, :] = inner[None]
            msk[:, 1, :] = inner[None]
            c16 = np.arange(16)
            msk[:, 2, :16] = (c16 % 2 == 0)[None]
            msk[:, 3, :16] = (c16 % 2 == 1)[None]
        vec[:, V_CV:V_CV + 8] = _col(cv)
        xT = np.ascontiguousarray(x.reshape(NT, 8, 128).transpose(2, 1, 0))
        in_maps.append(dict(xT=xT, vec=vec, cf=cf, cb=cb.astype(ml_dtypes.bfloat16), msk=msk, ada=ada, ewin=ewin, ewout=ewout, w1c=w1c,
                            w2p=w2p, s_wkv=s_w, owin=owin, owout=owout, g1c=g1c, g2p=g2p, s_gla=s_g))
    nc = build()
    res = run_bass_kernel_spmd(nc, in_maps, core_ids=list(range(8)))
    y_p = np.zeros((16, 256, 1024), np.float32)
    y_s = np.zeros((2, 2048, 1024), np.float32)
    n_w = np.zeros((16, 1, 2, 16, 64, 64), np.float32)
    n_g = np.zeros((16, 1, 2, 4, 128, 256), np.float32)
    for core, items in enumerate(plan):
        r = res.results[core]
        y = np.asarray(r["yT"]).transpose(2, 1, 0).reshape(NT, 1024)
        ow = np.asarray(r["o_wkv"])
        og = np.asarray(r["o_gla"])
        if items[0][0] == 's':
            y_s[items[0][1]] = y
        else:
            for si, (_, pi) in enumerate(items):
                y_p[pi] = y[256 * si:256 * si + 256]
                for z in range(2):
                    for h in range(16):
                        jj, hl = divmod(h, 2)
                        n_w[pi, 0, z, h] = ow[si, z, jj, 64 * hl:64 * hl + 64, 64 * hl:64 * hl + 64].T
                    n_g[pi, 0, z] = og[si, z]
    return (y_p, y_s, n_w, n_g)
```

```python
import contextlib
import numpy as np
import ml_dtypes
import concourse.bass as bass
import concourse.mybir as mybir
from concourse.bass_utils import run_bass_kernel_spmd

ACT = mybir.ActivationFunctionType
ALU = mybir.AluOpType
F32 = mybir.dt.float32
BF16 = mybir.dt.bfloat16
AX = mybir.AxisListType

ENGS = ['pe', 'act', 'dve', 'pool', 'sp']
EPOCH = 4000
NDS = 8
NT = 2048
NCH = 16
LAM = 0.6065306597126334
NORM_EPS = 1e-6
GN_EPS = 64e-5


class Prog:
    def __init__(self, nc):
        self.nc = nc
        self.ops = {e: [] for e in ENGS}
        self.count = {e: 0 for e in ENGS}
        self.dcount = {e: 0 for e in ENGS}
        self.last_w = {}
        self.readers = {}
        self.waited = {e: {} for e in ENGS}
        self.pending = {e: [] for e in ENGS}

    def _deps(self, eng, reads, writes):
        deps = set()
        for r in reads:
            if r in self.last_w:
                deps.add(self.last_w[r])
        for w in writes:
            if w in self.last_w:
                deps.add(self.last_w[w])
            for rd in self.readers.get(w, ()):
                deps.add(rd)
        best = {}
        for d in deps:
            if eng == 'pe' and d[:-1] == ('e', 'pe'):
                continue
            best[d[:-1]] = max(best.get(d[:-1], 0), d[-1])
        for d in self.pending[eng]:
            best[d[:-1]] = max(best.get(d[:-1], 0), d[-1])
        self.pending[eng] = []
        final = []
        for key, i in best.items():
            if self.waited[eng].get(key, 0) < i:
                self.waited[eng][key] = i
                final.append(key + (i,))
        return final

    def _mark(self, tok, reads, writes):
        for r in reads:
            self.readers.setdefault(r, []).append(tok)
        for w in writes:
            self.last_w[w] = tok
            self.readers[w] = []

    def op(self, eng, fn, reads=(), writes=()):
        writes = list(writes) + [r for r in reads if isinstance(r, tuple) and r[0] == 'ps']
        waits = self._deps(eng, reads, writes)
        idx = self.count[eng] + 1
        self.count[eng] = idx
        self.ops[eng].append(('c', fn, waits, idx))
        self._mark(('e', eng, idx), reads, writes)

    def dma(self, eng, fn, reads=(), writes=()):
        waits = self._deps(eng, reads, writes)
        j = self.dcount[eng]
        self.dcount[eng] = j + 1
        slot = j % NDS
        if j >= NDS:
            key = ('d', eng, slot)
            need = j // NDS
            if self.waited[eng].get(key, 0) < need:
                self.waited[eng][key] = need
                waits.append(key + (need,))
        self.ops[eng].append(('d', fn, waits, (slot, j // NDS + 1)))
        self._mark(('d', eng, slot, j // NDS + 1), reads, writes)

    def barrier(self):
        snap = [('e', e, self.count[e]) for e in ENGS if self.count[e]]
        for q in ENGS:
            n = self.dcount[q]
            for slot in range(min(n, NDS)):
                snap.append(('d', q, slot, (n - 1 - slot) // NDS + 1))
        for e in ENGS:
            self.pending[e] = list(snap)

    def finish_waits(self, eng='sp'):
        waits = []
        for q in ENGS:
            n = self.dcount[q]
            for slot in range(min(n, NDS)):
                waits.append(('d', q, slot, (n - 1 - slot) // NDS + 1))
        self.ops[eng].append(('w', None, waits, None))

    def emit(self):
        nc = self.nc
        with contextlib.ExitStack() as st:
            esem = {e: [st.enter_context(nc.semaphore(f"s_{e}_{k}")) for k in range(self.count[e] // EPOCH + 1)]
                    for e in ENGS}
            dsem = {e: [st.enter_context(nc.semaphore(f"d_{e}_{k}")) for k in range(NDS)]
                    for e in ENGS if self.dcount[e]}
            block = st.enter_context(nc.Block())

            def run(handle, e):
                for kind, fn, waits, info in self.ops[e]:
                    for w in waits:
                        if w[0] == 'e':
                            handle.wait_ge(esem[w[1]][(w[2] - 1) // EPOCH], (w[2] - 1) % EPOCH + 1)
                        else:
                            handle.wait_ge(dsem[w[1]][w[2]], 16 * w[3])
                    if kind == 'c':
                        fn(handle).then_inc(esem[e][(info - 1) // EPOCH], 1)
                    elif kind == 'd':
                        fn(handle).then_inc(dsem[e][info[0]], 16)

            @block.tensor
            def _(h):
                run(h, 'pe')

            @block.scalar
            def _(h):
                run(h, 'act')

            @block.vector
            def _(h):
                run(h, 'dve')

            @block.gpsimd
            def _(h):
                run(h, 'pool')

            @block.sync
            def _(h):
                run(h, 'sp')


V_NG, V_FG, V_CW, V_W0, V_A0, V_KK, V_KA, V_RK, V_LW, V_LB, V_AB, V_GB, V_GN, V_CV = \
    0, 16, 24, 48, 64, 80, 88, 96, 104, 112, 120, 168, 176, 178
NV = 186
C_ID, C_BD, C_HM, C_ONE = 0, 128, 256, 258
NCF = 386
B_RST, B_MAB, B_MN, B_MG = 0, 512, 1536, 2048
NCB = 2304


def _col(v):
    v = np.asarray(v, np.float32).reshape(-1, 128)
    return np.ascontiguousarray(v.T)


def _consts():
    u = np.arange(128)[:, None]
    t = np.arange(128)[None, :]
    LT, LE, GT, GE = (u < t), (u <= t), (u > t), (u >= t)
    cf = np.zeros((128, NCF), np.float32)
    cf[:, C_ID:C_ID + 128] = np.eye(128)
    cf[:, C_BD:C_BD + 128] = (u // 64 == t // 64)
    cf[:, C_HM] = (np.arange(128) < 64)
    cf[:, C_HM + 1] = (np.arange(128) >= 64)
    cf[:, C_ONE:C_ONE + 128] = 1.0
    cb = np.zeros((128, NCB), np.float32)
    rst = np.ones(512, np.float32)
    rst[::128] = 0
    cb[:, B_RST:B_RST + 512] = rst[None]
    cb[:, B_MAB:B_MAB + 512] = np.concatenate([LT, LE, LT, LE], 1)
    cb[:, B_MAB + 512:B_MAB + 1024] = np.concatenate([GT, GE, GT, GE], 1)
    cb[:, B_MN:B_MN + 256] = np.concatenate([GT, GT], 1)
    cb[:, B_MN + 256:B_MN + 512] = np.concatenate([LT, LT], 1)
    cb[:, B_MG:B_MG + 128] = LE
    cb[:, B_MG + 128:B_MG + 256] = GE
    return cf, cb


def _assign():
    plan = [[('s', 0)], [('s', 1)]]
    p = 0
    for n in (3, 3, 3, 3, 2, 2):
        plan.append([('p', p + i) for i in range(n)])
        p += n
    return plan


def build(stop_after=99):
    nc = bass.Bass("TRN2", target_bir_lowering=False)
    dt_in = lambda n, s: nc.dram_tensor(n, s, F32, kind="ExternalInput").ap()
    dt_out = lambda n, s: nc.dram_tensor(n, s, F32, kind="ExternalOutput").ap()
    d_x = dt_in("xT", [128, 8, NT])
    d_vec = dt_in("vec", [128, NV])
    d_cf = dt_in("cf", [128, NCF])
    d_cb = nc.dram_tensor("cb", [128, NCB], BF16, kind="ExternalInput").ap()
    d_msk = dt_in("msk", [128, 4, 32])
    d_ada = dt_in("ada", [2, 128, 8, 3072])
    d_ewin = dt_in("ewin", [128, 8, 8192])
    d_ewout = dt_in("ewout", [128, 16, 1024])
    d_w1 = dt_in("w1c", [128, 8, 256])
    d_w2 = dt_in("w2p", [128, 4, 1024])
    d_sw = dt_in("s_wkv", [128, 8, 2, 128])
    d_owin = dt_in("owin", [128, 8, 3072])
    d_owout = dt_in("owout", [128, 8, 1024])
    d_g1 = dt_in("g1c", [128, 8, 32])
    d_g2 = dt_in("g2p", [32, 2, 512])
    d_sg = dt_in("s_gla", [128, 4, 2, 256])
    o_y = dt_out("yT", [128, 8, NT])
    o_sw = dt_out("o_wkv", [8, 2, 8, 128, 128])
    o_sg = dt_out("o_gla", [8, 2, 4, 128, 256])

    st = contextlib.ExitStack()
    sb = lambda n, s, d=F32: st.enter_context(nc.sbuf_tensor(n, s, d))
    X = sb("X", [128, 8, NT])
    HT = sb("HT", [128, 8, NT], BF16)
    VEC = sb("VEC", [128, NV])
    CF = sb("CF", [128, NCF])
    CB = sb("CB", [128, NCB], BF16)
    IDB = sb("IDB", [128, 128], BF16)
    ONEB = sb("ONEB", [128, 128], BF16)
    BDB = sb("BDB", [128, 128], BF16)
    MSK = sb("MSK", [128, 4, 32])
    MOD = sb("MOD", [128, 2, 24])
    G1 = sb("G1", [128, 2, 8])
    CS_ = sb("CSIL", [128, 8])
    EPS = sb("EPS", [128, 2])
    FA = sb("FA", [128, NT])
    FB = sb("FB", [128, NT])
    BA = [sb(f"BA{i}", [128, NT], BF16) for i in range(4)]
    WRAW = sb("WRAW", [128, 3072])
    WS = WRAW[:, 0:1024].rearrange("p (a b) -> p a b", a=8)
    WB = WRAW[:, 1024:3072].bitcast(BF16).rearrange("p (a b c) -> p a b c", a=4, b=8)
    WOB = WB[:, 3, :, :].rearrange("p a b -> p (a b)")
    UNI = sb("UNI", [128, 8960])
    ub = lambda a, b, p=128: UNI[0:p, a:b].bitcast(BF16)
    T512 = [sb(f"T512_{i}", [128, 512]) for i in range(4)] + [WRAW[:, 512 * i:512 * (i + 1)] for i in range(6)]
    G1B = ub(1024, 1152).rearrange("p (a b) -> p a b", a=8)
    G2B = ub(1152, 1664, 32).rearrange("p (a b) -> p a b", a=2)
    GT1 = ub(0, 1024, 32)
    VTG = sb("VTG", [128, 16, 256], BF16)
    KKF = sb("KKF", [128, NT], BF16)
    GS = sb("GS", [128, 3, 256])
    WLG = sb("WLG", [128, 2, 16])
    NGB = sb("NGB", [128, 8])
    RSG = sb("RSG", [128, 16])
    PRB = ub(0, 2560)
    CHB = ub(2560, 6080)
    BKT = ub(6080, 7104).rearrange("p (a b) -> p a b", a=4)
    PRS = ub(7104, 8128)
    W2S = WRAW[:, 0:512].rearrange("p (a b) -> p a b", a=4)
    W2B = ub(8128, 8384).rearrange("p (a b) -> p a b", a=4)
    W_XAM = 8384
    WLW = sb("WLW", [128, 2, 16])
    OMKA = sb("OMKA", [128, 8])
    GNS = sb("GNS", [128, 8])
    WOT = [T512[2], T512[3]]
    PS = [st.enter_context(nc.psum_tensor(f"ps{i}", [128, 512], F32)) for i in range(8)]

    P = Prog(nc)
    cnt = {'rr': 0}

    def rr(engs=('act', 'dve')):
        cnt['rr'] += 1
        return engs[cnt['rr'] % len(engs)]

    def mm(out, lhsT, rhs, start, stop, r, w):
        P.op('pe', lambda e: e.matmul(out, lhsT, rhs, start=start, stop=stop), reads=r, writes=w)

    def copy(eng, out, in_, r, w):
        if eng == 'act':
            P.op('act', lambda e: e.activation(out=out, in_=in_, func=ACT.Copy), reads=r, writes=w)
        else:
            P.op(eng, lambda e: e.tensor_copy(out=out, in_=in_), reads=r, writes=w)

    def tt(eng, out, a, b, op, r, w):
        P.op(eng, lambda e: e.tensor_tensor(out=out, in0=a, in1=b, op=op), reads=r, writes=w)

    def ts(eng, out, a, s1, s2, op0, op1, r, w):
        if s2 is None:
            P.op(eng, lambda e: e.tensor_scalar(out=out, in0=a, scalar1=s1, scalar2=None, op0=op0), reads=r, writes=w)
        else:
            P.op(eng, lambda e: e.tensor_scalar(out=out, in0=a, scalar1=s1, scalar2=s2, op0=op0, op1=op1), reads=r, writes=w)

    def stt(out, a, s, b, op0, op1, r, w):
        P.op('dve', lambda e: e.scalar_tensor_tensor(out=out, in0=a, scalar=s, in1=b, op0=op0, op1=op1), reads=r, writes=w)

    def act(out, in_, func, r, w, bias=None, scale=None):
        kw = {}
        if bias is not None:
            kw['bias'] = bias
        if scale is not None:
            kw['scale'] = scale
        P.op('act', lambda e: e.activation(out=out, in_=in_, func=func, **kw), reads=r, writes=w)

    def ld(out, in_, w, r=()):
        P.dma('sp', lambda e: e.dma_start(out=out, in_=in_), reads=r, writes=w)

    for c in range(8):
        ld(X[:, c, :], d_x[:, c, :], [('X', c)])
    ld(VEC[:], d_vec, ['VEC'])
    ld(CF[:], d_cf, ['CF'])
    ld(CB[:], d_cb, ['CB'])
    ld(MSK[:], d_msk, ['MSK'])
    copy('pool', IDB[:], CF[:, C_ID:C_ID + 128], ['CF'], ['IDB'])
    copy('pool', ONEB[:], CF[:, C_ONE:C_ONE + 128], ['CF'], ['ONEB'])
    copy('pool', BDB[:], CF[:, C_BD:C_BD + 128], ['CF'], ['BDB'])
    P.op('pool', lambda e: e.memset(EPS[:, 0:1], NORM_EPS), writes=['EPS'])
    P.op('pool', lambda e: e.memset(EPS[:, 1:2], GN_EPS), writes=['EPS'])
    act(CS_[:], VEC[:, V_CV:V_CV + 8], ACT.Silu, ['VEC'], ['CSIL'])
    def ada_layer(l, ACCQ, an, STG, sn):
        steps = []
        for c in range(8):
            for q in range(3):
                def st_(c=c, q=q, i=len(steps)):
                    sg, sr = STG[i % 4], (sn, i % 4)
                    ld(sg, d_ada[l, :, c, q * 1024:(q + 1) * 1024], [sr])
                    if c == 0:
                        ts('dve', ACCQ[q], sg, CS_[:, c:c + 1], None, ALU.mult, None, [sr, 'CSIL'], [(an, q)])
                    else:
                        stt(ACCQ[q], sg, CS_[:, c:c + 1], ACCQ[q], ALU.mult, ALU.add, [sr, 'CSIL', (an, q)], [(an, q)])
                steps.append(st_)

        def fin():
            for j in range(24):
                mm(PS[0][:, j:j + 1], ACCQ[j // 8][:, (j % 8) * 128:(j % 8 + 1) * 128], CF[:, C_ONE:C_ONE + 1],
                   True, True, [(an, j // 8), 'CF'], [('ps', 0)])
            tt('dve', MOD[:, l, :], PS[0][:, 0:24], VEC[:, V_AB + 24 * l:V_AB + 24 * l + 24], ALU.add,
               [('ps', 0), 'VEC'], ['MOD'])
            ts('dve', G1[:, l, :], MOD[:, l, 8:16], 1.0, None, ALU.add, None, ['MOD'], ['G1'])
            tt('dve', G1[:, l, :], G1[:, l, :], VEC[:, V_NG + 8 * l:V_NG + 8 * l + 8], ALU.mult, ['G1', 'VEC'], ['G1'])
        return steps, fin

    st0, fin0 = ada_layer(0, [FA[:, 0:1024], FA[:, 1024:2048], FB[:, 1024:2048]], 'ACC',
                          [FB[:, 0:1024], WRAW[:, 0:1024], WRAW[:, 1024:2048], WRAW[:, 2048:3072]], 'STG')
    for f_ in st0:
        f_()
    fin0()
    ada1_steps, ada1_fin = ada_layer(1, [UNI[:, 1024 * i:1024 * (i + 1)] for i in range(3)], 'uACC',
                                     [UNI[:, 3072 + 1024 * i:4096 + 1024 * i] for i in range(4)], 'uSTG')

    XR = [('X', c) for c in range(8)]
    HR = [('HT', c) for c in range(8)]

    SQB = [WRAW[:, 0:256].bitcast(BF16), WRAW[:, 1024:1280].bitcast(BF16)]
    SQR = ['WS', ('WB', 0)]

    RSTD = [(T512[2], ('T', 2)), (T512[3], ('T', 3))]

    def sumsq_rstd(tsl, ri=0):
        rb_, rn_ = RSTD[ri]
        for c in range(8):
            act(SQB[c % 2], X[:, c, tsl], ACT.Square, [('X', c)], [SQR[c % 2]])
            mm(PS[1][:], ONEB[:], SQB[c % 2], c == 0, c == 7, [SQR[c % 2], 'ONEB'], [('ps', 1)])
        act(rb_[:], PS[1][:], ACT.Sqrt, [('ps', 1), 'EPS'], [rn_], bias=EPS[:, 0:1], scale=1.0 / 1024)
        P.op('dve', lambda e: e.reciprocal(out=rb_[:], in_=rb_[:]), reads=[rn_], writes=[rn_])

    def norm_mod(gfn, sfn, out_fn, out_res):
        tsls = [slice(t4 * 512, (t4 + 1) * 512) for t4 in range(4)]
        sumsq_rstd(tsls[0], 0)
        for t4 in range(4):
            tsl = tsls[t4]
            if t4 + 1 < 4:
                sumsq_rstd(tsls[t4 + 1], (t4 + 1) % 2)
            rb_, rn_ = RSTD[t4 % 2]
            for c in range(8):
                tmp = T512[c % 2]
                o = out_fn(c, tsl)
                if c % 2 == 0:
                    tt('pool', tmp[:], X[:, c, tsl], rb_[:], ALU.mult, [('X', c), rn_], [('T', 0)])
                    ts('dve', o, tmp[:], gfn(c), sfn(c), ALU.mult, ALU.add, [('T', 0), 'VEC', 'G1', 'MOD'], out_res(c, t4))
                else:
                    tt('dve', tmp[:], X[:, c, tsl], rb_[:], ALU.mult, [('X', c), rn_], [('T', 1)])
                    act(o, tmp[:], ACT.Identity, [('T', 1), 'VEC', 'G1', 'MOD'], out_res(c, t4), bias=sfn(c), scale=gfn(c))

    def load_w(dram, col0, br):
        ld(WS[:], dram[:, :, col0:col0 + 128], ['WS'])
        copy(rr(('act', 'dve')), WB[:, br, :, :], WS[:], ['WS'], [('WB', br)])

    def proj(br, evac, banks=(2, 3, 4, 5)):
        for t4 in range(4):
            tsl = slice(t4 * 512, (t4 + 1) * 512)
            pb = banks[cnt['rr'] % len(banks)]
            cnt['rr'] += 1
            for c in range(8):
                mm(PS[pb][:], WB[:, br, c, :], HT[:, c, tsl], c == 0, c == 7, [('WB', br), ('HT', c)], [('ps', pb)])
            evac(t4, tsl, PS[pb][:], ('ps', pb))

    def wout_partial(l, dram_wout, j, OB, ores):
        ld(WS[:].rearrange("p a b -> p (a b)"), dram_wout[:, j, :], ['WS'])
        copy('pool', WOB[:], WS[:].rearrange("p a b -> p (a b)"), ['WS'], [('WB', 3)])
        for ft in range(8):
            for t4 in range(4):
                tsl = slice(t4 * 512, (t4 + 1) * 512)
                pb = 2 + (cnt['rr'] % 4)
                cnt['rr'] += 1
                mm(PS[pb][:], WOB[:, ft * 128:(ft + 1) * 128], OB[:, tsl], True, True, [('WB', 3), ores], [('ps', pb)])
                if (ft * 4 + t4) % 5 < 3:
                    stt(X[:, ft, tsl], PS[pb][:], MOD[:, l, 16 + ft:17 + ft], X[:, ft, tsl], ALU.mult, ALU.add,
                        [('ps', pb), 'MOD', ('X', ft)], [('X', ft)])
                else:
                    wt = WOT[t4 % 2]
                    act(wt[:], PS[pb][:], ACT.Copy, [('ps', pb), 'MOD'], [('T', 2 + t4 % 2)], scale=MOD[:, l, 16 + ft:17 + ft])
                    tt('pool', X[:, ft, tsl], X[:, ft, tsl], wt[:], ALU.add, [('T', 2 + t4 % 2), ('X', ft)], [('X', ft)])

    P.barrier()
    norm_mod(lambda c: G1[:, 0, c:c + 1], lambda c: MOD[:, 0, c:c + 1], lambda c, tsl: HT[:, c, tsl],
             lambda c, t4: [('HT', c)])

    def v3(t, a, b):
        return t[:].rearrange("p (g w) -> p g w", w=64)[:, a, b]

    for j in range(8):
        for br in range(4):
            load_w(d_ewin, br * 1024 + j * 128, br)
        for f_ in ada1_steps[3 * j:3 * j + 3]:
            f_()
        U, Pm, Y = FA, FB, FA
        proj(0, lambda t4, tsl, ps, pr: copy(rr(), FA[:, tsl], ps, [pr], ['FA']))
        proj(2, lambda t4, tsl, ps, pr: tt('dve', FB[:, tsl], ps, FA[:, tsl], ALU.mult, [pr, 'FA'], ['FB']))
        proj(1, lambda t4, tsl, ps, pr: copy(rr(), BA[0][:, tsl], ps, [pr], ['BA0']))
        proj(3, lambda t4, tsl, ps, pr: act(BA[1][:, tsl], ps, ACT.Silu, [pr], ['BA1']))
        w0, w1, w2 = (VEC[:, V_CW + 8 * k + j:V_CW + 8 * k + j + 1] for k in range(3))
        act(FA[:], FB[:], ACT.Copy, ['FB', 'VEC'], ['FA'], scale=w1)
        g_all, g_lo, g_hi = slice(0, 32), slice(0, 31), slice(1, 32)
        stt(v3(FA, g_all, slice(1, 64)), v3(FB, g_all, slice(0, 63)), w0, v3(FA, g_all, slice(1, 64)),
            ALU.mult, ALU.add, ['FB', 'FA', 'VEC'], ['FA'])
        stt(v3(FA, g_all, slice(0, 63)), v3(FB, g_all, slice(1, 64)), w2, v3(FA, g_all, slice(0, 63)),
            ALU.mult, ALU.add, ['FB', 'FA', 'VEC'], ['FA'])
        tb = T512[0]
        tt('pool', tb[:, 0:31], v3(FB, g_lo, 63), MSK[:, 0, 1:32], ALU.mult, ['FB', 'MSK'], [('T', 0)])
        stt(v3(FA, g_hi, 0), tb[:, 0:31], w0, v3(FA, g_hi, 0), ALU.mult, ALU.add, [('T', 0), 'FA', 'VEC'], ['FA'])
        tt('pool', tb[:, 32:63], v3(FB, g_hi, 0), MSK[:, 1, 1:32], ALU.mult, ['FB', 'MSK'], [('T', 0)])
        stt(v3(FA, g_lo, 63), tb[:, 32:63], w2, v3(FA, g_lo, 63), ALU.mult, ALU.add, [('T', 0), 'FA', 'VEC'], ['FA'])
        tt('pool', FA[:], FA[:], BA[0][:], ALU.mult, ['FA', 'BA0'], ['FA'])
        tt('dve', BA[2][:], FA[:], BA[1][:], ALU.mult, ['FA', 'BA1'], ['BA2'])
        wout_partial(0, d_ewout, j, BA[2], 'BA2')

    ada1_fin()
    P.barrier()
    load_w(d_w1, 0, 0)
    load_w(d_w1, 128, 1)
    proj(0, lambda t4, tsl, ps, pr: act(BA[2][:, tsl], ps, ACT.Tanh, [pr], ['BA2']))
    proj(1, lambda t4, tsl, ps, pr: copy('act', BA[3][:, tsl], ps, [pr], ['BA3']))
    ts('pool', OMKA[:], VEC[:, V_KA:V_KA + 8], -1.0, 1.0, ALU.mult, ALU.add, ['VEC'], ['OMKA'])
    KKf = KKF[:]
    Rb = FA[:, 0:1024].bitcast(BF16)
    Kb = FA[:, 1024:2048].bitcast(BF16)
    TB = T512
    c3 = lambda ap: ap.rearrange("p (k t) -> p k t", t=128)
    FBb = FB[:].bitcast(BF16)
    PRBs = [PRB, FBb]
    XAm = ub(W_XAM, W_XAM + 512)
    def opnd(par):
        base = PRBs[par]
        d = dict(AR=base[:, 0:1024], Bh=base[:, 1024:1536], Kh=base[:, 1536:2048],
                 Btm=[base[:, 2048:2560], base[:, 2560:3072]], Ktm=[base[:, 3072:3584], base[:, 3584:4096]])
        d['Am'] = [base[:, 4096:4608], base[:, 4608:5120]] if par == 0 else [XAm[:, 0:512], XAm[:, 512:1024]]
        d['AR4'] = d['AR'].rearrange("p (k s t) -> p k s t", s=2, t=128)
        return d
    OPN = [opnd(0), opnd(1)]
    NM0 = [CHB[:, 1024 * i:1024 * i + 512] for i in range(3)]
    NM1 = [CHB[:, 1024 * i + 512:1024 * i + 1024] for i in range(3)]
    lv4 = lambda ap: ap.rearrange("p (h s t) -> p h s t", h=2, s=2)
    LVs = [[lv4(CHB[:, 3072 + 1536 * c + 512 * i:3072 + 1536 * c + 512 * (i + 1)]) for i in range(2)] for c in range(2)]
    PPs = [[CHB[:, 4096 + 1536 * c + 256 * i:4096 + 1536 * c + 256 * (i + 1)] for i in range(2)] for c in range(2)]
    TTf = [CHB[:, 6144:6400], CHB[:, 6400:6656]]
    Z0B, UB, SBw = CHB[:, 6656:6784], CHB[:, 6784:6912], CHB[:, 6912:7040]
    BKTs = [BKT[:, 0:2, :], BKT[:, 2:4, :]]
    SFw = GS[:, 0, 0:128]
    BDm = CF[:, C_BD:C_BD + 128]
    HM = [CF[:, C_HM:C_HM + 1], CF[:, C_HM + 1:C_HM + 2]]
    hs = lambda h: slice(h * 64, (h + 1) * 64)

    def interleave(lists):
        lists = [l for l in lists if l]
        pos = [0] * len(lists)
        n = max(len(l) for l in lists) if lists else 0
        for step in range(n):
            for li, l in enumerate(lists):
                tgt = (step + 1) * len(l) // n
                while pos[li] < tgt:
                    l[pos[li]]()
                    pos[li] += 1

    KT = [UNI[:, 512 * i:512 * (i + 1)] for i in range(3)]
    ET = [FB[:, 512 * i:512 * (i + 1)] for i in range(2)]

    def start_rk(jj, banks=(2, 3, 4, 5), kb=1):
        vcol = lambda base: VEC[:, base + jj:base + jj + 1]
        G = []

        def g_load():
            load_w(d_ewin, 4096 + 0 * 1024 + jj * 128, 0)
            load_w(d_ewin, 4096 + 1 * 1024 + jj * 128, 1)
            ld(W2S[:], d_w2[:, :, jj * 128:(jj + 1) * 128], ['WS'])
            copy('pool', W2B[:], W2S[:], ['WS'], ['W2B'])
        G.append(g_load)
        G.append(lambda: proj(0, lambda t4, tsl, ps, pr: copy(rr(), Rb[:, tsl], ps, [pr], ['FA']), banks))
        G.append(lambda: proj(1, lambda t4, tsl, ps, pr: copy(rr(), Kb[:, tsl], ps, [pr], ['FA']), banks))

        def g_kk(t4):
            def g():
                tsl = slice(t4 * 512, (t4 + 1) * 512)
                o0 = ('OPN', 0)
                sqb = KT[1].bitcast(BF16)[:, 0:512]
                act(KT[0][:], Kb[:, tsl], ACT.Copy, ['FA', 'VEC'], [o0], scale=vcol(V_KK))
                act(sqb, KT[0][:], ACT.Square, [], [o0])
                mm(PS[kb][:], BDB[:], sqb, True, True, [o0, 'BDB'], [('ps', kb)])
                act(KT[2][:], PS[kb][:], ACT.Sqrt, [('ps', kb)], [o0])
                ts('dve', KT[2][:], KT[2][:], 1e-12, None, ALU.max, None, [], [o0])
                P.op('dve', lambda e: e.reciprocal(out=KT[2][:], in_=KT[2][:]), reads=[], writes=[o0])
                tt('pool', KKf[:, tsl], KT[0][:], KT[2][:], ALU.mult, [o0], ['KKf'])
            return g
        for t4 in range(4):
            G.append(g_kk(t4))
        return G

    def start_vz(jj):
        G = []

        def g_l():
            load_w(d_ewin, 4096 + 2 * 1024 + jj * 128, 2)
            load_w(d_ewin, 4096 + 3 * 1024 + jj * 128, 3)
        G.append(g_l)
        G.append(lambda: proj(2, lambda t4, tsl, ps, pr: copy(rr(), BA[0][:, tsl], ps, [pr], ['BA0'])))
        G.append(lambda: proj(3, lambda t4, tsl, ps, pr: act(BA[1][:, tsl], ps, ACT.Silu, [pr], ['BA1'])))

        def g_vt(g):
            def f():
                for k in range(4):
                    mm(PS[4][:, k * 128:(k + 1) * 128], BA[0][:, (4 * g + k) * 128:(4 * g + k + 1) * 128], IDB[:], True, True,
                       ['BA0', 'IDB'], [('ps', 4)])
                copy('act', VTG[:, 4 * g:4 * g + 4, 0:128], c3(PS[4][:]), [('ps', 4)], ['VTG'])
            return f
        for g in range(4):
            G.append(g_vt(g))
        return G

    NRK_AT = {26: [0], 27: [1], 28: [2], 29: [3], 30: [4, 5], 31: [6]}

    def pair_chain(jj, vz, nrk=None):
        vcol = lambda base: VEC[:, base + jj:base + jj + 1]
        if True:
            blocks = [(0, b) for b in range(4)] + [(1, b) for b in range(3, -1, -1)]
            chunks = [(0, b, k) for b in range(4) for k in range(4)] + [(1, b, k) for b in range(3, -1, -1) for k in range(3, -1, -1)]
            mab = lambda z: CB[:, B_MAB + 512 * z:B_MAB + 512 * z + 512]
            mnm = lambda z: CB[:, B_MN + 256 * z:B_MN + 256 * z + 256]

            def init_state(z, jj=jj):
                def g():
                    ld(SFw, d_sw[:, jj, z, :], ['SFw'])
                    copy('pool', SBw, SFw, ['SFw'], ['SBw'])
                return g

            def prep_groups(bi, jj=jj, vcol=vcol):
                z, blk = blocks[bi]
                par = bi % 2
                O_ = OPN[par]
                opr = ('OPN', par)
                bkt = BKTs[par]
                bkr = ('BKT', par)
                tsl = slice(blk * 512, (blk + 1) * 512)
                SIG, CSw, CRw, CSBw, AI, KM, BV = TB[0], TB[1], TB[3], TB[4], TB[9], TB[7], TB[8]
                rAI, rBV = ('WB', 3), ('WB', 2)
                if bi == 0:
                    AI, BV = FB[:, 1024:1536], FB[:, 1536:2048]
                    rAI = rBV = ('OPN', 1)
                E1, E3 = TB[2], TB[6]
                if z == 0:
                    inc, ex, rest = (CSw, ('T', 1)), (SIG, ('T', 0)), (CRw, ('T', 3))
                else:
                    inc, ex, rest = (CSBw, 'WS'), (CRw, ('T', 3)), (SIG, ('T', 0))
                def w0():
                    mm(PS[4][:], W2B[:, z, :], BA[2][:, tsl], True, True, ['W2B', 'BA2'], [('ps', 4)])
                    mm(PS[5][:], W2B[:, 2 + z, :], BA[3][:, tsl], True, True, ['W2B', 'BA3'], [('ps', 5)])

                def w1():
                    act(SIG[:], PS[4][:], ACT.Sigmoid, [('ps', 4), 'VEC'], [('T', 0)], bias=vcol(V_W0 + 8 * z))
                    act(AI[:], PS[5][:], ACT.Sigmoid, [('ps', 5), 'VEC'], [rAI], bias=vcol(V_A0 + 8 * z))

                def w2():
                    P.op('dve', lambda e: e.tensor_tensor_scan(out=CSw[:], data0=CB[:, B_RST:B_RST + 512], data1=SIG[:],
                                                               initial=0.0, op0=ALU.mult, op1=ALU.add),
                         reads=[('T', 0), 'CB'], writes=[('T', 1)])
                    act(E3[:], AI[:], ACT.Identity, [rAI, 'VEC', 'OMKA'], [('WB', 0)], bias=OMKA[:, jj:jj + 1], scale=vcol(V_KA))
                    stt(BV[:], KKf[:, tsl], -1.0, AI[:], ALU.mult, ALU.mult, ['KKf', rAI], [rBV])

                def w3():
                    totb = bass.AP(CSw, 127, [[512, 128], [128, 4], [0, 128]])
                    tt('pool', c3(CRw[:]), totb, c3(CSw[:]), ALU.subtract, [('T', 1)], [('T', 3)])
                    act(WLW[:, z, blk * 4:blk * 4 + 4], bass.AP(CSw, 127, [[512, 128], [128, 4]]), ACT.Exp, [('T', 1)], ['WLW'], scale=-LAM)
                    tt('pool', KM[:], Kb[:, tsl], E3[:], ALU.mult, ['FA', ('WB', 0)], [('WB', 1)])

                def w4():
                    if z == 1:
                        tt('pool', CSBw[:], CRw[:], SIG[:], ALU.add, [('T', 3), ('T', 0)], ['WS'])
                    tt('pool', SIG[:], CSw[:], SIG[:], ALU.subtract, [('T', 1), ('T', 0)], [('T', 0)])
                    if z == 0:
                        tt('pool', PRS[:, tsl], Rb[:, tsl], KM[:], ALU.mult, ['FA', ('WB', 1)], ['PRS'])
                    else:
                        tt('pool', E3[:], Rb[:, tsl], KM[:], ALU.mult, ['FA', ('WB', 1)], [('WB', 0)])
                        tt('pool', PRS[:, tsl], PRS[:, tsl], E3[:], ALU.add, ['PRS', ('WB', 0)], ['PRS'])

                def w5():
                    act(E1[:], ex[0][:], ACT.Exp, [ex[1]], [('T', 2)], scale=-LAM)
                    act(E3[:], inc[0][:], ACT.Exp, [inc[1]], [('WB', 0)], scale=-LAM)

                def w6():
                    tt('pool', O_['AR4'][:, :, 0, :], c3(KKf[:, tsl]), c3(E1[:]), ALU.mult, ['KKf', ('T', 2)], [opr])
                    for h in range(2):
                        stt(O_['Am'][h], KKf[:, tsl], HM[h], E1[:], ALU.mult, ALU.mult, ['KKf', 'CF', ('T', 2)], [opr])
                    tt('pool', O_['AR4'][:, :, 1, :], c3(Rb[:, tsl]), c3(E3[:]), ALU.mult, ['FA', ('WB', 0)], [opr])

                def w7():
                    act(E1[:], rest[0][:], ACT.Exp, [rest[1]], [('T', 2)], scale=-LAM)
                    act(E3[:], inc[0][:], ACT.Exp, [inc[1]], [('WB', 0)], scale=LAM)

                def w8():
                    for h in range(2):
                        stt(O_['Btm'][h], BV[:], HM[h], E3[:], ALU.mult, ALU.mult, [rBV, 'CF', ('WB', 0)], [opr])
                        stt(O_['Ktm'][h], KM[:], HM[h], E3[:], ALU.mult, ALU.mult, [('WB', 1), 'CF', ('WB', 0)], [opr])
                    tt('pool', O_['Bh'], BV[:], E1[:], ALU.mult, [rBV, ('T', 2)], [opr])
                    tt('pool', O_['Kh'], KM[:], E1[:], ALU.mult, [('WB', 1), ('T', 2)], [opr])

                def w9():
                    for k in range(4):
                        mm(PS[4][:, k * 128:(k + 1) * 128], O_['Bh'][:, k * 128:(k + 1) * 128], IDB[:], True, True, [opr, 'IDB'], [('ps', 4)])

                def w10():
                    copy('act', bkt[:, 0, :], PS[4][:], [('ps', 4)], [bkr])
                    for k in range(4):
                        mm(PS[5][:, k * 128:(k + 1) * 128], O_['Kh'][:, k * 128:(k + 1) * 128], IDB[:], True, True, [opr, 'IDB'], [('ps', 5)])

                def w11():
                    copy('act', bkt[:, 1, :], PS[5][:], [('ps', 5)], [bkr])
                return [w0, w1, w2, w3, w4, w5, w6, w7, w8, w9, w10, w11]

            def a_groups(ci):
                bi, (z, blk, k) = ci // 4, chunks[ci]
                MAB, MNm = mab(z), mnm(z)
                par, q, m, cx = bi % 2, ci % 2, ci % 3, ci % 2
                O_ = OPN[par]
                opr = ('OPN', par)
                ksl = slice(k * 128, (k + 1) * 128)
                ARk = O_['AR'][:, k * 256:(k + 1) * 256]
                nm0, nm1 = NM0[m], NM1[m]
                LV, PPp = LVs[cx], PPs[cx]
                na, nb = (6, 7) if cx == 0 else (0, 1)
                PA_, PB_ = PS[na], PS[nb]
                ra, rb = ('ps', na), ('ps', nb)
                lvp = lambda i: ('LVp', cx, i)
                lvt = lambda i: ('LVt', cx, i)
                ppr = lambda i: ('PPp', cx, i)
                G = []

                def g0():
                    for h in range(2):
                        mm(PA_[:, h * 256:(h + 1) * 256], O_['Btm'][h][:, ksl], ARk, True, True, [opr], [ra])
                        mm(PB_[:, h * 256:(h + 1) * 256], O_['Ktm'][h][:, ksl], ARk, True, True, [opr], [rb])
                        mm(PS[3][:, 256 + h * 128:384 + h * 128], O_['Am'][h][:, ksl], O_['Btm'][h][:, ksl], True, True, [opr], [('ps', 3)])
                    tt('dve', nm0, PA_[:], MAB, ALU.mult, [ra, 'CB'], [('NM0', m)])
                    tt('dve', PPp[0], PS[3][:, 256:512], MNm, ALU.mult, [('ps', 3), 'CB'], [ppr(0)])
                    tt('dve', nm1, PB_[:], MAB, ALU.mult, [rb, 'CB'], [('NM1', m)])
                G.append(g0)
                pt0 = nm0.rearrange("p (h s t) -> p h s t", h=2, s=2)[:, :, 0, :]

                def g1():
                    idb2 = bass.AP(IDB, 0, [[128, 128], [0, 2], [1, 128]])
                    tt('pool', LV[1][:, :, 1, :], pt0, idb2, ALU.add, [('NM0', m), 'IDB'], [lvt(1)])
                    for h in range(2):
                        mm(PA_[:, h * 128:(h + 1) * 128], nm0[:, h * 256:h * 256 + 128], PPp[0][:, h * 128:(h + 1) * 128], True, True,
                           [('NM0', m), ppr(0)], [ra])
                        mm(PB_[:, h * 256:h * 256 + 128], PPp[0][:, h * 128:(h + 1) * 128], nm0[:, h * 256:h * 256 + 128], True, True,
                           [('NM0', m), ppr(0)], [rb])
                    copy('act', PPp[1], PA_[:, 0:256], [ra], [ppr(1)])
                    copy('dve', LV[1][:, :, 0, :], PB_[:].rearrange("p (h s t) -> p h s t", h=2, s=2)[:, :, 0, :], [rb], [lvp(1)])
                G.append(g1)

                def lvl(kk_):
                    def g():
                        a, b = kk_ % 2, (kk_ + 1) % 2
                        psb = PB_[:].rearrange("p (h s t) -> p h s t", h=2, s=2)
                        for h in range(2):
                            pk = PPp[a][:, h * 128:(h + 1) * 128]
                            if kk_ < 5:
                                mm(PB_[:, h * 256:(h + 1) * 256], pk, LV[a][:, h, :, :].rearrange("p s t -> p (s t)"), True, True,
                                   [ppr(a), lvp(a), lvt(a)], [rb])
                            else:
                                mm(PB_[:, h * 256 + 128:(h + 1) * 256], pk, LV[a][:, h, 1, :], True, True, [ppr(a), lvt(a)], [rb])
                            mm(PA_[:, h * 128:(h + 1) * 128], LV[a][:, h, 0, :], pk, True, True, [ppr(a), lvp(a)], [ra])
                        copy('act', PPp[b], PA_[:, 0:256], [ra], [ppr(b)])
                        if kk_ < 5:
                            copy('dve', LV[b][:, :, 0, :], psb[:, :, 0, :], [rb], [lvp(b)])
                        tt('dve', LV[b][:, :, 1, :], psb[:, :, 1, :], LV[a][:, :, 1, :], ALU.add, [rb, lvt(a)], [lvt(b)])
                    return g
                for kk_ in range(1, 6):
                    G.append(lvl(kk_))

                def g7():
                    for h in range(2):
                        mm(PB_[:, h * 128:(h + 1) * 128], PPp[0][:, h * 128:(h + 1) * 128], LV[0][:, h, 1, :], True, True,
                           [ppr(0), lvt(0)], [rb])
                    tt('dve', TTf[q].rearrange("p (h t) -> p h t", h=2), PB_[:, 0:256].rearrange("p (h t) -> p h t", h=2),
                       LV[0][:, :, 1, :], ALU.add, [rb, lvt(0)], [('TTf', q)])
                G.append(g7)
                return G

            def b_groups(ci, jj=jj):
                bi, (z, blk, k) = ci // 4, chunks[ci]
                par, q, m = bi % 2, ci % 2, ci % 3
                O_ = OPN[par]
                opr = ('OPN', par)
                bkt, bkr = BKTs[par], ('BKT', par)
                c16 = blk * 4 + k
                ksl = slice(k * 128, (k + 1) * 128)
                ARk = O_['AR'][:, k * 256:(k + 1) * 256]
                nm0, nm1, TT = NM0[m], NM1[m], TTf[q]
                vt = lambda h: VTG[:, c16, h * 64:(h + 1) * 64]
                G = []

                def g0():
                    mm(PS[2][:, 0:128], ARk[:, 0:128], SBw, True, False, [opr, 'SBw'], [('ps', 2)])
                    for h in range(2):
                        mm(PS[2][:, hs(h)], nm1[:, h * 256:h * 256 + 128], vt(h), False, h == 1, [('NM1', m), 'VTG'], [('ps', 2)])
                    copy('act', Z0B, PS[2][:, 0:128], [('ps', 2)], ['Z0B'])
                G.append(g0)

                def g1():
                    for h in range(2):
                        mm(PS[2][:, 128 + h * 64:192 + h * 64], TT[:, h * 128:(h + 1) * 128], Z0B[:, hs(h)], True, True,
                           [('TTf', q), 'Z0B'], [('ps', 2)])
                    copy('act', UB, PS[2][:, 128:256], [('ps', 2)], ['UB'])
                G.append(g1)

                def g2():
                    mm(PS[3][:, 0:128], ARk[:, 128:256], SBw, True, False, [opr, 'SBw'], [('ps', 3)])
                    for h in range(2):
                        mm(PS[3][:, hs(h)], nm0[:, h * 256 + 128:h * 256 + 256], UB[:, hs(h)], False, False, [('NM0', m), 'UB'], [('ps', 3)])
                        mm(PS[3][:, hs(h)], nm1[:, h * 256 + 128:h * 256 + 256], vt(h), False, h == 1, [('NM1', m), 'VTG'], [('ps', 3)])
                    mm(PS[2][:, 256:384], bkt[:, 0, ksl], UB, True, False, [bkr, 'UB'], [('ps', 2)])
                    mm(PS[2][:, 256:384], bkt[:, 1, ksl], VTG[:, c16, 0:128], False, True, [bkr, 'VTG'], [('ps', 2)])
                G.append(g2)

                def g3():
                    tmpw = GS[:, 1 + (c16 % 2), 0:128]
                    tr = ('TMPw', c16 % 2)
                    stt(tmpw, SFw, WLW[:, z, c16:c16 + 1], PS[2][:, 256:384], ALU.mult, ALU.add, ['SFw', 'WLW', ('ps', 2)], [tr])
                    stt(SFw, tmpw, MSK[:, 2 + z, c16:c16 + 1], BDm, ALU.mult, ALU.mult, [tr, 'MSK', 'CF'], ['SFw'])
                    copy('act', SBw, SFw, ['SFw'], ['SBw'])
                    if (c16 % 2 == 1) == (z == 0):
                        P.dma('sp', lambda e, tmpw=tmpw, z=z, jj=jj, c16=c16: e.dma_start(out=o_sw[c16 // 2, z, jj], in_=tmpw), reads=[tr])
                    ofc = VTG[:, c16, 128:256]
                    vo = ('VO', c16)
                    if z == 0:
                        copy('act', ofc, PS[3][:, 0:128], [('ps', 3)], [vo])
                    else:
                        to, sqo = GS[:, 0, 128:256], GS[:, 1, 128:256]
                        h3 = lambda ap: ap.rearrange("p (g w) -> p g w", w=64)
                        gb = lambda off: bass.AP(GNS, off, [[8, 128], [1, 2], [0, 64]])
                        tt('dve', to, ofc, PS[3][:, 0:128], ALU.add, [('ps', 3), vo], ['TO'])
                        P.op('dve', lambda e: e.tensor_reduce(out=GNS[:, 0:2], in_=h3(to), axis=AX.X, op=ALU.add),
                             reads=['TO'], writes=['GNS'])
                        tt('dve', sqo, to, to, ALU.mult, ['TO'], ['SQO'])
                        P.op('dve', lambda e: e.tensor_reduce(out=GNS[:, 2:4], in_=h3(sqo), axis=AX.X, op=ALU.add),
                             reads=['SQO'], writes=['GNS'])
                        ts('dve', GNS[:, 0:2], GNS[:, 0:2], 1.0 / 64, None, ALU.mult, None, ['GNS'], ['GNS'])
                        tt('dve', GNS[:, 4:6], GNS[:, 0:2], GNS[:, 0:2], ALU.mult, ['GNS'], ['GNS'])
                        stt(GNS[:, 2:4], GNS[:, 2:4], 1.0 / 64, GNS[:, 4:6], ALU.mult, ALU.subtract, ['GNS'], ['GNS'])

                        def tail():
                            act(GNS[:, 2:4], GNS[:, 2:4], ACT.Ln, ['GNS', 'EPS'], ['GNS'], bias=EPS[:, 1:2], scale=1.0)
                            act(GNS[:, 2:4], GNS[:, 2:4], ACT.Exp, ['GNS'], ['GNS'], scale=-0.5)
                            tt('pool', h3(to), h3(to), gb(0), ALU.subtract, ['TO', 'GNS'], ['TO'])
                            tt('pool', h3(ofc), h3(to), gb(2), ALU.mult, ['TO', 'GNS'], [vo])
                        DEFER[ci] = tail
                G.append(g3)
                return G

            NCK = 32
            DEFER = {}
            init_state(0)()
            AG = {0: a_groups(0), 1: a_groups(1)}
            PG = {0: prep_groups(0), 1: prep_groups(1)}
            lock = [(lambda i=i: (AG[0][i](), AG[1][i]() if i < 4 else None)) for i in range(8)]
            interleave([vz, PG[0] + lock])
            for ci in range(NCK):
                bg = b_groups(ci)
                if ci - 1 in DEFER:
                    bg = [DEFER.pop(ci - 1)] + bg
                if ci == 16:
                    bg = [init_state(1)] + bg
                lists = [bg]
                if ci + 1 < NCK:
                    lists.append(AG[ci + 1][4:])
                if ci + 2 < NCK:
                    AG[ci + 2] = a_groups(ci + 2)
                    lists.append(AG[ci + 2][:4])
                b_, r_ = ci // 4, ci % 4
                if r_ < 2 and b_ + 1 < 8:
                    if b_ + 1 not in PG:
                        PG[b_ + 1] = prep_groups(b_ + 1)
                    if b_ == 0:
                        lists.append(PG[1][6 * r_:6 * r_ + 6])
                    else:
                        lists.append(PG[b_ + 1][6 + 3 * r_:9 + 3 * r_])
                elif r_ >= 2 and b_ + 2 < 8:
                    if b_ + 2 not in PG:
                        PG[b_ + 2] = prep_groups(b_ + 2)
                    lists.append(PG[b_ + 2][3 * (r_ - 2):3 * (r_ - 2) + 3])
                if nrk is not None and ci in NRK_AT:
                    lists.append([nrk[i] for i in NRK_AT[ci]])
                interleave(lists)
            for k_ in sorted(DEFER):
                DEFER.pop(k_)()

    def pair_end(jj):
        vcol = lambda base: VEC[:, base + jj:base + jj + 1]
        G = []
        o1 = ('OPN', 1)

        def g_t(t4):
            def g():
                tsl = slice(t4 * 512, (t4 + 1) * 512)
                for k in range(4):
                    mm(PS[4][:, k * 128:(k + 1) * 128], VTG[:, t4 * 4 + k, 128:256], IDB[:], True, True,
                       [('VO', t4 * 4 + k), 'IDB'], [('ps', 4)])
                ts('dve', TB[0][:], PS[4][:], vcol(V_LW), vcol(V_LB), ALU.mult, ALU.add, [('ps', 4), 'VEC'], [('T', 0)])
                etb = ET[0].bitcast(BF16)[:, 0:512]
                act(etb, PRS[:, tsl], ACT.Copy, ['PRS', 'VEC'], [o1], scale=vcol(V_RK))
                mm(PS[5][:], BDB[:], etb, True, True, [o1, 'BDB'], [('ps', 5)])
                tt('dve', ET[1][:], PS[5][:], BA[0][:, tsl], ALU.mult, [('ps', 5), 'BA0'], [o1])
                tt('pool', TB[0][:], TB[0][:], ET[1][:], ALU.add, [('T', 0), o1], [('T', 0)])
                tt('pool', BA[1][:, tsl], TB[0][:], BA[1][:, tsl], ALU.mult, [('T', 0), 'BA1'], ['BA1'])
            return g
        for t4 in range(4):
            G.append(g_t(t4))

        def g_w():
            ld(WS[:].rearrange("p a b -> p (a b)"), d_ewout[:, 8 + jj, :], ['WS'])
            copy('pool', WOB[:], WS[:].rearrange("p a b -> p (a b)"), ['WS'], [('WB', 3)])

        def g_o(t4, half):
            def g():
                tsl = slice(t4 * 512, (t4 + 1) * 512)
                for ft in range(4 * half, 4 * half + 4):
                    pb = 2 + (cnt['rr'] % 4)
                    cnt['rr'] += 1
                    mm(PS[pb][:], WOB[:, ft * 128:(ft + 1) * 128], BA[1][:, tsl], True, True, [('WB', 3), 'BA1'], [('ps', pb)])
                    if (ft * 4 + t4) % 5 < 3:
                        stt(X[:, ft, tsl], PS[pb][:], MOD[:, 0, 16 + ft:17 + ft], X[:, ft, tsl], ALU.mult, ALU.add,
                            [('ps', pb), 'MOD', ('X', ft)], [('X', ft)])
                    else:
                        wt = WOT[ft % 2]
                        act(wt[:], PS[pb][:], ACT.Copy, [('ps', pb), 'MOD'], [('T', 2 + ft % 2)], scale=MOD[:, 0, 16 + ft:17 + ft])
                        tt('pool', X[:, ft, tsl], X[:, ft, tsl], wt[:], ALU.add, [('T', 2 + ft % 2), ('X', ft)], [('X', ft)])
            return g
        G = [g_w, G[0], G[1], g_o(0, 0), g_o(0, 1), G[2], g_o(1, 0), g_o(1, 1), G[3], g_o(2, 0), g_o(2, 1), g_o(3, 0), g_o(3, 1)]
        return G

    for g in start_rk(0):
        g()
    vz = start_vz(0)
    for jj in range(8):
        nrk = start_rk(jj + 1, banks=(4, 5), kb=4) if jj + 1 < 8 else None
        pair_chain(jj, vz, nrk)
        for g in pair_end(jj):
            g()
        if jj + 1 < 8:
            vz = start_vz(jj + 1)
    P.barrier()

    norm_mod(lambda c: G1[:, 1, c:c + 1], lambda c: MOD[:, 1, c:c + 1], lambda c, tsl: HT[:, c, tsl],
             lambda c, t4: [('HT', c)])
    ld(WS[:].rearrange("p a b -> p (a b)")[:, 0:256], d_g1.rearrange('p a b -> p (a b)'), ['WS'])
    copy('pool', G1B[:].rearrange('p a b -> p (a b)'), WS[:].rearrange("p a b -> p (a b)")[:, 0:256], ['WS'], ['G1B'])
    ld(WS[:].rearrange("p a b -> p (a b)")[0:32, :], d_g2.rearrange('p a b -> p (a b)'), ['WS'])
    copy('pool', G2B[:].rearrange('p a b -> p (a b)'), WS[:].rearrange("p a b -> p (a b)")[0:32, :], ['WS'], ['G2B'])
    ts('pool', NGB[:], VEC[:, V_GB:V_GB + 8], -1.0, None, ALU.mult, None, ['VEC'], ['NGB'])
    for t4 in range(4):
        tsl = slice(t4 * 512, (t4 + 1) * 512)
        for c in range(8):
            mm(PS[0][0:32, :], G1B[:, c, :], HT[:, c, tsl], c == 0, c == 7, ['G1B', ('HT', c)], [('ps', 0)])
        copy('act', GT1[:, tsl], PS[0][0:32, :], [('ps', 0)], ['GT1'])
    SP, CSg, CR0, INC, RST, EXg = T512[:6]
    g2 = ub(2048, 3328)
    GSET = [(BA[0][:, 0:512], BA[0][:, 512:1024], BA[0][:, 1024:1536], BA[1][:, 0:512],
             BA[1][:, 512:1024].rearrange('p (k t) -> p k t', k=4)),
            (g2[:, 0:512], g2[:, 512:1024], g2[:, 1024:1536], g2[:, 1536:2048],
             g2[:, 2048:2560].rearrange('p (k t) -> p k t', k=4))]
    GEX = [UNI[:, 4096 + 512 * i:4096 + 512 * (i + 1)] for i in range(3)]
    GLTf = KKF[:].bitcast(F32)
    GLT = [GLTf[:, 0:512], GLTf[:, 512:1024]]
    SBg = BA[1][:, 1024:1280]
    SFg = GS[:, 0, :]
    for hd in range(4):
        load_w(d_owin, hd * 128, 0)
        load_w(d_owin, 512 + hd * 128, 1)
        load_w(d_owin, 1024 + hd * 256, 2)
        load_w(d_owin, 1024 + hd * 256 + 128, 3)
        proj(0, lambda t4, tsl, ps, pr: copy(rr(), FA[:, tsl], ps, [pr], ['FA']))
        proj(1, lambda t4, tsl, ps, pr: copy(rr(), FB[:, tsl], ps, [pr], ['FB']))
        vgr = []
        for half in range(2):
            vgr.append(lambda half=half: proj(2 + half, lambda t4, tsl, ps, pr: copy(rr(), BA[2][:, tsl], ps, [pr], ['BA2'])))

            def vt(g, half=half):
                def f():
                    for k in range(4):
                        mm(PS[1][:, k * 128:(k + 1) * 128], BA[2][:, (4 * g + k) * 128:(4 * g + k + 1) * 128], IDB[:], True, True,
                           ['BA2', 'IDB'], [('ps', 1)])
                    copy('act', VTG[:, 4 * g:4 * g + 4, half * 128:(half + 1) * 128], PS[1][:].rearrange("p (k t) -> p k t", k=4),
                         [('ps', 1)], [('VTG', 4 * g + i) for i in range(4)])
                return f
            for g in range(4):
                vgr.append(vt(g))
        gblocks = [(0, b) for b in range(4)] + [(1, b) for b in range(3, -1, -1)]

        def g_init(z, hd=hd):
            def g():
                ld(SFg, d_sg[:, hd, z, :], ['SFg'])
                copy('pool', SBg, SFg, ['SFg'], ['SBg', 'BA1'])
            return g

        def g_prep(bi, hd=hd):
            z, blk = gblocks[bi]
            sb_ = bi % 2
            QE, KE, KD, KDT, ATT4 = GSET[sb_]
            rq, rk, rd, rt, ra_ = (('gQE', sb_), ('gKE', sb_), ('gKD', sb_), ('gKDT', sb_), ('gATT', sb_))
            MB = [(PS[0], 0), (PS[1], 1)] if sb_ == 0 else [(PS[2], 2), (PS[5], 5)]
            tsl = slice(blk * 512, (blk + 1) * 512)
            EA, EB, EC = GEX
            if z == 0:
                inc, rst, ri, rr_ = CSg, CR0, ('T', 1), ('T', 2)
            else:
                inc, rst, ri, rr_ = INC, RST, ('T', 3), 'WS'

            def w0():
                mm(PS[4][:], G2B[:, z, hd * 128:(hd + 1) * 128], GT1[:, tsl], True, True, ['G2B', 'GT1'], [('ps', 4)])

            def w1():
                act(EXg[:], PS[4][:], ACT.Exp, [('ps', 4), 'NGB'], ['WS'], bias=NGB[:, 4 * z + hd:4 * z + hd + 1], scale=-1.0)

            def w2():
                act(SP[:], EXg[:], ACT.Ln, ['WS', 'CF'], [('T', 0)], bias=CF[:, C_ONE:C_ONE + 1], scale=1.0)

            def w3():
                P.op('dve', lambda e: e.tensor_tensor_scan(out=CSg[:], data0=CB[:, B_RST:B_RST + 512], data1=SP[:],
                                                           initial=0.0, op0=ALU.mult, op1=ALU.add),
                     reads=[('T', 0), 'CB'], writes=[('T', 1)])

            def w4():
                totb = bass.AP(CSg, 127, [[512, 128], [128, 4], [0, 128]])
                cs3 = bass.AP(CSg, 0, [[512, 128], [128, 4], [1, 128]])
                cr3 = bass.AP(CR0, 0, [[512, 128], [128, 4], [1, 128]])
                tt('pool', cr3, totb, cs3, ALU.subtract, [('T', 1)], [('T', 2)])
                tot4 = bass.AP(CSg, 127, [[512, 128], [128, 4]])
                act(WLG[:, z, blk * 4:blk * 4 + 4], tot4, ACT.Exp, [('T', 1)], ['WLG'], scale=-1.0 / 16)
                if z == 1:
                    tt('pool', INC[:], CR0[:], SP[:], ALU.add, [('T', 2), ('T', 0)], [('T', 3)])
                    tt('pool', RST[:], CSg[:], SP[:], ALU.subtract, [('T', 1), ('T', 0)], ['WS'])

            def w5():
                act(EA, inc[:], ACT.Exp, [ri], [('gE', 0)], scale=-1.0 / 16)
                act(EB, inc[:], ACT.Exp, [ri], [('gE', 1)], scale=1.0 / 16)
                act(EC, rst[:], ACT.Exp, [rr_], [('gE', 2)], scale=-1.0 / 16)

            def w6():
                stt(QE, FA[:, tsl], 128 ** -0.5, EA, ALU.mult, ALU.mult, ['FA', ('gE', 0)], [rq])
                tt('dve', KE, FB[:, tsl], EB, ALU.mult, ['FB', ('gE', 1)], [rk])
                tt('pool', KD, FB[:, tsl], EC, ALU.mult, ['FB', ('gE', 2)], [rd])

            def w7():
                for k in range(4):
                    ksl = slice(k * 128, (k + 1) * 128)
                    mm(PS[6][:, ksl], KE[:, ksl], QE[:, ksl], True, True, [rk, rq], [('ps', 6)])
                for k in range(4):
                    mm(PS[4][:, k * 128:(k + 1) * 128], KD[:, k * 128:(k + 1) * 128], IDB[:], True, True, [rd, 'IDB'], [('ps', 4)])

            def w8():
                mg = bass.AP(CB, B_MG + 128 * z, [[NCB, 128], [0, 4], [1, 128]])
                tt('dve', ATT4, PS[6][:].rearrange("p (k t) -> p k t", k=4), mg, ALU.mult, [('ps', 6), 'CB'], [ra_])
                copy('act', KDT, PS[4][:], [('ps', 4)], [rt])

            def w9():
                for k in range(4):
                    ksl = slice(k * 128, (k + 1) * 128)
                    mb, mbn = MB[k // 2]
                    mm(mb[:, (k % 2) * 256:(k % 2 + 1) * 256], KDT[:, ksl], VTG[:, blk * 4 + k, :], True, True,
                       [rt, ('VTG', blk * 4 + k)], [('ps', mbn)])
            return [w0, w1, w2, w3, w4, w5, w6, w7, w8, w9]

        def g_chain(bi, hd=hd):
            z, blk = gblocks[bi]
            sb_ = bi % 2
            QE, KE, KD, KDT, ATT4 = GSET[sb_]
            rq, ra_ = ('gQE', sb_), ('gATT', sb_)
            MB = [(PS[0], 0), (PS[1], 1)] if sb_ == 0 else [(PS[2], 2), (PS[5], 5)]
            G = []

            def ch(k):
                def g():
                    c16 = blk * 4 + k
                    ksl = slice(k * 128, (k + 1) * 128)
                    mb, mbn = MB[k // 2]
                    ob, obn = (PS[7], 7) if k % 2 == 0 else (PS[3], 3)
                    mm(ob[:, 0:256], ATT4[:, k, :], VTG[:, c16, :], True, False, [ra_, ('VTG', c16)], [('ps', obn)])
                    mm(ob[:, 0:256], QE[:, ksl], SBg, False, True, [rq, 'SBg'], [('ps', obn)])
                    tmpg = GS[:, 1 + (c16 % 2), :]
                    tr = ('TMPg', c16 % 2)
                    stt(tmpg, SFg, WLG[:, z, c16:c16 + 1], mb[:, (k % 2) * 256:(k % 2 + 1) * 256], ALU.mult, ALU.add,
                        ['SFg', 'WLG', ('ps', mbn)], [tr])
                    ts('dve', SFg, tmpg, MSK[:, 2 + z, c16:c16 + 1], None, ALU.mult, None, [tr, 'MSK'], ['SFg'])
                    act(SBg, tmpg, ACT.Copy, [tr, 'MSK'], ['SBg'], scale=MSK[:, 2 + z, c16:c16 + 1])
                    if (c16 % 2 == 1) == (z == 0):
                        P.dma('sp', lambda e, z=z, hd=hd, c16=c16, tmpg=tmpg: e.dma_start(out=o_sg[c16 // 2, z, hd], in_=tmpg), reads=[tr])
                    obf = BA[2 + c16 // 8][:, (c16 % 8) * 256:(c16 % 8 + 1) * 256]
                    obr = 'BA2' if c16 < 8 else 'BA3'
                    if z == 0:
                        copy('act', obf, ob[:, 0:256], [('ps', obn)], [obr])
                    else:
                        tog, sqg = GLT[c16 % 2][:, 0:256], GLT[c16 % 2][:, 256:512]
                        gr = ('GLT', c16 % 2)
                        tt('dve', tog, obf, ob[:, 0:256], ALU.add, [('ps', obn), obr], [gr])
                        tt('pool', sqg, tog, tog, ALU.mult, [gr], [gr])
                        P.op('dve', lambda e, sqg=sqg, c16=c16: e.tensor_reduce(out=RSG[:, c16:c16 + 1], in_=sqg, axis=AX.X, op=ALU.add),
                             reads=[gr], writes=[('RSG', c16)])
                        act(RSG[:, c16:c16 + 1], RSG[:, c16:c16 + 1], ACT.Ln, [('RSG', c16), 'EPS'], [('RSG', c16)], bias=EPS[:, 0:1], scale=1.0 / 256)
                        act(RSG[:, c16:c16 + 1], RSG[:, c16:c16 + 1], ACT.Exp, [('RSG', c16)], [('RSG', c16)], scale=-0.5)
                        act(VTG[:, c16, :], tog, ACT.Copy, [gr, ('RSG', c16)], [('VTG', c16)], scale=RSG[:, c16:c16 + 1])
                return g
            for k in (range(4) if z == 0 else range(3, -1, -1)):
                G.append(ch(k))
            return G

        p0_ = g_prep(0)
        interleave([vgr, [g_init(0)] + p0_[:9]])
        p0_[9]()
        for bi in range(8):
            cg = g_chain(bi)
            if bi == 4:
                cg = [g_init(1)] + cg
            lists = [cg]
            if bi + 1 < 8:
                lists.append(g_prep(bi + 1))
            interleave(lists)
        def half_groups(half, hd=hd):
            slot, SZ, nSZ, TF, nTF, OBh, nOB, pT, nT = ((0, BA[2], 'BA2', FB, 'FB', BA[3], 'BA3', PS[1], 1) if half == 0 else
                                                         (1, BA[0], 'BA0', FA, 'FA', BA[1], 'BA1', PS[0], 0))
            G = []

            def ga():
                load_w(d_owin, 2048 + hd * 256 + half * 128, slot)
                proj(slot, lambda t4, tsl, ps, pr: act(SZ[:, tsl], ps, ACT.Silu, [pr], [nSZ]))
            G.append(ga)

            def gt(t4):
                def f():
                    tsl = slice(t4 * 512, (t4 + 1) * 512)
                    for k in range(4):
                        mm(pT[:, k * 128:(k + 1) * 128], VTG[:, t4 * 4 + k, half * 128:(half + 1) * 128], IDB[:], True, True,
                           [('VTG', t4 * 4 + k), 'IDB'], [('ps', nT)])
                    ts('dve', TF[:, tsl], pT[:], VEC[:, V_GN + half:V_GN + half + 1], None, ALU.mult, None, [('ps', nT), 'VEC'], [nTF])
                return f
            for t4 in range(4):
                G.append(gt(t4))

            def gm():
                tt('pool', OBh[:], TF[:], SZ[:], ALU.mult, [nTF, nSZ], [nOB])
            G.append(gm)
            G.append(lambda: wout_partial(1, d_owout, hd * 2 + half, OBh, nOB))
            return G
        interleave([half_groups(0), half_groups(1)])

    ftsl = [slice(t4 * 512, (t4 + 1) * 512) for t4 in range(4)]
    sumsq_rstd(ftsl[0], 0)
    for t4 in range(4):
        tsl = ftsl[t4]
        if t4 + 1 < 4:
            sumsq_rstd(ftsl[t4 + 1], (t4 + 1) % 2)
        rb_, rn_ = RSTD[t4 % 2]
        for c in range(8):
            stt(FA[:, (c % 4) * 512:(c % 4 + 1) * 512], X[:, c, tsl], VEC[:, V_FG + c:V_FG + c + 1], rb_[:],
                ALU.mult, ALU.mult, [('X', c), 'VEC', rn_], [('FAq', c % 4)])
            P.dma('sp', lambda e, c=c, tsl=tsl: e.dma_start(out=o_y[:, c, tsl], in_=FA[:, (c % 4) * 512:(c % 4 + 1) * 512]),
                  reads=[('FAq', c % 4)])
    P.finish_waits('sp')
    P.emit()
    global _LAST_P
    _LAST_P = P
    st.close()
    return nc


def kernel(**inp):
    f = lambda k: np.asarray(inp[k], np.float32)
    plan = _assign()
    cf, cb = _consts()
    vec0 = np.zeros((128, NV), np.float32)
    vec0[:, V_NG:V_NG + 16] = np.concatenate([_col(f('norm_g')[0]), _col(f('norm_g')[1])], 1)
    vec0[:, V_FG:V_FG + 8] = _col(f('final_g'))
    cw = f('conv_w')[0]
    for k in range(3):
        vec0[:, V_CW + 8 * k:V_CW + 8 * k + 8] = _col(cw[k])
    for z in range(2):
        vec0[:, V_W0 + 8 * z:V_W0 + 8 * z + 8] = _col(f('wkv_w0')[0, z])
        vec0[:, V_A0 + 8 * z:V_A0 + 8 * z + 8] = _col(f('wkv_a0')[0, z])
        vec0[:, V_GB + 4 * z:V_GB + 4 * z + 4] = _col(f('gla_gk_b')[0, z])
    vec0[:, V_KK:V_KK + 8] = _col(f('wkv_k_k')[0])
    vec0[:, V_KA:V_KA + 8] = _col(f('wkv_k_a')[0])
    vec0[:, V_RK:V_RK + 8] = _col(f('wkv_r_k')[0].reshape(-1))
    vec0[:, V_LW:V_LW + 8] = _col(f('wkv_ln_w')[0])
    vec0[:, V_LB:V_LB + 8] = _col(f('wkv_ln_b')[0])
    for l in range(2):
        vec0[:, V_AB + 24 * l:V_AB + 24 * l + 24] = _col(f('ada_b')[l])
    vec0[:, V_GN:V_GN + 2] = _col(f('gla_g_norm')[0])
    r3 = lambda w: np.ascontiguousarray(w.reshape(-1, 128, w.shape[-1]).transpose(1, 0, 2))
    ada = np.stack([r3(f('ada_w')[l]) for l in range(2)])
    ewin = r3(f('e_w_in')[0])
    ewout = r3(f('e_w_out')[0])
    owin = r3(f('o_w_in')[0])
    owout = r3(f('o_w_out')[0])
    w1c = np.concatenate([r3(f('wkv_w1')[0, 0]), r3(f('wkv_w1')[0, 1]), r3(f('wkv_a1')[0, 0]), r3(f('wkv_a1')[0, 1])], 2)
    w2p = np.zeros((128, 4, 1024), np.float32)
    for z in range(2):
        w2p[64 * z:64 * z + 64, z] = f('wkv_w2')[0, z]
        w2p[64 * z:64 * z + 64, 2 + z] = f('wkv_a2')[0, z]
    g1c = np.concatenate([r3(f('gla_gk1')[0, 0]), r3(f('gla_gk1')[0, 1])], 2)
    g2p = np.zeros((32, 2, 512), np.float32)
    for z in range(2):
        g2p[16 * z:16 * z + 16, z] = f('gla_gk2')[0, z]
    xp, xs = f('x_prompt'), f('x_sample')
    swkv, sgla = f('state_wkv'), f('state_gla')
    in_maps = []
    for core, items in enumerate(plan):
        x = np.zeros((NT, 1024), np.float32)
        msk = np.zeros((128, 4, 32), np.float32)
        s_w = np.zeros((128, 8, 2, 128), np.float32)
        s_g = np.zeros((128, 4, 2, 256), np.float32)
        vec = vec0.copy()
        if items[0][0] == 's':
            b = items[0][1]
            x[:] = xs[b]
            cv = f('c')[b]
            msk[:, 2, :16] = 1.0
            msk[:, 3, :16] = 1.0
            for z in range(2):
                for h in range(16):
                    jj, hl = divmod(h, 2)
                    s_w[64 * hl:64 * hl + 64, jj, z, 64 * hl:64 * hl + 64] = swkv[b, 0, z, h].T
                for h in range(4):
                    s_g[:, h, z, :] = sgla[b, 0, z, h]
        else:
            for si in range(8):
                x[256 * si:256 * si + 256] = xp[items[si % len(items)][1]]
            cv = f('c_ctx')
            g = np.arange(32)
            inner = (g % 4 != 0).astype(np.float32)
            msk[:, 0, :] = inner[None]
            msk[:, 1, :] = inner[None]
            c16 = np.arange(16)
            msk[:, 2, :16] = (c16 % 2 == 0)[None]
            msk[:, 3, :16] = (c16 % 2 == 1)[None]
        vec[:, V_CV:V_CV + 8] = _col(cv)
        xT = np.ascontiguousarray(x.reshape(NT, 8, 128).transpose(2, 1, 0))
        in_maps.append(dict(xT=xT, vec=vec, cf=cf, cb=cb.astype(ml_dtypes.bfloat16), msk=msk, ada=ada, ewin=ewin, ewout=ewout, w1c=w1c,
                            w2p=w2p, s_wkv=s_w, owin=owin, owout=owout, g1c=g1c, g2p=g2p, s_gla=s_g))
    nc = build()
    res = run_bass_kernel_spmd(nc, in_maps, core_ids=list(range(8)))
    y_p = np.zeros((16, 256, 1024), np.float32)
    y_s = np.zeros((2, 2048, 1024), np.float32)
    n_w = np.zeros((16, 1, 2, 16, 64, 64), np.float32)
    n_g = np.zeros((16, 1, 2, 4, 128, 256), np.float32)
    for core, items in enumerate(plan):
        r = res.results[core]
        y = np.asarray(r["yT"]).transpose(2, 1, 0).reshape(NT, 1024)
        ow = np.asarray(r["o_wkv"])
        og = np.asarray(r["o_gla"])
        if items[0][0] == 's':
            y_s[items[0][1]] = y
        else:
            for si, (_, pi) in enumerate(items):
                y_p[pi] = y[256 * si:256 * si + 256]
                for z in range(2):
                    for h in range(16):
                        jj, hl = divmod(h, 2)
                        n_w[pi, 0, z, h] = ow[si, z, jj, 64 * hl:64 * hl + 64, 64 * hl:64 * hl + 64].T
                    n_g[pi, 0, z] = og[si, z]
    return (y_p, y_s, n_w, n_g)
```

```python
import contextlib
import numpy as np
import ml_dtypes
import concourse.bass as bass
import concourse.mybir as mybir
from concourse.bass_utils import run_bass_kernel_spmd

ACT = mybir.ActivationFunctionType
ALU = mybir.AluOpType
F32 = mybir.dt.float32
BF16 = mybir.dt.bfloat16
AX = mybir.AxisListType

ENGS = ['pe', 'act', 'dve', 'pool', 'sp']
EPOCH = 4000
NDS = 8
NT = 2048
NCH = 16
LAM = 0.6065306597126334
NORM_EPS = 1e-6
GN_EPS = 64e-5


class Prog:
    def __init__(self, nc):
        self.nc = nc
        self.ops = {e: [] for e in ENGS}
        self.count = {e: 0 for e in ENGS}
        self.dcount = {e: 0 for e in ENGS}
        self.last_w = {}
        self.readers = {}
        self.waited = {e: {} for e in ENGS}
        self.pending = {e: [] for e in ENGS}

    def _deps(self, eng, reads, writes):
        deps = set()
        for r in reads:
            if r in self.last_w:
                deps.add(self.last_w[r])
        for w in writes:
            if w in self.last_w:
                deps.add(self.last_w[w])
            for rd in self.readers.get(w, ()):
                deps.add(rd)
        best = {}
        for d in deps:
            if eng == 'pe' and d[:-1] == ('e', 'pe'):
                continue
            best[d[:-1]] = max(best.get(d[:-1], 0), d[-1])
        for d in self.pending[eng]:
            best[d[:-1]] = max(best.get(d[:-1], 0), d[-1])
        self.pending[eng] = []
        final = []
        for key, i in best.items():
            if self.waited[eng].get(key, 0) < i:
                self.waited[eng][key] = i
                final.append(key + (i,))
        return final

    def _mark(self, tok, reads, writes):
        for r in reads:
            self.readers.setdefault(r, []).append(tok)
        for w in writes:
            self.last_w[w] = tok
            self.readers[w] = []

    def op(self, eng, fn, reads=(), writes=()):
        writes = list(writes) + [r for r in reads if isinstance(r, tuple) and r[0] == 'ps']
        waits = self._deps(eng, reads, writes)
        idx = self.count[eng] + 1
        self.count[eng] = idx
        self.ops[eng].append(('c', fn, waits, idx))
        self._mark(('e', eng, idx), reads, writes)

    def dma(self, eng, fn, reads=(), writes=()):
        waits = self._deps(eng, reads, writes)
        j = self.dcount[eng]
        self.dcount[eng] = j + 1
        slot = j % NDS
        if j >= NDS:
            key = ('d', eng, slot)
            need = j // NDS
            if self.waited[eng].get(key, 0) < need:
                self.waited[eng][key] = need
                waits.append(key + (need,))
        self.ops[eng].append(('d', fn, waits, (slot, j // NDS + 1)))
        self._mark(('d', eng, slot, j // NDS + 1), reads, writes)

    def barrier(self):
        snap = [('e', e, self.count[e]) for e in ENGS if self.count[e]]
        for q in ENGS:
            n = self.dcount[q]
            for slot in range(min(n, NDS)):
                snap.append(('d', q, slot, (n - 1 - slot) // NDS + 1))
        for e in ENGS:
            self.pending[e] = list(snap)

    def finish_waits(self, eng='sp'):
        waits = []
        for q in ENGS:
            n = self.dcount[q]
            for slot in range(min(n, NDS)):
                waits.append(('d', q, slot, (n - 1 - slot) // NDS + 1))
        self.ops[eng].append(('w', None, waits, None))

    def emit(self):
        nc = self.nc
        with contextlib.ExitStack() as st:
            esem = {e: [st.enter_context(nc.semaphore(f"s_{e}_{k}")) for k in range(self.count[e] // EPOCH + 1)]
                    for e in ENGS}
            dsem = {e: [st.enter_context(nc.semaphore(f"d_{e}_{k}")) for k in range(NDS)]
                    for e in ENGS if self.dcount[e]}
            block = st.enter_context(nc.Block())

            def run(handle, e):
                for kind, fn, waits, info in self.ops[e]:
                    for w in waits:
                        if w[0] == 'e':
                            handle.wait_ge(esem[w[1]][(w[2] - 1) // EPOCH], (w[2] - 1) % EPOCH + 1)
                        else:
                            handle.wait_ge(dsem[w[1]][w[2]], 16 * w[3])
                    if kind == 'c':
                        fn(handle).then_inc(esem[e][(info - 1) // EPOCH], 1)
                    elif kind == 'd':
                        fn(handle).then_inc(dsem[e][info[0]], 16)

            @block.tensor
            def _(h):
                run(h, 'pe')

            @block.scalar
            def _(h):
                run(h, 'act')

            @block.vector
            def _(h):
                run(h, 'dve')

            @block.gpsimd
            def _(h):
                run(h, 'pool')

            @block.sync
            def _(h):
                run(h, 'sp')


V_NG, V_FG, V_CW, V_W0, V_A0, V_KK, V_KA, V_RK, V_LW, V_LB, V_AB, V_GB, V_GN, V_CV = \
    0, 16, 24, 48, 64, 80, 88, 96, 104, 112, 120, 168, 176, 178
NV = 186
C_ID, C_BD, C_HM, C_ONE = 0, 128, 256, 258
NCF = 386
B_RST, B_MAB, B_MN, B_MG = 0, 512, 1536, 2048
NCB = 2304


def _col(v):
    v = np.asarray(v, np.float32).reshape(-1, 128)
    return np.ascontiguousarray(v.T)


def _consts():
    u = np.arange(128)[:, None]
    t = np.arange(128)[None, :]
    LT, LE, GT, GE = (u < t), (u <= t), (u > t), (u >= t)
    cf = np.zeros((128, NCF), np.float32)
    cf[:, C_ID:C_ID + 128] = np.eye(128)
    cf[:, C_BD:C_BD + 128] = (u // 64 == t // 64)
    cf[:, C_HM] = (np.arange(128) < 64)
    cf[:, C_HM + 1] = (np.arange(128) >= 64)
    cf[:, C_ONE:C_ONE + 128] = 1.0
    cb = np.zeros((128, NCB), np.float32)
    rst = np.ones(512, np.float32)
    rst[::128] = 0
    cb[:, B_RST:B_RST + 512] = rst[None]
    cb[:, B_MAB:B_MAB + 512] = np.concatenate([LT, LE, LT, LE], 1)
    cb[:, B_MAB + 512:B_MAB + 1024] = np.concatenate([GT, GE, GT, GE], 1)
    cb[:, B_MN:B_MN + 256] = np.concatenate([GT, GT], 1)
    cb[:, B_MN + 256:B_MN + 512] = np.concatenate([LT, LT], 1)
    cb[:, B_MG:B_MG + 128] = LE
    cb[:, B_MG + 128:B_MG + 256] = GE
    return cf, cb


def _assign():
    plan = [[('s', 0)], [('s', 1)]]
    p = 0
    for n in (3, 3, 3, 3, 2, 2):
        plan.append([('p', p + i) for i in range(n)])
        p += n
    return plan


def build(stop_after=99):
    nc = bass.Bass("TRN2", target_bir_lowering=False)
    dt_in = lambda n, s: nc.dram_tensor(n, s, F32, kind="ExternalInput").ap()
    dt_out = lambda n, s: nc.dram_tensor(n, s, F32, kind="ExternalOutput").ap()
    d_x = dt_in("xT", [128, 8, NT])
    d_vec = dt_in("vec", [128, NV])
    d_cf = dt_in("cf", [128, NCF])
    d_cb = nc.dram_tensor("cb", [128, NCB], BF16, kind="ExternalInput").ap()
    d_msk = dt_in("msk", [128, 4, 32])
    d_ada = dt_in("ada", [2, 128, 8, 3072])
    d_ewin = dt_in("ewin", [128, 8, 8192])
    d_ewout = dt_in("ewout", [128, 16, 1024])
    d_w1 = dt_in("w1c", [128, 8, 256])
    d_w2 = dt_in("w2p", [128, 4, 1024])
    d_sw = dt_in("s_wkv", [128, 8, 2, 128])
    d_owin = dt_in("owin", [128, 8, 3072])
    d_owout = dt_in("owout", [128, 8, 1024])
    d_g1 = dt_in("g1c", [128, 8, 32])
    d_g2 = dt_in("g2p", [32, 2, 512])
    d_sg = dt_in("s_gla", [128, 4, 2, 256])
    o_y = dt_out("yT", [128, 8, NT])
    o_sw = dt_out("o_wkv", [8, 2, 8, 128, 128])
    o_sg = dt_out("o_gla", [8, 2, 4, 128, 256])

    st = contextlib.ExitStack()
    sb = lambda n, s, d=F32: st.enter_context(nc.sbuf_tensor(n, s, d))
    X = sb("X", [128, 8, NT])
    HT = sb("HT", [128, 8, NT], BF16)
    VEC = sb("VEC", [128, NV])
    CF = sb("CF", [128, NCF])
    CB = sb("CB", [128, NCB], BF16)
    IDB = sb("IDB", [128, 128], BF16)
    ONEB = sb("ONEB", [128, 128], BF16)
    BDB = sb("BDB", [128, 128], BF16)
    MSK = sb("MSK", [128, 4, 32])
    MOD = sb("MOD", [128, 2, 24])
    G1 = sb("G1", [128, 2, 8])
    CS_ = sb("CSIL", [128, 8])
    EPS = sb("EPS", [128, 2])
    FA = sb("FA", [128, NT])
    FB = sb("FB", [128, NT])
    BA = [sb(f"BA{i}", [128, NT], BF16) for i in range(4)]
    WRAW = sb("WRAW", [128, 3072])
    WS = WRAW[:, 0:1024].rearrange("p (a b) -> p a b", a=8)
    WB = WRAW[:, 1024:3072].bitcast(BF16).rearrange("p (a b c) -> p a b c", a=4, b=8)
    WOB = WB[:, 3, :, :].rearrange("p a b -> p (a b)")
    UNI = sb("UNI", [128, 8960])
    ub = lambda a, b, p=128: UNI[0:p, a:b].bitcast(BF16)
    T512 = [sb(f"T512_{i}", [128, 512]) for i in range(4)] + [WRAW[:, 512 * i:512 * (i + 1)] for i in range(6)]
    G1B = ub(1024, 1152).rearrange("p (a b) -> p a b", a=8)
    G2B = ub(1152, 1664, 32).rearrange("p (a b) -> p a b", a=2)
    GT1 = ub(0, 1024, 32)
    VTG = sb("VTG", [128, 16, 256], BF16)
    KKF = sb("KKF", [128, NT], BF16)
    GS = sb("GS", [128, 3, 256])
    WLG = sb("WLG", [128, 2, 16])
    NGB = sb("NGB", [128, 8])
    RSG = sb("RSG", [128, 16])
    PRB = ub(0, 2560)
    CHB = ub(2560, 6080)
    BKT = ub(6080, 7104).rearrange("p (a b) -> p a b", a=4)
    PRS = ub(7104, 8128)
    W2S = WRAW[:, 0:512].rearrange("p (a b) -> p a b", a=4)
    W2B = ub(8128, 8384).rearrange("p (a b) -> p a b", a=4)
    W_XAM = 8384
    WLW = sb("WLW", [128, 2, 16])
    OMKA = sb("OMKA", [128, 8])
    GNS = sb("GNS", [128, 8])
    WOT = [T512[2], T512[3]]
    PS = [st.enter_context(nc.psum_tensor(f"ps{i}", [128, 512], F32)) for i in range(8)]

    P = Prog(nc)
    cnt = {'rr': 0}

    def rr(engs=('act', 'dve')):
        cnt['rr'] += 1
        return engs[cnt['rr'] % len(engs)]

    def mm(out, lhsT, rhs, start, stop, r, w):
        P.op('pe', lambda e: e.matmul(out, lhsT, rhs, start=start, stop=stop), reads=r, writes=w)

    def copy(eng, out, in_, r, w):
        if eng == 'act':
            P.op('act', lambda e: e.activation(out=out, in_=in_, func=ACT.Copy), reads=r, writes=w)
        else:
            P.op(eng, lambda e: e.tensor_copy(out=out, in_=in_), reads=r, writes=w)

    def tt(eng, out, a, b, op, r, w):
        P.op(eng, lambda e: e.tensor_tensor(out=out, in0=a, in1=b, op=op), reads=r, writes=w)

    def ts(eng, out, a, s1, s2, op0, op1, r, w):
        if s2 is None:
            P.op(eng, lambda e: e.tensor_scalar(out=out, in0=a, scalar1=s1, scalar2=None, op0=op0), reads=r, writes=w)
        else:
            P.op(eng, lambda e: e.tensor_scalar(out=out, in0=a, scalar1=s1, scalar2=s2, op0=op0, op1=op1), reads=r, writes=w)

    def stt(out, a, s, b, op0, op1, r, w):
        P.op('dve', lambda e: e.scalar_tensor_tensor(out=out, in0=a, scalar=s, in1=b, op0=op0, op1=op1), reads=r, writes=w)

    def act(out, in_, func, r, w, bias=None, scale=None):
        kw = {}
        if bias is not None:
            kw['bias'] = bias
        if scale is not None:
            kw['scale'] = scale
        P.op('act', lambda e: e.activation(out=out, in_=in_, func=func, **kw), reads=r, writes=w)

    def ld(out, in_, w, r=()):
        P.dma('sp', lambda e: e.dma_start(out=out, in_=in_), reads=r, writes=w)

    for c in range(8):
        ld(X[:, c, :], d_x[:, c, :], [('X', c)])
    ld(VEC[:], d_vec, ['VEC'])
    ld(CF[:], d_cf, ['CF'])
    ld(CB[:], d_cb, ['CB'])
    ld(MSK[:], d_msk, ['MSK'])
    copy('pool', IDB[:], CF[:, C_ID:C_ID + 128], ['CF'], ['IDB'])
    copy('pool', ONEB[:], CF[:, C_ONE:C_ONE + 128], ['CF'], ['ONEB'])
    copy('pool', BDB[:], CF[:, C_BD:C_BD + 128], ['CF'], ['BDB'])
    P.op('pool', lambda e: e.memset(EPS[:, 0:1], NORM_EPS), writes=['EPS'])
    P.op('pool', lambda e: e.memset(EPS[:, 1:2], GN_EPS), writes=['EPS'])
    act(CS_[:], VEC[:, V_CV:V_CV + 8], ACT.Silu, ['VEC'], ['CSIL'])
    def ada_layer(l, ACCQ, an, STG, sn):
        steps = []
        for c in range(8):
            for q in range(3):
                def st_(c=c, q=q, i=len(steps)):
                    sg, sr = STG[i % 4], (sn, i % 4)
                    ld(sg, d_ada[l, :, c, q * 1024:(q + 1) * 1024], [sr])
                    if c == 0:
                        ts('dve', ACCQ[q], sg, CS_[:, c:c + 1], None, ALU.mult, None, [sr, 'CSIL'], [(an, q)])
                    else:
                        stt(ACCQ[q], sg, CS_[:, c:c + 1], ACCQ[q], ALU.mult, ALU.add, [sr, 'CSIL', (an, q)], [(an, q)])
                steps.append(st_)

        def fin():
            for j in range(24):
                mm(PS[0][:, j:j + 1], ACCQ[j // 8][:, (j % 8) * 128:(j % 8 + 1) * 128], CF[:, C_ONE:C_ONE + 1],
                   True, True, [(an, j // 8), 'CF'], [('ps', 0)])
            tt('dve', MOD[:, l, :], PS[0][:, 0:24], VEC[:, V_AB + 24 * l:V_AB + 24 * l + 24], ALU.add,
               [('ps', 0), 'VEC'], ['MOD'])
            ts('dve', G1[:, l, :], MOD[:, l, 8:16], 1.0, None, ALU.add, None, ['MOD'], ['G1'])
            tt('dve', G1[:, l, :], G1[:, l, :], VEC[:, V_NG + 8 * l:V_NG + 8 * l + 8], ALU.mult, ['G1', 'VEC'], ['G1'])
        return steps, fin

    st0, fin0 = ada_layer(0, [FA[:, 0:1024], FA[:, 1024:2048], FB[:, 1024:2048]], 'ACC',
                          [FB[:, 0:1024], WRAW[:, 0:1024], WRAW[:, 1024:2048], WRAW[:, 2048:3072]], 'STG')
    for f_ in st0:
        f_()
    fin0()
    ada1_steps, ada1_fin = ada_layer(1, [UNI[:, 1024 * i:1024 * (i + 1)] for i in range(3)], 'uACC',
                                     [UNI[:, 3072 + 1024 * i:4096 + 1024 * i] for i in range(4)], 'uSTG')

    XR = [('X', c) for c in range(8)]
    HR = [('HT', c) for c in range(8)]

    SQB = [WRAW[:, 0:256].bitcast(BF16), WRAW[:, 1024:1280].bitcast(BF16)]
    SQR = ['WS', ('WB', 0)]

    RSTD = [(T512[2], ('T', 2)), (T512[3], ('T', 3))]

    def sumsq_rstd(tsl, ri=0):
        rb_, rn_ = RSTD[ri]
        for c in range(8):
            act(SQB[c % 2], X[:, c, tsl], ACT.Square, [('X', c)], [SQR[c % 2]])
            mm(PS[1][:], ONEB[:], SQB[c % 2], c == 0, c == 7, [SQR[c % 2], 'ONEB'], [('ps', 1)])
        act(rb_[:], PS[1][:], ACT.Sqrt, [('ps', 1), 'EPS'], [rn_], bias=EPS[:, 0:1], scale=1.0 / 1024)
        P.op('dve', lambda e: e.reciprocal(out=rb_[:], in_=rb_[:]), reads=[rn_], writes=[rn_])

    def norm_mod(gfn, sfn, out_fn, out_res):
        tsls = [slice(t4 * 512, (t4 + 1) * 512) for t4 in range(4)]
        sumsq_rstd(tsls[0], 0)
        for t4 in range(4):
            tsl = tsls[t4]
            if t4 + 1 < 4:
                sumsq_rstd(tsls[t4 + 1], (t4 + 1) % 2)
            rb_, rn_ = RSTD[t4 % 2]
            for c in range(8):
                tmp = T512[c % 2]
                o = out_fn(c, tsl)
                if c % 2 == 0:
                    tt('pool', tmp[:], X[:, c, tsl], rb_[:], ALU.mult, [('X', c), rn_], [('T', 0)])
                    ts('dve', o, tmp[:], gfn(c), sfn(c), ALU.mult, ALU.add, [('T', 0), 'VEC', 'G1', 'MOD'], out_res(c, t4))
                else:
                    tt('dve', tmp[:], X[:, c, tsl], rb_[:], ALU.mult, [('X', c), rn_], [('T', 1)])
                    act(o, tmp[:], ACT.Identity, [('T', 1), 'VEC', 'G1', 'MOD'], out_res(c, t4), bias=sfn(c), scale=gfn(c))

    def load_w(dram, col0, br):
        ld(WS[:], dram[:, :, col0:col0 + 128], ['WS'])
        copy(rr(('act', 'dve')), WB[:, br, :, :], WS[:], ['WS'], [('WB', br)])

    def proj(br, evac, banks=(2, 3, 4, 5)):
        for t4 in range(4):
            tsl = slice(t4 * 512, (t4 + 1) * 512)
            pb = banks[cnt['rr'] % len(banks)]
            cnt['rr'] += 1
            for c in range(8):
                mm(PS[pb][:], WB[:, br, c, :], HT[:, c, tsl], c == 0, c == 7, [('WB', br), ('HT', c)], [('ps', pb)])
            evac(t4, tsl, PS[pb][:], ('ps', pb))

    def wout_partial(l, dram_wout, j, OB, ores):
        ld(WS[:].rearrange("p a b -> p (a b)"), dram_wout[:, j, :], ['WS'])
        copy('pool', WOB[:], WS[:].rearrange("p a b -> p (a b)"), ['WS'], [('WB', 3)])
        for ft in range(8):
            for t4 in range(4):
                tsl = slice(t4 * 512, (t4 + 1) * 512)
                pb = 2 + (cnt['rr'] % 4)
                cnt['rr'] += 1
                mm(PS[pb][:], WOB[:, ft * 128:(ft + 1) * 128], OB[:, tsl], True, True, [('WB', 3), ores], [('ps', pb)])
                if (ft * 4 + t4) % 5 < 3:
                    stt(X[:, ft, tsl], PS[pb][:], MOD[:, l, 16 + ft:17 + ft], X[:, ft, tsl], ALU.mult, ALU.add,
                        [('ps', pb), 'MOD', ('X', ft)], [('X', ft)])
                else:
                    wt = WOT[t4 % 2]
                    act(wt[:], PS[pb][:], ACT.Copy, [('ps', pb), 'MOD'], [('T', 2 + t4 % 2)], scale=MOD[:, l, 16 + ft:17 + ft])
                    tt('pool', X[:, ft, tsl], X[:, ft, tsl], wt[:], ALU.add, [('T', 2 + t4 % 2), ('X', ft)], [('X', ft)])

    P.barrier()
    norm_mod(lambda c: G1[:, 0, c:c + 1], lambda c: MOD[:, 0, c:c + 1], lambda c, tsl: HT[:, c, tsl],
             lambda c, t4: [('HT', c)])

    def v3(t, a, b):
        return t[:].rearrange("p (g w) -> p g w", w=64)[:, a, b]

    for j in range(8):
        for br in range(4):
            load_w(d_ewin, br * 1024 + j * 128, br)
        for f_ in ada1_steps[3 * j:3 * j + 3]:
            f_()
        U, Pm, Y = FA, FB, FA
        proj(0, lambda t4, tsl, ps, pr: copy(rr(), FA[:, tsl], ps, [pr], ['FA']))
        proj(2, lambda t4, tsl, ps, pr: tt('dve', FB[:, tsl], ps, FA[:, tsl], ALU.mult, [pr, 'FA'], ['FB']))
        proj(1, lambda t4, tsl, ps, pr: copy(rr(), BA[0][:, tsl], ps, [pr], ['BA0']))
        proj(3, lambda t4, tsl, ps, pr: act(BA[1][:, tsl], ps, ACT.Silu, [pr], ['BA1']))
        w0, w1, w2 = (VEC[:, V_CW + 8 * k + j:V_CW + 8 * k + j + 1] for k in range(3))
        act(FA[:], FB[:], ACT.Copy, ['FB', 'VEC'], ['FA'], scale=w1)
        g_all, g_lo, g_hi = slice(0, 32), slice(0, 31), slice(1, 32)
        stt(v3(FA, g_all, slice(1, 64)), v3(FB, g_all, slice(0, 63)), w0, v3(FA, g_all, slice(1, 64)),
            ALU.mult, ALU.add, ['FB', 'FA', 'VEC'], ['FA'])
        stt(v3(FA, g_all, slice(0, 63)), v3(FB, g_all, slice(1, 64)), w2, v3(FA, g_all, slice(0, 63)),
            ALU.mult, ALU.add, ['FB', 'FA', 'VEC'], ['FA'])
        tb = T512[0]
        tt('pool', tb[:, 0:31], v3(FB, g_lo, 63), MSK[:, 0, 1:32], ALU.mult, ['FB', 'MSK'], [('T', 0)])
        stt(v3(FA, g_hi, 0), tb[:, 0:31], w0, v3(FA, g_hi, 0), ALU.mult, ALU.add, [('T', 0), 'FA', 'VEC'], ['FA'])
        tt('pool', tb[:, 32:63], v3(FB, g_hi, 0), MSK[:, 1, 1:32], ALU.mult, ['FB', 'MSK'], [('T', 0)])
        stt(v3(FA, g_lo, 63), tb[:, 32:63], w2, v3(FA, g_lo, 63), ALU.mult, ALU.add, [('T', 0), 'FA', 'VEC'], ['FA'])
        tt('pool', FA[:], FA[:], BA[0][:], ALU.mult, ['FA', 'BA0'], ['FA'])
        tt('dve', BA[2][:], FA[:], BA[1][:], ALU.mult, ['FA', 'BA1'], ['BA2'])
        wout_partial(0, d_ewout, j, BA[2], 'BA2')

    ada1_fin()
    P.barrier()
    load_w(d_w1, 0, 0)
    load_w(d_w1, 128, 1)
    proj(0, lambda t4, tsl, ps, pr: act(BA[2][:, tsl], ps, ACT.Tanh, [pr], ['BA2']))
    proj(1, lambda t4, tsl, ps, pr: copy('act', BA[3][:, tsl], ps, [pr], ['BA3']))
    ts('pool', OMKA[:], VEC[:, V_KA:V_KA + 8], -1.0, 1.0, ALU.mult, ALU.add, ['VEC'], ['OMKA'])
    KKf = KKF[:]
    Rb = FA[:, 0:1024].bitcast(BF16)
    Kb = FA[:, 1024:2048].bitcast(BF16)
    TB = T512
    c3 = lambda ap: ap.rearrange("p (k t) -> p k t", t=128)
    FBb = FB[:].bitcast(BF16)
    PRBs = [PRB, FBb]
    XAm = ub(W_XAM, W_XAM + 512)
    def opnd(par):
        base = PRBs[par]
        d = dict(AR=base[:, 0:1024], Bh=base[:, 1024:1536], Kh=base[:, 1536:2048],
                 Btm=[base[:, 2048:2560], base[:, 2560:3072]], Ktm=[base[:, 3072:3584], base[:, 3584:4096]])
        d['Am'] = [base[:, 4096:4608], base[:, 4608:5120]] if par == 0 else [XAm[:, 0:512], XAm[:, 512:1024]]
        d['AR4'] = d['AR'].rearrange("p (k s t) -> p k s t", s=2, t=128)
        return d
    OPN = [opnd(0), opnd(1)]
    NM0 = [CHB[:, 1024 * i:1024 * i + 512] for i in range(3)]
    NM1 = [CHB[:, 1024 * i + 512:1024 * i + 1024] for i in range(3)]
    lv4 = lambda ap: ap.rearrange("p (h s t) -> p h s t", h=2, s=2)
    LVs = [[lv4(CHB[:, 3072 + 1536 * c + 512 * i:3072 + 1536 * c + 512 * (i + 1)]) for i in range(2)] for c in range(2)]
    PPs = [[CHB[:, 4096 + 1536 * c + 256 * i:4096 + 1536 * c + 256 * (i + 1)] for i in range(2)] for c in range(2)]
    TTf = [CHB[:, 6144:6400], CHB[:, 6400:6656]]
    Z0B, UB, SBw = CHB[:, 6656:6784], CHB[:, 6784:6912], CHB[:, 6912:7040]
    BKTs = [BKT[:, 0:2, :], BKT[:, 2:4, :]]
    SFw = GS[:, 0, 0:128]
    BDm = CF[:, C_BD:C_BD + 128]
    HM = [CF[:, C_HM:C_HM + 1], CF[:, C_HM + 1:C_HM + 2]]
    hs = lambda h: slice(h * 64, (h + 1) * 64)

    def interleave(lists):
        lists = [l for l in lists if l]
        pos = [0] * len(lists)
        n = max(len(l) for l in lists) if lists else 0
        for step in range(n):
            for li, l in enumerate(lists):
                tgt = (step + 1) * len(l) // n
                while pos[li] < tgt:
                    l[pos[li]]()
                    pos[li] += 1

    KT = [UNI[:, 512 * i:512 * (i + 1)] for i in range(3)]
    ET = [FB[:, 512 * i:512 * (i + 1)] for i in range(2)]

    def start_rk(jj, banks=(2, 3, 4, 5), kb=1):
        vcol = lambda base: VEC[:, base + jj:base + jj + 1]
        G = []

        def g_load():
            load_w(d_ewin, 4096 + 0 * 1024 + jj * 128, 0)
            load_w(d_ewin, 4096 + 1 * 1024 + jj * 128, 1)
            ld(W2S[:], d_w2[:, :, jj * 128:(jj + 1) * 128], ['WS'])
            copy('pool', W2B[:], W2S[:], ['WS'], ['W2B'])
        G.append(g_load)
        G.append(lambda: proj(0, lambda t4, tsl, ps, pr: copy(rr(), Rb[:, tsl], ps, [pr], ['FA']), banks))
        G.append(lambda: proj(1, lambda t4, tsl, ps, pr: copy(rr(), Kb[:, tsl], ps, [pr], ['FA']), banks))

        def g_kk(t4):
            def g():
                tsl = slice(t4 * 512, (t4 + 1) * 512)
                o0 = ('OPN', 0)
                sqb = KT[1].bitcast(BF16)[:, 0:512]
                act(KT[0][:], Kb[:, tsl], ACT.Copy, ['FA', 'VEC'], [o0], scale=vcol(V_KK))
                act(sqb, KT[0][:], ACT.Square, [], [o0])
                mm(PS[kb][:], BDB[:], sqb, True, True, [o0, 'BDB'], [('ps', kb)])
                act(KT[2][:], PS[kb][:], ACT.Sqrt, [('ps', kb)], [o0])
                ts('dve', KT[2][:], KT[2][:], 1e-12, None, ALU.max, None, [], [o0])
                P.op('dve', lambda e: e.reciprocal(out=KT[2][:], in_=KT[2][:]), reads=[], writes=[o0])
                tt('pool', KKf[:, tsl], KT[0][:], KT[2][:], ALU.mult, [o0], ['KKf'])
            return g
        for t4 in range(4):
            G.append(g_kk(t4))
        return G

    def start_vz(jj):
        G = []

        def g_l():
            load_w(d_ewin, 4096 + 2 * 1024 + jj * 128, 2)
            load_w(d_ewin, 4096 + 3 * 1024 + jj * 128, 3)
        G.append(g_l)
        G.append(lambda: proj(2, lambda t4, tsl, ps, pr: copy(rr(), BA[0][:, tsl], ps, [pr], ['BA0'])))
        G.append(lambda: proj(3, lambda t4, tsl, ps, pr: act(BA[1][:, tsl], ps, ACT.Silu, [pr], ['BA1'])))

        def g_vt(g):
            def f():
                for k in range(4):
                    mm(PS[4][:, k * 128:(k + 1) * 128], BA[0][:, (4 * g + k) * 128:(4 * g + k + 1) * 128], IDB[:], True, True,
                       ['BA0', 'IDB'], [('ps', 4)])
                copy('act', VTG[:, 4 * g:4 * g + 4, 0:128], c3(PS[4][:]), [('ps', 4)], ['VTG'])
            return f
        for g in range(4):
            G.append(g_vt(g))
        return G

    NRK_AT = {26: [0], 27: [1], 28: [2], 29: [3], 30: [4, 5], 31: [6]}

    def pair_chain(jj, vz, nrk=None):
        vcol = lambda base: VEC[:, base + jj:base + jj + 1]
        if True:
            blocks = [(0, b) for b in range(4)] + [(1, b) for b in range(3, -1, -1)]
            chunks = [(0, b, k) for b in range(4) for k in range(4)] + [(1, b, k) for b in range(3, -1, -1) for k in range(3, -1, -1)]
            mab = lambda z: CB[:, B_MAB + 512 * z:B_MAB + 512 * z + 512]
            mnm = lambda z: CB[:, B_MN + 256 * z:B_MN + 256 * z + 256]

            def init_state(z, jj=jj):
                def g():
                    ld(SFw, d_sw[:, jj, z, :], ['SFw'])
                    copy('pool', SBw, SFw, ['SFw'], ['SBw'])
                return g

            def prep_groups(bi, jj=jj, vcol=vcol):
                z, blk = blocks[bi]
                par = bi % 2
                O_ = OPN[par]
                opr = ('OPN', par)
                bkt = BKTs[par]
                bkr = ('BKT', par)
                tsl = slice(blk * 512, (blk + 1) * 512)
                SIG, CSw, CRw, CSBw, AI, KM, BV = TB[0], TB[1], TB[3], TB[4], TB[9], TB[7], TB[8]
                rAI, rBV = ('WB', 3), ('WB', 2)
                if bi == 0:
                    AI, BV = FB[:, 1024:1536], FB[:, 1536:2048]
                    rAI = rBV = ('OPN', 1)
                E1, E3 = TB[2], TB[6]
                if z == 0:
                    inc, ex, rest = (CSw, ('T', 1)), (SIG, ('T', 0)), (CRw, ('T', 3))
                else:
                    inc, ex, rest = (CSBw, 'WS'), (CRw, ('T', 3)), (SIG, ('T', 0))
                def w0():
                    mm(PS[4][:], W2B[:, z, :], BA[2][:, tsl], True, True, ['W2B', 'BA2'], [('ps', 4)])
                    mm(PS[5][:], W2B[:, 2 + z, :], BA[3][:, tsl], True, True, ['W2B', 'BA3'], [('ps', 5)])

                def w1():
                    act(SIG[:], PS[4][:], ACT.Sigmoid, [('ps', 4), 'VEC'], [('T', 0)], bias=vcol(V_W0 + 8 * z))
                    act(AI[:], PS[5][:], ACT.Sigmoid, [('ps', 5), 'VEC'], [rAI], bias=vcol(V_A0 + 8 * z))

                def w2():
                    P.op('dve', lambda e: e.tensor_tensor_scan(out=CSw[:], data0=CB[:, B_RST:B_RST + 512], data1=SIG[:],
                                                               initial=0.0, op0=ALU.mult, op1=ALU.add),
                         reads=[('T', 0), 'CB'], writes=[('T', 1)])
                    act(E3[:], AI[:], ACT.Identity, [rAI, 'VEC', 'OMKA'], [('WB', 0)], bias=OMKA[:, jj:jj + 1], scale=vcol(V_KA))
                    stt(BV[:], KKf[:, tsl], -1.0, AI[:], ALU.mult, ALU.mult, ['KKf', rAI], [rBV])

                def w3():
                    totb = bass.AP(CSw, 127, [[512, 128], [128, 4], [0, 128]])
                    tt('pool', c3(CRw[:]), totb, c3(CSw[:]), ALU.subtract, [('T', 1)], [('T', 3)])
                    act(WLW[:, z, blk * 4:blk * 4 + 4], bass.AP(CSw, 127, [[512, 128], [128, 4]]), ACT.Exp, [('T', 1)], ['WLW'], scale=-LAM)
                    tt('pool', KM[:], Kb[:, tsl], E3[:], ALU.mult, ['FA', ('WB', 0)], [('WB', 1)])

                def w4():
                    if z == 1:
                        tt('pool', CSBw[:], CRw[:], SIG[:], ALU.add, [('T', 3), ('T', 0)], ['WS'])
                    tt('pool', SIG[:], CSw[:], SIG[:], ALU.subtract, [('T', 1), ('T', 0)], [('T', 0)])
                    if z == 0:
                        tt('pool', PRS[:, tsl], Rb[:, tsl], KM[:], ALU.mult, ['FA', ('WB', 1)], ['PRS'])
                    else:
                        tt('pool', E3[:], Rb[:, tsl], KM[:], ALU.mult, ['FA', ('WB', 1)], [('WB', 0)])
                        tt('pool', PRS[:, tsl], PRS[:, tsl], E3[:], ALU.add, ['PRS', ('WB', 0)], ['PRS'])

                def w5():
                    act(E1[:], ex[0][:], ACT.Exp, [ex[1]], [('T', 2)], scale=-LAM)
                    act(E3[:], inc[0][:], ACT.Exp, [inc[1]], [('WB', 0)], scale=-LAM)

                def w6():
                    tt('pool', O_['AR4'][:, :, 0, :], c3(KKf[:, tsl]), c3(E1[:]), ALU.mult, ['KKf', ('T', 2)], [opr])
                    for h in range(2):
                        stt(O_['Am'][h], KKf[:, tsl], HM[h], E1[:], ALU.mult, ALU.mult, ['KKf', 'CF', ('T', 2)], [opr])
                    tt('pool', O_['AR4'][:, :, 1, :], c3(Rb[:, tsl]), c3(E3[:]), ALU.mult, ['FA', ('WB', 0)], [opr])

                def w7():
                    act(E1[:], rest[0][:], ACT.Exp, [rest[1]], [('T', 2)], scale=-LAM)
                    act(E3[:], inc[0][:], ACT.Exp, [inc[1]], [('WB', 0)], scale=LAM)

                def w8():
                    for h in range(2):
                        stt(O_['Btm'][h], BV[:], HM[h], E3[:], ALU.mult, ALU.mult, [rBV, 'CF', ('WB', 0)], [opr])
                        stt(O_['Ktm'][h], KM[:], HM[h], E3[:], ALU.mult, ALU.mult, [('WB', 1), 'CF', ('WB', 0)], [opr])
                    tt('pool', O_['Bh'], BV[:], E1[:], ALU.mult, [rBV, ('T', 2)], [opr])
                    tt('pool', O_['Kh'], KM[:], E1[:], ALU.mult, [('WB', 1), ('T', 2)], [opr])

                def w9():
                    for k in range(4):
                        mm(PS[4][:, k * 128:(k + 1) * 128], O_['Bh'][:, k * 128:(k + 1) * 128], IDB[:], True, True, [opr, 'IDB'], [('ps', 4)])

                def w10():
                    copy('act', bkt[:, 0, :], PS[4][:], [('ps', 4)], [bkr])
                    for k in range(4):
                        mm(PS[5][:, k * 128:(k + 1) * 128], O_['Kh'][:, k * 128:(k + 1) * 128], IDB[:], True, True, [opr, 'IDB'], [('ps', 5)])

                def w11():
                    copy('act', bkt[:, 1, :], PS[5][:], [('ps', 5)], [bkr])
                return [w0, w1, w2, w3, w4, w5, w6, w7, w8, w9, w10, w11]

            def a_groups(ci):
                bi, (z, blk, k) = ci // 4, chunks[ci]
                MAB, MNm = mab(z), mnm(z)
                par, q, m, cx = bi % 2, ci % 2, ci % 3, ci % 2
                O_ = OPN[par]
                opr = ('OPN', par)
                ksl = slice(k * 128, (k + 1) * 128)
                ARk = O_['AR'][:, k * 256:(k + 1) * 256]
                nm0, nm1 = NM0[m], NM1[m]
                LV, PPp = LVs[cx], PPs[cx]
                na, nb = (6, 7) if cx == 0 else (0, 1)
                PA_, PB_ = PS[na], PS[nb]
                ra, rb = ('ps', na), ('ps', nb)
                lvp = lambda i: ('LVp', cx, i)
                lvt = lambda i: ('LVt', cx, i)
                ppr = lambda i: ('PPp', cx, i)
                G = []

                def g0():
                    for h in range(2):
                        mm(PA_[:, h * 256:(h + 1) * 256], O_['Btm'][h][:, ksl], ARk, True, True, [opr], [ra])
                        mm(PB_[:, h * 256:(h + 1) * 256], O_['Ktm'][h][:, ksl], ARk, True, True, [opr], [rb])
                        mm(PS[3][:, 256 + h * 128:384 + h * 128], O_['Am'][h][:, ksl], O_['Btm'][h][:, ksl], True, True, [opr], [('ps', 3)])
                    tt('dve', nm0, PA_[:], MAB, ALU.mult, [ra, 'CB'], [('NM0', m)])
                    tt('dve', PPp[0], PS[3][:, 256:512], MNm, ALU.mult, [('ps', 3), 'CB'], [ppr(0)])
                    tt('dve', nm1, PB_[:], MAB, ALU.mult, [rb, 'CB'], [('NM1', m)])
                G.append(g0)
                pt0 = nm0.rearrange("p (h s t) -> p h s t", h=2, s=2)[:, :, 0, :]

                def g1():
                    idb2 = bass.AP(IDB, 0, [[128, 128], [0, 2], [1, 128]])
                    tt('pool', LV[1][:, :, 1, :], pt0, idb2, ALU.add, [('NM0', m), 'IDB'], [lvt(1)])
                    for h in range(2):
                        mm(PA_[:, h * 128:(h + 1) * 128], nm0[:, h * 256:h * 256 + 128], PPp[0][:, h * 128:(h + 1) * 128], True, True,
                           [('NM0', m), ppr(0)], [ra])
                        mm(PB_[:, h * 256:h * 256 + 128], PPp[0][:, h * 128:(h + 1) * 128], nm0[:, h * 256:h * 256 + 128], True, True,
                           [('NM0', m), ppr(0)], [rb])
                    copy('act', PPp[1], PA_[:, 0:256], [ra], [ppr(1)])
                    copy('dve', LV[1][:, :, 0, :], PB_[:].rearrange("p (h s t) -> p h s t", h=2, s=2)[:, :, 0, :], [rb], [lvp(1)])
                G.append(g1)

                def lvl(kk_):
                    def g():
                        a, b = kk_ % 2, (kk_ + 1) % 2
                        psb = PB_[:].rearrange("p (h s t) -> p h s t", h=2, s=2)
                        for h in range(2):
                            pk = PPp[a][:, h * 128:(h + 1) * 128]
                            if kk_ < 5:
                                mm(PB_[:, h * 256:(h + 1) * 256], pk, LV[a][:, h, :, :].rearrange("p s t -> p (s t)"), True, True,
                                   [ppr(a), lvp(a), lvt(a)], [rb])
                            else:
                                mm(PB_[:, h * 256 + 128:(h + 1) * 256], pk, LV[a][:, h, 1, :], True, True, [ppr(a), lvt(a)], [rb])
                            mm(PA_[:, h * 128:(h + 1) * 128], LV[a][:, h, 0, :], pk, True, True, [ppr(a), lvp(a)], [ra])
                        copy('act', PPp[b], PA_[:, 0:256], [ra], [ppr(b)])
                        if kk_ < 5:
                            copy('dve', LV[b][:, :, 0, :], psb[:, :, 0, :], [rb], [lvp(b)])
                        tt('dve', LV[b][:, :, 1, :], psb[:, :, 1, :], LV[a][:, :, 1, :], ALU.add, [rb, lvt(a)], [lvt(b)])
                    return g
                for kk_ in range(1, 6):
                    G.append(lvl(kk_))

                def g7():
                    for h in range(2):
                        mm(PB_[:, h * 128:(h + 1) * 128], PPp[0][:, h * 128:(h + 1) * 128], LV[0][:, h, 1, :], True, True,
                           [ppr(0), lvt(0)], [rb])
                    tt('dve', TTf[q].rearrange("p (h t) -> p h t", h=2), PB_[:, 0:256].rearrange("p (h t) -> p h t", h=2),
                       LV[0][:, :, 1, :], ALU.add, [rb, lvt(0)], [('TTf', q)])
                G.append(g7)
                return G

            def b_groups(ci, jj=jj):
                bi, (z, blk, k) = ci // 4, chunks[ci]
                par, q, m = bi % 2, ci % 2, ci % 3
                O_ = OPN[par]
                opr = ('OPN', par)
                bkt, bkr = BKTs[par], ('BKT', par)
                c16 = blk * 4 + k
                ksl = slice(k * 128, (k + 1) * 128)
                ARk = O_['AR'][:, k * 256:(k + 1) * 256]
                nm0, nm1, TT = NM0[m], NM1[m], TTf[q]
                vt = lambda h: VTG[:, c16, h * 64:(h + 1) * 64]
                G = []

                def g0():
                    mm(PS[2][:, 0:128], ARk[:, 0:128], SBw, True, False, [opr, 'SBw'], [('ps', 2)])
                    for h in range(2):
                        mm(PS[2][:, hs(h)], nm1[:, h * 256:h * 256 + 128], vt(h), False, h == 1, [('NM1', m), 'VTG'], [('ps', 2)])
                    copy('act', Z0B, PS[2][:, 0:128], [('ps', 2)], ['Z0B'])
                G.append(g0)

                def g1():
                    for h in range(2):
                        mm(PS[2][:, 128 + h * 64:192 + h * 64], TT[:, h * 128:(h + 1) * 128], Z0B[:, hs(h)], True, True,
                           [('TTf', q), 'Z0B'], [('ps', 2)])
                    copy('act', UB, PS[2][:, 128:256], [('ps', 2)], ['UB'])
                G.append(g1)

                def g2():
                    mm(PS[3][:, 0:128], ARk[:, 128:256], SBw, True, False, [opr, 'SBw'], [('ps', 3)])
                    for h in range(2):
                        mm(PS[3][:, hs(h)], nm0[:, h * 256 + 128:h * 256 + 256], UB[:, hs(h)], False, False, [('NM0', m), 'UB'], [('ps', 3)])
                        mm(PS[3][:, hs(h)], nm1[:, h * 256 + 128:h * 256 + 256], vt(h), False, h == 1, [('NM1', m), 'VTG'], [('ps', 3)])
                    mm(PS[2][:, 256:384], bkt[:, 0, ksl], UB, True, False, [bkr, 'UB'], [('ps', 2)])
                    mm(PS[2][:, 256:384], bkt[:, 1, ksl], VTG[:, c16, 0:128], False, True, [bkr, 'VTG'], [('ps', 2)])
                G.append(g2)

                def g3():
                    tmpw = GS[:, 1 + (c16 % 2), 0:128]
                    tr = ('TMPw', c16 % 2)
                    stt(tmpw, SFw, WLW[:, z, c16:c16 + 1], PS[2][:, 256:384], ALU.mult, ALU.add, ['SFw', 'WLW', ('ps', 2)], [tr])
                    stt(SFw, tmpw, MSK[:, 2 + z, c16:c16 + 1], BDm, ALU.mult, ALU.mult, [tr, 'MSK', 'CF'], ['SFw'])
                    copy('act', SBw, SFw, ['SFw'], ['SBw'])
                    if (c16 % 2 == 1) == (z == 0):
                        P.dma('sp', lambda e, tmpw=tmpw, z=z, jj=jj, c16=c16: e.dma_start(out=o_sw[c16 // 2, z, jj], in_=tmpw), reads=[tr])
                    ofc = VTG[:, c16, 128:256]
                    vo = ('VO', c16)
                    if z == 0:
                        copy('act', ofc, PS[3][:, 0:128], [('ps', 3)], [vo])
                    else:
                        to, sqo = GS[:, 0, 128:256], GS[:, 1, 128:256]
                        h3 = lambda ap: ap.rearrange("p (g w) -> p g w", w=64)
                        gb = lambda off: bass.AP(GNS, off, [[8, 128], [1, 2], [0, 64]])
                        tt('dve', to, ofc, PS[3][:, 0:128], ALU.add, [('ps', 3), vo], ['TO'])
                        P.op('dve', lambda e: e.tensor_reduce(out=GNS[:, 0:2], in_=h3(to), axis=AX.X, op=ALU.add),
                             reads=['TO'], writes=['GNS'])
                        tt('dve', sqo, to, to, ALU.mult, ['TO'], ['SQO'])
                        P.op('dve', lambda e: e.tensor_reduce(out=GNS[:, 2:4], in_=h3(sqo), axis=AX.X, op=ALU.add),
                             reads=['SQO'], writes=['GNS'])
                        ts('dve', GNS[:, 0:2], GNS[:, 0:2], 1.0 / 64, None, ALU.mult, None, ['GNS'], ['GNS'])
                        tt('dve', GNS[:, 4:6], GNS[:, 0:2], GNS[:, 0:2], ALU.mult, ['GNS'], ['GNS'])
                        stt(GNS[:, 2:4], GNS[:, 2:4], 1.0 / 64, GNS[:, 4:6], ALU.mult, ALU.subtract, ['GNS'], ['GNS'])

                        def tail():
                            act(GNS[:, 2:4], GNS[:, 2:4], ACT.Ln, ['GNS', 'EPS'], ['GNS'], bias=EPS[:, 1:2], scale=1.0)
                            act(GNS[:, 2:4], GNS[:, 2:4], ACT.Exp, ['GNS'], ['GNS'], scale=-0.5)
                            tt('pool', h3(to), h3(to), gb(0), ALU.subtract, ['TO', 'GNS'], ['TO'])
                            tt('pool', h3(ofc), h3(to), gb(2), ALU.mult, ['TO', 'GNS'], [vo])
                        DEFER[ci] = tail
                G.append(g3)
                return G

            NCK = 32
            DEFER = {}
            init_state(0)()
            AG = {0: a_groups(0), 1: a_groups(1)}
            PG = {0: prep_groups(0), 1: prep_groups(1)}
            lock = [(lambda i=i: (AG[0][i](), AG[1][i]() if i < 4 else None)) for i in range(8)]
            interleave([vz, PG[0] + lock])
            for ci in range(NCK):
                bg = b_groups(ci)
                if ci - 1 in DEFER:
                    bg = [DEFER.pop(ci - 1)] + bg
                if ci == 16:
                    bg = [init_state(1)] + bg
                lists = [bg]
                if ci + 1 < NCK:
                    lists.append(AG[ci + 1][4:])
                if ci + 2 < NCK:
                    AG[ci + 2] = a_groups(ci + 2)
                    lists.append(AG[ci + 2][:4])
                b_, r_ = ci // 4, ci % 4
                if r_ < 2 and b_ + 1 < 8:
                    if b_ + 1 not in PG:
                        PG[b_ + 1] = prep_groups(b_ + 1)
                    if b_ == 0:
                        lists.append(PG[1][6 * r_:6 * r_ + 6])
                    else:
                        lists.append(PG[b_ + 1][6 + 3 * r_:9 + 3 * r_])
                elif r_ >= 2 and b_ + 2 < 8:
                    if b_ + 2 not in PG:
                        PG[b_ + 2] = prep_groups(b_ + 2)
                    lists.append(PG[b_ + 2][3 * (r_ - 2):3 * (r_ - 2) + 3])
                if nrk is not None and ci in NRK_AT:
                    lists.append([nrk[i] for i in NRK_AT[ci]])
                interleave(lists)
            for k_ in sorted(DEFER):
                DEFER.pop(k_)()

    def pair_end(jj):
        vcol = lambda base: VEC[:, base + jj:base + jj + 1]
        G = []
        o1 = ('OPN', 1)

        def g_t(t4):
            def g():
                tsl = slice(t4 * 512, (t4 + 1) * 512)
                for k in range(4):
                    mm(PS[4][:, k * 128:(k + 1) * 128], VTG[:, t4 * 4 + k, 128:256], IDB[:], True, True,
                       [('VO', t4 * 4 + k), 'IDB'], [('ps', 4)])
                ts('dve', TB[0][:], PS[4][:], vcol(V_LW), vcol(V_LB), ALU.mult, ALU.add, [('ps', 4), 'VEC'], [('T', 0)])
                etb = ET[0].bitcast(BF16)[:, 0:512]
                act(etb, PRS[:, tsl], ACT.Copy, ['PRS', 'VEC'], [o1], scale=vcol(V_RK))
                mm(PS[5][:], BDB[:], etb, True, True, [o1, 'BDB'], [('ps', 5)])
                tt('dve', ET[1][:], PS[5][:], BA[0][:, tsl], ALU.mult, [('ps', 5), 'BA0'], [o1])
                tt('pool', TB[0][:], TB[0][:], ET[1][:], ALU.add, [('T', 0), o1], [('T', 0)])
                tt('pool', BA[1][:, tsl], TB[0][:], BA[1][:, tsl], ALU.mult, [('T', 0), 'BA1'], ['BA1'])
            return g
        for t4 in range(4):
            G.append(g_t(t4))

        def g_w():
            ld(WS[:].rearrange("p a b -> p (a b)"), d_ewout[:, 8 + jj, :], ['WS'])
            copy('pool', WOB[:], WS[:].rearrange("p a b -> p (a b)"), ['WS'], [('WB', 3)])

        def g_o(t4, half):
            def g():
                tsl = slice(t4 * 512, (t4 + 1) * 512)
                for ft in range(4 * half, 4 * half + 4):
                    pb = 2 + (cnt['rr'] % 4)
                    cnt['rr'] += 1
                    mm(PS[pb][:], WOB[:, ft * 128:(ft + 1) * 128], BA[1][:, tsl], True, True, [('WB', 3), 'BA1'], [('ps', pb)])
                    if (ft * 4 + t4) % 5 < 3:
                        stt(X[:, ft, tsl], PS[pb][:], MOD[:, 0, 16 + ft:17 + ft], X[:, ft, tsl], ALU.mult, ALU.add,
                            [('ps', pb), 'MOD', ('X', ft)], [('X', ft)])
                    else:
                        wt = WOT[ft % 2]
                        act(wt[:], PS[pb][:], ACT.Copy, [('ps', pb), 'MOD'], [('T', 2 + ft % 2)], scale=MOD[:, 0, 16 + ft:17 + ft])
                        tt('pool', X[:, ft, tsl], X[:, ft, tsl], wt[:], ALU.add, [('T', 2 + ft % 2), ('X', ft)], [('X', ft)])
            return g
        G = [g_w, G[0], G[1], g_o(0, 0), g_o(0, 1), G[2], g_o(1, 0), g_o(1, 1), G[3], g_o(2, 0), g_o(2, 1), g_o(3, 0), g_o(3, 1)]
        return G

    for g in start_rk(0):
        g()
    vz = start_vz(0)
    for jj in range(8):
        nrk = start_rk(jj + 1, banks=(4, 5), kb=4) if jj + 1 < 8 else None
        pair_chain(jj, vz, nrk)
        for g in pair_end(jj):
            g()
        if jj + 1 < 8:
            vz = start_vz(jj + 1)
    P.barrier()

    norm_mod(lambda c: G1[:, 1, c:c + 1], lambda c: MOD[:, 1, c:c + 1], lambda c, tsl: HT[:, c, tsl],
             lambda c, t4: [('HT', c)])
    ld(WS[:].rearrange("p a b -> p (a b)")[:, 0:256], d_g1.rearrange('p a b -> p (a b)'), ['WS'])
    copy('pool', G1B[:].rearrange('p a b -> p (a b)'), WS[:].rearrange("p a b -> p (a b)")[:, 0:256], ['WS'], ['G1B'])
    ld(WS[:].rearrange("p a b -> p (a b)")[0:32, :], d_g2.rearrange('p a b -> p (a b)'), ['WS'])
    copy('pool', G2B[:].rearrange('p a b -> p (a b)'), WS[:].rearrange("p a b -> p (a b)")[0:32, :], ['WS'], ['G2B'])
    ts('pool', NGB[:], VEC[:, V_GB:V_GB + 8], -1.0, None, ALU.mult, None, ['VEC'], ['NGB'])
    for t4 in range(4):
        tsl = slice(t4 * 512, (t4 + 1) * 512)
        for c in range(8):
            mm(PS[0][0:32, :], G1B[:, c, :], HT[:, c, tsl], c == 0, c == 7, ['G1B', ('HT', c)], [('ps', 0)])
        copy('act', GT1[:, tsl], PS[0][0:32, :], [('ps', 0)], ['GT1'])
    SP, CSg, CR0, INC, RST, EXg = T512[:6]
    g2 = ub(2048, 3328)
    GSET = [(BA[0][:, 0:512], BA[0][:, 512:1024], BA[0][:, 1024:1536], BA[1][:, 0:512],
             BA[1][:, 512:1024].rearrange('p (k t) -> p k t', k=4)),
            (g2[:, 0:512], g2[:, 512:1024], g2[:, 1024:1536], g2[:, 1536:2048],
             g2[:, 2048:2560].rearrange('p (k t) -> p k t', k=4))]
    GEX = [UNI[:, 4096 + 512 * i:4096 + 512 * (i + 1)] for i in range(3)]
    GLTf = KKF[:].bitcast(F32)
    GLT = [GLTf[:, 0:512], GLTf[:, 512:1024]]
    SBg = BA[1][:, 1024:1280]
    SFg = GS[:, 0, :]
    for hd in range(4):
        load_w(d_owin, hd * 128, 0)
        load_w(d_owin, 512 + hd * 128, 1)
        load_w(d_owin, 1024 + hd * 256, 2)
        load_w(d_owin, 1024 + hd * 256 + 128, 3)
        proj(0, lambda t4, tsl, ps, pr: copy(rr(), FA[:, tsl], ps, [pr], ['FA']))
        proj(1, lambda t4, tsl, ps, pr: copy(rr(), FB[:, tsl], ps, [pr], ['FB']))
        vgr = []
        for half in range(2):
            vgr.append(lambda half=half: proj(2 + half, lambda t4, tsl, ps, pr: copy(rr(), BA[2][:, tsl], ps, [pr], ['BA2'])))

            def vt(g, half=half):
                def f():
                    for k in range(4):
                        mm(PS[1][:, k * 128:(k + 1) * 128], BA[2][:, (4 * g + k) * 128:(4 * g + k + 1) * 128], IDB[:], True, True,
                           ['BA2', 'IDB'], [('ps', 1)])
                    copy('act', VTG[:, 4 * g:4 * g + 4, half * 128:(half + 1) * 128], PS[1][:].rearrange("p (k t) -> p k t", k=4),
                         [('ps', 1)], [('VTG', 4 * g + i) for i in range(4)])
                return f
            for g in range(4):
                vgr.append(vt(g))
        gblocks = [(0, b) for b in range(4)] + [(1, b) for b in range(3, -1, -1)]
        GDEF = []

        def g_init(z, hd=hd):
            def g():
                ld(SFg, d_sg[:, hd, z, :], ['SFg'])
                copy('pool', SBg, SFg, ['SFg'], ['SBg', 'BA1'])
            return g

        def g_prep(bi, hd=hd):
            z, blk = gblocks[bi]
            sb_ = bi % 2
            QE, KE, KD, KDT, ATT4 = GSET[sb_]
            rq, rk, rd, rt, ra_ = (('gQE', sb_), ('gKE', sb_), ('gKD', sb_), ('gKDT', sb_), ('gATT', sb_))
            MB = [(PS[0], 0), (PS[1], 1)] if sb_ == 0 else [(PS[2], 2), (PS[5], 5)]
            tsl = slice(blk * 512, (blk + 1) * 512)
            EA, EB, EC = GEX
            if z == 0:
                inc, rst, ri, rr_ = CSg, CR0, ('T', 1), ('T', 2)
            else:
                inc, rst, ri, rr_ = INC, RST, ('T', 3), 'WS'

            def w0():
                mm(PS[4][:], G2B[:, z, hd * 128:(hd + 1) * 128], GT1[:, tsl], True, True, ['G2B', 'GT1'], [('ps', 4)])

            def w1():
                act(EXg[:], PS[4][:], ACT.Exp, [('ps', 4), 'NGB'], ['WS'], bias=NGB[:, 4 * z + hd:4 * z + hd + 1], scale=-1.0)

            def w2():
                act(SP[:], EXg[:], ACT.Ln, ['WS', 'CF'], [('T', 0)], bias=CF[:, C_ONE:C_ONE + 1], scale=1.0)

            def w3():
                P.op('dve', lambda e: e.tensor_tensor_scan(out=CSg[:], data0=CB[:, B_RST:B_RST + 512], data1=SP[:],
                                                           initial=0.0, op0=ALU.mult, op1=ALU.add),
                     reads=[('T', 0), 'CB'], writes=[('T', 1)])

            def w4():
                totb = bass.AP(CSg, 127, [[512, 128], [128, 4], [0, 128]])
                cs3 = bass.AP(CSg, 0, [[512, 128], [128, 4], [1, 128]])
                cr3 = bass.AP(CR0, 0, [[512, 128], [128, 4], [1, 128]])
                tt('pool', cr3, totb, cs3, ALU.subtract, [('T', 1)], [('T', 2)])
                tot4 = bass.AP(CSg, 127, [[512, 128], [128, 4]])
                act(WLG[:, z, blk * 4:blk * 4 + 4], tot4, ACT.Exp, [('T', 1)], ['WLG'], scale=-1.0 / 16)
                if z == 1:
                    tt('pool', INC[:], CR0[:], SP[:], ALU.add, [('T', 2), ('T', 0)], [('T', 3)])
                    tt('pool', RST[:], CSg[:], SP[:], ALU.subtract, [('T', 1), ('T', 0)], ['WS'])

            def w5():
                act(EA, inc[:], ACT.Exp, [ri], [('gE', 0)], scale=-1.0 / 16)
                act(EB, inc[:], ACT.Exp, [ri], [('gE', 1)], scale=1.0 / 16)
                act(EC, rst[:], ACT.Exp, [rr_], [('gE', 2)], scale=-1.0 / 16)

            def w6():
                stt(QE, FA[:, tsl], 128 ** -0.5, EA, ALU.mult, ALU.mult, ['FA', ('gE', 0)], [rq])
                tt('dve', KE, FB[:, tsl], EB, ALU.mult, ['FB', ('gE', 1)], [rk])
                tt('pool', KD, FB[:, tsl], EC, ALU.mult, ['FB', ('gE', 2)], [rd])

            def w7():
                for k in range(4):
                    ksl = slice(k * 128, (k + 1) * 128)
                    mm(PS[6][:, ksl], KE[:, ksl], QE[:, ksl], True, True, [rk, rq], [('ps', 6)])
                for k in range(4):
                    mm(PS[4][:, k * 128:(k + 1) * 128], KD[:, k * 128:(k + 1) * 128], IDB[:], True, True, [rd, 'IDB'], [('ps', 4)])

            def w8():
                mg = bass.AP(CB, B_MG + 128 * z, [[NCB, 128], [0, 4], [1, 128]])
                tt('dve', ATT4, PS[6][:].rearrange("p (k t) -> p k t", k=4), mg, ALU.mult, [('ps', 6), 'CB'], [ra_])
                copy('act', KDT, PS[4][:], [('ps', 4)], [rt])

            def w9():
                for k in range(4):
                    ksl = slice(k * 128, (k + 1) * 128)
                    mb, mbn = MB[k // 2]
                    mm(mb[:, (k % 2) * 256:(k % 2 + 1) * 256], KDT[:, ksl], VTG[:, blk * 4 + k, :], True, True,
                       [rt, ('VTG', blk * 4 + k)], [('ps', mbn)])
            return [w0, w1, w2, w3, w4, w5, w6, w7, w8, w9]

        def g_chain(bi, hd=hd):
            z, blk = gblocks[bi]
            sb_ = bi % 2
            QE, KE, KD, KDT, ATT4 = GSET[sb_]
            rq, ra_ = ('gQE', sb_), ('gATT', sb_)
            MB = [(PS[0], 0), (PS[1], 1)] if sb_ == 0 else [(PS[2], 2), (PS[5], 5)]
            G = []

            def ch(k):
                def g():
                    while GDEF:
                        GDEF.pop(0)()
                    c16 = blk * 4 + k
                    ksl = slice(k * 128, (k + 1) * 128)
                    mb, mbn = MB[k // 2]
                    ob, obn = (PS[7], 7) if k % 2 == 0 else (PS[3], 3)
                    mm(ob[:, 0:256], ATT4[:, k, :], VTG[:, c16, :], True, False, [ra_, ('VTG', c16)], [('ps', obn)])
                    mm(ob[:, 0:256], QE[:, ksl], SBg, False, True, [rq, 'SBg'], [('ps', obn)])
                    tmpg = GS[:, 1 + (c16 % 2), :]
                    tr = ('TMPg', c16 % 2)
                    stt(tmpg, SFg, WLG[:, z, c16:c16 + 1], mb[:, (k % 2) * 256:(k % 2 + 1) * 256], ALU.mult, ALU.add,
                        ['SFg', 'WLG', ('ps', mbn)], [tr])
                    ts('dve', SFg, tmpg, MSK[:, 2 + z, c16:c16 + 1], None, ALU.mult, None, [tr, 'MSK'], ['SFg'])
                    act(SBg, tmpg, ACT.Copy, [tr, 'MSK'], ['SBg'], scale=MSK[:, 2 + z, c16:c16 + 1])
                    if (c16 % 2 == 1) == (z == 0):
                        P.dma('sp', lambda e, z=z, hd=hd, c16=c16, tmpg=tmpg: e.dma_start(out=o_sg[c16 // 2, z, hd], in_=tmpg), reads=[tr])
                    obf = BA[2 + c16 // 8][:, (c16 % 8) * 256:(c16 % 8 + 1) * 256]
                    obr = 'BA2' if c16 < 8 else 'BA3'
                    if z == 0:
                        copy('act', obf, ob[:, 0:256], [('ps', obn)], [obr])
                    else:
                        tog, sqg = GLT[c16 % 2][:, 0:256], GLT[c16 % 2][:, 256:512]
                        gr = ('GLT', c16 % 2)
                        tt('dve', tog, obf, ob[:, 0:256], ALU.add, [('ps', obn), obr], [gr])
                        tt('dve', sqg, tog, tog, ALU.mult, [gr], [gr])
                        P.op('dve', lambda e, sqg=sqg, c16=c16: e.tensor_reduce(out=RSG[:, c16:c16 + 1], in_=sqg, axis=AX.X, op=ALU.add),
                             reads=[gr], writes=[('RSG', c16)])

                        def tail(c16=c16, tog=tog, gr=gr):
                            act(RSG[:, c16:c16 + 1], RSG[:, c16:c16 + 1], ACT.Ln, [('RSG', c16), 'EPS'], [('RSG', c16)], bias=EPS[:, 0:1], scale=1.0 / 256)
                            act(RSG[:, c16:c16 + 1], RSG[:, c16:c16 + 1], ACT.Exp, [('RSG', c16)], [('RSG', c16)], scale=-0.5)
                            act(VTG[:, c16, :], tog, ACT.Copy, [gr, ('RSG', c16)], [('VTG', c16)], scale=RSG[:, c16:c16 + 1])
                        GDEF.append(tail)
                return g
            for k in (range(4) if z == 0 else range(3, -1, -1)):
                G.append(ch(k))
            return G

        p0_ = g_prep(0)
        interleave([vgr, [g_init(0)] + p0_[:9]])
        p0_[9]()
        for bi in range(8):
            cg = g_chain(bi)
            if bi == 4:
                cg = [g_init(1)] + cg
            lists = [cg]
            if bi + 1 < 8:
                lists.append(g_prep(bi + 1))
            interleave(lists)
        while GDEF:
            GDEF.pop(0)()
        def half_groups(half, hd=hd):
            slot, SZ, nSZ, TF, nTF, OBh, nOB, pT, nT = ((0, BA[2], 'BA2', FB, 'FB', BA[3], 'BA3', PS[1], 1) if half == 0 else
                                                         (1, BA[0], 'BA0', FA, 'FA', BA[1], 'BA1', PS[0], 0))
            G = []

            def ga():
                load_w(d_owin, 2048 + hd * 256 + half * 128, slot)
                proj(slot, lambda t4, tsl, ps, pr: act(SZ[:, tsl], ps, ACT.Silu, [pr], [nSZ]))
            G.append(ga)

            def gt(t4):
                def f():
                    tsl = slice(t4 * 512, (t4 + 1) * 512)
                    for k in range(4):
                        mm(pT[:, k * 128:(k + 1) * 128], VTG[:, t4 * 4 + k, half * 128:(half + 1) * 128], IDB[:], True, True,
                           [('VTG', t4 * 4 + k), 'IDB'], [('ps', nT)])
                    ts('dve', TF[:, tsl], pT[:], VEC[:, V_GN + half:V_GN + half + 1], None, ALU.mult, None, [('ps', nT), 'VEC'], [nTF])
                return f
            for t4 in range(4):
                G.append(gt(t4))

            def gm():
                tt('pool', OBh[:], TF[:], SZ[:], ALU.mult, [nTF, nSZ], [nOB])
            G.append(gm)
            G.append(lambda: wout_partial(1, d_owout, hd * 2 + half, OBh, nOB))
            return G
        interleave([half_groups(0), half_groups(1)])

    ftsl = [slice(t4 * 512, (t4 + 1) * 512) for t4 in range(4)]
    sumsq_rstd(ftsl[0], 0)
    for t4 in range(4):
        tsl = ftsl[t4]
        if t4 + 1 < 4:
            sumsq_rstd(ftsl[t4 + 1], (t4 + 1) % 2)
        rb_, rn_ = RSTD[t4 % 2]
        for c in range(8):
            stt(FA[:, (c % 4) * 512:(c % 4 + 1) * 512], X[:, c, tsl], VEC[:, V_FG + c:V_FG + c + 1], rb_[:],
                ALU.mult, ALU.mult, [('X', c), 'VEC', rn_], [('FAq', c % 4)])
            P.dma('sp', lambda e, c=c, tsl=tsl: e.dma_start(out=o_y[:, c, tsl], in_=FA[:, (c % 4) * 512:(c % 4 + 1) * 512]),
                  reads=[('FAq', c % 4)])
    P.finish_waits('sp')
    P.emit()
    global _LAST_P
    _LAST_P = P
    st.close()
    return nc


def kernel(**inp):
    f = lambda k: np.asarray(inp[k], np.float32)
    plan = _assign()
    cf, cb = _consts()
    vec0 = np.zeros((128, NV), np.float32)
    vec0[:, V_NG:V_NG + 16] = np.concatenate([_col(f('norm_g')[0]), _col(f('norm_g')[1])], 1)
    vec0[:, V_FG:V_FG + 8] = _col(f('final_g'))
    cw = f('conv_w')[0]
    for k in range(3):
        vec0[:, V_CW + 8 * k:V_CW + 8 * k + 8] = _col(cw[k])
    for z in range(2):
        vec0[:, V_W0 + 8 * z:V_W0 + 8 * z + 8] = _col(f('wkv_w0')[0, z])
        vec0[:, V_A0 + 8 * z:V_A0 + 8 * z + 8] = _col(f('wkv_a0')[0, z])
        vec0[:, V_GB + 4 * z:V_GB + 4 * z + 4] = _col(f('gla_gk_b')[0, z])
    vec0[:, V_KK:V_KK + 8] = _col(f('wkv_k_k')[0])
    vec0[:, V_KA:V_KA + 8] = _col(f('wkv_k_a')[0])
    vec0[:, V_RK:V_RK + 8] = _col(f('wkv_r_k')[0].reshape(-1))
    vec0[:, V_LW:V_LW + 8] = _col(f('wkv_ln_w')[0])
    vec0[:, V_LB:V_LB + 8] = _col(f('wkv_ln_b')[0])
    for l in range(2):
        vec0[:, V_AB + 24 * l:V_AB + 24 * l + 24] = _col(f('ada_b')[l])
    vec0[:, V_GN:V_GN + 2] = _col(f('gla_g_norm')[0])
    r3 = lambda w: np.ascontiguousarray(w.reshape(-1, 128, w.shape[-1]).transpose(1, 0, 2))
    ada = np.stack([r3(f('ada_w')[l]) for l in range(2)])
    ewin = r3(f('e_w_in')[0])
    ewout = r3(f('e_w_out')[0])
    owin = r3(f('o_w_in')[0])
    owout = r3(f('o_w_out')[0])
    w1c = np.concatenate([r3(f('wkv_w1')[0, 0]), r3(f('wkv_w1')[0, 1]), r3(f('wkv_a1')[0, 0]), r3(f('wkv_a1')[0, 1])], 2)
    w2p = np.zeros((128, 4, 1024), np.float32)
    for z in range(2):
        w2p[64 * z:64 * z + 64, z] = f('wkv_w2')[0, z]
        w2p[64 * z:64 * z + 64, 2 + z] = f('wkv_a2')[0, z]
    g1c = np.concatenate([r3(f('gla_gk1')[0, 0]), r3(f('gla_gk1')[0, 1])], 2)
    g2p = np.zeros((32, 2, 512), np.float32)
    for z in range(2):
        g2p[16 * z:16 * z + 16, z] = f('gla_gk2')[0, z]
    xp, xs = f('x_prompt'), f('x_sample')
    swkv, sgla = f('state_wkv'), f('state_gla')
    in_maps = []
    for core, items in enumerate(plan):
        x = np.zeros((NT, 1024), np.float32)
        msk = np.zeros((128, 4, 32), np.float32)
        s_w = np.zeros((128, 8, 2, 128), np.float32)
        s_g = np.zeros((128, 4, 2, 256), np.float32)
        vec = vec0.copy()
        if items[0][0] == 's':
            b = items[0][1]
            x[:] = xs[b]
            cv = f('c')[b]
            msk[:, 2, :16] = 1.0
            msk[:, 3, :16] = 1.0
            for z in range(2):
                for h in range(16):
                    jj, hl = divmod(h, 2)
                    s_w[64 * hl:64 * hl + 64, jj, z, 64 * hl:64 * hl + 64] = swkv[b, 0, z, h].T
                for h in range(4):
                    s_g[:, h, z, :] = sgla[b, 0, z, h]
        else:
            for si in range(8):
                x[256 * si:256 * si + 256] = xp[items[si % len(items)][1]]
            cv = f('c_ctx')
            g = np.arange(32)
            inner = (g % 4 != 0).astype(np.float32)
            msk[:, 0, :] = inner[None]
            msk[:, 1, :] = inner[None]
            c16 = np.arange(16)
            msk[:, 2, :16] = (c16 % 2 == 0)[None]
            msk[:, 3, :16] = (c16 % 2 == 1)[None]
        vec[:, V_CV:V_CV + 8] = _col(cv)
        xT = np.ascontiguousarray(x.reshape(NT, 8, 128).transpose(2, 1, 0))
        in_maps.append(dict(xT=xT, vec=vec, cf=cf, cb=cb.astype(ml_dtypes.bfloat16), msk=msk, ada=ada, ewin=ewin, ewout=ewout, w1c=w1c,
                            w2p=w2p, s_wkv=s_w, owin=owin, owout=owout, g1c=g1c, g2p=g2p, s_gla=s_g))
    nc = build()
    res = run_bass_kernel_spmd(nc, in_maps, core_ids=list(range(8)))
    y_p = np.zeros((16, 256, 1024), np.float32)
    y_s = np.zeros((2, 2048, 1024), np.float32)
    n_w = np.zeros((16, 1, 2, 16, 64, 64), np.float32)
    n_g = np.zeros((16, 1, 2, 4, 128, 256), np.float32)
    for core, items in enumerate(plan):
        r = res.results[core]
        y = np.asarray(r["yT"]).transpose(2, 1, 0).reshape(NT, 1024)
        ow = np.asarray(r["o_wkv"])
        og = np.asarray(r["o_gla"])
        if items[0][0] == 's':
            y_s[items[0][1]] = y
        else:
            for si, (_, pi) in enumerate(items):
                y_p[pi] = y[256 * si:256 * si + 256]
                for z in range(2):
                    for h in range(16):
                        jj, hl = divmod(h, 2)
                        n_w[pi, 0, z, h] = ow[si, z, jj, 64 * hl:64 * hl + 64, 64 * hl:64 * hl + 64].T
                    n_g[pi, 0, z] = og[si, z]
    return (y_p, y_s, n_w, n_g)
```

```python
import contextlib
import numpy as np
import ml_dtypes
import concourse.bass as bass
import concourse.mybir as mybir
from concourse.bass_utils import run_bass_kernel_spmd

ACT = mybir.ActivationFunctionType
ALU = mybir.AluOpType
F32 = mybir.dt.float32
BF16 = mybir.dt.bfloat16
AX = mybir.AxisListType

ENGS = ['pe', 'act', 'dve', 'pool', 'sp']
EPOCH = 4000
NDS = 8
NT = 2048
NCH = 16
LAM = 0.6065306597126334
NORM_EPS = 1e-6
GN_EPS = 64e-5


class Prog:
    def __init__(self, nc):
        self.nc = nc
        self.ops = {e: [] for e in ENGS}
        self.count = {e: 0 for e in ENGS}
        self.dcount = {e: 0 for e in ENGS}
        self.last_w = {}
        self.readers = {}
        self.waited = {e: {} for e in ENGS}
        self.pending = {e: [] for e in ENGS}

    def _deps(self, eng, reads, writes):
        deps = set()
        for r in reads:
            if r in self.last_w:
                deps.add(self.last_w[r])
        for w in writes:
            if w in self.last_w:
                deps.add(self.last_w[w])
            for rd in self.readers.get(w, ()):
                deps.add(rd)
        best = {}
        for d in deps:
            if eng == 'pe' and d[:-1] == ('e', 'pe'):
                continue
            best[d[:-1]] = max(best.get(d[:-1], 0), d[-1])
        for d in self.pending[eng]:
            best[d[:-1]] = max(best.get(d[:-1], 0), d[-1])
        self.pending[eng] = []
        final = []
        for key, i in best.items():
            if self.waited[eng].get(key, 0) < i:
                self.waited[eng][key] = i
                final.append(key + (i,))
        return final

    def _mark(self, tok, reads, writes):
        for r in reads:
            self.readers.setdefault(r, []).append(tok)
        for w in writes:
            self.last_w[w] = tok
            self.readers[w] = []

    def op(self, eng, fn, reads=(), writes=()):
        writes = list(writes) + [r for r in reads if isinstance(r, tuple) and r[0] == 'ps']
        waits = self._deps(eng, reads, writes)
        idx = self.count[eng] + 1
        self.count[eng] = idx
        self.ops[eng].append(('c', fn, waits, idx))
        self._mark(('e', eng, idx), reads, writes)

    def dma(self, eng, fn, reads=(), writes=()):
        waits = self._deps(eng, reads, writes)
        j = self.dcount[eng]
        self.dcount[eng] = j + 1
        slot = j % NDS
        if j >= NDS:
            key = ('d', eng, slot)
            need = j // NDS
            if self.waited[eng].get(key, 0) < need:
                self.waited[eng][key] = need
                waits.append(key + (need,))
        self.ops[eng].append(('d', fn, waits, (slot, j // NDS + 1)))
        self._mark(('d', eng, slot, j // NDS + 1), reads, writes)

    def barrier(self):
        snap = [('e', e, self.count[e]) for e in ENGS if self.count[e]]
        for q in ENGS:
            n = self.dcount[q]
            for slot in range(min(n, NDS)):
                snap.append(('d', q, slot, (n - 1 - slot) // NDS + 1))
        for e in ENGS:
            self.pending[e] = list(snap)

    def finish_waits(self, eng='sp'):
        waits = []
        for q in ENGS:
            n = self.dcount[q]
            for slot in range(min(n, NDS)):
                waits.append(('d', q, slot, (n - 1 - slot) // NDS + 1))
        self.ops[eng].append(('w', None, waits, None))

    def emit(self):
        nc = self.nc
        with contextlib.ExitStack() as st:
            esem = {e: [st.enter_context(nc.semaphore(f"s_{e}_{k}")) for k in range(self.count[e] // EPOCH + 1)]
                    for e in ENGS}
            dsem = {e: [st.enter_context(nc.semaphore(f"d_{e}_{k}")) for k in range(NDS)]
                    for e in ENGS if self.dcount[e]}
            block = st.enter_context(nc.Block())

            def run(handle, e):
                for kind, fn, waits, info in self.ops[e]:
                    for w in waits:
                        if w[0] == 'e':
                            handle.wait_ge(esem[w[1]][(w[2] - 1) // EPOCH], (w[2] - 1) % EPOCH + 1)
                        else:
                            handle.wait_ge(dsem[w[1]][w[2]], 16 * w[3])
                    if kind == 'c':
                        fn(handle).then_inc(esem[e][(info - 1) // EPOCH], 1)
                    elif kind == 'd':
                        fn(handle).then_inc(dsem[e][info[0]], 16)

            @block.tensor
            def _(h):
                run(h, 'pe')

            @block.scalar
            def _(h):
                run(h, 'act')

            @block.vector
            def _(h):
                run(h, 'dve')

            @block.gpsimd
            def _(h):
                run(h, 'pool')

            @block.sync
            def _(h):
                run(h, 'sp')


V_NG, V_FG, V_CW, V_W0, V_A0, V_KK, V_KA, V_RK, V_LW, V_LB, V_AB, V_GB, V_GN, V_CV = \
    0, 16, 24, 48, 64, 80, 88, 96, 104, 112, 120, 168, 176, 178
NV = 186
C_ID, C_BD, C_HM, C_ONE = 0, 128, 256, 258
NCF = 386
B_RST, B_MAB, B_MN, B_MG = 0, 512, 1536, 2048
NCB = 2304


def _col(v):
    v = np.asarray(v, np.float32).reshape(-1, 128)
    return np.ascontiguousarray(v.T)


def _consts():
    u = np.arange(128)[:, None]
    t = np.arange(128)[None, :]
    LT, LE, GT, GE = (u < t), (u <= t), (u > t), (u >= t)
    cf = np.zeros((128, NCF), np.float32)
    cf[:, C_ID:C_ID + 128] = np.eye(128)
    cf[:, C_BD:C_BD + 128] = (u // 64 == t // 64)
    cf[:, C_HM] = (np.arange(128) < 64)
    cf[:, C_HM + 1] = (np.arange(128) >= 64)
    cf[:, C_ONE:C_ONE + 128] = 1.0
    cb = np.zeros((128, NCB), np.float32)
    rst = np.ones(512, np.float32)
    rst[::128] = 0
    cb[:, B_RST:B_RST + 512] = rst[None]
    cb[:, B_MAB:B_MAB + 512] = np.concatenate([LT, LE, LT, LE], 1)
    cb[:, B_MAB + 512:B_MAB + 1024] = np.concatenate([GT, GE, GT, GE], 1)
    cb[:, B_MN:B_MN + 256] = np.concatenate([GT, GT], 1)
    cb[:, B_MN + 256:B_MN + 512] = np.concatenate([LT, LT], 1)
    cb[:, B_MG:B_MG + 128] = LE
    cb[:, B_MG + 128:B_MG + 256] = GE
    return cf, cb


def _assign():
    plan = [[('s', 0)], [('s', 1)]]
    p = 0
    for n in (3, 3, 3, 3, 2, 2):
        plan.append([('p', p + i) for i in range(n)])
        p += n
    return plan


def build(stop_after=99):
    nc = bass.Bass("TRN2", target_bir_lowering=False)
    dt_in = lambda n, s: nc.dram_tensor(n, s, F32, kind="ExternalInput").ap()
    dt_out = lambda n, s: nc.dram_tensor(n, s, F32, kind="ExternalOutput").ap()
    d_x = dt_in("xT", [128, 8, NT])
    d_vec = dt_in("vec", [128, NV])
    d_cf = dt_in("cf", [128, NCF])
    d_cb = nc.dram_tensor("cb", [128, NCB], BF16, kind="ExternalInput").ap()
    d_msk = dt_in("msk", [128, 4, 32])
    d_ada = dt_in("ada", [2, 128, 8, 3072])
    d_ewin = dt_in("ewin", [128, 8, 8192])
    d_ewout = dt_in("ewout", [128, 16, 1024])
    d_w1 = dt_in("w1c", [128, 8, 256])
    d_w2 = dt_in("w2p", [128, 4, 1024])
    d_sw = dt_in("s_wkv", [128, 8, 2, 128])
    d_owin = dt_in("owin", [128, 8, 3072])
    d_owout = dt_in("owout", [128, 8, 1024])
    d_g1 = dt_in("g1c", [128, 8, 32])
    d_g2 = dt_in("g2p", [32, 2, 512])
    d_sg = dt_in("s_gla", [128, 4, 2, 256])
    o_y = dt_out("yT", [128, 8, NT])
    o_sw = dt_out("o_wkv", [8, 2, 8, 128, 128])
    o_sg = dt_out("o_gla", [8, 2, 4, 128, 256])

    st = contextlib.ExitStack()
    sb = lambda n, s, d=F32: st.enter_context(nc.sbuf_tensor(n, s, d))
    X = sb("X", [128, 8, NT])
    HT = sb("HT", [128, 8, NT], BF16)
    VEC = sb("VEC", [128, NV])
    CF = sb("CF", [128, NCF])
    CB = sb("CB", [128, NCB], BF16)
    IDB = sb("IDB", [128, 128], BF16)
    ONEB = sb("ONEB", [128, 128], BF16)
    BDB = sb("BDB", [128, 128], BF16)
    MSK = sb("MSK", [128, 4, 32])
    MOD = sb("MOD", [128, 2, 24])
    G1 = sb("G1", [128, 2, 8])
    CS_ = sb("CSIL", [128, 8])
    EPS = sb("EPS", [128, 2])
    FA = sb("FA", [128, NT])
    FB = sb("FB", [128, NT])
    BA = [sb(f"BA{i}", [128, NT], BF16) for i in range(4)]
    WRAW = sb("WRAW", [128, 3072])
    WS = WRAW[:, 0:1024].rearrange("p (a b) -> p a b", a=8)
    WB = WRAW[:, 1024:3072].bitcast(BF16).rearrange("p (a b c) -> p a b c", a=4, b=8)
    WOB = WB[:, 3, :, :].rearrange("p a b -> p (a b)")
    UNI = sb("UNI", [128, 8960])
    ub = lambda a, b, p=128: UNI[0:p, a:b].bitcast(BF16)
    T512 = [sb(f"T512_{i}", [128, 512]) for i in range(4)] + [WRAW[:, 512 * i:512 * (i + 1)] for i in range(6)]
    G1B = ub(1024, 1152).rearrange("p (a b) -> p a b", a=8)
    G2B = ub(1152, 1664, 32).rearrange("p (a b) -> p a b", a=2)
    GT1 = ub(0, 1024, 32)
    VTG = sb("VTG", [128, 16, 256], BF16)
    KKF = sb("KKF", [128, NT], BF16)
    GS = sb("GS", [128, 3, 256])
    WLG = sb("WLG", [128, 2, 16])
    NGB = sb("NGB", [128, 8])
    RSG = sb("RSG", [128, 16])
    PRB = ub(0, 2560)
    CHB = ub(2560, 6080)
    BKT = ub(6080, 7104).rearrange("p (a b) -> p a b", a=4)
    PRS = ub(7104, 8128)
    W2S = WRAW[:, 0:512].rearrange("p (a b) -> p a b", a=4)
    W2B = ub(8128, 8384).rearrange("p (a b) -> p a b", a=4)
    W_XAM = 8384
    WLW = sb("WLW", [128, 2, 16])
    OMKA = sb("OMKA", [128, 8])
    GNS = sb("GNS", [128, 8])
    WOT = [T512[2], T512[3]]
    PS = [st.enter_context(nc.psum_tensor(f"ps{i}", [128, 512], F32)) for i in range(8)]

    P = Prog(nc)
    cnt = {'rr': 0}

    def rr(engs=('act', 'dve')):
        cnt['rr'] += 1
        return engs[cnt['rr'] % len(engs)]

    def mm(out, lhsT, rhs, start, stop, r, w):
        P.op('pe', lambda e: e.matmul(out, lhsT, rhs, start=start, stop=stop), reads=r, writes=w)

    def copy(eng, out, in_, r, w):
        if eng == 'act':
            P.op('act', lambda e: e.activation(out=out, in_=in_, func=ACT.Copy), reads=r, writes=w)
        else:
            P.op(eng, lambda e: e.tensor_copy(out=out, in_=in_), reads=r, writes=w)

    def tt(eng, out, a, b, op, r, w):
        P.op(eng, lambda e: e.tensor_tensor(out=out, in0=a, in1=b, op=op), reads=r, writes=w)

    def ts(eng, out, a, s1, s2, op0, op1, r, w):
        if s2 is None:
            P.op(eng, lambda e: e.tensor_scalar(out=out, in0=a, scalar1=s1, scalar2=None, op0=op0), reads=r, writes=w)
        else:
            P.op(eng, lambda e: e.tensor_scalar(out=out, in0=a, scalar1=s1, scalar2=s2, op0=op0, op1=op1), reads=r, writes=w)

    def stt(out, a, s, b, op0, op1, r, w):
        P.op('dve', lambda e: e.scalar_tensor_tensor(out=out, in0=a, scalar=s, in1=b, op0=op0, op1=op1), reads=r, writes=w)

    def act(out, in_, func, r, w, bias=None, scale=None):
        kw = {}
        if bias is not None:
            kw['bias'] = bias
        if scale is not None:
            kw['scale'] = scale
        P.op('act', lambda e: e.activation(out=out, in_=in_, func=func, **kw), reads=r, writes=w)

    def ld(out, in_, w, r=()):
        P.dma('sp', lambda e: e.dma_start(out=out, in_=in_), reads=r, writes=w)

    for c in range(8):
        ld(X[:, c, :], d_x[:, c, :], [('X', c)])
    ld(VEC[:], d_vec, ['VEC'])
    ld(CF[:], d_cf, ['CF'])
    ld(CB[:], d_cb, ['CB'])
    ld(MSK[:], d_msk, ['MSK'])
    copy('pool', IDB[:], CF[:, C_ID:C_ID + 128], ['CF'], ['IDB'])
    copy('pool', ONEB[:], CF[:, C_ONE:C_ONE + 128], ['CF'], ['ONEB'])
    copy('pool', BDB[:], CF[:, C_BD:C_BD + 128], ['CF'], ['BDB'])
    P.op('pool', lambda e: e.memset(EPS[:, 0:1], NORM_EPS), writes=['EPS'])
    P.op('pool', lambda e: e.memset(EPS[:, 1:2], GN_EPS), writes=['EPS'])
    act(CS_[:], VEC[:, V_CV:V_CV + 8], ACT.Silu, ['VEC'], ['CSIL'])
    def ada_layer(l, ACCQ, an, STG, sn):
        steps = []
        for c in range(8):
            for q in range(3):
                def st_(c=c, q=q, i=len(steps)):
                    sg, sr = STG[i % 4], (sn, i % 4)
                    ld(sg, d_ada[l, :, c, q * 1024:(q + 1) * 1024], [sr])
                    if c == 0:
                        ts('dve', ACCQ[q], sg, CS_[:, c:c + 1], None, ALU.mult, None, [sr, 'CSIL'], [(an, q)])
                    else:
                        stt(ACCQ[q], sg, CS_[:, c:c + 1], ACCQ[q], ALU.mult, ALU.add, [sr, 'CSIL', (an, q)], [(an, q)])
                steps.append(st_)

        def fin():
            for j in range(24):
                mm(PS[0][:, j:j + 1], ACCQ[j // 8][:, (j % 8) * 128:(j % 8 + 1) * 128], CF[:, C_ONE:C_ONE + 1],
                   True, True, [(an, j // 8), 'CF'], [('ps', 0)])
            tt('dve', MOD[:, l, :], PS[0][:, 0:24], VEC[:, V_AB + 24 * l:V_AB + 24 * l + 24], ALU.add,
               [('ps', 0), 'VEC'], ['MOD'])
            ts('dve', G1[:, l, :], MOD[:, l, 8:16], 1.0, None, ALU.add, None, ['MOD'], ['G1'])
            tt('dve', G1[:, l, :], G1[:, l, :], VEC[:, V_NG + 8 * l:V_NG + 8 * l + 8], ALU.mult, ['G1', 'VEC'], ['G1'])
        return steps, fin

    st0, fin0 = ada_layer(0, [FA[:, 0:1024], FA[:, 1024:2048], FB[:, 1024:2048]], 'ACC',
                          [FB[:, 0:1024], WRAW[:, 0:1024], WRAW[:, 1024:2048], WRAW[:, 2048:3072]], 'STG')
    for f_ in st0:
        f_()
    fin0()
    ada1_steps, ada1_fin = ada_layer(1, [UNI[:, 1024 * i:1024 * (i + 1)] for i in range(3)], 'uACC',
                                     [UNI[:, 3072 + 1024 * i:4096 + 1024 * i] for i in range(4)], 'uSTG')

    XR = [('X', c) for c in range(8)]
    HR = [('HT', c) for c in range(8)]

    SQB = [WRAW[:, 0:256].bitcast(BF16), WRAW[:, 1024:1280].bitcast(BF16)]
    SQR = ['WS', ('WB', 0)]

    RSTD = [(T512[2], ('T', 2)), (T512[3], ('T', 3))]

    def sumsq_rstd(tsl, ri=0):
        rb_, rn_ = RSTD[ri]
        for c in range(8):
            act(SQB[c % 2], X[:, c, tsl], ACT.Square, [('X', c)], [SQR[c % 2]])
            mm(PS[1][:], ONEB[:], SQB[c % 2], c == 0, c == 7, [SQR[c % 2], 'ONEB'], [('ps', 1)])
        act(rb_[:], PS[1][:], ACT.Sqrt, [('ps', 1), 'EPS'], [rn_], bias=EPS[:, 0:1], scale=1.0 / 1024)
        P.op('dve', lambda e: e.reciprocal(out=rb_[:], in_=rb_[:]), reads=[rn_], writes=[rn_])

    def norm_mod(gfn, sfn, out_fn, out_res):
        tsls = [slice(t4 * 512, (t4 + 1) * 512) for t4 in range(4)]
        sumsq_rstd(tsls[0], 0)
        for t4 in range(4):
            tsl = tsls[t4]
            if t4 + 1 < 4:
                sumsq_rstd(tsls[t4 + 1], (t4 + 1) % 2)
            rb_, rn_ = RSTD[t4 % 2]
            for c in range(8):
                tmp = T512[c % 2]
                o = out_fn(c, tsl)
                if c % 2 == 0:
                    tt('pool', tmp[:], X[:, c, tsl], rb_[:], ALU.mult, [('X', c), rn_], [('T', 0)])
                    ts('dve', o, tmp[:], gfn(c), sfn(c), ALU.mult, ALU.add, [('T', 0), 'VEC', 'G1', 'MOD'], out_res(c, t4))
                else:
                    tt('dve', tmp[:], X[:, c, tsl], rb_[:], ALU.mult, [('X', c), rn_], [('T', 1)])
                    act(o, tmp[:], ACT.Identity, [('T', 1), 'VEC', 'G1', 'MOD'], out_res(c, t4), bias=sfn(c), scale=gfn(c))

    def load_w(dram, col0, br):
        ld(WS[:], dram[:, :, col0:col0 + 128], ['WS'])
        copy(rr(('act', 'dve')), WB[:, br, :, :], WS[:], ['WS'], [('WB', br)])

    def proj(br, evac, banks=(2, 3, 4, 5)):
        for t4 in range(4):
            tsl = slice(t4 * 512, (t4 + 1) * 512)
            pb = banks[cnt['rr'] % len(banks)]
            cnt['rr'] += 1
            for c in range(8):
                mm(PS[pb][:], WB[:, br, c, :], HT[:, c, tsl], c == 0, c == 7, [('WB', br), ('HT', c)], [('ps', pb)])
            evac(t4, tsl, PS[pb][:], ('ps', pb))

    def wout_partial(l, dram_wout, j, OB, ores):
        ld(WS[:].rearrange("p a b -> p (a b)"), dram_wout[:, j, :], ['WS'])
        copy('pool', WOB[:], WS[:].rearrange("p a b -> p (a b)"), ['WS'], [('WB', 3)])
        for ft in range(8):
            for t4 in range(4):
                tsl = slice(t4 * 512, (t4 + 1) * 512)
                pb = 2 + (cnt['rr'] % 4)
                cnt['rr'] += 1
                mm(PS[pb][:], WOB[:, ft * 128:(ft + 1) * 128], OB[:, tsl], True, True, [('WB', 3), ores], [('ps', pb)])
                if (ft * 4 + t4) % 5 < 3:
                    stt(X[:, ft, tsl], PS[pb][:], MOD[:, l, 16 + ft:17 + ft], X[:, ft, tsl], ALU.mult, ALU.add,
                        [('ps', pb), 'MOD', ('X', ft)], [('X', ft)])
                else:
                    wt = WOT[t4 % 2]
                    act(wt[:], PS[pb][:], ACT.Copy, [('ps', pb), 'MOD'], [('T', 2 + t4 % 2)], scale=MOD[:, l, 16 + ft:17 + ft])
                    tt('pool', X[:, ft, tsl], X[:, ft, tsl], wt[:], ALU.add, [('T', 2 + t4 % 2), ('X', ft)], [('X', ft)])

    P.barrier()
    norm_mod(lambda c: G1[:, 0, c:c + 1], lambda c: MOD[:, 0, c:c + 1], lambda c, tsl: HT[:, c, tsl],
             lambda c, t4: [('HT', c)])

    def v3(t, a, b):
        return t[:].rearrange("p (g w) -> p g w", w=64)[:, a, b]

    for j in range(8):
        for br in range(4):
            load_w(d_ewin, br * 1024 + j * 128, br)
        for f_ in ada1_steps[3 * j:3 * j + 3]:
            f_()
        U, Pm, Y = FA, FB, FA
        proj(0, lambda t4, tsl, ps, pr: copy(rr(), FA[:, tsl], ps, [pr], ['FA']))
        proj(2, lambda t4, tsl, ps, pr: tt('dve', FB[:, tsl], ps, FA[:, tsl], ALU.mult, [pr, 'FA'], ['FB']))
        proj(1, lambda t4, tsl, ps, pr: copy(rr(), BA[0][:, tsl], ps, [pr], ['BA0']))
        proj(3, lambda t4, tsl, ps, pr: act(BA[1][:, tsl], ps, ACT.Silu, [pr], ['BA1']))
        w0, w1, w2 = (VEC[:, V_CW + 8 * k + j:V_CW + 8 * k + j + 1] for k in range(3))
        act(FA[:], FB[:], ACT.Copy, ['FB', 'VEC'], ['FA'], scale=w1)
        g_all, g_lo, g_hi = slice(0, 32), slice(0, 31), slice(1, 32)
        stt(v3(FA, g_all, slice(1, 64)), v3(FB, g_all, slice(0, 63)), w0, v3(FA, g_all, slice(1, 64)),
            ALU.mult, ALU.add, ['FB', 'FA', 'VEC'], ['FA'])
        stt(v3(FA, g_all, slice(0, 63)), v3(FB, g_all, slice(1, 64)), w2, v3(FA, g_all, slice(0, 63)),
            ALU.mult, ALU.add, ['FB', 'FA', 'VEC'], ['FA'])
        tb = T512[0]
        tt('pool', tb[:, 0:31], v3(FB, g_lo, 63), MSK[:, 0, 1:32], ALU.mult, ['FB', 'MSK'], [('T', 0)])
        stt(v3(FA, g_hi, 0), tb[:, 0:31], w0, v3(FA, g_hi, 0), ALU.mult, ALU.add, [('T', 0), 'FA', 'VEC'], ['FA'])
        tt('pool', tb[:, 32:63], v3(FB, g_hi, 0), MSK[:, 1, 1:32], ALU.mult, ['FB', 'MSK'], [('T', 0)])
        stt(v3(FA, g_lo, 63), tb[:, 32:63], w2, v3(FA, g_lo, 63), ALU.mult, ALU.add, [('T', 0), 'FA', 'VEC'], ['FA'])
        tt('pool', FA[:], FA[:], BA[0][:], ALU.mult, ['FA', 'BA0'], ['FA'])
        tt('dve', BA[2][:], FA[:], BA[1][:], ALU.mult, ['FA', 'BA1'], ['BA2'])
        wout_partial(0, d_ewout, j, BA[2], 'BA2')

    ada1_fin()
    P.barrier()
    load_w(d_w1, 0, 0)
    load_w(d_w1, 128, 1)
    proj(0, lambda t4, tsl, ps, pr: act(BA[2][:, tsl], ps, ACT.Tanh, [pr], ['BA2']))
    proj(1, lambda t4, tsl, ps, pr: copy('act', BA[3][:, tsl], ps, [pr], ['BA3']))
    ts('pool', OMKA[:], VEC[:, V_KA:V_KA + 8], -1.0, 1.0, ALU.mult, ALU.add, ['VEC'], ['OMKA'])
    KKf = KKF[:]
    Rb = FA[:, 0:1024].bitcast(BF16)
    Kb = FA[:, 1024:2048].bitcast(BF16)
    TB = T512
    c3 = lambda ap: ap.rearrange("p (k t) -> p k t", t=128)
    FBb = FB[:].bitcast(BF16)
    PRBs = [PRB, FBb]
    XAm = ub(W_XAM, W_XAM + 512)
    def opnd(par):
        base = PRBs[par]
        d = dict(AR=base[:, 0:1024], Bh=base[:, 1024:1536], Kh=base[:, 1536:2048],
                 Btm=[base[:, 2048:2560], base[:, 2560:3072]], Ktm=[base[:, 3072:3584], base[:, 3584:4096]])
        d['Am'] = [base[:, 4096:4608], base[:, 4608:5120]] if par == 0 else [XAm[:, 0:512], XAm[:, 512:1024]]
        d['AR4'] = d['AR'].rearrange("p (k s t) -> p k s t", s=2, t=128)
        return d
    OPN = [opnd(0), opnd(1)]
    NM0 = [CHB[:, 1024 * i:1024 * i + 512] for i in range(3)]
    NM1 = [CHB[:, 1024 * i + 512:1024 * i + 1024] for i in range(3)]
    lv4 = lambda ap: ap.rearrange("p (h s t) -> p h s t", h=2, s=2)
    LVs = [[lv4(CHB[:, 3072 + 1536 * c + 512 * i:3072 + 1536 * c + 512 * (i + 1)]) for i in range(2)] for c in range(2)]
    PPs = [[CHB[:, 4096 + 1536 * c + 256 * i:4096 + 1536 * c + 256 * (i + 1)] for i in range(2)] for c in range(2)]
    TTf = [CHB[:, 6144:6400], CHB[:, 6400:6656]]
    Z0B, UB, SBw = CHB[:, 6656:6784], CHB[:, 6784:6912], CHB[:, 6912:7040]
    BKTs = [BKT[:, 0:2, :], BKT[:, 2:4, :]]
    SFw = GS[:, 0, 0:128]
    BDm = CF[:, C_BD:C_BD + 128]
    HM = [CF[:, C_HM:C_HM + 1], CF[:, C_HM + 1:C_HM + 2]]
    hs = lambda h: slice(h * 64, (h + 1) * 64)

    def interleave(lists):
        lists = [l for l in lists if l]
        pos = [0] * len(lists)
        n = max(len(l) for l in lists) if lists else 0
        for step in range(n):
            for li, l in enumerate(lists):
                tgt = (step + 1) * len(l) // n
                while pos[li] < tgt:
                    l[pos[li]]()
                    pos[li] += 1

    KT = [UNI[:, 512 * i:512 * (i + 1)] for i in range(3)]
    ET = [FB[:, 512 * i:512 * (i + 1)] for i in range(2)]

    def start_rk(jj, banks=(2, 3, 4, 5), kb=1):
        vcol = lambda base: VEC[:, base + jj:base + jj + 1]
        G = []

        def g_load():
            load_w(d_ewin, 4096 + 0 * 1024 + jj * 128, 0)
            load_w(d_ewin, 4096 + 1 * 1024 + jj * 128, 1)
            ld(W2S[:], d_w2[:, :, jj * 128:(jj + 1) * 128], ['WS'])
            copy('pool', W2B[:], W2S[:], ['WS'], ['W2B'])
        G.append(g_load)
        G.append(lambda: proj(0, lambda t4, tsl, ps, pr: copy(rr(), Rb[:, tsl], ps, [pr], ['FA']), banks))
        G.append(lambda: proj(1, lambda t4, tsl, ps, pr: copy(rr(), Kb[:, tsl], ps, [pr], ['FA']), banks))

        def g_kk(t4):
            def g():
                tsl = slice(t4 * 512, (t4 + 1) * 512)
                o0 = ('OPN', 0)
                sqb = KT[1].bitcast(BF16)[:, 0:512]
                act(KT[0][:], Kb[:, tsl], ACT.Copy, ['FA', 'VEC'], [o0], scale=vcol(V_KK))
                act(sqb, KT[0][:], ACT.Square, [], [o0])
                mm(PS[kb][:], BDB[:], sqb, True, True, [o0, 'BDB'], [('ps', kb)])
                act(KT[2][:], PS[kb][:], ACT.Sqrt, [('ps', kb)], [o0])
                ts('dve', KT[2][:], KT[2][:], 1e-12, None, ALU.max, None, [], [o0])
                P.op('dve', lambda e: e.reciprocal(out=KT[2][:], in_=KT[2][:]), reads=[], writes=[o0])
                tt('pool', KKf[:, tsl], KT[0][:], KT[2][:], ALU.mult, [o0], ['KKf'])
            return g
        for t4 in range(4):
            G.append(g_kk(t4))
        return G

    def start_vz(jj):
        G = []

        def g_l():
            load_w(d_ewin, 4096 + 2 * 1024 + jj * 128, 2)
            load_w(d_ewin, 4096 + 3 * 1024 + jj * 128, 3)
        G.append(g_l)
        G.append(lambda: proj(2, lambda t4, tsl, ps, pr: copy(rr(), BA[0][:, tsl], ps, [pr], ['BA0'])))
        G.append(lambda: proj(3, lambda t4, tsl, ps, pr: act(BA[1][:, tsl], ps, ACT.Silu, [pr], ['BA1'])))

        def g_vt(g):
            def f():
                for k in range(4):
                    mm(PS[4][:, k * 128:(k + 1) * 128], BA[0][:, (4 * g + k) * 128:(4 * g + k + 1) * 128], IDB[:], True, True,
                       ['BA0', 'IDB'], [('ps', 4)])
                copy('act', VTG[:, 4 * g:4 * g + 4, 0:128], c3(PS[4][:]), [('ps', 4)], ['VTG'])
            return f
        for g in range(4):
            G.append(g_vt(g))
        return G

    NRK_AT = {26: [0], 27: [1], 28: [2], 29: [3], 30: [4, 5], 31: [6]}

    def pair_chain(jj, vz, nrk=None):
        vcol = lambda base: VEC[:, base + jj:base + jj + 1]
        if True:
            blocks = [(0, b) for b in range(4)] + [(1, b) for b in range(3, -1, -1)]
            chunks = [(0, b, k) for b in range(4) for k in range(4)] + [(1, b, k) for b in range(3, -1, -1) for k in range(3, -1, -1)]
            mab = lambda z: CB[:, B_MAB + 512 * z:B_MAB + 512 * z + 512]
            mnm = lambda z: CB[:, B_MN + 256 * z:B_MN + 256 * z + 256]

            def init_state(z, jj=jj):
                def g():
                    ld(SFw, d_sw[:, jj, z, :], ['SFw'])
                    copy('pool', SBw, SFw, ['SFw'], ['SBw'])
                return g

            def prep_groups(bi, jj=jj, vcol=vcol):
                z, blk = blocks[bi]
                par = bi % 2
                O_ = OPN[par]
                opr = ('OPN', par)
                bkt = BKTs[par]
                bkr = ('BKT', par)
                tsl = slice(blk * 512, (blk + 1) * 512)
                SIG, CSw, CRw, CSBw, AI, KM, BV = TB[0], TB[1], TB[3], TB[4], TB[9], TB[7], TB[8]
                rAI, rBV = ('WB', 3), ('WB', 2)
                if bi == 0:
                    AI, BV = FB[:, 1024:1536], FB[:, 1536:2048]
                    rAI = rBV = ('OPN', 1)
                E1, E3 = TB[2], TB[6]
                if z == 0:
                    inc, ex, rest = (CSw, ('T', 1)), (SIG, ('T', 0)), (CRw, ('T', 3))
                else:
                    inc, ex, rest = (CSBw, 'WS'), (CRw, ('T', 3)), (SIG, ('T', 0))
                def w0():
                    mm(PS[4][:], W2B[:, z, :], BA[2][:, tsl], True, True, ['W2B', 'BA2'], [('ps', 4)])
                    mm(PS[5][:], W2B[:, 2 + z, :], BA[3][:, tsl], True, True, ['W2B', 'BA3'], [('ps', 5)])

                def w1():
                    act(SIG[:], PS[4][:], ACT.Sigmoid, [('ps', 4), 'VEC'], [('T', 0)], bias=vcol(V_W0 + 8 * z))
                    act(AI[:], PS[5][:], ACT.Sigmoid, [('ps', 5), 'VEC'], [rAI], bias=vcol(V_A0 + 8 * z))

                def w2():
                    P.op('dve', lambda e: e.tensor_tensor_scan(out=CSw[:], data0=CB[:, B_RST:B_RST + 512], data1=SIG[:],
                                                               initial=0.0, op0=ALU.mult, op1=ALU.add),
                         reads=[('T', 0), 'CB'], writes=[('T', 1)])
                    act(E3[:], AI[:], ACT.Identity, [rAI, 'VEC', 'OMKA'], [('WB', 0)], bias=OMKA[:, jj:jj + 1], scale=vcol(V_KA))
                    stt(BV[:], KKf[:, tsl], -1.0, AI[:], ALU.mult, ALU.mult, ['KKf', rAI], [rBV])

                def w3():
                    totb = bass.AP(CSw, 127, [[512, 128], [128, 4], [0, 128]])
                    tt('pool', c3(CRw[:]), totb, c3(CSw[:]), ALU.subtract, [('T', 1)], [('T', 3)])
                    act(WLW[:, z, blk * 4:blk * 4 + 4], bass.AP(CSw, 127, [[512, 128], [128, 4]]), ACT.Exp, [('T', 1)], ['WLW'], scale=-LAM)
                    tt('pool', KM[:], Kb[:, tsl], E3[:], ALU.mult, ['FA', ('WB', 0)], [('WB', 1)])

                def w4():
                    if z == 1:
                        tt('pool', CSBw[:], CRw[:], SIG[:], ALU.add, [('T', 3), ('T', 0)], ['WS'])
                    tt('pool', SIG[:], CSw[:], SIG[:], ALU.subtract, [('T', 1), ('T', 0)], [('T', 0)])
                    if z == 0:
                        tt('pool', PRS[:, tsl], Rb[:, tsl], KM[:], ALU.mult, ['FA', ('WB', 1)], ['PRS'])
                    else:
                        tt('pool', E3[:], Rb[:, tsl], KM[:], ALU.mult, ['FA', ('WB', 1)], [('WB', 0)])
                        tt('pool', PRS[:, tsl], PRS[:, tsl], E3[:], ALU.add, ['PRS', ('WB', 0)], ['PRS'])

                def w5():
                    act(E1[:], ex[0][:], ACT.Exp, [ex[1]], [('T', 2)], scale=-LAM)
                    act(E3[:], inc[0][:], ACT.Exp, [inc[1]], [('WB', 0)], scale=-LAM)

                def w6():
                    tt('pool', O_['AR4'][:, :, 0, :], c3(KKf[:, tsl]), c3(E1[:]), ALU.mult, ['KKf', ('T', 2)], [opr])
                    for h in range(2):
                        stt(O_['Am'][h], KKf[:, tsl], HM[h], E1[:], ALU.mult, ALU.mult, ['KKf', 'CF', ('T', 2)], [opr])
                    tt('pool', O_['AR4'][:, :, 1, :], c3(Rb[:, tsl]), c3(E3[:]), ALU.mult, ['FA', ('WB', 0)], [opr])

                def w7():
                    act(E1[:], rest[0][:], ACT.Exp, [rest[1]], [('T', 2)], scale=-LAM)
                    act(E3[:], inc[0][:], ACT.Exp, [inc[1]], [('WB', 0)], scale=LAM)

                def w8():
                    for h in range(2):
                        stt(O_['Btm'][h], BV[:], HM[h], E3[:], ALU.mult, ALU.mult, [rBV, 'CF', ('WB', 0)], [opr])
                        stt(O_['Ktm'][h], KM[:], HM[h], E3[:], ALU.mult, ALU.mult, [('WB', 1), 'CF', ('WB', 0)], [opr])
                    tt('pool', O_['Bh'], BV[:], E1[:], ALU.mult, [rBV, ('T', 2)], [opr])
                    tt('pool', O_['Kh'], KM[:], E1[:], ALU.mult, [('WB', 1), ('T', 2)], [opr])

                def w9():
                    for k in range(4):
                        mm(PS[4][:, k * 128:(k + 1) * 128], O_['Bh'][:, k * 128:(k + 1) * 128], IDB[:], True, True, [opr, 'IDB'], [('ps', 4)])

                def w10():
                    copy('act', bkt[:, 0, :], PS[4][:], [('ps', 4)], [bkr])
                    for k in range(4):
                        mm(PS[5][:, k * 128:(k + 1) * 128], O_['Kh'][:, k * 128:(k + 1) * 128], IDB[:], True, True, [opr, 'IDB'], [('ps', 5)])

                def w11():
                    copy('act', bkt[:, 1, :], PS[5][:], [('ps', 5)], [bkr])
                return [w0, w1, w2, w3, w4, w5, w6, w7, w8, w9, w10, w11]

            def a_groups(ci):
                bi, (z, blk, k) = ci // 4, chunks[ci]
                MAB, MNm = mab(z), mnm(z)
                par, q, m, cx = bi % 2, ci % 2, ci % 3, ci % 2
                O_ = OPN[par]
                opr = ('OPN', par)
                ksl = slice(k * 128, (k + 1) * 128)
                ARk = O_['AR'][:, k * 256:(k + 1) * 256]
                nm0, nm1 = NM0[m], NM1[m]
                LV, PPp = LVs[cx], PPs[cx]
                na, nb = (6, 7) if cx == 0 else (0, 1)
                PA_, PB_ = PS[na], PS[nb]
                ra, rb = ('ps', na), ('ps', nb)
                lvp = lambda i: ('LVp', cx, i)
                lvt = lambda i: ('LVt', cx, i)
                ppr = lambda i: ('PPp', cx, i)
                G = []

                def g0():
                    for h in range(2):
                        mm(PA_[:, h * 256:(h + 1) * 256], O_['Btm'][h][:, ksl], ARk, True, True, [opr], [ra])
                        mm(PB_[:, h * 256:(h + 1) * 256], O_['Ktm'][h][:, ksl], ARk, True, True, [opr], [rb])
                        mm(PS[3][:, 256 + h * 128:384 + h * 128], O_['Am'][h][:, ksl], O_['Btm'][h][:, ksl], True, True, [opr], [('ps', 3)])
                    tt('dve', nm0, PA_[:], MAB, ALU.mult, [ra, 'CB'], [('NM0', m)])
                    tt('dve', PPp[0], PS[3][:, 256:512], MNm, ALU.mult, [('ps', 3), 'CB'], [ppr(0)])
                    tt('dve', nm1, PB_[:], MAB, ALU.mult, [rb, 'CB'], [('NM1', m)])
                G.append(g0)
                pt0 = nm0.rearrange("p (h s t) -> p h s t", h=2, s=2)[:, :, 0, :]

                def g1():
                    idb2 = bass.AP(IDB, 0, [[128, 128], [0, 2], [1, 128]])
                    tt('pool', LV[1][:, :, 1, :], pt0, idb2, ALU.add, [('NM0', m), 'IDB'], [lvt(1)])
                    for h in range(2):
                        mm(PA_[:, h * 128:(h + 1) * 128], nm0[:, h * 256:h * 256 + 128], PPp[0][:, h * 128:(h + 1) * 128], True, True,
                           [('NM0', m), ppr(0)], [ra])
                        mm(PB_[:, h * 256:h * 256 + 128], PPp[0][:, h * 128:(h + 1) * 128], nm0[:, h * 256:h * 256 + 128], True, True,
                           [('NM0', m), ppr(0)], [rb])
                    copy('act', PPp[1], PA_[:, 0:256], [ra], [ppr(1)])
                    copy('dve', LV[1][:, :, 0, :], PB_[:].rearrange("p (h s t) -> p h s t", h=2, s=2)[:, :, 0, :], [rb], [lvp(1)])
                G.append(g1)

                def lvl(kk_):
                    def g():
                        a, b = kk_ % 2, (kk_ + 1) % 2
                        psb = PB_[:].rearrange("p (h s t) -> p h s t", h=2, s=2)
                        for h in range(2):
                            pk = PPp[a][:, h * 128:(h + 1) * 128]
                            if kk_ < 5:
                                mm(PB_[:, h * 256:(h + 1) * 256], pk, LV[a][:, h, :, :].rearrange("p s t -> p (s t)"), True, True,
                                   [ppr(a), lvp(a), lvt(a)], [rb])
                            else:
                                mm(PB_[:, h * 256 + 128:(h + 1) * 256], pk, LV[a][:, h, 1, :], True, True, [ppr(a), lvt(a)], [rb])
                            mm(PA_[:, h * 128:(h + 1) * 128], LV[a][:, h, 0, :], pk, True, True, [ppr(a), lvp(a)], [ra])
                        copy('act', PPp[b], PA_[:, 0:256], [ra], [ppr(b)])
                        if kk_ < 5:
                            copy('dve', LV[b][:, :, 0, :], psb[:, :, 0, :], [rb], [lvp(b)])
                        tt('dve', LV[b][:, :, 1, :], psb[:, :, 1, :], LV[a][:, :, 1, :], ALU.add, [rb, lvt(a)], [lvt(b)])
                    return g
                for kk_ in range(1, 6):
                    G.append(lvl(kk_))

                def g7():
                    for h in range(2):
                        mm(PB_[:, h * 128:(h + 1) * 128], PPp[0][:, h * 128:(h + 1) * 128], LV[0][:, h, 1, :], True, True,
                           [ppr(0), lvt(0)], [rb])
                    tt('dve', TTf[q].rearrange("p (h t) -> p h t", h=2), PB_[:, 0:256].rearrange("p (h t) -> p h t", h=2),
                       LV[0][:, :, 1, :], ALU.add, [rb, lvt(0)], [('TTf', q)])
                G.append(g7)
                return G

            def b_groups(ci, jj=jj):
                bi, (z, blk, k) = ci // 4, chunks[ci]
                par, q, m = bi % 2, ci % 2, ci % 3
                O_ = OPN[par]
                opr = ('OPN', par)
                bkt, bkr = BKTs[par], ('BKT', par)
                c16 = blk * 4 + k
                ksl = slice(k * 128, (k + 1) * 128)
                ARk = O_['AR'][:, k * 256:(k + 1) * 256]
                nm0, nm1, TT = NM0[m], NM1[m], TTf[q]
                vt = lambda h: VTG[:, c16, h * 64:(h + 1) * 64]
                G = []

                def g0():
                    mm(PS[2][:, 0:128], ARk[:, 0:128], SBw, True, False, [opr, 'SBw'], [('ps', 2)])
                    for h in range(2):
                        mm(PS[2][:, hs(h)], nm1[:, h * 256:h * 256 + 128], vt(h), False, h == 1, [('NM1', m), 'VTG'], [('ps', 2)])
                    copy('act', Z0B, PS[2][:, 0:128], [('ps', 2)], ['Z0B'])
                G.append(g0)

                def g1():
                    for h in range(2):
                        mm(PS[2][:, 128 + h * 64:192 + h * 64], TT[:, h * 128:(h + 1) * 128], Z0B[:, hs(h)], True, True,
                           [('TTf', q), 'Z0B'], [('ps', 2)])
                    copy('act', UB, PS[2][:, 128:256], [('ps', 2)], ['UB'])
                G.append(g1)

                def g2():
                    mm(PS[3][:, 0:128], ARk[:, 128:256], SBw, True, False, [opr, 'SBw'], [('ps', 3)])
                    for h in range(2):
                        mm(PS[3][:, hs(h)], nm0[:, h * 256 + 128:h * 256 + 256], UB[:, hs(h)], False, False, [('NM0', m), 'UB'], [('ps', 3)])
                        mm(PS[3][:, hs(h)], nm1[:, h * 256 + 128:h * 256 + 256], vt(h), False, h == 1, [('NM1', m), 'VTG'], [('ps', 3)])
                    mm(PS[2][:, 256:384], bkt[:, 0, ksl], UB, True, False, [bkr, 'UB'], [('ps', 2)])
                    mm(PS[2][:, 256:384], bkt[:, 1, ksl], VTG[:, c16, 0:128], False, True, [bkr, 'VTG'], [('ps', 2)])
                G.append(g2)

                def g3():
                    tmpw = GS[:, 1 + (c16 % 2), 0:128]
                    tr = ('TMPw', c16 % 2)
                    stt(tmpw, SFw, WLW[:, z, c16:c16 + 1], PS[2][:, 256:384], ALU.mult, ALU.add, ['SFw', 'WLW', ('ps', 2)], [tr])
                    stt(SFw, tmpw, MSK[:, 2 + z, c16:c16 + 1], BDm, ALU.mult, ALU.mult, [tr, 'MSK', 'CF'], ['SFw'])
                    copy('act', SBw, SFw, ['SFw'], ['SBw'])
                    if (c16 % 2 == 1) == (z == 0):
                        P.dma('sp', lambda e, tmpw=tmpw, z=z, jj=jj, c16=c16: e.dma_start(out=o_sw[c16 // 2, z, jj], in_=tmpw), reads=[tr])
                    ofc = VTG[:, c16, 128:256]
                    vo = ('VO', c16)
                    if z == 0:
                        copy('act', ofc, PS[3][:, 0:128], [('ps', 3)], [vo])
                    else:
                        to, sqo = GS[:, 0, 128:256], GS[:, 1, 128:256]
                        h3 = lambda ap: ap.rearrange("p (g w) -> p g w", w=64)
                        gb = lambda off: bass.AP(GNS, off, [[8, 128], [1, 2], [0, 64]])
                        tt('dve', to, ofc, PS[3][:, 0:128], ALU.add, [('ps', 3), vo], ['TO'])
                        tt('dve', sqo, to, to, ALU.mult, ['TO'], ['SQO'])
                        both = bass.AP(GS, 128, [[768, 128], [256, 2], [64, 2], [1, 64]])
                        P.op('dve', lambda e: e.tensor_reduce(out=GNS[:, 0:4].rearrange("p (a b) -> p a b", a=2), in_=both,
                                                              axis=AX.X, op=ALU.add), reads=['TO', 'SQO'], writes=['GNS'])
                        tt('dve', GNS[:, 4:6], GNS[:, 0:2], GNS[:, 0:2], ALU.mult, ['GNS'], ['GNS'])
                        stt(GNS[:, 2:4], GNS[:, 2:4], 64.0, GNS[:, 4:6], ALU.mult, ALU.subtract, ['GNS'], ['GNS'])

                        def tail():
                            act(GNS[:, 0:2], GNS[:, 0:2], ACT.Copy, ['GNS'], ['GNS'], scale=1.0 / 64)
                            act(GNS[:, 2:4], GNS[:, 2:4], ACT.Ln, ['GNS', 'EPS'], ['GNS'], bias=EPS[:, 1:2], scale=1.0 / 4096)
                            act(GNS[:, 2:4], GNS[:, 2:4], ACT.Exp, ['GNS'], ['GNS'], scale=-0.5)
                            tt('pool', h3(to), h3(to), gb(0), ALU.subtract, ['TO', 'GNS'], ['TO'])
                            tt('pool', h3(ofc), h3(to), gb(2), ALU.mult, ['TO', 'GNS'], [vo])
                        DEFER[ci] = tail
                G.append(g3)
                return G

            NCK = 32
            DEFER = {}
            init_state(0)()
            AG = {0: a_groups(0), 1: a_groups(1)}
            PG = {0: prep_groups(0), 1: prep_groups(1)}
            lock = [(lambda i=i: (AG[0][i](), AG[1][i]() if i < 4 else None)) for i in range(8)]
            interleave([vz, PG[0] + lock])
            for ci in range(NCK):
                bg = b_groups(ci)
                if ci - 1 in DEFER:
                    bg = [DEFER.pop(ci - 1)] + bg
                if ci == 16:
                    bg = [init_state(1)] + bg
                lists = [bg]
                if ci + 1 < NCK:
                    lists.append(AG[ci + 1][4:])
                if ci + 2 < NCK:
                    AG[ci + 2] = a_groups(ci + 2)
                    lists.append(AG[ci + 2][:4])
                b_, r_ = ci // 4, ci % 4
                if r_ < 2 and b_ + 1 < 8:
                    if b_ + 1 not in PG:
                        PG[b_ + 1] = prep_groups(b_ + 1)
                    if b_ == 0:
                        lists.append(PG[1][6 * r_:6 * r_ + 6])
                    else:
                        lists.append(PG[b_ + 1][6 + 3 * r_:9 + 3 * r_])
                elif r_ >= 2 and b_ + 2 < 8:
                    if b_ + 2 not in PG:
                        PG[b_ + 2] = prep_groups(b_ + 2)
                    lists.append(PG[b_ + 2][3 * (r_ - 2):3 * (r_ - 2) + 3])
                if nrk is not None and ci in NRK_AT:
                    lists.append([nrk[i] for i in NRK_AT[ci]])
                interleave(lists)
            for k_ in sorted(DEFER):
                DEFER.pop(k_)()

    def pair_end(jj):
        vcol = lambda base: VEC[:, base + jj:base + jj + 1]
        G = []
        o1 = ('OPN', 1)

        def g_t(t4):
            def g():
                tsl = slice(t4 * 512, (t4 + 1) * 512)
                for k in range(4):
                    mm(PS[4][:, k * 128:(k + 1) * 128], VTG[:, t4 * 4 + k, 128:256], IDB[:], True, True,
                       [('VO', t4 * 4 + k), 'IDB'], [('ps', 4)])
                ts('dve', TB[0][:], PS[4][:], vcol(V_LW), vcol(V_LB), ALU.mult, ALU.add, [('ps', 4), 'VEC'], [('T', 0)])
                etb = ET[0].bitcast(BF16)[:, 0:512]
                act(etb, PRS[:, tsl], ACT.Copy, ['PRS', 'VEC'], [o1], scale=vcol(V_RK))
                mm(PS[5][:], BDB[:], etb, True, True, [o1, 'BDB'], [('ps', 5)])
                tt('dve', ET[1][:], PS[5][:], BA[0][:, tsl], ALU.mult, [('ps', 5), 'BA0'], [o1])
                tt('pool', TB[0][:], TB[0][:], ET[1][:], ALU.add, [('T', 0), o1], [('T', 0)])
                tt('pool', BA[1][:, tsl], TB[0][:], BA[1][:, tsl], ALU.mult, [('T', 0), 'BA1'], ['BA1'])
            return g
        for t4 in range(4):
            G.append(g_t(t4))

        def g_w():
            ld(WS[:].rearrange("p a b -> p (a b)"), d_ewout[:, 8 + jj, :], ['WS'])
            copy('pool', WOB[:], WS[:].rearrange("p a b -> p (a b)"), ['WS'], [('WB', 3)])

        def g_o(t4, half):
            def g():
                tsl = slice(t4 * 512, (t4 + 1) * 512)
                for ft in range(4 * half, 4 * half + 4):
                    pb = 2 + (cnt['rr'] % 4)
                    cnt['rr'] += 1
                    mm(PS[pb][:], WOB[:, ft * 128:(ft + 1) * 128], BA[1][:, tsl], True, True, [('WB', 3), 'BA1'], [('ps', pb)])
                    if (ft * 4 + t4) % 5 < 3:
                        stt(X[:, ft, tsl], PS[pb][:], MOD[:, 0, 16 + ft:17 + ft], X[:, ft, tsl], ALU.mult, ALU.add,
                            [('ps', pb), 'MOD', ('X', ft)], [('X', ft)])
                    else:
                        wt = WOT[ft % 2]
                        act(wt[:], PS[pb][:], ACT.Copy, [('ps', pb), 'MOD'], [('T', 2 + ft % 2)], scale=MOD[:, 0, 16 + ft:17 + ft])
                        tt('pool', X[:, ft, tsl], X[:, ft, tsl], wt[:], ALU.add, [('T', 2 + ft % 2), ('X', ft)], [('X', ft)])
            return g
        G = [g_w, G[0], G[1], g_o(0, 0), g_o(0, 1), G[2], g_o(1, 0), g_o(1, 1), G[3], g_o(2, 0), g_o(2, 1), g_o(3, 0), g_o(3, 1)]
        return G

    for g in start_rk(0):
        g()
    vz = start_vz(0)
    for jj in range(8):
        nrk = start_rk(jj + 1, banks=(4, 5), kb=4) if jj + 1 < 8 else None
        pair_chain(jj, vz, nrk)
        for g in pair_end(jj):
            g()
        if jj + 1 < 8:
            vz = start_vz(jj + 1)
    P.barrier()

    norm_mod(lambda c: G1[:, 1, c:c + 1], lambda c: MOD[:, 1, c:c + 1], lambda c, tsl: HT[:, c, tsl],
             lambda c, t4: [('HT', c)])
    ld(WS[:].rearrange("p a b -> p (a b)")[:, 0:256], d_g1.rearrange('p a b -> p (a b)'), ['WS'])
    copy('pool', G1B[:].rearrange('p a b -> p (a b)'), WS[:].rearrange("p a b -> p (a b)")[:, 0:256], ['WS'], ['G1B'])
    ld(WS[:].rearrange("p a b -> p (a b)")[0:32, :], d_g2.rearrange('p a b -> p (a b)'), ['WS'])
    copy('pool', G2B[:].rearrange('p a b -> p (a b)'), WS[:].rearrange("p a b -> p (a b)")[0:32, :], ['WS'], ['G2B'])
    ts('pool', NGB[:], VEC[:, V_GB:V_GB + 8], -1.0, None, ALU.mult, None, ['VEC'], ['NGB'])
    for t4 in range(4):
        tsl = slice(t4 * 512, (t4 + 1) * 512)
        for c in range(8):
            mm(PS[0][0:32, :], G1B[:, c, :], HT[:, c, tsl], c == 0, c == 7, ['G1B', ('HT', c)], [('ps', 0)])
        copy('act', GT1[:, tsl], PS[0][0:32, :], [('ps', 0)], ['GT1'])
    SP, CSg, CR0, INC, RST, EXg = T512[:6]
    g2 = ub(2048, 3328)
    GSET = [(BA[0][:, 0:512], BA[0][:, 512:1024], BA[0][:, 1024:1536], BA[1][:, 0:512],
             BA[1][:, 512:1024].rearrange('p (k t) -> p k t', k=4)),
            (g2[:, 0:512], g2[:, 512:1024], g2[:, 1024:1536], g2[:, 1536:2048],
             g2[:, 2048:2560].rearrange('p (k t) -> p k t', k=4))]
    GEX = [UNI[:, 4096 + 512 * i:4096 + 512 * (i + 1)] for i in range(3)]
    GLTf = KKF[:].bitcast(F32)
    GLT = [GLTf[:, 0:512], GLTf[:, 512:1024]]
    SBg = BA[1][:, 1024:1280]
    SFg = GS[:, 0, :]
    for hd in range(4):
        load_w(d_owin, hd * 128, 0)
        load_w(d_owin, 512 + hd * 128, 1)
        load_w(d_owin, 1024 + hd * 256, 2)
        load_w(d_owin, 1024 + hd * 256 + 128, 3)
        proj(0, lambda t4, tsl, ps, pr: copy(rr(), FA[:, tsl], ps, [pr], ['FA']))
        proj(1, lambda t4, tsl, ps, pr: copy(rr(), FB[:, tsl], ps, [pr], ['FB']))
        vgr = []
        for half in range(2):
            vgr.append(lambda half=half: proj(2 + half, lambda t4, tsl, ps, pr: copy(rr(), BA[2][:, tsl], ps, [pr], ['BA2'])))

            def vt(g, half=half):
                def f():
                    for k in range(4):
                        mm(PS[1][:, k * 128:(k + 1) * 128], BA[2][:, (4 * g + k) * 128:(4 * g + k + 1) * 128], IDB[:], True, True,
                           ['BA2', 'IDB'], [('ps', 1)])
                    copy('act', VTG[:, 4 * g:4 * g + 4, half * 128:(half + 1) * 128], PS[1][:].rearrange("p (k t) -> p k t", k=4),
                         [('ps', 1)], [('VTG', 4 * g + i) for i in range(4)])
                return f
            for g in range(4):
                vgr.append(vt(g))
        gblocks = [(0, b) for b in range(4)] + [(1, b) for b in range(3, -1, -1)]
        GDEF = []

        def g_init(z, hd=hd):
            def g():
                ld(SFg, d_sg[:, hd, z, :], ['SFg'])
                copy('pool', SBg, SFg, ['SFg'], ['SBg', 'BA1'])
            return g

        def g_prep(bi, hd=hd):
            z, blk = gblocks[bi]
            sb_ = bi % 2
            QE, KE, KD, KDT, ATT4 = GSET[sb_]
            rq, rk, rd, rt, ra_ = (('gQE', sb_), ('gKE', sb_), ('gKD', sb_), ('gKDT', sb_), ('gATT', sb_))
            MB = [(PS[0], 0), (PS[1], 1)] if sb_ == 0 else [(PS[2], 2), (PS[5], 5)]
            tsl = slice(blk * 512, (blk + 1) * 512)
            EA, EB, EC = GEX
            if z == 0:
                inc, rst, ri, rr_ = CSg, CR0, ('T', 1), ('T', 2)
            else:
                inc, rst, ri, rr_ = INC, RST, ('T', 3), 'WS'

            def w0():
                mm(PS[4][:], G2B[:, z, hd * 128:(hd + 1) * 128], GT1[:, tsl], True, True, ['G2B', 'GT1'], [('ps', 4)])

            def w1():
                act(EXg[:], PS[4][:], ACT.Exp, [('ps', 4), 'NGB'], ['WS'], bias=NGB[:, 4 * z + hd:4 * z + hd + 1], scale=-1.0)

            def w2():
                act(SP[:], EXg[:], ACT.Ln, ['WS', 'CF'], [('T', 0)], bias=CF[:, C_ONE:C_ONE + 1], scale=1.0)

            def w3():
                P.op('dve', lambda e: e.tensor_tensor_scan(out=CSg[:], data0=CB[:, B_RST:B_RST + 512], data1=SP[:],
                                                           initial=0.0, op0=ALU.mult, op1=ALU.add),
                     reads=[('T', 0), 'CB'], writes=[('T', 1)])

            def w4():
                totb = bass.AP(CSg, 127, [[512, 128], [128, 4], [0, 128]])
                cs3 = bass.AP(CSg, 0, [[512, 128], [128, 4], [1, 128]])
                cr3 = bass.AP(CR0, 0, [[512, 128], [128, 4], [1, 128]])
                tt('pool', cr3, totb, cs3, ALU.subtract, [('T', 1)], [('T', 2)])
                tot4 = bass.AP(CSg, 127, [[512, 128], [128, 4]])
                act(WLG[:, z, blk * 4:blk * 4 + 4], tot4, ACT.Exp, [('T', 1)], ['WLG'], scale=-1.0 / 16)
                if z == 1:
                    tt('pool', INC[:], CR0[:], SP[:], ALU.add, [('T', 2), ('T', 0)], [('T', 3)])
                    tt('pool', RST[:], CSg[:], SP[:], ALU.subtract, [('T', 1), ('T', 0)], ['WS'])

            def w5():
                act(EA, inc[:], ACT.Exp, [ri], [('gE', 0)], scale=-1.0 / 16)
                act(EB, inc[:], ACT.Exp, [ri], [('gE', 1)], scale=1.0 / 16)
                act(EC, rst[:], ACT.Exp, [rr_], [('gE', 2)], scale=-1.0 / 16)

            def w6():
                stt(QE, FA[:, tsl], 128 ** -0.5, EA, ALU.mult, ALU.mult, ['FA', ('gE', 0)], [rq])
                tt('dve', KE, FB[:, tsl], EB, ALU.mult, ['FB', ('gE', 1)], [rk])
                tt('pool', KD, FB[:, tsl], EC, ALU.mult, ['FB', ('gE', 2)], [rd])

            def w7():
                for k in range(4):
                    ksl = slice(k * 128, (k + 1) * 128)
                    mm(PS[6][:, ksl], KE[:, ksl], QE[:, ksl], True, True, [rk, rq], [('ps', 6)])
                for k in range(4):
                    mm(PS[4][:, k * 128:(k + 1) * 128], KD[:, k * 128:(k + 1) * 128], IDB[:], True, True, [rd, 'IDB'], [('ps', 4)])

            def w8():
                mg = bass.AP(CB, B_MG + 128 * z, [[NCB, 128], [0, 4], [1, 128]])
                tt('dve', ATT4, PS[6][:].rearrange("p (k t) -> p k t", k=4), mg, ALU.mult, [('ps', 6), 'CB'], [ra_])
                copy('act', KDT, PS[4][:], [('ps', 4)], [rt])

            def w9():
                for k in range(4):
                    ksl = slice(k * 128, (k + 1) * 128)
                    mb, mbn = MB[k // 2]
                    mm(mb[:, (k % 2) * 256:(k % 2 + 1) * 256], KDT[:, ksl], VTG[:, blk * 4 + k, :], True, True,
                       [rt, ('VTG', blk * 4 + k)], [('ps', mbn)])
            return [w0, w1, w2, w3, w4, w5, w6, w7, w8, w9]

        def g_chain(bi, hd=hd):
            z, blk = gblocks[bi]
            sb_ = bi % 2
            QE, KE, KD, KDT, ATT4 = GSET[sb_]
            rq, ra_ = ('gQE', sb_), ('gATT', sb_)
            MB = [(PS[0], 0), (PS[1], 1)] if sb_ == 0 else [(PS[2], 2), (PS[5], 5)]
            G = []

            def ch(k):
                def g():
                    while GDEF:
                        GDEF.pop(0)()
                    c16 = blk * 4 + k
                    ksl = slice(k * 128, (k + 1) * 128)
                    mb, mbn = MB[k // 2]
                    ob, obn = (PS[7], 7) if k % 2 == 0 else (PS[3], 3)
                    mm(ob[:, 0:256], ATT4[:, k, :], VTG[:, c16, :], True, False, [ra_, ('VTG', c16)], [('ps', obn)])
                    mm(ob[:, 0:256], QE[:, ksl], SBg, False, True, [rq, 'SBg'], [('ps', obn)])
                    tmpg = GS[:, 1 + (c16 % 2), :]
                    tr = ('TMPg', c16 % 2)
                    stt(tmpg, SFg, WLG[:, z, c16:c16 + 1], mb[:, (k % 2) * 256:(k % 2 + 1) * 256], ALU.mult, ALU.add,
                        ['SFg', 'WLG', ('ps', mbn)], [tr])
                    ts('dve', SFg, tmpg, MSK[:, 2 + z, c16:c16 + 1], None, ALU.mult, None, [tr, 'MSK'], ['SFg'])
                    act(SBg, tmpg, ACT.Copy, [tr, 'MSK'], ['SBg'], scale=MSK[:, 2 + z, c16:c16 + 1])
                    if (c16 % 2 == 1) == (z == 0):
                        P.dma('sp', lambda e, z=z, hd=hd, c16=c16, tmpg=tmpg: e.dma_start(out=o_sg[c16 // 2, z, hd], in_=tmpg), reads=[tr])
                    obf = BA[2 + c16 // 8][:, (c16 % 8) * 256:(c16 % 8 + 1) * 256]
                    obr = 'BA2' if c16 < 8 else 'BA3'
                    if z == 0:
                        copy('act', obf, ob[:, 0:256], [('ps', obn)], [obr])
                    else:
                        tog, sqg = GLT[c16 % 2][:, 0:256], GLT[c16 % 2][:, 256:512]
                        gr = ('GLT', c16 % 2)
                        tt('dve', tog, obf, ob[:, 0:256], ALU.add, [('ps', obn), obr], [gr])
                        tt('dve', sqg, tog, tog, ALU.mult, [gr], [gr])
                        P.op('dve', lambda e, sqg=sqg, c16=c16: e.tensor_reduce(out=RSG[:, c16:c16 + 1], in_=sqg, axis=AX.X, op=ALU.add),
                             reads=[gr], writes=[('RSG', c16)])

                        def tail(c16=c16, tog=tog, gr=gr):
                            act(RSG[:, c16:c16 + 1], RSG[:, c16:c16 + 1], ACT.Ln, [('RSG', c16), 'EPS'], [('RSG', c16)], bias=EPS[:, 0:1], scale=1.0 / 256)
                            act(RSG[:, c16:c16 + 1], RSG[:, c16:c16 + 1], ACT.Exp, [('RSG', c16)], [('RSG', c16)], scale=-0.5)
                            act(VTG[:, c16, :], tog, ACT.Copy, [gr, ('RSG', c16)], [('VTG', c16)], scale=RSG[:, c16:c16 + 1])
                        GDEF.append(tail)
                return g
            for k in (range(4) if z == 0 else range(3, -1, -1)):
                G.append(ch(k))
            return G

        p0_ = g_prep(0)
        interleave([vgr, [g_init(0)] + p0_[:9]])
        p0_[9]()
        for bi in range(8):
            cg = g_chain(bi)
            if bi == 4:
                cg = [g_init(1)] + cg
            lists = [cg]
            if bi + 1 < 8:
                lists.append(g_prep(bi + 1))
            interleave(lists)
        while GDEF:
            GDEF.pop(0)()
        def half_groups(half, hd=hd):
            slot, SZ, nSZ, TF, nTF, OBh, nOB, pT, nT = ((0, BA[2], 'BA2', FB, 'FB', BA[3], 'BA3', PS[1], 1) if half == 0 else
                                                         (1, BA[0], 'BA0', FA, 'FA', BA[1], 'BA1', PS[0], 0))
            G = []

            def ga():
                load_w(d_owin, 2048 + hd * 256 + half * 128, slot)
                proj(slot, lambda t4, tsl, ps, pr: act(SZ[:, tsl], ps, ACT.Silu, [pr], [nSZ]))
            G.append(ga)

            def gt(t4):
                def f():
                    tsl = slice(t4 * 512, (t4 + 1) * 512)
                    for k in range(4):
                        mm(pT[:, k * 128:(k + 1) * 128], VTG[:, t4 * 4 + k, half * 128:(half + 1) * 128], IDB[:], True, True,
                           [('VTG', t4 * 4 + k), 'IDB'], [('ps', nT)])
                    ts('dve', TF[:, tsl], pT[:], VEC[:, V_GN + half:V_GN + half + 1], None, ALU.mult, None, [('ps', nT), 'VEC'], [nTF])
                return f
            for t4 in range(4):
                G.append(gt(t4))

            def gm():
                tt('pool', OBh[:], TF[:], SZ[:], ALU.mult, [nTF, nSZ], [nOB])
            G.append(gm)
            G.append(lambda: wout_partial(1, d_owout, hd * 2 + half, OBh, nOB))
            return G
        interleave([half_groups(0), half_groups(1)])

    ftsl = [slice(t4 * 512, (t4 + 1) * 512) for t4 in range(4)]
    sumsq_rstd(ftsl[0], 0)
    for t4 in range(4):
        tsl = ftsl[t4]
        if t4 + 1 < 4:
            sumsq_rstd(ftsl[t4 + 1], (t4 + 1) % 2)
        rb_, rn_ = RSTD[t4 % 2]
        for c in range(8):
            stt(FA[:, (c % 4) * 512:(c % 4 + 1) * 512], X[:, c, tsl], VEC[:, V_FG + c:V_FG + c + 1], rb_[:],
                ALU.mult, ALU.mult, [('X', c), 'VEC', rn_], [('FAq', c % 4)])
            P.dma('sp', lambda e, c=c, tsl=tsl: e.dma_start(out=o_y[:, c, tsl], in_=FA[:, (c % 4) * 512:(c % 4 + 1) * 512]),
                  reads=[('FAq', c % 4)])
    P.finish_waits('sp')
    P.emit()
    global _LAST_P
    _LAST_P = P
    st.close()
    return nc


def kernel(**inp):
    f = lambda k: np.asarray(inp[k], np.float32)
    plan = _assign()
    cf, cb = _consts()
    vec0 = np.zeros((128, NV), np.float32)
    vec0[:, V_NG:V_NG + 16] = np.concatenate([_col(f('norm_g')[0]), _col(f('norm_g')[1])], 1)
    vec0[:, V_FG:V_FG + 8] = _col(f('final_g'))
    cw = f('conv_w')[0]
    for k in range(3):
        vec0[:, V_CW + 8 * k:V_CW + 8 * k + 8] = _col(cw[k])
    for z in range(2):
        vec0[:, V_W0 + 8 * z:V_W0 + 8 * z + 8] = _col(f('wkv_w0')[0, z])
        vec0[:, V_A0 + 8 * z:V_A0 + 8 * z + 8] = _col(f('wkv_a0')[0, z])
        vec0[:, V_GB + 4 * z:V_GB + 4 * z + 4] = _col(f('gla_gk_b')[0, z])
    vec0[:, V_KK:V_KK + 8] = _col(f('wkv_k_k')[0])
    vec0[:, V_KA:V_KA + 8] = _col(f('wkv_k_a')[0])
    vec0[:, V_RK:V_RK + 8] = _col(f('wkv_r_k')[0].reshape(-1))
    vec0[:, V_LW:V_LW + 8] = _col(f('wkv_ln_w')[0])
    vec0[:, V_LB:V_LB + 8] = _col(f('wkv_ln_b')[0])
    for l in range(2):
        vec0[:, V_AB + 24 * l:V_AB + 24 * l + 24] = _col(f('ada_b')[l])
    vec0[:, V_GN:V_GN + 2] = _col(f('gla_g_norm')[0])
    r3 = lambda w: np.ascontiguousarray(w.reshape(-1, 128, w.shape[-1]).transpose(1, 0, 2))
    ada = np.stack([r3(f('ada_w')[l]) for l in range(2)])
    ewin = r3(f('e_w_in')[0])
    ewout = r3(f('e_w_out')[0])
    owin = r3(f('o_w_in')[0])
    owout = r3(f('o_w_out')[0])
    w1c = np.concatenate([r3(f('wkv_w1')[0, 0]), r3(f('wkv_w1')[0, 1]), r3(f('wkv_a1')[0, 0]), r3(f('wkv_a1')[0, 1])], 2)
    w2p = np.zeros((128, 4, 1024), np.float32)
    for z in range(2):
        w2p[64 * z:64 * z + 64, z] = f('wkv_w2')[0, z]
        w2p[64 * z:64 * z + 64, 2 + z] = f('wkv_a2')[0, z]
    g1c = np.concatenate([r3(f('gla_gk1')[0, 0]), r3(f('gla_gk1')[0, 1])], 2)
    g2p = np.zeros((32, 2, 512), np.float32)
    for z in range(2):
        g2p[16 * z:16 * z + 16, z] = f('gla_gk2')[0, z]
    xp, xs = f('x_prompt'), f('x_sample')
    swkv, sgla = f('state_wkv'), f('state_gla')
    in_maps = []
    for core, items in enumerate(plan):
        x = np.zeros((NT, 1024), np.float32)
        msk = np.zeros((128, 4, 32), np.float32)
        s_w = np.zeros((128, 8, 2, 128), np.float32)
        s_g = np.zeros((128, 4, 2, 256), np.float32)
        vec = vec0.copy()
        if items[0][0] == 's':
            b = items[0][1]
            x[:] = xs[b]
            cv = f('c')[b]
            msk[:, 2, :16] = 1.0
            msk[:, 3, :16] = 1.0
            for z in range(2):
                for h in range(16):
                    jj, hl = divmod(h, 2)
                    s_w[64 * hl:64 * hl + 64, jj, z, 64 * hl:64 * hl + 64] = swkv[b, 0, z, h].T
                for h in range(4):
                    s_g[:, h, z, :] = sgla[b, 0, z, h]
        else:
            for si in range(8):
                x[256 * si:256 * si + 256] = xp[items[si % len(items)][1]]
            cv = f('c_ctx')
            g = np.arange(32)
            inner = (g % 4 != 0).astype(np.float32)
            msk[:, 0, :] = inner[None]
            msk[:, 1, :] = inner[None]
            c16 = np.arange(16)
            msk[:, 2, :16] = (c16 % 2 == 0)[None]
            msk[:, 3, :16] = (c16 % 2 == 1)[None]
        vec[:, V_CV:V_CV + 8] = _col(cv)
        xT = np.ascontiguousarray(x.reshape(NT, 8, 128).transpose(2, 1, 0))
        in_maps.append(dict(xT=xT, vec=vec, cf=cf, cb=cb.astype(ml_dtypes.bfloat16), msk=msk, ada=ada, ewin=ewin, ewout=ewout, w1c=w1c,
                            w2p=w2p, s_wkv=s_w, owin=owin, owout=owout, g1c=g1c, g2p=g2p, s_gla=s_g))
    nc = build()
    res = run_bass_kernel_spmd(nc, in_maps, core_ids=list(range(8)))
    y_p = np.zeros((16, 256, 1024), np.float32)
    y_s = np.zeros((2, 2048, 1024), np.float32)
    n_w = np.zeros((16, 1, 2, 16, 64, 64), np.float32)
    n_g = np.zeros((16, 1, 2, 4, 128, 256), np.float32)
    for core, items in enumerate(plan):
        r = res.results[core]
        y = np.asarray(r["yT"]).transpose(2, 1, 0).reshape(NT, 1024)
        ow = np.asarray(r["o_wkv"])
        og = np.asarray(r["o_gla"])
        if items[0][0] == 's':
            y_s[items[0][1]] = y
        else:
            for si, (_, pi) in enumerate(items):
                y_p[pi] = y[256 * si:256 * si + 256]
                for z in range(2):
                    for h in range(16):
                        jj, hl = divmod(h, 2)
                        n_w[pi, 0, z, h] = ow[si, z, jj, 64 * hl:64 * hl + 64, 64 * hl:64 * hl + 64].T
                    n_g[pi, 0, z] = og[si, z]
    return (y_p, y_s, n_w, n_g)
```

```python
import contextlib
import numpy as np
import ml_dtypes
import concourse.bass as bass
import concourse.mybir as mybir
from concourse.bass_utils import run_bass_kernel_spmd

ACT = mybir.ActivationFunctionType
ALU = mybir.AluOpType
F32 = mybir.dt.float32
BF16 = mybir.dt.bfloat16
AX = mybir.AxisListType

ENGS = ['pe', 'act', 'dve', 'pool', 'sp']
EPOCH = 4000
NDS = 8
NT = 2048
NCH = 16
LAM = 0.6065306597126334
NORM_EPS = 1e-6
GN_EPS = 64e-5


class Prog:
    def __init__(self, nc):
        self.nc = nc
        self.ops = {e: [] for e in ENGS}
        self.count = {e: 0 for e in ENGS}
        self.dcount = {e: 0 for e in ENGS}
        self.last_w = {}
        self.readers = {}
        self.waited = {e: {} for e in ENGS}
        self.pending = {e: [] for e in ENGS}

    def _deps(self, eng, reads, writes):
        deps = set()
        for r in reads:
            if r in self.last_w:
                deps.add(self.last_w[r])
        for w in writes:
            if w in self.last_w:
                deps.add(self.last_w[w])
            for rd in self.readers.get(w, ()):
                deps.add(rd)
        best = {}
        for d in deps:
            if eng == 'pe' and d[:-1] == ('e', 'pe'):
                continue
            best[d[:-1]] = max(best.get(d[:-1], 0), d[-1])
        for d in self.pending[eng]:
            best[d[:-1]] = max(best.get(d[:-1], 0), d[-1])
        self.pending[eng] = []
        final = []
        for key, i in best.items():
            if self.waited[eng].get(key, 0) < i:
                self.waited[eng][key] = i
                final.append(key + (i,))
        return final

    def _mark(self, tok, reads, writes):
        for r in reads:
            self.readers.setdefault(r, []).append(tok)
        for w in writes:
            self.last_w[w] = tok
            self.readers[w] = []

    def op(self, eng, fn, reads=(), writes=()):
        writes = list(writes) + [r for r in reads if isinstance(r, tuple) and r[0] == 'ps']
        waits = self._deps(eng, reads, writes)
        idx = self.count[eng] + 1
        self.count[eng] = idx
        self.ops[eng].append(('c', fn, waits, idx))
        self._mark(('e', eng, idx), reads, writes)

    def dma(self, eng, fn, reads=(), writes=()):
        waits = self._deps(eng, reads, writes)
        j = self.dcount[eng]
        self.dcount[eng] = j + 1
        slot = j % NDS
        if j >= NDS:
            key = ('d', eng, slot)
            need = j // NDS
            if self.waited[eng].get(key, 0) < need:
                self.waited[eng][key] = need
                waits.append(key + (need,))
        self.ops[eng].append(('d', fn, waits, (slot, j // NDS + 1)))
        self._mark(('d', eng, slot, j // NDS + 1), reads, writes)

    def barrier(self):
        snap = [('e', e, self.count[e]) for e in ENGS if self.count[e]]
        for q in ENGS:
            n = self.dcount[q]
            for slot in range(min(n, NDS)):
                snap.append(('d', q, slot, (n - 1 - slot) // NDS + 1))
        for e in ENGS:
            self.pending[e] = list(snap)

    def finish_waits(self, eng='sp'):
        waits = []
        for q in ENGS:
            n = self.dcount[q]
            for slot in range(min(n, NDS)):
                waits.append(('d', q, slot, (n - 1 - slot) // NDS + 1))
        self.ops[eng].append(('w', None, waits, None))

    def emit(self):
        nc = self.nc
        with contextlib.ExitStack() as st:
            esem = {e: [st.enter_context(nc.semaphore(f"s_{e}_{k}")) for k in range(self.count[e] // EPOCH + 1)]
                    for e in ENGS}
            dsem = {e: [st.enter_context(nc.semaphore(f"d_{e}_{k}")) for k in range(NDS)]
                    for e in ENGS if self.dcount[e]}
            block = st.enter_context(nc.Block())

            def run(handle, e):
                for kind, fn, waits, info in self.ops[e]:
                    for w in waits:
                        if w[0] == 'e':
                            handle.wait_ge(esem[w[1]][(w[2] - 1) // EPOCH], (w[2] - 1) % EPOCH + 1)
                        else:
                            handle.wait_ge(dsem[w[1]][w[2]], 16 * w[3])
                    if kind == 'c':
                        fn(handle).then_inc(esem[e][(info - 1) // EPOCH], 1)
                    elif kind == 'd':
                        fn(handle).then_inc(dsem[e][info[0]], 16)

            @block.tensor
            def _(h):
                run(h, 'pe')

            @block.scalar
            def _(h):
                run(h, 'act')

            @block.vector
            def _(h):
                run(h, 'dve')

            @block.gpsimd
            def _(h):
                run(h, 'pool')

            @block.sync
            def _(h):
                run(h, 'sp')


V_NG, V_FG, V_CW, V_W0, V_A0, V_KK, V_KA, V_RK, V_LW, V_LB, V_AB, V_GB, V_GN, V_CV = \
    0, 16, 24, 48, 64, 80, 88, 96, 104, 112, 120, 168, 176, 178
NV = 186
C_ID, C_BD, C_HM, C_ONE = 0, 128, 256, 258
NCF = 386
B_RST, B_MAB, B_MN, B_MG = 0, 512, 1536, 2048
NCB = 2304


def _col(v):
    v = np.asarray(v, np.float32).reshape(-1, 128)
    return np.ascontiguousarray(v.T)


def _consts():
    u = np.arange(128)[:, None]
    t = np.arange(128)[None, :]
    LT, LE, GT, GE = (u < t), (u <= t), (u > t), (u >= t)
    cf = np.zeros((128, NCF), np.float32)
    cf[:, C_ID:C_ID + 128] = np.eye(128)
    cf[:, C_BD:C_BD + 128] = (u // 64 == t // 64)
    cf[:, C_HM] = (np.arange(128) < 64)
    cf[:, C_HM + 1] = (np.arange(128) >= 64)
    cf[:, C_ONE:C_ONE + 128] = 1.0
    cb = np.zeros((128, NCB), np.float32)
    rst = np.ones(512, np.float32)
    rst[::128] = 0
    cb[:, B_RST:B_RST + 512] = rst[None]
    cb[:, B_MAB:B_MAB + 512] = np.concatenate([LT, LE, LT, LE], 1)
    cb[:, B_MAB + 512:B_MAB + 1024] = np.concatenate([GT, GE, GT, GE], 1)
    cb[:, B_MN:B_MN + 256] = np.concatenate([GT, GT], 1)
    cb[:, B_MN + 256:B_MN + 512] = np.concatenate([LT, LT], 1)
    cb[:, B_MG:B_MG + 128] = LE
    cb[:, B_MG + 128:B_MG + 256] = GE
    return cf, cb


def _assign():
    plan = [[('s', 0)], [('s', 1)]]
    p = 0
    for n in (3, 3, 3, 3, 2, 2):
        plan.append([('p', p + i) for i in range(n)])
        p += n
    return plan


def build(stop_after=99):
    nc = bass.Bass("TRN2", target_bir_lowering=False)
    dt_in = lambda n, s: nc.dram_tensor(n, s, F32, kind="ExternalInput").ap()
    dt_out = lambda n, s: nc.dram_tensor(n, s, F32, kind="ExternalOutput").ap()
    d_x = dt_in("xT", [128, 8, NT])
    d_vec = dt_in("vec", [128, NV])
    d_cf = dt_in("cf", [128, NCF])
    d_cb = nc.dram_tensor("cb", [128, NCB], BF16, kind="ExternalInput").ap()
    d_msk = dt_in("msk", [128, 4, 32])
    d_ada = dt_in("ada", [2, 128, 8, 3072])
    d_ewin = dt_in("ewin", [128, 8, 8192])
    d_ewout = dt_in("ewout", [128, 16, 1024])
    d_w1 = dt_in("w1c", [128, 8, 256])
    d_w2 = dt_in("w2p", [128, 4, 1024])
    d_sw = dt_in("s_wkv", [128, 8, 2, 128])
    d_owin = dt_in("owin", [128, 8, 3072])
    d_owout = dt_in("owout", [128, 8, 1024])
    d_g1 = dt_in("g1c", [128, 8, 32])
    d_g2 = dt_in("g2p", [32, 2, 512])
    d_sg = dt_in("s_gla", [128, 4, 2, 256])
    o_y = dt_out("yT", [128, 8, NT])
    o_sw = dt_out("o_wkv", [8, 2, 8, 128, 128])
    o_sg = dt_out("o_gla", [8, 2, 4, 128, 256])

    st = contextlib.ExitStack()
    sb = lambda n, s, d=F32: st.enter_context(nc.sbuf_tensor(n, s, d))
    X = sb("X", [128, 8, NT])
    HT = sb("HT", [128, 8, NT], BF16)
    VEC = sb("VEC", [128, NV])
    CF = sb("CF", [128, NCF])
    CB = sb("CB", [128, NCB], BF16)
    IDB = sb("IDB", [128, 128], BF16)
    ONEB = sb("ONEB", [128, 128], BF16)
    BDB = sb("BDB", [128, 128], BF16)
    MSK = sb("MSK", [128, 4, 32])
    MOD = sb("MOD", [128, 2, 24])
    G1 = sb("G1", [128, 2, 8])
    CS_ = sb("CSIL", [128, 8])
    EPS = sb("EPS", [128, 2])
    FA = sb("FA", [128, NT])
    FB = sb("FB", [128, NT])
    BA = [sb(f"BA{i}", [128, NT], BF16) for i in range(4)]
    WRAW = sb("WRAW", [128, 3072])
    WS = WRAW[:, 0:1024].rearrange("p (a b) -> p a b", a=8)
    WB = WRAW[:, 1024:3072].bitcast(BF16).rearrange("p (a b c) -> p a b c", a=4, b=8)
    WOB = WB[:, 3, :, :].rearrange("p a b -> p (a b)")
    UNI = sb("UNI", [128, 8960])
    ub = lambda a, b, p=128: UNI[0:p, a:b].bitcast(BF16)
    T512 = [sb(f"T512_{i}", [128, 512]) for i in range(4)] + [WRAW[:, 512 * i:512 * (i + 1)] for i in range(6)]
    G1B = ub(1024, 1152).rearrange("p (a b) -> p a b", a=8)
    G2B = ub(1152, 1664, 32).rearrange("p (a b) -> p a b", a=2)
    GT1 = ub(0, 1024, 32)
    VTG = sb("VTG", [128, 16, 256], BF16)
    KKF = sb("KKF", [128, NT], BF16)
    GS = sb("GS", [128, 3, 256])
    WLG = sb("WLG", [128, 2, 16])
    NGB = sb("NGB", [128, 8])
    RSG = sb("RSG", [128, 16])
    PRB = ub(0, 2560)
    CHB = ub(2560, 6080)
    BKT = ub(6080, 7104).rearrange("p (a b) -> p a b", a=4)
    PRS = ub(7104, 8128)
    W2S = WRAW[:, 0:512].rearrange("p (a b) -> p a b", a=4)
    W2B = ub(8128, 8384).rearrange("p (a b) -> p a b", a=4)
    W_XAM = 8384
    WLW = sb("WLW", [128, 2, 16])
    OMKA = sb("OMKA", [128, 8])
    GNS = sb("GNS", [128, 8])
    WOT = [T512[2], T512[3]]
    PS = [st.enter_context(nc.psum_tensor(f"ps{i}", [128, 512], F32)) for i in range(8)]

    P = Prog(nc)
    cnt = {'rr': 0}

    def rr(engs=('act', 'dve')):
        cnt['rr'] += 1
        return engs[cnt['rr'] % len(engs)]

    def mm(out, lhsT, rhs, start, stop, r, w):
        P.op('pe', lambda e: e.matmul(out, lhsT, rhs, start=start, stop=stop), reads=r, writes=w)

    def copy(eng, out, in_, r, w):
        if eng == 'act':
            P.op('act', lambda e: e.activation(out=out, in_=in_, func=ACT.Copy), reads=r, writes=w)
        else:
            P.op(eng, lambda e: e.tensor_copy(out=out, in_=in_), reads=r, writes=w)

    def tt(eng, out, a, b, op, r, w):
        P.op(eng, lambda e: e.tensor_tensor(out=out, in0=a, in1=b, op=op), reads=r, writes=w)

    def ts(eng, out, a, s1, s2, op0, op1, r, w):
        if s2 is None:
            P.op(eng, lambda e: e.tensor_scalar(out=out, in0=a, scalar1=s1, scalar2=None, op0=op0), reads=r, writes=w)
        else:
            P.op(eng, lambda e: e.tensor_scalar(out=out, in0=a, scalar1=s1, scalar2=s2, op0=op0, op1=op1), reads=r, writes=w)

    def stt(out, a, s, b, op0, op1, r, w):
        P.op('dve', lambda e: e.scalar_tensor_tensor(out=out, in0=a, scalar=s, in1=b, op0=op0, op1=op1), reads=r, writes=w)

    def act(out, in_, func, r, w, bias=None, scale=None):
        kw = {}
        if bias is not None:
            kw['bias'] = bias
        if scale is not None:
            kw['scale'] = scale
        P.op('act', lambda e: e.activation(out=out, in_=in_, func=func, **kw), reads=r, writes=w)

    def ld(out, in_, w, r=()):
        P.dma('sp', lambda e: e.dma_start(out=out, in_=in_), reads=r, writes=w)

    for c in range(8):
        ld(X[:, c, :], d_x[:, c, :], [('X', c)])
    ld(VEC[:], d_vec, ['VEC'])
    ld(CF[:], d_cf, ['CF'])
    ld(CB[:], d_cb, ['CB'])
    ld(MSK[:], d_msk, ['MSK'])
    copy('pool', IDB[:], CF[:, C_ID:C_ID + 128], ['CF'], ['IDB'])
    copy('pool', ONEB[:], CF[:, C_ONE:C_ONE + 128], ['CF'], ['ONEB'])
    copy('pool', BDB[:], CF[:, C_BD:C_BD + 128], ['CF'], ['BDB'])
    P.op('pool', lambda e: e.memset(EPS[:, 0:1], NORM_EPS), writes=['EPS'])
    P.op('pool', lambda e: e.memset(EPS[:, 1:2], GN_EPS), writes=['EPS'])
    act(CS_[:], VEC[:, V_CV:V_CV + 8], ACT.Silu, ['VEC'], ['CSIL'])
    def ada_layer(l, ACCQ, an, STG, sn):
        steps = []
        for c in range(8):
            for q in range(3):
                def st_(c=c, q=q, i=len(steps)):
                    sg, sr = STG[i % 4], (sn, i % 4)
                    ld(sg, d_ada[l, :, c, q * 1024:(q + 1) * 1024], [sr])
                    if c == 0:
                        ts('dve', ACCQ[q], sg, CS_[:, c:c + 1], None, ALU.mult, None, [sr, 'CSIL'], [(an, q)])
                    else:
                        stt(ACCQ[q], sg, CS_[:, c:c + 1], ACCQ[q], ALU.mult, ALU.add, [sr, 'CSIL', (an, q)], [(an, q)])
                steps.append(st_)

        def fin():
            for j in range(24):
                mm(PS[0][:, j:j + 1], ACCQ[j // 8][:, (j % 8) * 128:(j % 8 + 1) * 128], CF[:, C_ONE:C_ONE + 1],
                   True, True, [(an, j // 8), 'CF'], [('ps', 0)])
            tt('dve', MOD[:, l, :], PS[0][:, 0:24], VEC[:, V_AB + 24 * l:V_AB + 24 * l + 24], ALU.add,
               [('ps', 0), 'VEC'], ['MOD'])
            ts('dve', G1[:, l, :], MOD[:, l, 8:16], 1.0, None, ALU.add, None, ['MOD'], ['G1'])
            tt('dve', G1[:, l, :], G1[:, l, :], VEC[:, V_NG + 8 * l:V_NG + 8 * l + 8], ALU.mult, ['G1', 'VEC'], ['G1'])
        return steps, fin

    st0, fin0 = ada_layer(0, [FA[:, 0:1024], FA[:, 1024:2048], FB[:, 1024:2048]], 'ACC',
                          [FB[:, 0:1024], WRAW[:, 0:1024], WRAW[:, 1024:2048], WRAW[:, 2048:3072]], 'STG')
    for f_ in st0:
        f_()
    fin0()
    ada1_steps, ada1_fin = ada_layer(1, [UNI[:, 1024 * i:1024 * (i + 1)] for i in range(3)], 'uACC',
                                     [UNI[:, 3072 + 1024 * i:4096 + 1024 * i] for i in range(4)], 'uSTG')

    XR = [('X', c) for c in range(8)]
    HR = [('HT', c) for c in range(8)]

    SQB = [WRAW[:, 0:256].bitcast(BF16), WRAW[:, 1024:1280].bitcast(BF16)]
    SQR = ['WS', ('WB', 0)]

    RSTD = [(T512[2], ('T', 2)), (T512[3], ('T', 3))]

    def sumsq_rstd(tsl, ri=0):
        rb_, rn_ = RSTD[ri]
        for c in range(8):
            act(SQB[c % 2], X[:, c, tsl], ACT.Square, [('X', c)], [SQR[c % 2]])
            mm(PS[1][:], ONEB[:], SQB[c % 2], c == 0, c == 7, [SQR[c % 2], 'ONEB'], [('ps', 1)])
        act(rb_[:], PS[1][:], ACT.Sqrt, [('ps', 1), 'EPS'], [rn_], bias=EPS[:, 0:1], scale=1.0 / 1024)
        P.op('dve', lambda e: e.reciprocal(out=rb_[:], in_=rb_[:]), reads=[rn_], writes=[rn_])

    def norm_mod(gfn, sfn, out_fn, out_res):
        tsls = [slice(t4 * 512, (t4 + 1) * 512) for t4 in range(4)]
        sumsq_rstd(tsls[0], 0)
        for t4 in range(4):
            tsl = tsls[t4]
            if t4 + 1 < 4:
                sumsq_rstd(tsls[t4 + 1], (t4 + 1) % 2)
            rb_, rn_ = RSTD[t4 % 2]
            for c in range(8):
                tmp = T512[c % 2]
                o = out_fn(c, tsl)
                if c % 2 == 0:
                    tt('pool', tmp[:], X[:, c, tsl], rb_[:], ALU.mult, [('X', c), rn_], [('T', 0)])
                    ts('dve', o, tmp[:], gfn(c), sfn(c), ALU.mult, ALU.add, [('T', 0), 'VEC', 'G1', 'MOD'], out_res(c, t4))
                else:
                    tt('dve', tmp[:], X[:, c, tsl], rb_[:], ALU.mult, [('X', c), rn_], [('T', 1)])
                    act(o, tmp[:], ACT.Identity, [('T', 1), 'VEC', 'G1', 'MOD'], out_res(c, t4), bias=sfn(c), scale=gfn(c))

    def load_w(dram, col0, br):
        ld(WS[:], dram[:, :, col0:col0 + 128], ['WS'])
        copy(rr(('act', 'dve')), WB[:, br, :, :], WS[:], ['WS'], [('WB', br)])

    def proj(br, evac, banks=(2, 3, 4, 5)):
        for t4 in range(4):
            tsl = slice(t4 * 512, (t4 + 1) * 512)
            pb = banks[cnt['rr'] % len(banks)]
            cnt['rr'] += 1
            for c in range(8):
                mm(PS[pb][:], WB[:, br, c, :], HT[:, c, tsl], c == 0, c == 7, [('WB', br), ('HT', c)], [('ps', pb)])
            evac(t4, tsl, PS[pb][:], ('ps', pb))

    def wout_partial(l, dram_wout, j, OB, ores):
        ld(WS[:].rearrange("p a b -> p (a b)"), dram_wout[:, j, :], ['WS'])
        copy('pool', WOB[:], WS[:].rearrange("p a b -> p (a b)"), ['WS'], [('WB', 3)])
        for ft in range(8):
            for t4 in range(4):
                tsl = slice(t4 * 512, (t4 + 1) * 512)
                pb = 2 + (cnt['rr'] % 4)
                cnt['rr'] += 1
                mm(PS[pb][:], WOB[:, ft * 128:(ft + 1) * 128], OB[:, tsl], True, True, [('WB', 3), ores], [('ps', pb)])
                if (ft * 4 + t4) % 5 < 3:
                    stt(X[:, ft, tsl], PS[pb][:], MOD[:, l, 16 + ft:17 + ft], X[:, ft, tsl], ALU.mult, ALU.add,
                        [('ps', pb), 'MOD', ('X', ft)], [('X', ft)])
                else:
                    wt = WOT[t4 % 2]
                    act(wt[:], PS[pb][:], ACT.Copy, [('ps', pb), 'MOD'], [('T', 2 + t4 % 2)], scale=MOD[:, l, 16 + ft:17 + ft])
                    tt('pool', X[:, ft, tsl], X[:, ft, tsl], wt[:], ALU.add, [('T', 2 + t4 % 2), ('X', ft)], [('X', ft)])

    P.barrier()
    norm_mod(lambda c: G1[:, 0, c:c + 1], lambda c: MOD[:, 0, c:c + 1], lambda c, tsl: HT[:, c, tsl],
             lambda c, t4: [('HT', c)])

    def v3(t, a, b):
        return t[:].rearrange("p (g w) -> p g w", w=64)[:, a, b]

    for j in range(8):
        for br in range(4):
            load_w(d_ewin, br * 1024 + j * 128, br)
        for f_ in ada1_steps[3 * j:3 * j + 3]:
            f_()
        U, Pm, Y = FA, FB, FA
        proj(0, lambda t4, tsl, ps, pr: copy(rr(), FA[:, tsl], ps, [pr], ['FA']))
        proj(2, lambda t4, tsl, ps, pr: tt('dve', FB[:, tsl], ps, FA[:, tsl], ALU.mult, [pr, 'FA'], ['FB']))
        proj(1, lambda t4, tsl, ps, pr: copy(rr(), BA[0][:, tsl], ps, [pr], ['BA0']))
        proj(3, lambda t4, tsl, ps, pr: act(BA[1][:, tsl], ps, ACT.Silu, [pr], ['BA1']))
        w0, w1, w2 = (VEC[:, V_CW + 8 * k + j:V_CW + 8 * k + j + 1] for k in range(3))
        act(FA[:], FB[:], ACT.Copy, ['FB', 'VEC'], ['FA'], scale=w1)
        g_all, g_lo, g_hi = slice(0, 32), slice(0, 31), slice(1, 32)
        stt(v3(FA, g_all, slice(1, 64)), v3(FB, g_all, slice(0, 63)), w0, v3(FA, g_all, slice(1, 64)),
            ALU.mult, ALU.add, ['FB', 'FA', 'VEC'], ['FA'])
        stt(v3(FA, g_all, slice(0, 63)), v3(FB, g_all, slice(1, 64)), w2, v3(FA, g_all, slice(0, 63)),
            ALU.mult, ALU.add, ['FB', 'FA', 'VEC'], ['FA'])
        tb = T512[0]
        tt('pool', tb[:, 0:31], v3(FB, g_lo, 63), MSK[:, 0, 1:32], ALU.mult, ['FB', 'MSK'], [('T', 0)])
        stt(v3(FA, g_hi, 0), tb[:, 0:31], w0, v3(FA, g_hi, 0), ALU.mult, ALU.add, [('T', 0), 'FA', 'VEC'], ['FA'])
        tt('pool', tb[:, 32:63], v3(FB, g_hi, 0), MSK[:, 1, 1:32], ALU.mult, ['FB', 'MSK'], [('T', 0)])
        stt(v3(FA, g_lo, 63), tb[:, 32:63], w2, v3(FA, g_lo, 63), ALU.mult, ALU.add, [('T', 0), 'FA', 'VEC'], ['FA'])
        tt('pool', FA[:], FA[:], BA[0][:], ALU.mult, ['FA', 'BA0'], ['FA'])
        tt('dve', BA[2][:], FA[:], BA[1][:], ALU.mult, ['FA', 'BA1'], ['BA2'])
        wout_partial(0, d_ewout, j, BA[2], 'BA2')

    ada1_fin()
    P.barrier()
    load_w(d_w1, 0, 0)
    load_w(d_w1, 128, 1)
    proj(0, lambda t4, tsl, ps, pr: act(BA[2][:, tsl], ps, ACT.Tanh, [pr], ['BA2']))
    proj(1, lambda t4, tsl, ps, pr: copy('act', BA[3][:, tsl], ps, [pr], ['BA3']))
    ts('pool', OMKA[:], VEC[:, V_KA:V_KA + 8], -1.0, 1.0, ALU.mult, ALU.add, ['VEC'], ['OMKA'])
    KKf = KKF[:]
    Rb = FA[:, 0:1024].bitcast(BF16)
    Kb = FA[:, 1024:2048].bitcast(BF16)
    TB = T512
    c3 = lambda ap: ap.rearrange("p (k t) -> p k t", t=128)
    FBb = FB[:].bitcast(BF16)
    PRBs = [PRB, FBb]
    XAm = ub(W_XAM, W_XAM + 512)
    def opnd(par):
        base = PRBs[par]
        d = dict(AR=base[:, 0:1024], Bh=base[:, 1024:1536], Kh=base[:, 1536:2048],
                 Btm=[base[:, 2048:2560], base[:, 2560:3072]], Ktm=[base[:, 3072:3584], base[:, 3584:4096]])
        d['Am'] = [base[:, 4096:4608], base[:, 4608:5120]] if par == 0 else [XAm[:, 0:512], XAm[:, 512:1024]]
        d['AR4'] = d['AR'].rearrange("p (k s t) -> p k s t", s=2, t=128)
        return d
    OPN = [opnd(0), opnd(1)]
    NM0 = [CHB[:, 1024 * i:1024 * i + 512] for i in range(3)]
    NM1 = [CHB[:, 1024 * i + 512:1024 * i + 1024] for i in range(3)]
    lv4 = lambda ap: ap.rearrange("p (h s t) -> p h s t", h=2, s=2)
    LVs = [[lv4(CHB[:, 3072 + 1536 * c + 512 * i:3072 + 1536 * c + 512 * (i + 1)]) for i in range(2)] for c in range(2)]
    PPs = [[CHB[:, 4096 + 1536 * c + 256 * i:4096 + 1536 * c + 256 * (i + 1)] for i in range(2)] for c in range(2)]
    TTf = [CHB[:, 6144:6400], CHB[:, 6400:6656]]
    Z0B, UB, SBw = CHB[:, 6656:6784], CHB[:, 6784:6912], CHB[:, 6912:7040]
    BKTs = [BKT[:, 0:2, :], BKT[:, 2:4, :]]
    SFw = GS[:, 0, 0:128]
    BDm = CF[:, C_BD:C_BD + 128]
    HM = [CF[:, C_HM:C_HM + 1], CF[:, C_HM + 1:C_HM + 2]]
    hs = lambda h: slice(h * 64, (h + 1) * 64)

    def interleave(lists):
        lists = [l for l in lists if l]
        pos = [0] * len(lists)
        n = max(len(l) for l in lists) if lists else 0
        for step in range(n):
            for li, l in enumerate(lists):
                tgt = (step + 1) * len(l) // n
                while pos[li] < tgt:
                    l[pos[li]]()
                    pos[li] += 1

    KT = [UNI[:, 512 * i:512 * (i + 1)] for i in range(3)]
    ET = [FB[:, 512 * i:512 * (i + 1)] for i in range(2)]

    def start_rk(jj, banks=(2, 3, 4, 5), kb=1):
        vcol = lambda base: VEC[:, base + jj:base + jj + 1]
        G = []

        def g_load():
            load_w(d_ewin, 4096 + 0 * 1024 + jj * 128, 0)
            load_w(d_ewin, 4096 + 1 * 1024 + jj * 128, 1)
            ld(W2S[:], d_w2[:, :, jj * 128:(jj + 1) * 128], ['WS'])
            copy('pool', W2B[:], W2S[:], ['WS'], ['W2B'])
        G.append(g_load)
        G.append(lambda: proj(0, lambda t4, tsl, ps, pr: copy(rr(), Rb[:, tsl], ps, [pr], ['FA']), banks))
        G.append(lambda: proj(1, lambda t4, tsl, ps, pr: copy(rr(), Kb[:, tsl], ps, [pr], ['FA']), banks))

        def g_kk(t4):
            def g():
                tsl = slice(t4 * 512, (t4 + 1) * 512)
                o0 = ('OPN', 0)
                sqb = KT[1].bitcast(BF16)[:, 0:512]
                act(KT[0][:], Kb[:, tsl], ACT.Copy, ['FA', 'VEC'], [o0], scale=vcol(V_KK))
                act(sqb, KT[0][:], ACT.Square, [], [o0])
                mm(PS[kb][:], BDB[:], sqb, True, True, [o0, 'BDB'], [('ps', kb)])
                act(KT[2][:], PS[kb][:], ACT.Sqrt, [('ps', kb)], [o0])
                ts('dve', KT[2][:], KT[2][:], 1e-12, None, ALU.max, None, [], [o0])
                P.op('dve', lambda e: e.reciprocal(out=KT[2][:], in_=KT[2][:]), reads=[], writes=[o0])
                tt('pool', KKf[:, tsl], KT[0][:], KT[2][:], ALU.mult, [o0], ['KKf'])
            return g
        for t4 in range(4):
            G.append(g_kk(t4))
        return G

    def start_vz(jj):
        G = []

        def g_l():
            load_w(d_ewin, 4096 + 2 * 1024 + jj * 128, 2)
            load_w(d_ewin, 4096 + 3 * 1024 + jj * 128, 3)
        G.append(g_l)
        G.append(lambda: proj(2, lambda t4, tsl, ps, pr: copy(rr(), BA[0][:, tsl], ps, [pr], ['BA0'])))
        G.append(lambda: proj(3, lambda t4, tsl, ps, pr: act(BA[1][:, tsl], ps, ACT.Silu, [pr], ['BA1'])))

        def g_vt(g):
            def f():
                for k in range(4):
                    mm(PS[4][:, k * 128:(k + 1) * 128], BA[0][:, (4 * g + k) * 128:(4 * g + k + 1) * 128], IDB[:], True, True,
                       ['BA0', 'IDB'], [('ps', 4)])
                copy('act', VTG[:, 4 * g:4 * g + 4, 0:128], c3(PS[4][:]), [('ps', 4)], ['VTG'])
            return f
        for g in range(4):
            G.append(g_vt(g))
        return G

    NRK_AT = {26: [0], 27: [1], 28: [2], 29: [3], 30: [4, 5], 31: [6]}

    def pair_chain(jj, vz, nrk=None):
        vcol = lambda base: VEC[:, base + jj:base + jj + 1]
        if True:
            blocks = [(0, b) for b in range(4)] + [(1, b) for b in range(3, -1, -1)]
            chunks = [(0, b, k) for b in range(4) for k in range(4)] + [(1, b, k) for b in range(3, -1, -1) for k in range(3, -1, -1)]
            mab = lambda z: CB[:, B_MAB + 512 * z:B_MAB + 512 * z + 512]
            mnm = lambda z: CB[:, B_MN + 256 * z:B_MN + 256 * z + 256]

            def init_state(z, jj=jj):
                def g():
                    ld(SFw, d_sw[:, jj, z, :], ['SFw'])
                    copy('pool', SBw, SFw, ['SFw'], ['SBw'])
                return g

            def prep_groups(bi, jj=jj, vcol=vcol):
                z, blk = blocks[bi]
                par = bi % 2
                O_ = OPN[par]
                opr = ('OPN', par)
                bkt = BKTs[par]
                bkr = ('BKT', par)
                tsl = slice(blk * 512, (blk + 1) * 512)
                SIG, CSw, CRw, CSBw, AI, KM, BV = TB[0], TB[1], TB[3], TB[4], TB[9], TB[7], TB[8]
                rAI, rBV = ('WB', 3), ('WB', 2)
                if bi == 0:
                    AI, BV = FB[:, 1024:1536], FB[:, 1536:2048]
                    rAI = rBV = ('OPN', 1)
                E1, E3 = TB[2], TB[6]
                if z == 0:
                    inc, ex, rest = (CSw, ('T', 1)), (SIG, ('T', 0)), (CRw, ('T', 3))
                else:
                    inc, ex, rest = (CSBw, 'WS'), (CRw, ('T', 3)), (SIG, ('T', 0))
                def w0():
                    mm(PS[4][:], W2B[:, z, :], BA[2][:, tsl], True, True, ['W2B', 'BA2'], [('ps', 4)])
                    mm(PS[5][:], W2B[:, 2 + z, :], BA[3][:, tsl], True, True, ['W2B', 'BA3'], [('ps', 5)])

                def w1():
                    act(SIG[:], PS[4][:], ACT.Sigmoid, [('ps', 4), 'VEC'], [('T', 0)], bias=vcol(V_W0 + 8 * z))
                    act(AI[:], PS[5][:], ACT.Sigmoid, [('ps', 5), 'VEC'], [rAI], bias=vcol(V_A0 + 8 * z))

                def w2():
                    P.op('dve', lambda e: e.tensor_tensor_scan(out=CSw[:], data0=CB[:, B_RST:B_RST + 512], data1=SIG[:],
                                                               initial=0.0, op0=ALU.mult, op1=ALU.add),
                         reads=[('T', 0), 'CB'], writes=[('T', 1)])
                    act(E3[:], AI[:], ACT.Identity, [rAI, 'VEC', 'OMKA'], [('WB', 0)], bias=OMKA[:, jj:jj + 1], scale=vcol(V_KA))
                    stt(BV[:], KKf[:, tsl], -1.0, AI[:], ALU.mult, ALU.mult, ['KKf', rAI], [rBV])

                def w3():
                    totb = bass.AP(CSw, 127, [[512, 128], [128, 4], [0, 128]])
                    tt('pool', c3(CRw[:]), totb, c3(CSw[:]), ALU.subtract, [('T', 1)], [('T', 3)])
                    act(WLW[:, z, blk * 4:blk * 4 + 4], bass.AP(CSw, 127, [[512, 128], [128, 4]]), ACT.Exp, [('T', 1)], ['WLW'], scale=-LAM)
                    tt('pool', KM[:], Kb[:, tsl], E3[:], ALU.mult, ['FA', ('WB', 0)], [('WB', 1)])

                def w4():
                    if z == 1:
                        tt('pool', CSBw[:], CRw[:], SIG[:], ALU.add, [('T', 3), ('T', 0)], ['WS'])
                    tt('pool', SIG[:], CSw[:], SIG[:], ALU.subtract, [('T', 1), ('T', 0)], [('T', 0)])
                    if z == 0:
                        tt('pool', PRS[:, tsl], Rb[:, tsl], KM[:], ALU.mult, ['FA', ('WB', 1)], ['PRS'])
                    else:
                        tt('pool', E3[:], Rb[:, tsl], KM[:], ALU.mult, ['FA', ('WB', 1)], [('WB', 0)])
                        tt('pool', PRS[:, tsl], PRS[:, tsl], E3[:], ALU.add, ['PRS', ('WB', 0)], ['PRS'])

                def w5():
                    act(E1[:], ex[0][:], ACT.Exp, [ex[1]], [('T', 2)], scale=-LAM)
                    act(E3[:], inc[0][:], ACT.Exp, [inc[1]], [('WB', 0)], scale=-LAM)

                def w6():
                    tt('pool', O_['AR4'][:, :, 0, :], c3(KKf[:, tsl]), c3(E1[:]), ALU.mult, ['KKf', ('T', 2)], [opr])
                    tt('pool', O_['AR4'][:, :, 1, :], c3(Rb[:, tsl]), c3(E3[:]), ALU.mult, ['FA', ('WB', 0)], [opr])

                def w7():
                    act(E1[:], rest[0][:], ACT.Exp, [rest[1]], [('T', 2)], scale=-LAM)
                    act(E3[:], inc[0][:], ACT.Exp, [inc[1]], [('WB', 0)], scale=LAM)
                    for h in range(2):
                        act(c3(O_['Am'][h]), O_['AR4'][:, :, 0, :], ACT.Copy, [opr, 'CF'], [opr], scale=HM[h])

                def w8():
                    for h in range(2):
                        stt(O_['Btm'][h], BV[:], HM[h], E3[:], ALU.mult, ALU.mult, [rBV, 'CF', ('WB', 0)], [opr])
                        stt(O_['Ktm'][h], KM[:], HM[h], E3[:], ALU.mult, ALU.mult, [('WB', 1), 'CF', ('WB', 0)], [opr])
                    tt('pool', O_['Bh'], BV[:], E1[:], ALU.mult, [rBV, ('T', 2)], [opr])
                    tt('pool', O_['Kh'], KM[:], E1[:], ALU.mult, [('WB', 1), ('T', 2)], [opr])

                def w9():
                    for k in range(4):
                        mm(PS[4][:, k * 128:(k + 1) * 128], O_['Bh'][:, k * 128:(k + 1) * 128], IDB[:], True, True, [opr, 'IDB'], [('ps', 4)])

                def w10():
                    copy('act', bkt[:, 0, :], PS[4][:], [('ps', 4)], [bkr])
                    for k in range(4):
                        mm(PS[5][:, k * 128:(k + 1) * 128], O_['Kh'][:, k * 128:(k + 1) * 128], IDB[:], True, True, [opr, 'IDB'], [('ps', 5)])

                def w11():
                    copy('act', bkt[:, 1, :], PS[5][:], [('ps', 5)], [bkr])
                return [w0, w1, w2, w3, w4, w5, w6, w7, w8, w9, w10, w11]

            def a_groups(ci):
                bi, (z, blk, k) = ci // 4, chunks[ci]
                MAB, MNm = mab(z), mnm(z)
                par, q, m, cx = bi % 2, ci % 2, ci % 3, ci % 2
                O_ = OPN[par]
                opr = ('OPN', par)
                ksl = slice(k * 128, (k + 1) * 128)
                ARk = O_['AR'][:, k * 256:(k + 1) * 256]
                nm0, nm1 = NM0[m], NM1[m]
                LV, PPp = LVs[cx], PPs[cx]
                na, nb = (6, 7) if cx == 0 else (0, 1)
                PA_, PB_ = PS[na], PS[nb]
                ra, rb = ('ps', na), ('ps', nb)
                lvp = lambda i: ('LVp', cx, i)
                lvt = lambda i: ('LVt', cx, i)
                ppr = lambda i: ('PPp', cx, i)
                G = []

                def g0():
                    for h in range(2):
                        mm(PA_[:, h * 256:(h + 1) * 256], O_['Btm'][h][:, ksl], ARk, True, True, [opr], [ra])
                        mm(PB_[:, h * 256:(h + 1) * 256], O_['Ktm'][h][:, ksl], ARk, True, True, [opr], [rb])
                        mm(PS[3][:, 256 + h * 128:384 + h * 128], O_['Am'][h][:, ksl], O_['Btm'][h][:, ksl], True, True, [opr], [('ps', 3)])
                    tt('dve', nm0, PA_[:], MAB, ALU.mult, [ra, 'CB'], [('NM0', m)])
                    tt('dve', PPp[0], PS[3][:, 256:512], MNm, ALU.mult, [('ps', 3), 'CB'], [ppr(0)])
                    tt('dve', nm1, PB_[:], MAB, ALU.mult, [rb, 'CB'], [('NM1', m)])
                G.append(g0)
                pt0 = nm0.rearrange("p (h s t) -> p h s t", h=2, s=2)[:, :, 0, :]

                def g1():
                    idb2 = bass.AP(IDB, 0, [[128, 128], [0, 2], [1, 128]])
                    tt('pool', LV[1][:, :, 1, :], pt0, idb2, ALU.add, [('NM0', m), 'IDB'], [lvt(1)])
                    for h in range(2):
                        mm(PA_[:, h * 128:(h + 1) * 128], nm0[:, h * 256:h * 256 + 128], PPp[0][:, h * 128:(h + 1) * 128], True, True,
                           [('NM0', m), ppr(0)], [ra])
                        mm(PB_[:, h * 256:h * 256 + 128], PPp[0][:, h * 128:(h + 1) * 128], nm0[:, h * 256:h * 256 + 128], True, True,
                           [('NM0', m), ppr(0)], [rb])
                    copy('act', PPp[1], PA_[:, 0:256], [ra], [ppr(1)])
                    copy('dve', LV[1][:, :, 0, :], PB_[:].rearrange("p (h s t) -> p h s t", h=2, s=2)[:, :, 0, :], [rb], [lvp(1)])
                G.append(g1)

                def lvl(kk_):
                    def g():
                        a, b = kk_ % 2, (kk_ + 1) % 2
                        psb = PB_[:].rearrange("p (h s t) -> p h s t", h=2, s=2)
                        for h in range(2):
                            pk = PPp[a][:, h * 128:(h + 1) * 128]
                            if kk_ < 5:
                                mm(PB_[:, h * 256:(h + 1) * 256], pk, LV[a][:, h, :, :].rearrange("p s t -> p (s t)"), True, True,
                                   [ppr(a), lvp(a), lvt(a)], [rb])
                            else:
                                mm(PB_[:, h * 256 + 128:(h + 1) * 256], pk, LV[a][:, h, 1, :], True, True, [ppr(a), lvt(a)], [rb])
                            mm(PA_[:, h * 128:(h + 1) * 128], LV[a][:, h, 0, :], pk, True, True, [ppr(a), lvp(a)], [ra])
                        copy('act', PPp[b], PA_[:, 0:256], [ra], [ppr(b)])
                        if kk_ < 5:
                            copy('dve', LV[b][:, :, 0, :], psb[:, :, 0, :], [rb], [lvp(b)])
                        tt('dve', LV[b][:, :, 1, :], psb[:, :, 1, :], LV[a][:, :, 1, :], ALU.add, [rb, lvt(a)], [lvt(b)])
                    return g
                for kk_ in range(1, 6):
                    G.append(lvl(kk_))

                def g7():
                    for h in range(2):
                        mm(PB_[:, h * 128:(h + 1) * 128], PPp[0][:, h * 128:(h + 1) * 128], LV[0][:, h, 1, :], True, True,
                           [ppr(0), lvt(0)], [rb])
                    tt('dve', TTf[q].rearrange("p (h t) -> p h t", h=2), PB_[:, 0:256].rearrange("p (h t) -> p h t", h=2),
                       LV[0][:, :, 1, :], ALU.add, [rb, lvt(0)], [('TTf', q)])
                G.append(g7)
                return G

            def b_groups(ci, jj=jj):
                bi, (z, blk, k) = ci // 4, chunks[ci]
                par, q, m = bi % 2, ci % 2, ci % 3
                O_ = OPN[par]
                opr = ('OPN', par)
                bkt, bkr = BKTs[par], ('BKT', par)
                c16 = blk * 4 + k
                ksl = slice(k * 128, (k + 1) * 128)
                ARk = O_['AR'][:, k * 256:(k + 1) * 256]
                nm0, nm1, TT = NM0[m], NM1[m], TTf[q]
                vt = lambda h: VTG[:, c16, h * 64:(h + 1) * 64]
                G = []

                def g0():
                    mm(PS[2][:, 0:128], ARk[:, 0:128], SBw, True, False, [opr, 'SBw'], [('ps', 2)])
                    for h in range(2):
                        mm(PS[2][:, hs(h)], nm1[:, h * 256:h * 256 + 128], vt(h), False, h == 1, [('NM1', m), 'VTG'], [('ps', 2)])
                    copy('act', Z0B, PS[2][:, 0:128], [('ps', 2)], ['Z0B'])
                G.append(g0)

                def g1():
                    for h in range(2):
                        mm(PS[2][:, 128 + h * 64:192 + h * 64], TT[:, h * 128:(h + 1) * 128], Z0B[:, hs(h)], True, True,
                           [('TTf', q), 'Z0B'], [('ps', 2)])
                    copy('act', UB, PS[2][:, 128:256], [('ps', 2)], ['UB'])
                G.append(g1)

                def g2():
                    mm(PS[3][:, 0:128], ARk[:, 128:256], SBw, True, False, [opr, 'SBw'], [('ps', 3)])
                    for h in range(2):
                        mm(PS[3][:, hs(h)], nm0[:, h * 256 + 128:h * 256 + 256], UB[:, hs(h)], False, False, [('NM0', m), 'UB'], [('ps', 3)])
                        mm(PS[3][:, hs(h)], nm1[:, h * 256 + 128:h * 256 + 256], vt(h), False, h == 1, [('NM1', m), 'VTG'], [('ps', 3)])
                    mm(PS[2][:, 256:384], bkt[:, 0, ksl], UB, True, False, [bkr, 'UB'], [('ps', 2)])
                    mm(PS[2][:, 256:384], bkt[:, 1, ksl], VTG[:, c16, 0:128], False, True, [bkr, 'VTG'], [('ps', 2)])
                G.append(g2)

                def g3():
                    tmpw = GS[:, 1 + (c16 % 2), 0:128]
                    tr = ('TMPw', c16 % 2)
                    stt(tmpw, SFw, WLW[:, z, c16:c16 + 1], PS[2][:, 256:384], ALU.mult, ALU.add, ['SFw', 'WLW', ('ps', 2)], [tr])
                    stt(SFw, tmpw, MSK[:, 2 + z, c16:c16 + 1], BDm, ALU.mult, ALU.mult, [tr, 'MSK', 'CF'], ['SFw'])
                    copy('act', SBw, SFw, ['SFw'], ['SBw'])
                    if (c16 % 2 == 1) == (z == 0):
                        P.dma('sp', lambda e, tmpw=tmpw, z=z, jj=jj, c16=c16: e.dma_start(out=o_sw[c16 // 2, z, jj], in_=tmpw), reads=[tr])
                    ofc = VTG[:, c16, 128:256]
                    vo = ('VO', c16)
                    if z == 0:
                        copy('act', ofc, PS[3][:, 0:128], [('ps', 3)], [vo])
                    else:
                        to, sqo = GS[:, 0, 128:256], GS[:, 1, 128:256]
                        h3 = lambda ap: ap.rearrange("p (g w) -> p g w", w=64)
                        gb = lambda off: bass.AP(GNS, off, [[8, 128], [1, 2], [0, 64]])
                        tt('dve', to, ofc, PS[3][:, 0:128], ALU.add, [('ps', 3), vo], ['TO'])
                        tt('dve', sqo, to, to, ALU.mult, ['TO'], ['SQO'])
                        both = bass.AP(GS, 128, [[768, 128], [256, 2], [64, 2], [1, 64]])
                        P.op('dve', lambda e: e.tensor_reduce(out=GNS[:, 0:4].rearrange("p (a b) -> p a b", a=2), in_=both,
                                                              axis=AX.X, op=ALU.add), reads=['TO', 'SQO'], writes=['GNS'])
                        tt('dve', GNS[:, 4:6], GNS[:, 0:2], GNS[:, 0:2], ALU.mult, ['GNS'], ['GNS'])
                        stt(GNS[:, 2:4], GNS[:, 2:4], 64.0, GNS[:, 4:6], ALU.mult, ALU.subtract, ['GNS'], ['GNS'])

                        def tail():
                            act(GNS[:, 0:2], GNS[:, 0:2], ACT.Copy, ['GNS'], ['GNS'], scale=1.0 / 64)
                            act(GNS[:, 2:4], GNS[:, 2:4], ACT.Ln, ['GNS', 'EPS'], ['GNS'], bias=EPS[:, 1:2], scale=1.0 / 4096)
                            act(GNS[:, 2:4], GNS[:, 2:4], ACT.Exp, ['GNS'], ['GNS'], scale=-0.5)
                            tt('pool', h3(to), h3(to), gb(0), ALU.subtract, ['TO', 'GNS'], ['TO'])
                            tt('pool', h3(ofc), h3(to), gb(2), ALU.mult, ['TO', 'GNS'], [vo])
                        DEFER[ci] = tail
                G.append(g3)
                return G

            NCK = 32
            DEFER = {}
            init_state(0)()
            AG = {0: a_groups(0), 1: a_groups(1)}
            PG = {0: prep_groups(0), 1: prep_groups(1)}
            lock = [(lambda i=i: (AG[0][i](), AG[1][i]() if i < 4 else None)) for i in range(8)]
            interleave([vz, PG[0] + lock])
            for ci in range(NCK):
                bg = b_groups(ci)
                if ci - 1 in DEFER:
                    bg = [DEFER.pop(ci - 1)] + bg
                if ci == 16:
                    bg = [init_state(1)] + bg
                lists = [bg]
                if ci + 1 < NCK:
                    lists.append(AG[ci + 1][4:])
                if ci + 2 < NCK:
                    AG[ci + 2] = a_groups(ci + 2)
                    lists.append(AG[ci + 2][:4])
                b_, r_ = ci // 4, ci % 4
                if r_ < 2 and b_ + 1 < 8:
                    if b_ + 1 not in PG:
                        PG[b_ + 1] = prep_groups(b_ + 1)
                    if b_ == 0:
                        lists.append(PG[1][6 * r_:6 * r_ + 6])
                    else:
                        lists.append(PG[b_ + 1][6 + 3 * r_:9 + 3 * r_])
                elif r_ >= 2 and b_ + 2 < 8:
                    if b_ + 2 not in PG:
                        PG[b_ + 2] = prep_groups(b_ + 2)
                    lists.append(PG[b_ + 2][3 * (r_ - 2):3 * (r_ - 2) + 3])
                if nrk is not None and ci in NRK_AT:
                    lists.append([nrk[i] for i in NRK_AT[ci]])
                interleave(lists)
            for k_ in sorted(DEFER):
                DEFER.pop(k_)()

    def pair_end(jj):
        vcol = lambda base: VEC[:, base + jj:base + jj + 1]
        G = []
        o1 = ('OPN', 1)

        def g_t(t4):
            def g():
                tsl = slice(t4 * 512, (t4 + 1) * 512)
                for k in range(4):
                    mm(PS[4][:, k * 128:(k + 1) * 128], VTG[:, t4 * 4 + k, 128:256], IDB[:], True, True,
                       [('VO', t4 * 4 + k), 'IDB'], [('ps', 4)])
                ts('dve', TB[0][:], PS[4][:], vcol(V_LW), vcol(V_LB), ALU.mult, ALU.add, [('ps', 4), 'VEC'], [('T', 0)])
                etb = ET[0].bitcast(BF16)[:, 0:512]
                act(etb, PRS[:, tsl], ACT.Copy, ['PRS', 'VEC'], [o1], scale=vcol(V_RK))
                mm(PS[5][:], BDB[:], etb, True, True, [o1, 'BDB'], [('ps', 5)])
                tt('dve', ET[1][:], PS[5][:], BA[0][:, tsl], ALU.mult, [('ps', 5), 'BA0'], [o1])
                tt('pool', TB[0][:], TB[0][:], ET[1][:], ALU.add, [('T', 0), o1], [('T', 0)])
                tt('pool', BA[1][:, tsl], TB[0][:], BA[1][:, tsl], ALU.mult, [('T', 0), 'BA1'], ['BA1'])
            return g
        for t4 in range(4):
            G.append(g_t(t4))

        def g_w():
            ld(WS[:].rearrange("p a b -> p (a b)"), d_ewout[:, 8 + jj, :], ['WS'])
            copy('pool', WOB[:], WS[:].rearrange("p a b -> p (a b)"), ['WS'], [('WB', 3)])

        def g_o(t4, half):
            def g():
                tsl = slice(t4 * 512, (t4 + 1) * 512)
                for ft in range(4 * half, 4 * half + 4):
                    pb = 2 + (cnt['rr'] % 4)
                    cnt['rr'] += 1
                    mm(PS[pb][:], WOB[:, ft * 128:(ft + 1) * 128], BA[1][:, tsl], True, True, [('WB', 3), 'BA1'], [('ps', pb)])
                    if (ft * 4 + t4) % 5 < 3:
                        stt(X[:, ft, tsl], PS[pb][:], MOD[:, 0, 16 + ft:17 + ft], X[:, ft, tsl], ALU.mult, ALU.add,
                            [('ps', pb), 'MOD', ('X', ft)], [('X', ft)])
                    else:
                        wt = WOT[ft % 2]
                        act(wt[:], PS[pb][:], ACT.Copy, [('ps', pb), 'MOD'], [('T', 2 + ft % 2)], scale=MOD[:, 0, 16 + ft:17 + ft])
                        tt('pool', X[:, ft, tsl], X[:, ft, tsl], wt[:], ALU.add, [('T', 2 + ft % 2), ('X', ft)], [('X', ft)])
            return g
        G = [g_w, G[0], G[1], g_o(0, 0), g_o(0, 1), G[2], g_o(1, 0), g_o(1, 1), G[3], g_o(2, 0), g_o(2, 1), g_o(3, 0), g_o(3, 1)]
        return G

    for g in start_rk(0):
        g()
    vz = start_vz(0)
    for jj in range(8):
        nrk = start_rk(jj + 1, banks=(4, 5), kb=4) if jj + 1 < 8 else None
        pair_chain(jj, vz, nrk)
        for g in pair_end(jj):
            g()
        if jj + 1 < 8:
            vz = start_vz(jj + 1)
    P.barrier()

    norm_mod(lambda c: G1[:, 1, c:c + 1], lambda c: MOD[:, 1, c:c + 1], lambda c, tsl: HT[:, c, tsl],
             lambda c, t4: [('HT', c)])
    ld(WS[:].rearrange("p a b -> p (a b)")[:, 0:256], d_g1.rearrange('p a b -> p (a b)'), ['WS'])
    copy('pool', G1B[:].rearrange('p a b -> p (a b)'), WS[:].rearrange("p a b -> p (a b)")[:, 0:256], ['WS'], ['G1B'])
    ld(WS[:].rearrange("p a b -> p (a b)")[0:32, :], d_g2.rearrange('p a b -> p (a b)'), ['WS'])
    copy('pool', G2B[:].rearrange('p a b -> p (a b)'), WS[:].rearrange("p a b -> p (a b)")[0:32, :], ['WS'], ['G2B'])
    ts('pool', NGB[:], VEC[:, V_GB:V_GB + 8], -1.0, None, ALU.mult, None, ['VEC'], ['NGB'])
    for t4 in range(4):
        tsl = slice(t4 * 512, (t4 + 1) * 512)
        for c in range(8):
            mm(PS[0][0:32, :], G1B[:, c, :], HT[:, c, tsl], c == 0, c == 7, ['G1B', ('HT', c)], [('ps', 0)])
        copy('act', GT1[:, tsl], PS[0][0:32, :], [('ps', 0)], ['GT1'])
    SP, CSg, CR0, INC, RST, EXg = T512[:6]
    g2 = ub(2048, 3328)
    GSET = [(BA[0][:, 0:512], BA[0][:, 512:1024], BA[0][:, 1024:1536], BA[1][:, 0:512],
             BA[1][:, 512:1024].rearrange('p (k t) -> p k t', k=4)),
            (g2[:, 0:512], g2[:, 512:1024], g2[:, 1024:1536], g2[:, 1536:2048],
             g2[:, 2048:2560].rearrange('p (k t) -> p k t', k=4))]
    GEX = [UNI[:, 4096 + 512 * i:4096 + 512 * (i + 1)] for i in range(3)]
    GLTf = KKF[:].bitcast(F32)
    GLT = [GLTf[:, 0:512], GLTf[:, 512:1024]]
    SBg = BA[1][:, 1024:1280]
    SFg = GS[:, 0, :]
    for hd in range(4):
        load_w(d_owin, hd * 128, 0)
        load_w(d_owin, 512 + hd * 128, 1)
        load_w(d_owin, 1024 + hd * 256, 2)
        load_w(d_owin, 1024 + hd * 256 + 128, 3)
        proj(0, lambda t4, tsl, ps, pr: copy(rr(), FA[:, tsl], ps, [pr], ['FA']))
        proj(1, lambda t4, tsl, ps, pr: copy(rr(), FB[:, tsl], ps, [pr], ['FB']))
        vgr = []
        for half in range(2):
            vgr.append(lambda half=half: proj(2 + half, lambda t4, tsl, ps, pr: copy(rr(), BA[2][:, tsl], ps, [pr], ['BA2'])))

            def vt(g, half=half):
                def f():
                    for k in range(4):
                        mm(PS[1][:, k * 128:(k + 1) * 128], BA[2][:, (4 * g + k) * 128:(4 * g + k + 1) * 128], IDB[:], True, True,
                           ['BA2', 'IDB'], [('ps', 1)])
                    copy('act', VTG[:, 4 * g:4 * g + 4, half * 128:(half + 1) * 128], PS[1][:].rearrange("p (k t) -> p k t", k=4),
                         [('ps', 1)], [('VTG', 4 * g + i) for i in range(4)])
                return f
            for g in range(4):
                vgr.append(vt(g))
        gblocks = [(0, b) for b in range(4)] + [(1, b) for b in range(3, -1, -1)]
        GDEF = []

        def g_init(z, hd=hd):
            def g():
                ld(SFg, d_sg[:, hd, z, :], ['SFg'])
                copy('pool', SBg, SFg, ['SFg'], ['SBg', 'BA1'])
            return g

        def g_prep(bi, hd=hd):
            z, blk = gblocks[bi]
            sb_ = bi % 2
            QE, KE, KD, KDT, ATT4 = GSET[sb_]
            rq, rk, rd, rt, ra_ = (('gQE', sb_), ('gKE', sb_), ('gKD', sb_), ('gKDT', sb_), ('gATT', sb_))
            MB = [(PS[0], 0), (PS[1], 1)] if sb_ == 0 else [(PS[2], 2), (PS[5], 5)]
            tsl = slice(blk * 512, (blk + 1) * 512)
            EA, EB, EC = GEX
            if z == 0:
                inc, rst, ri, rr_ = CSg, CR0, ('T', 1), ('T', 2)
            else:
                inc, rst, ri, rr_ = INC, RST, ('T', 3), 'WS'

            def w0():
                mm(PS[4][:], G2B[:, z, hd * 128:(hd + 1) * 128], GT1[:, tsl], True, True, ['G2B', 'GT1'], [('ps', 4)])

            def w1():
                act(EXg[:], PS[4][:], ACT.Exp, [('ps', 4), 'NGB'], ['WS'], bias=NGB[:, 4 * z + hd:4 * z + hd + 1], scale=-1.0)

            def w2():
                act(SP[:], EXg[:], ACT.Ln, ['WS', 'CF'], [('T', 0)], bias=CF[:, C_ONE:C_ONE + 1], scale=1.0)

            def w3():
                P.op('dve', lambda e: e.tensor_tensor_scan(out=CSg[:], data0=CB[:, B_RST:B_RST + 512], data1=SP[:],
                                                           initial=0.0, op0=ALU.mult, op1=ALU.add),
                     reads=[('T', 0), 'CB'], writes=[('T', 1)])

            def w4():
                totb = bass.AP(CSg, 127, [[512, 128], [128, 4], [0, 128]])
                cs3 = bass.AP(CSg, 0, [[512, 128], [128, 4], [1, 128]])
                cr3 = bass.AP(CR0, 0, [[512, 128], [128, 4], [1, 128]])
                tt('pool', cr3, totb, cs3, ALU.subtract, [('T', 1)], [('T', 2)])
                tot4 = bass.AP(CSg, 127, [[512, 128], [128, 4]])
                act(WLG[:, z, blk * 4:blk * 4 + 4], tot4, ACT.Exp, [('T', 1)], ['WLG'], scale=-1.0 / 16)
                if z == 1:
                    tt('pool', INC[:], CR0[:], SP[:], ALU.add, [('T', 2), ('T', 0)], [('T', 3)])
                    tt('pool', RST[:], CSg[:], SP[:], ALU.subtract, [('T', 1), ('T', 0)], ['WS'])

            def w5():
                act(EA, inc[:], ACT.Exp, [ri], [('gE', 0)], scale=-1.0 / 16)
                act(EB, inc[:], ACT.Exp, [ri], [('gE', 1)], scale=1.0 / 16)
                act(EC, rst[:], ACT.Exp, [rr_], [('gE', 2)], scale=-1.0 / 16)

            def w6():
                stt(QE, FA[:, tsl], 128 ** -0.5, EA, ALU.mult, ALU.mult, ['FA', ('gE', 0)], [rq])
                tt('dve', KE, FB[:, tsl], EB, ALU.mult, ['FB', ('gE', 1)], [rk])
                tt('pool', KD, FB[:, tsl], EC, ALU.mult, ['FB', ('gE', 2)], [rd])

            def w7():
                for k in range(4):
                    ksl = slice(k * 128, (k + 1) * 128)
                    mm(PS[6][:, ksl], KE[:, ksl], QE[:, ksl], True, True, [rk, rq], [('ps', 6)])
                for k in range(4):
                    mm(PS[4][:, k * 128:(k + 1) * 128], KD[:, k * 128:(k + 1) * 128], IDB[:], True, True, [rd, 'IDB'], [('ps', 4)])

            def w8():
                mg = bass.AP(CB, B_MG + 128 * z, [[NCB, 128], [0, 4], [1, 128]])
                tt('dve', ATT4, PS[6][:].rearrange("p (k t) -> p k t", k=4), mg, ALU.mult, [('ps', 6), 'CB'], [ra_])
                copy('act', KDT, PS[4][:], [('ps', 4)], [rt])

            def w9():
                for k in range(4):
                    ksl = slice(k * 128, (k + 1) * 128)
                    mb, mbn = MB[k // 2]
                    mm(mb[:, (k % 2) * 256:(k % 2 + 1) * 256], KDT[:, ksl], VTG[:, blk * 4 + k, :], True, True,
                       [rt, ('VTG', blk * 4 + k)], [('ps', mbn)])
            return [w0, w1, w2, w3, w4, w5, w6, w7, w8, w9]

        def g_chain(bi, hd=hd):
            z, blk = gblocks[bi]
            sb_ = bi % 2
            QE, KE, KD, KDT, ATT4 = GSET[sb_]
            rq, ra_ = ('gQE', sb_), ('gATT', sb_)
            MB = [(PS[0], 0), (PS[1], 1)] if sb_ == 0 else [(PS[2], 2), (PS[5], 5)]
            G = []

            def ch(k):
                def g():
                    while GDEF:
                        GDEF.pop(0)()
                    c16 = blk * 4 + k
                    ksl = slice(k * 128, (k + 1) * 128)
                    mb, mbn = MB[k // 2]
                    ob, obn = (PS[7], 7) if k % 2 == 0 else (PS[3], 3)
                    mm(ob[:, 0:256], ATT4[:, k, :], VTG[:, c16, :], True, False, [ra_, ('VTG', c16)], [('ps', obn)])
                    mm(ob[:, 0:256], QE[:, ksl], SBg, False, True, [rq, 'SBg'], [('ps', obn)])
                    tmpg = GS[:, 1 + (c16 % 2), :]
                    tr = ('TMPg', c16 % 2)
                    stt(tmpg, SFg, WLG[:, z, c16:c16 + 1], mb[:, (k % 2) * 256:(k % 2 + 1) * 256], ALU.mult, ALU.add,
                        ['SFg', 'WLG', ('ps', mbn)], [tr])
                    ts('dve', SFg, tmpg, MSK[:, 2 + z, c16:c16 + 1], None, ALU.mult, None, [tr, 'MSK'], ['SFg'])
                    act(SBg, tmpg, ACT.Copy, [tr, 'MSK'], ['SBg'], scale=MSK[:, 2 + z, c16:c16 + 1])
                    if (c16 % 2 == 1) == (z == 0):
                        P.dma('sp', lambda e, z=z, hd=hd, c16=c16, tmpg=tmpg: e.dma_start(out=o_sg[c16 // 2, z, hd], in_=tmpg), reads=[tr])
                    obf = BA[2 + c16 // 8][:, (c16 % 8) * 256:(c16 % 8 + 1) * 256]
                    obr = 'BA2' if c16 < 8 else 'BA3'
                    if z == 0:
                        copy('act', obf, ob[:, 0:256], [('ps', obn)], [obr])
                    else:
                        tog, sqg = GLT[c16 % 2][:, 0:256], GLT[c16 % 2][:, 256:512]
                        gr = ('GLT', c16 % 2)
                        tt('dve', tog, obf, ob[:, 0:256], ALU.add, [('ps', obn), obr], [gr])
                        tt('dve', sqg, tog, tog, ALU.mult, [gr], [gr])
                        P.op('dve', lambda e, sqg=sqg, c16=c16: e.tensor_reduce(out=RSG[:, c16:c16 + 1], in_=sqg, axis=AX.X, op=ALU.add),
                             reads=[gr], writes=[('RSG', c16)])

                        def tail(c16=c16, tog=tog, gr=gr):
                            act(RSG[:, c16:c16 + 1], RSG[:, c16:c16 + 1], ACT.Ln, [('RSG', c16), 'EPS'], [('RSG', c16)], bias=EPS[:, 0:1], scale=1.0 / 256)
                            act(RSG[:, c16:c16 + 1], RSG[:, c16:c16 + 1], ACT.Exp, [('RSG', c16)], [('RSG', c16)], scale=-0.5)
                            act(VTG[:, c16, :], tog, ACT.Copy, [gr, ('RSG', c16)], [('VTG', c16)], scale=RSG[:, c16:c16 + 1])
                        GDEF.append(tail)
                return g
            for k in (range(4) if z == 0 else range(3, -1, -1)):
                G.append(ch(k))
            return G

        p0_ = g_prep(0)
        interleave([vgr, [g_init(0)] + p0_[:9]])
        p0_[9]()
        for bi in range(8):
            cg = g_chain(bi)
            if bi == 4:
                cg = [g_init(1)] + cg
            lists = [cg]
            if bi + 1 < 8:
                lists.append(g_prep(bi + 1))
            interleave(lists)
        while GDEF:
            GDEF.pop(0)()
        def half_groups(half, hd=hd):
            slot, SZ, nSZ, TF, nTF, OBh, nOB, pT, nT = ((0, BA[2], 'BA2', FB, 'FB', BA[3], 'BA3', PS[1], 1) if half == 0 else
                                                         (1, BA[0], 'BA0', FA, 'FA', BA[1], 'BA1', PS[0], 0))
            G = []

            def ga():
                load_w(d_owin, 2048 + hd * 256 + half * 128, slot)
                proj(slot, lambda t4, tsl, ps, pr: act(SZ[:, tsl], ps, ACT.Silu, [pr], [nSZ]))
            G.append(ga)

            def gt(t4):
                def f():
                    tsl = slice(t4 * 512, (t4 + 1) * 512)
                    for k in range(4):
                        mm(pT[:, k * 128:(k + 1) * 128], VTG[:, t4 * 4 + k, half * 128:(half + 1) * 128], IDB[:], True, True,
                           [('VTG', t4 * 4 + k), 'IDB'], [('ps', nT)])
                    ts('dve', TF[:, tsl], pT[:], VEC[:, V_GN + half:V_GN + half + 1], None, ALU.mult, None, [('ps', nT), 'VEC'], [nTF])
                return f
            for t4 in range(4):
                G.append(gt(t4))

            def gm():
                tt('pool', OBh[:], TF[:], SZ[:], ALU.mult, [nTF, nSZ], [nOB])
            G.append(gm)
            G.append(lambda: wout_partial(1, d_owout, hd * 2 + half, OBh, nOB))
            return G
        interleave([half_groups(0), half_groups(1)])

    ftsl = [slice(t4 * 512, (t4 + 1) * 512) for t4 in range(4)]
    sumsq_rstd(ftsl[0], 0)
    for t4 in range(4):
        tsl = ftsl[t4]
        if t4 + 1 < 4:
            sumsq_rstd(ftsl[t4 + 1], (t4 + 1) % 2)
        rb_, rn_ = RSTD[t4 % 2]
        for c in range(8):
            stt(FA[:, (c % 4) * 512:(c % 4 + 1) * 512], X[:, c, tsl], VEC[:, V_FG + c:V_FG + c + 1], rb_[:],
                ALU.mult, ALU.mult, [('X', c), 'VEC', rn_], [('FAq', c % 4)])
            P.dma('sp', lambda e, c=c, tsl=tsl: e.dma_start(out=o_y[:, c, tsl], in_=FA[:, (c % 4) * 512:(c % 4 + 1) * 512]),
                  reads=[('FAq', c % 4)])
    P.finish_waits('sp')
    P.emit()
    global _LAST_P
    _LAST_P = P
    st.close()
    return nc


def kernel(**inp):
    f = lambda k: np.asarray(inp[k], np.float32)
    plan = _assign()
    cf, cb = _consts()
    vec0 = np.zeros((128, NV), np.float32)
    vec0[:, V_NG:V_NG + 16] = np.concatenate([_col(f('norm_g')[0]), _col(f('norm_g')[1])], 1)
    vec0[:, V_FG:V_FG + 8] = _col(f('final_g'))
    cw = f('conv_w')[0]
    for k in range(3):
        vec0[:, V_CW + 8 * k:V_CW + 8 * k + 8] = _col(cw[k])
    for z in range(2):
        vec0[:, V_W0 + 8 * z:V_W0 + 8 * z + 8] = _col(f('wkv_w0')[0, z])
        vec0[:, V_A0 + 8 * z:V_A0 + 8 * z + 8] = _col(f('wkv_a0')[0, z])
        vec0[:, V_GB + 4 * z:V_GB + 4 * z + 4] = _col(f('gla_gk_b')[0, z])
    vec0[:, V_KK:V_KK + 8] = _col(f('wkv_k_k')[0])
    vec0[:, V_KA:V_KA + 8] = _col(f('wkv_k_a')[0])
    vec0[:, V_RK:V_RK + 8] = _col(f('wkv_r_k')[0].reshape(-1))
    vec0[:, V_LW:V_LW + 8] = _col(f('wkv_ln_w')[0])
    vec0[:, V_LB:V_LB + 8] = _col(f('wkv_ln_b')[0])
    for l in range(2):
        vec0[:, V_AB + 24 * l:V_AB + 24 * l + 24] = _col(f('ada_b')[l])
    vec0[:, V_GN:V_GN + 2] = _col(f('gla_g_norm')[0])
    r3 = lambda w: np.ascontiguousarray(w.reshape(-1, 128, w.shape[-1]).transpose(1, 0, 2))
    ada = np.stack([r3(f('ada_w')[l]) for l in range(2)])
    ewin = r3(f('e_w_in')[0])
    ewout = r3(f('e_w_out')[0])
    owin = r3(f('o_w_in')[0])
    owout = r3(f('o_w_out')[0])
    w1c = np.concatenate([r3(f('wkv_w1')[0, 0]), r3(f('wkv_w1')[0, 1]), r3(f('wkv_a1')[0, 0]), r3(f('wkv_a1')[0, 1])], 2)
    w2p = np.zeros((128, 4, 1024), np.float32)
    for z in range(2):
        w2p[64 * z:64 * z + 64, z] = f('wkv_w2')[0, z]
        w2p[64 * z:64 * z + 64, 2 + z] = f('wkv_a2')[0, z]
    g1c = np.concatenate([r3(f('gla_gk1')[0, 0]), r3(f('gla_gk1')[0, 1])], 2)
    g2p = np.zeros((32, 2, 512), np.float32)
    for z in range(2):
        g2p[16 * z:16 * z + 16, z] = f('gla_gk2')[0, z]
    xp, xs = f('x_prompt'), f('x_sample')
    swkv, sgla = f('state_wkv'), f('state_gla')
    in_maps = []
    for core, items in enumerate(plan):
        x = np.zeros((NT, 1024), np.float32)
        msk = np.zeros((128, 4, 32), np.float32)
        s_w = np.zeros((128, 8, 2, 128), np.float32)
        s_g = np.zeros((128, 4, 2, 256), np.float32)
        vec = vec0.copy()
        if items[0][0] == 's':
            b = items[0][1]
            x[:] = xs[b]
            cv = f('c')[b]
            msk[:, 2, :16] = 1.0
            msk[:, 3, :16] = 1.0
            for z in range(2):
                for h in range(16):
                    jj, hl = divmod(h, 2)
                    s_w[64 * hl:64 * hl + 64, jj, z, 64 * hl:64 * hl + 64] = swkv[b, 0, z, h].T
                for h in range(4):
                    s_g[:, h, z, :] = sgla[b, 0, z, h]
        else:
            for si in range(8):
                x[256 * si:256 * si + 256] = xp[items[si % len(items)][1]]
            cv = f('c_ctx')
            g = np.arange(32)
            inner = (g % 4 != 0).astype(np.float32)
            msk[:, 0, :] = inner[None]
            msk[:, 1, :] = inner[None]
            c16 = np.arange(16)
            msk[:, 2, :16] = (c16 % 2 == 0)[None]
            msk[:, 3, :16] = (c16 % 2 == 1)[None]
        vec[:, V_CV:V_CV + 8] = _col(cv)
        xT = np.ascontiguousarray(x.reshape(NT, 8, 128).transpose(2, 1, 0))
        in_maps.append(dict(xT=xT, vec=vec, cf=cf, cb=cb.astype(ml_dtypes.bfloat16), msk=msk, ada=ada, ewin=ewin, ewout=ewout, w1c=w1c,
                            w2p=w2p, s_wkv=s_w, owin=owin, owout=owout, g1c=g1c, g2p=g2p, s_gla=s_g))
    nc = build()
    res = run_bass_kernel_spmd(nc, in_maps, core_ids=list(range(8)))
    y_p = np.zeros((16, 256, 1024), np.float32)
    y_s = np.zeros((2, 2048, 1024), np.float32)
    n_w = np.zeros((16, 1, 2, 16, 64, 64), np.float32)
    n_g = np.zeros((16, 1, 2, 4, 128, 256), np.float32)
    for core, items in enumerate(plan):
        r = res.results[core]
        y = np.asarray(r["yT"]).transpose(2, 1, 0).reshape(NT, 1024)
        ow = np.asarray(r["o_wkv"])
        og = np.asarray(r["o_gla"])
        if items[0][0] == 's':
            y_s[items[0][1]] = y
        else:
            for si, (_, pi) in enumerate(items):
                y_p[pi] = y[256 * si:256 * si + 256]
                for z in range(2):
                    for h in range(16):
                        jj, hl = divmod(h, 2)
                        n_w[pi, 0, z, h] = ow[si, z, jj, 64 * hl:64 * hl + 64, 64 * hl:64 * hl + 64].T
                    n_g[pi, 0, z] = og[si, z]
    return (y_p, y_s, n_w, n_g)
```
